# Optimizing a Trainium2 kernel written in Bass

```python
import math
import jax
import jax.numpy as jnp
from jax import lax
import numpy as np

D_MODEL = 1024
BATCH = 16
SEQ = 2048
DEPTH = 4

CTX_LEN = 256
GRID_W = 64
EPS = 1e-6

D_SSM = 512
SSM_GROUP = 16
N_SSM_GROUPS = D_SSM // SSM_GROUP
SSM_STATE = 64
D_MLSTM = 512
N_MLSTM_HEADS = 4
MLSTM_HEAD = D_MLSTM // N_MLSTM_HEADS
MLSTM_CHUNK = 128
QK_CONV = 3
N_DIFF_HEADS = 4
DIFF_QK_HEAD = 64
DIFF_V_HEAD = 2 * DIFF_QK_HEAD
D_DIFF = N_DIFF_HEADS * DIFF_V_HEAD
D_DIFF_QK = N_DIFF_HEADS * 2 * DIFF_QK_HEAD
Q_BLOCK = 128
ROPE_BASE = 10000.0

N_BRANCH = 3
IN_SPLITS = (D_SSM, D_SSM, 2 * D_MLSTM, D_MLSTM, D_MLSTM, D_MLSTM, 4 * N_MLSTM_HEADS,
             D_DIFF_QK, D_DIFF_QK, D_DIFF, D_DIFF, N_BRANCH * D_MODEL)
D_IN = 2 * D_SSM + 5 * D_MLSTM + 4 * N_MLSTM_HEADS + 2 * D_DIFF_QK + 2 * D_DIFF + N_BRANCH * D_MODEL

kernel_name = 'hybrid_s5_mlstm_diffattn_prefix_block'


def rmsnorm(x, g):
    xf = x.astype(jnp.float32)
    y = xf * lax.rsqrt(jnp.mean(xf * xf, axis=-1, keepdims=True) + EPS)
    return (y * g.astype(jnp.float32)).astype(x.dtype)


def split_cols(p):
    out, off = [], 0
    for w in IN_SPLITS:
        out.append(p[..., off:off + w])
        off += w
    return out


def axial_rope_tables(n_tok):
    rows = n_tok // GRID_W
    row = jnp.repeat(jnp.arange(rows), GRID_W).astype(jnp.float32)
    col = jnp.tile(jnp.arange(GRID_W), rows).astype(jnp.float32)
    half = DIFF_QK_HEAD // 2
    inv = jnp.power(ROPE_BASE, -jnp.arange(0, half, 2, dtype=jnp.float32) / half)
    ang_r = row[:, None] * inv
    ang_c = col[:, None] * inv
    shp = (n_tok, 1, 1, half // 2)
    return (jnp.cos(ang_r).reshape(shp), jnp.sin(ang_r).reshape(shp),
            jnp.cos(ang_c).reshape(shp), jnp.sin(ang_c).reshape(shp))


def rope_1d(x, cos, sin):
    x1, x2 = jnp.split(x, 2, axis=-1)
    return jnp.concatenate([x1 * cos - x2 * sin, x2 * cos + x1 * sin], axis=-1)


def apply_axial_rope(x, rope):
    cos_r, sin_r, cos_c, sin_c = rope
    xr, xc = jnp.split(x, 2, axis=-1)
    y = jnp.concatenate([rope_1d(xr, cos_r, sin_r), rope_1d(xc, cos_c, sin_c)], axis=-1)
    return y.astype(x.dtype)


def s5_discretize(lam_re, lam_im, log_step, b_re, b_im):
    f32 = jnp.float32
    lre = jnp.minimum(lam_re.astype(f32), -1e-4)
    lim = lam_im.astype(f32)
    dt = jnp.exp(log_step.astype(f32))[:, None]
    ld_re, ld_im = lre * dt, lim * dt
    mag = jnp.exp(ld_re)
    ab_re, ab_im = mag * jnp.cos(ld_im), mag * jnp.sin(ld_im)
    den = lre * lre + lim * lim
    num_re, num_im = ab_re - 1.0, ab_im
    coef_re = (num_re * lre + num_im * lim) / den
    coef_im = (num_im * lre - num_re * lim) / den
    bre, bim = b_re.astype(f32), b_im.astype(f32)
    bb_re = coef_re[..., None] * bre - coef_im[..., None] * bim
    bb_im = coef_re[..., None] * bim + coef_im[..., None] * bre
    return ld_re, ld_im, ab_re, ab_im, bb_re, bb_im


def _cmul(ar, ai, br, bi):
    return ar * br - ai * bi, ar * bi + ai * br


def _ssm_combine(e1, e2):
    a1r, a1i, b1r, b1i = e1
    a2r, a2i, b2r, b2i = e2
    ar, ai = _cmul(a2r, a2i, a1r, a1i)
    br, bi = _cmul(a2r, a2i, b1r, b1i)
    return ar, ai, br + b2r, bi + b2i


def s5_states(u, disc, h0):
    ld_re, ld_im, ab_re, ab_im, bb_re, bb_im = disc
    bu_re = jnp.einsum('blgn,gpn->blgp', u, bb_re)
    bu_im = jnp.einsum('blgn,gpn->blgp', u, bb_im)
    a_re = jnp.broadcast_to(ab_re, bu_re.shape)
    a_im = jnp.broadcast_to(ab_im, bu_im.shape)
    _, _, x_re, x_im = lax.associative_scan(_ssm_combine, (a_re, a_im, bu_re, bu_im), axis=1)
    if h0 is not None:
        t = jnp.arange(1, u.shape[1] + 1, dtype=jnp.float32)[:, None, None]
        mag = jnp.exp(t * ld_re)
        ph = t * ld_im
        add_re, add_im = _cmul(mag * jnp.cos(ph), mag * jnp.sin(ph), h0[0][:, None], h0[1][:, None])
        x_re = x_re + add_re
        x_im = x_im + add_im
    return x_re, x_im


def s5_readout(x_re, x_im, c_re, c_im):
    return jnp.einsum('blgp,gnp->blgn', x_re, c_re) - jnp.einsum('blgp,gnp->blgn', x_im, c_im)


def s5_branch(u_c, u_l, lam_re, lam_im, log_step, b_re, b_im, c_re, c_im, d_skip, glu_w, glu_b, with_ctx):
    f32 = jnp.float32
    G, N = N_SSM_GROUPS, SSM_GROUP
    B = u_l.shape[0]
    uc = u_c.astype(f32).reshape(B, -1, G, N)
    ul = u_l.astype(f32).reshape(B, -1, G, N)
    dg = d_skip.astype(f32).reshape(G, N)
    y_l = dg * ul
    y_c = dg * uc
    for d in range(2):
        disc = s5_discretize(lam_re[d], lam_im[d], log_step[d], b_re[d], b_im[d])
        cr, ci = c_re[d].astype(f32), c_im[d].astype(f32)
        uc_d = uc if d == 0 else jnp.flip(uc, 1)
        ul_d = ul if d == 0 else jnp.flip(ul, 1)
        xc_re, xc_im = s5_states(uc_d, disc, None)
        xl_re, xl_im = s5_states(ul_d, disc, (xc_re[:, -1], xc_im[:, -1]))
        yl_d = s5_readout(xl_re, xl_im, cr, ci)
        y_l = y_l + (yl_d if d == 0 else jnp.flip(yl_d, 1))
        if with_ctx:
            yc_d = s5_readout(xc_re, xc_im, cr, ci)
            y_c = y_c + (yc_d if d == 0 else jnp.flip(yc_d, 1))

    def glu(y):
        y = jax.nn.gelu(y.reshape(B, -1, D_SSM))
        a, g = jnp.split(y @ glu_w.astype(f32) + glu_b.astype(f32), 2, axis=-1)
        return (a * jax.nn.sigmoid(g)).astype(u_l.dtype)

    return (glu(y_c) if with_ctx else None), glu(y_l)


def dwconv_centred(x, w, b):
    pad = w.shape[0] // 2
    y = lax.conv_general_dilated(x, w[:, None, :].astype(x.dtype), (1,), [(pad, pad)],
                                 dimension_numbers=('NWC', 'WIO', 'NWC'),
                                 feature_group_count=x.shape[-1])
    return y + b.astype(x.dtype)


def mlstm_chunkwise(q, k, v, i_pre, f_pre, state0):
    f32 = jnp.float32
    B, L, H, dh = q.shape
    T = MLSTM_CHUNK
    nc = L // T
    q = q.astype(f32).reshape(B, nc, T, H, dh)
    k = (k.astype(f32) * dh ** -0.5).reshape(B, nc, T, H, dh)
    v = v.astype(f32).reshape(B, nc, T, H, dh)
    ig = i_pre.astype(f32).reshape(B, nc, T, H)
    F = jnp.cumsum(jax.nn.log_sigmoid(f_pre.astype(f32)).reshape(B, nc, T, H), axis=2)
    F_end = F[:, :, -1]
    w_end = F_end[:, :, None] - F + ig
    m_loc = jnp.max(w_end, axis=2)
    e_end = jnp.exp(w_end - m_loc[:, :, None])
    dC = jnp.einsum('bcth,bcthd,bcthe->bchde', e_end, k, v)
    dn = jnp.einsum('bcth,bcthd->bchd', e_end, k)

    def step(carry, inp):
        C, n, m = carry
        dC_c, dn_c, m_c, fe = inp
        m_new = jnp.maximum(fe + m, m_c)
        a = jnp.exp(fe + m - m_new)
        b = jnp.exp(m_c - m_new)
        C_new = a[..., None, None] * C + b[..., None, None] * dC_c
        n_new = a[..., None] * n + b[..., None] * dn_c
        return (C_new, n_new, m_new), (C, n, m)

    xs = tuple(jnp.moveaxis(t, 1, 0) for t in (dC, dn, m_loc, F_end))
    state0 = tuple(s.astype(f32) for s in state0)
    final, starts = lax.scan(step, state0, xs)
    C0, n0, m0 = (jnp.moveaxis(t, 0, 1) for t in starts)

    tri = jnp.tril(jnp.ones((T, T), dtype=bool))
    log_d = F[:, :, :, None, :] - F[:, :, None, :, :] + ig[:, :, None, :, :]
    log_d = jnp.where(tri[None, None, :, :, None], log_d, -jnp.inf)
    log_inter = F + m0[:, :, None]
    m_t = jnp.maximum(jnp.max(log_d, axis=3), log_inter)
    s_qk = jnp.einsum('bcthd,bcshd->bctsh', q, k) * jnp.exp(log_d - m_t[:, :, :, None])
    e_inter = jnp.exp(log_inter - m_t)
    num = (jnp.einsum('bctsh,bcshe->bcthe', s_qk, v)
           + e_inter[..., None] * jnp.einsum('bcthd,bchde->bcthe', q, C0))
    den = jnp.sum(s_qk, axis=3) + e_inter * jnp.einsum('bcthd,bchd->bcth', q, n0)
    h = num / jnp.maximum(jnp.abs(den), jnp.exp(-m_t))[..., None]
    return h.reshape(B, L, H, dh), final


def mlstm_branch(in_c, in_l, conv_w, conv_b, gate_b, norm_g, with_ctx):
    H, dh = N_MLSTM_HEADS, MLSTM_HEAD

    def prep(qk, v, g):
        B, L, _ = qk.shape
        qk = jax.nn.silu(dwconv_centred(qk, conv_w, conv_b))
        q, k = jnp.split(qk, 2, axis=-1)
        gates = g.reshape(B, L, 4, H) + gate_b.astype(g.dtype)
        return (q.reshape(B, L, H, dh), k.reshape(B, L, H, dh), v.reshape(B, L, H, dh), gates)

    def orient(seq, d):
        q, k, v, g = seq
        args = (q, k, v, g[:, :, 2 * d], g[:, :, 2 * d + 1])
        return tuple(jnp.flip(a, 1) for a in args) if d == 1 else args

    qk_c, v_c, o_c, g_c = in_c
    qk_l, v_l, o_l, g_l = in_l
    seq_c = prep(qk_c, v_c, g_c)
    seq_l = prep(qk_l, v_l, g_l)
    B = qk_l.shape[0]
    zero_state = (jnp.zeros((B, H, dh, dh), jnp.float32), jnp.zeros((B, H, dh), jnp.float32),
                  jnp.zeros((B, H), jnp.float32))
    h_c = 0.0
    h_l = 0.0
    for d in range(2):
        hc_d, st = mlstm_chunkwise(*orient(seq_c, d), zero_state)
        hl_d, _ = mlstm_chunkwise(*orient(seq_l, d), st)
        h_l = h_l + (hl_d if d == 0 else jnp.flip(hl_d, 1))
        h_c = h_c + (hc_d if d == 0 else jnp.flip(hc_d, 1))
    g_heads = norm_g.reshape(H, dh)

    def finish(h, o):
        h = rmsnorm(h, g_heads).reshape(h.shape[0], h.shape[1], D_MLSTM)
        return (h * jax.nn.sigmoid(o.astype(jnp.float32))).astype(o.dtype)

    return (finish(h_c, o_c) if with_ctx else None), finish(h_l, o_l)


def _diff_attend(q, k, v, lam):
    s = jnp.einsum('bqhcd,bkhcd->bchqk', q, k).astype(jnp.float32) * DIFF_QK_HEAD ** -0.5
    p = jax.nn.softmax(s, axis=-1)
    w = p[:, 0] - lam * p[:, 1]
    return jnp.einsum('bhqk,bkhd->bqhd', w.astype(v.dtype), v)


def diff_attn_branch(in_c, in_l, qn_g, kn_g, lam_p, subln_g, rope, layer_idx, with_ctx):
    H, dk, dv = N_DIFF_HEADS, DIFF_QK_HEAD, DIFF_V_HEAD

    def heads(q, k, v):
        B, L, _ = q.shape
        q = rmsnorm(q.reshape(B, L, H, 2, dk), qn_g)
        k = rmsnorm(k.reshape(B, L, H, 2, dk), kn_g)
        return q, k, v.reshape(B, L, H, dv)

    qc, kc, vc = heads(*in_c)
    ql, kl, vl = heads(*in_l)
    ql = apply_axial_rope(ql, rope)
    kl = apply_axial_rope(kl, rope)
    lam_init = 0.8 - 0.6 * math.exp(-0.3 * layer_idx)
    lp = lam_p.astype(jnp.float32)
    lam = jnp.exp(jnp.sum(lp[0] * lp[1])) - jnp.exp(jnp.sum(lp[2] * lp[3])) + lam_init

    B, L = ql.shape[0], ql.shape[1]
    keys = jnp.concatenate([kl, kc], axis=1)
    vals = jnp.concatenate([vl, vc], axis=1)
    nb = L // Q_BLOCK
    qb = jnp.moveaxis(ql.reshape(B, nb, Q_BLOCK, H, 2, dk), 1, 0)
    ol = lax.map(lambda qq: _diff_attend(qq, keys, vals, lam), qb)
    ol = jnp.moveaxis(ol, 0, 1).reshape(B, L, H, dv)

    def post(o):
        return (rmsnorm(o, subln_g) * (1.0 - lam_init)).reshape(o.shape[0], o.shape[1], D_DIFF)

    out_c = post(_diff_attend(qc, kc, vc, lam)) if with_ctx else None
    return out_c, post(ol)


def merge_branches(ys, zs, gate_logits, w_a, w_b, w_c, w_o):
    gates = jnp.split(jax.nn.sigmoid(gate_logits), N_BRANCH, axis=-1)
    m = 0.0
    for g, y, z, w in zip(gates, ys, zs, (w_a, w_b, w_c)):
        m = m + g * ((y * jax.nn.silu(z)) @ w)
    return m @ w_o


def setup_inputs(seed: int = 0) -> dict:
    key = jax.random.key(seed)
    ks = iter(jax.random.split(key, 40))
    f32 = jnp.float32

    def nrm(shape, s):
        return s * jax.random.normal(next(ks), shape, f32)

    L, G, P, N, H = DEPTH, N_SSM_GROUPS, SSM_STATE, SSM_GROUP, N_MLSTM_HEADS
    x = nrm((BATCH, SEQ, D_MODEL), 1.0)
    c = nrm((BATCH, D_MODEL), 1.0)
    ctx = nrm((BATCH, CTX_LEN, D_MODEL), 1.0)
    c_ctx = nrm((D_MODEL,), 1.0)
    norm_g = 1.0 + nrm((L, D_MODEL), 0.02)
    ada_w = nrm((L, D_MODEL, 3 * D_MODEL), 0.5 * D_MODEL ** -0.5)
    ada_b = nrm((L, 3 * D_MODEL), 0.02)
    w_in = nrm((L, D_MODEL, D_IN), D_MODEL ** -0.5)
    ssm_lam_re = -0.5 + nrm((L, 2, G, P), 0.01)
    ssm_lam_im = jnp.pi * jnp.arange(P, dtype=f32) + nrm((L, 2, G, P), 0.01)
    ssm_log_step = jax.random.uniform(next(ks), (L, 2, G), f32, math.log(1e-3), math.log(1e-1))
    ssm_b_re = nrm((L, 2, G, P, N), (2 * N) ** -0.5)
    ssm_b_im = nrm((L, 2, G, P, N), (2 * N) ** -0.5)
    ssm_c_re = nrm((L, 2, G, N, P), P ** -0.5)
    ssm_c_im = nrm((L, 2, G, N, P), P ** -0.5)
    ssm_d = nrm((L, D_SSM), 1.0)
    ssm_glu_w = nrm((L, D_SSM, 2 * D_SSM), D_SSM ** -0.5)
    ssm_glu_b = nrm((L, 2 * D_SSM), 0.02)
    w_ssm_out = nrm((L, D_SSM, D_MODEL), D_SSM ** -0.5)
    ml_conv_w = nrm((L, QK_CONV, 2 * D_MLSTM), QK_CONV ** -0.5)
    ml_conv_b = nrm((L, 2 * D_MLSTM), 0.02)
    fb = jnp.linspace(3.0, 6.0, H, dtype=f32)
    zb = jnp.zeros((H,), f32)
    ml_gate_b = jnp.stack([zb, fb, zb, fb])[None] + nrm((L, 4, H), 0.1)
    ml_norm_g = 1.0 + nrm((L, D_MLSTM), 0.02)
    w_ml_out = nrm((L, D_MLSTM, D_MODEL), D_MLSTM ** -0.5)
    da_qnorm_g = 1.0 + nrm((L, DIFF_QK_HEAD), 0.02)
    da_knorm_g = 1.0 + nrm((L, DIFF_QK_HEAD), 0.02)
    da_lambda = nrm((L, 4, DIFF_QK_HEAD), 0.1)
    da_subln_g = 1.0 + nrm((L, DIFF_V_HEAD), 0.02)
    w_da_out = nrm((L, D_DIFF, D_MODEL), D_DIFF ** -0.5)
    w_out = nrm((L, D_MODEL, D_MODEL), D_MODEL ** -0.5)
    return {'x': x, 'c': c, 'ctx': ctx, 'c_ctx': c_ctx, 'norm_g': norm_g, 'ada_w': ada_w,
            'ada_b': ada_b, 'w_in': w_in, 'ssm_lam_re': ssm_lam_re, 'ssm_lam_im': ssm_lam_im,
            'ssm_log_step': ssm_log_step, 'ssm_b_re': ssm_b_re, 'ssm_b_im': ssm_b_im,
            'ssm_c_re': ssm_c_re, 'ssm_c_im': ssm_c_im, 'ssm_d': ssm_d, 'ssm_glu_w': ssm_glu_w,
            'ssm_glu_b': ssm_glu_b, 'w_ssm_out': w_ssm_out, 'ml_conv_w': ml_conv_w,
            'ml_conv_b': ml_conv_b, 'ml_gate_b': ml_gate_b, 'ml_norm_g': ml_norm_g,
            'w_ml_out': w_ml_out, 'da_qnorm_g': da_qnorm_g, 'da_knorm_g': da_knorm_g,
            'da_lambda': da_lambda, 'da_subln_g': da_subln_g, 'w_da_out': w_da_out, 'w_out': w_out}


def reference(x, c, ctx, c_ctx, norm_g, ada_w, ada_b, w_in, ssm_lam_re, ssm_lam_im, ssm_log_step,
              ssm_b_re, ssm_b_im, ssm_c_re, ssm_c_im, ssm_d, ssm_glu_w, ssm_glu_b, w_ssm_out,
              ml_conv_w, ml_conv_b, ml_gate_b, ml_norm_g, w_ml_out, da_qnorm_g, da_knorm_g,
              da_lambda, da_subln_g, w_da_out, w_out):
    rope = axial_rope_tables(x.shape[1])
    c_act = jax.nn.silu(c)
    cc_act = jax.nn.silu(c_ctx)
    xl, xc = x, ctx
    for li in range(DEPTH):
        with_ctx = li < DEPTH - 1
        sh_l, sc_l, gt_l = jnp.split(c_act @ ada_w[li] + ada_b[li], 3, axis=-1)
        sh_c, sc_c, gt_c = jnp.split(cc_act @ ada_w[li] + ada_b[li], 3, axis=-1)
        h_l = rmsnorm(xl, norm_g[li]) * (1.0 + sc_l[:, None]) + sh_l[:, None]
        h_c = rmsnorm(xc, norm_g[li]) * (1.0 + sc_c) + sh_c
        (su_l, sz_l, mqk_l, mv_l, mo_l, mz_l, mg_l, dq_l, dk_l, dv_l, dz_l, gl_l) = split_cols(h_l @ w_in[li])
        (su_c, sz_c, mqk_c, mv_c, mo_c, mz_c, mg_c, dq_c, dk_c, dv_c, dz_c, gl_c) = split_cols(h_c @ w_in[li])

        ya_c, ya_l = s5_branch(su_c, su_l, ssm_lam_re[li], ssm_lam_im[li], ssm_log_step[li],
                               ssm_b_re[li], ssm_b_im[li], ssm_c_re[li], ssm_c_im[li], ssm_d[li],
                               ssm_glu_w[li], ssm_glu_b[li], with_ctx)
        yb_c, yb_l = mlstm_branch((mqk_c, mv_c, mo_c, mg_c), (mqk_l, mv_l, mo_l, mg_l),
                                  ml_conv_w[li], ml_conv_b[li], ml_gate_b[li], ml_norm_g[li], with_ctx)
        yc_c, yc_l = diff_attn_branch((dq_c, dk_c, dv_c), (dq_l, dk_l, dv_l), da_qnorm_g[li],
                                      da_knorm_g[li], da_lambda[li], da_subln_g[li], rope, li, with_ctx)

        out_l = merge_branches((ya_l, yb_l, yc_l), (sz_l, mz_l, dz_l), gl_l,
                               w_ssm_out[li], w_ml_out[li], w_da_out[li], w_out[li])
        xl = xl + gt_l[:, None] * out_l
        if with_ctx:
            out_c = merge_branches((ya_c, yb_c, yc_c), (sz_c, mz_c, dz_c), gl_c,
                                   w_ssm_out[li], w_ml_out[li], w_da_out[li], w_out[li])
            xc = xc + gt_c * out_c
    return xl
```

```python
import math
from contextlib import ExitStack

import numpy as np
import concourse.bass as bass
import concourse.mybir as mybir
from concourse.bass_utils import run_bass_kernel_spmd

F32 = mybir.dt.float32
BF16 = mybir.dt.bfloat16
AF = mybir.ActivationFunctionType
ALU = mybir.AluOpType
AX = mybir.AxisListType

D = 1024
LC = 256
LL = 2048
LT = LC + LL
NT = LT // 128
DEPTH = 4
EPS = 1e-6
NSEQ = 2
O_SU, O_SZ, O_MQK, O_MV, O_MO, O_MZ, O_MG, O_DQ, O_DK, O_DV, O_DZ, O_GL = (
    0, 512, 1024, 2048, 2560, 3072, 3584, 3600, 4112, 4624, 5136, 5648)
D_IN = 8720
TOKCH = [(0, 256), (256, 512), (768, 512), (1280, 512), (1792, 512)]
TWO_PI = 2.0 * math.pi


class _Op:
    __slots__ = ("eng", "fn", "deps", "dma", "ms", "sem", "val", "idx", "ep")


class Sched:
    ENGS = ("pe", "act", "dve", "pool", "sp")
    R = 6

    def __init__(self):
        self.ops = []
        self.tw = {}
        self.tr = {}
        self.last = {e: None for e in self.ENGS}
        self.pending_barrier = {e: set() for e in self.ENGS}
        self.dma_hist = {e: [] for e in self.ENGS}
        self.epoch = 0

    def add(self, eng, fn, reads=(), writes=(), dma=False):
        op = _Op()
        op.eng, op.fn, op.dma, op.ms, op.sem, op.val = eng, fn, dma, False, None, None
        op.idx = len(self.ops)
        op.ep = self.epoch
        xs = [r for r in reads if isinstance(r, str) and (r.startswith("ps") or r.startswith("acc"))]
        if xs:
            writes = list(writes) + [x for x in xs if x not in writes]
        deps = set()
        for r in reads:
            w = self.tw.get(r)
            if w is not None:
                deps.add(w)
        for wt in writes:
            w = self.tw.get(wt)
            if w is not None:
                deps.add(w)
            for rr in self.tr.get(wt, ()):
                deps.add(rr)
        deps |= self.pending_barrier[eng]
        self.pending_barrier[eng] = set()
        keep = set()
        rset = None
        for d in deps:
            o = self.ops[d]
            if o.eng == eng and not o.dma and not dma:
                if eng == "pe":
                    continue
            keep.add(d)
        op.deps = keep
        for r in reads:
            self.tr.setdefault(r, []).append(op.idx)
        for wt in writes:
            self.tw[wt] = op.idx
            self.tr[wt] = []
        self.ops.append(op)
        self.last[eng] = op.idx
        if dma:
            self.dma_hist[eng].append(op.idx)
        return op.idx

    def barrier(self):
        s = set()
        for e in self.ENGS:
            if self.last[e] is not None:
                s.add(self.last[e])
            for i in self.dma_hist[e][-self.R:]:
                s.add(i)
        for e in self.ENGS:
            self.pending_barrier[e] |= s

    def emit(self, nc, stack):
        ops = self.ops
        for op in ops:
            for d in op.deps:
                ops[d].ms = True
        neps = max(op.ep for op in ops) + 1
        csem = {(e, ep): stack.enter_context(nc.semaphore("c_%s%d" % (e, ep))) for e in self.ENGS for ep in range(neps)}
        dsem = {e: [stack.enter_context(nc.semaphore("d_%s%d" % (e, i))) for i in range(self.R)]
                for e in ("sp", "pool", "act")}
        ccount = {(e, ep): 0 for e in self.ENGS for ep in range(neps)}
        dcount = {e: 0 for e in self.ENGS}
        per_eng = {e: [] for e in self.ENGS}
        for op in ops:
            if op.dma:
                i = dcount[op.eng]
                op.sem = dsem[op.eng][i % self.R]
                op.val = 16 * (i // self.R + 1)
                dcount[op.eng] += 1
            elif op.ms:
                ccount[(op.eng, op.ep)] += 1
                op.sem = csem[(op.eng, op.ep)]
                op.val = ccount[(op.eng, op.ep)]
            per_eng[op.eng].append(op)
        self.stats = {e: (len(per_eng[e]), sum(ccount[(e, ep)] for ep in range(neps)), dcount[e]) for e in self.ENGS}
        R = self.R

        def replay(ename, e):
            seen = {}
            ndma = 0
            for op in per_eng[ename]:
                waits = {}
                for d in op.deps:
                    o = ops[d]
                    k = o.sem
                    if waits.get(k, (None, 0))[1] < o.val:
                        waits[k] = (o.sem, o.val)
                if op.dma:
                    if ndma >= R:
                        k = op.sem
                        v = op.val - 16
                        if waits.get(k, (None, 0))[1] < v:
                            waits[k] = (op.sem, v)
                    ndma += 1
                for k, (sem, val) in waits.items():
                    if seen.get(k, 0) < val:
                        e.wait_ge(sem, val)
                        seen[k] = val
                ins = op.fn(e)
                if op.dma:
                    ins.then_inc(op.sem, 16)
                elif op.ms:
                    ins.then_inc(op.sem, 1)
            if ename in dsem:
                n = dcount[ename]
                for j in range(min(n, R)):
                    cnt = (n - 1 - j) // R + 1
                    e.wait_ge(dsem[ename][j], 16 * cnt)

        with nc.Block() as block:
            @block.tensor
            def _(e):
                replay("pe", e)

            @block.scalar
            def _(e):
                replay("act", e)

            @block.vector
            def _(e):
                replay("dve", e)

            @block.gpsimd
            def _(e):
                replay("pool", e)

            @block.sync
            def _(e):
                replay("sp", e)


class LayerBuilder:
    def __init__(self, nc, S, stack, dbg=None):
        self.nc, self.S, self.stack = nc, S, stack
        self.dbg = dbg or {}
        self.uid = 0
        self.wq = 0

    def sb(self, name, shape, dt, stack=None):
        self.uid += 1
        return (stack or self.stack).enter_context(self.nc.sbuf_tensor("%s_%d" % (name, self.uid), shape, dt))

    def ps(self, name, shape, dt=F32, stack=None):
        self.uid += 1
        full = 512 if dt == F32 else 1024
        t = (stack or self.stack).enter_context(self.nc.psum_tensor("%s_%d" % (name, self.uid), [128, full], dt))
        n = 1
        for d in shape[1:]:
            n *= d
        assert n <= full, (name, shape)
        v = t[0:shape[0], 0:n]
        if len(shape) == 3:
            v = v.rearrange("p (a b) -> p a b", b=shape[2])
        return v

    def dma(self, out, in_, reads, writes, q=None, slow=False):
        if q is None:
            q = "sp"
        if slow:
            fn = lambda e, o=out, i=in_: e.dma_start(out=o, in_=i, allow_slow_non_contiguous=True)
        else:
            fn = lambda e, o=out, i=in_: e.dma_start(out=o, in_=i)
        return self.S.add(q, fn, reads, writes, dma=True)

    def mm(self, out, lhsT, rhs, start, stop, reads, writes, skip=False):
        if skip:
            fn = lambda e: e.matmul(out, lhsT, rhs, start=start, stop=stop, skip_group_check=True)
        else:
            fn = lambda e: e.matmul(out, lhsT, rhs, start=start, stop=stop)
        return self.S.add("pe", fn, reads, writes)

    def tr(self, out, in_, ident, reads, writes):
        return self.S.add("pe", lambda e: e.transpose(out, in_, ident), reads, writes)

    def act(self, out, in_, func, reads, writes, bias=None, scale=None, accum_out=None):
        kw = {}
        if bias is not None:
            kw["bias"] = bias
        if scale is not None:
            kw["scale"] = scale
        if accum_out is not None:
            kw["accum_out"] = accum_out
        return self.S.add("act", lambda e: e.activation(out=out, in_=in_, func=func, **kw), reads, writes)

    def tt(self, out, in0, in1, op, reads, writes, eng="dve"):
        return self.S.add(eng, lambda e: e.tensor_tensor(out=out, in0=in0, in1=in1, op=op), reads, writes)

    def ts(self, out, in0, s1, s2, op0, op1, reads, writes, eng="dve"):
        if op1 is None:
            fn = lambda e: e.tensor_scalar(out=out, in0=in0, scalar1=s1, scalar2=None, op0=op0)
        else:
            fn = lambda e: e.tensor_scalar(out=out, in0=in0, scalar1=s1, scalar2=s2, op0=op0, op1=op1)
        return self.S.add(eng, fn, reads, writes)

    def stt(self, out, in0, scalar, in1, op0, op1, reads, writes):
        return self.S.add("dve", lambda e: e.scalar_tensor_tensor(out=out, in0=in0, scalar=scalar, in1=in1,
                                                                  op0=op0, op1=op1), reads, writes)

    def cp(self, out, in_, reads, writes, eng="dve"):
        if eng == "act":
            return self.S.add("act", lambda e: e.activation(out=out, in_=in_, func=AF.Copy), reads, writes)
        return self.S.add(eng, lambda e: e.tensor_copy(out=out, in_=in_), reads, writes)

    def rsqrt(self, out, in_, scale, eps, reads, writes):
        et = self.epsT[eps]
        np_ = out.shape[0]
        self.act(out, in_, AF.Sqrt, list(reads) + [("epsT", eps)], writes, bias=et[0:np_, :], scale=scale)
        return self.S.add("dve", lambda e: e.reciprocal(out=out, in_=out), writes, writes)

    def rsqrt_le(self, out, in_, scale, eps, reads, writes):
        et = self.epsT[eps]
        np_ = out.shape[0]
        self.act(out, in_, AF.Ln, list(reads) + [("epsT", eps)], writes, bias=et[0:np_, :], scale=scale)
        return self.act(out, out, AF.Exp, writes, writes, scale=-0.5)

    def memset(self, ap, val, writes, eng="dve"):
        return self.S.add(eng, lambda e: e.memset(ap, val), (), writes)

    def setup_consts(self, cin):
        S = self.S
        self.identF = self.sb("identF", [128, 128], F32)
        self.identB = self.sb("identB", [128, 128], BF16)
        self.onesF = self.sb("onesF", [128, 128], F32)
        self.triL = self.sb("triL", [128, 128], BF16)
        self.triU = self.sb("triU", [128, 128], BF16)
        self.dma(self.identF[:], cin["ident"], (), ["identF"])
        self.dma(self.onesF[:], cin["ones"], (), ["onesF"])
        tmp = self.sb("ctmp", [128, 256], F32)
        self.dma(tmp[:, 0:128], cin["tril"], (), ["ctmp0"])
        self.dma(tmp[:, 128:256], cin["triu"], (), ["ctmp1"])
        self.triLF = tmp[:, 0:128]
        self.triUF = tmp[:, 128:256]
        self.epsT = {}
        t = self.sb("epsT", [128, 1], F32)
        self.memset(t[:], EPS, [("epsT", EPS)])
        self.epsT[EPS] = t
        self.cp(self.identB[:], self.identF[:], ["identF"], ["identB"])
        self.cp(self.triL[:], tmp[:, 0:128], ["ctmp0"], ["triL"])
        self.cp(self.triU[:], tmp[:, 128:256], ["ctmp1"], ["triU"])

    def load_w(self, name, w_dram, c0, ncols, KT, dst, dst_tok, stage, stage_tok, q="pool", cast_eng="pool"):
        src = w_dram.rearrange("(k p) c -> p k c", p=128)[:, :, c0:c0 + ncols]
        self.dma(stage[:, 0:KT, 0:ncols], src, [], [stage_tok], q=q)
        self.cp(dst[:, 0:KT, 0:ncols], stage[:, 0:KT, 0:ncols], [stage_tok], [dst_tok], eng=cast_eng)


    def stage_mod(self, W):
        S, nc = self.S, self.nc
        self.modT = self.sb("modT", [128, 24, 3], F32)
        self.A1 = self.sb("A1", [128, 8, 3], F32)
        with ExitStack() as st:
            cS = self.sb("cS", [128, 8, 3], F32, st)
            adab = self.sb("adab", [128, 24], F32, st)
            normg = self.sb("normg", [128, 8], F32, st)
            stage = self.sb("stgA", [128, 8, 512], F32, st)
            stage2 = self.sb("stgA2", [128, 8, 512], F32, st)
            psM = self.ps("psM", [128, 72], F32, st)
            for v in range(3):
                self.dma(cS[:, :, v], W["cvec"][v].rearrange("(k p) -> p k", p=128), [], ["cS"], slow=True)
            self.dma(adab[:], W["ada_b"].rearrange("(j p) -> p j", p=128), [], ["adab"], slow=True)
            self.dma(normg[:], W["norm_g"].rearrange("(k p) -> p k", p=128), [], ["normg"], slow=True)
            self.act(cS[:], cS[:], AF.Silu, ["cS"], ["cS"])
            stgs = [(stage, "stgA"), (stage2, "stgA2")]
            for ch in range(6):
                stg, tok = stgs[ch % 2]
                src = W["ada_w"].rearrange("(k p) c -> p k c", p=128)[:, :, ch * 512:(ch + 1) * 512]
                self.dma(stg[:], src, [], [tok], q="pool")
                for j in range(4):
                    jj = ch * 4 + j
                    for k in range(8):
                        self.mm(psM[:, 3 * jj:3 * jj + 3], stg[:, k, j * 128:(j + 1) * 128], cS[:, k, :],
                                k == 0, k == 7, [tok, "cS"], ["psM"])
            self.tt(self.modT[:], psM[:].rearrange("p (j v) -> p j v", v=3),
                    adab[:].unsqueeze(2).to_broadcast([128, 24, 3]), ALU.add, ["psM", "adab"], ["modT"])
            self.stt(self.A1[:], self.modT[:, 8:16, :], 1.0, normg[:].unsqueeze(2).to_broadcast([128, 8, 3]),
                     ALU.add, ALU.mult, ["modT", "normg"], ["A1"])
            S.barrier()

    def stage_norm(self, xT_s, s, xtok_in="xin"):
        S = self.S
        with ExitStack() as st:
            xin = [self.sb("xin", [128, 8, 512], F32, st) for _ in range(2)]
            sqt = self.sb("sqt", [128, 8, 512], F32, st)
            rs = self.sb("rs", [128, 512], F32, st)
            tmpn = [self.sb("tmpn", [128, 512], F32, st) for _ in range(2)]
            pss = [self.ps("pss", [128, 512], F32, st) for _ in range(2)]
            for ci, (t0, n) in enumerate(TOKCH):
                v = 2 if t0 == 0 else s
                xi = xin[ci % 2]
                xtok = "xin%d" % (ci % 2)
                ps = pss[ci % 2]
                pstok = "pss%d" % (ci % 2)
                self.dma(xi[:, :, 0:n], xT_s.rearrange("(k p) t -> p k t", p=128)[:, :, t0:t0 + n], [(xtok_in, s)], [xtok])
                self.act(sqt[:, :, 0:n], xi[:, :, 0:n], AF.Square, [xtok], ["sqt"])
                for k in range(8):
                    self.mm(ps[:, 0:n], self.onesF[:], sqt[:, k, 0:n], k == 0, k == 7, ["sqt", "onesF"], [pstok])
                self.rsqrt(rs[:, 0:n], ps[:, 0:n], 1.0 / D, EPS, [pstok], ["rs"])
                for k in range(8):
                    tm = tmpn[k % 2]
                    ttok = "tmpn%d" % (k % 2)
                    self.tt(tm[:, 0:n], xi[:, k, 0:n], rs[:, 0:n], ALU.mult, [xtok, "rs"], [ttok])
                    self.act(self.hT[:, k, t0:t0 + n], tm[:, 0:n], AF.Identity, [ttok, "A1", "modT"], [("hT", ci)],
                             bias=self.modT[:, k, v:v + 1], scale=self.A1[:, k, v:v + 1])
            S.barrier()

    def stage_merge(self, W, xT_s, xo_s, ybr_s, s, xtok_in="xin", xtok_out="xout"):
        S = self.S
        hT = self.hT
        with ExitStack() as st:
            yg = [self.sb("yg%d" % b, [128, 4, LT], BF16, st) for b in range(3)]
            mT = self.sb("mT", [128, 8, LT], BF16, st)
            stage = self.sb("stgF", [128, 8, 512], F32, st)
            wb = [self.sb("wbF", [128, 8, 512], BF16, st) for _ in range(2)]
            wb2 = [self.sb("wbG", [128, 8, 384], BF16, st), self.sb("wbG", [128, 4, 384], BF16, st)]
            ych = [self.sb("ych", [128, 512], F32, st) for _ in range(2)]
            szt = [self.sb("szt", [128, 512], F32, st) for _ in range(2)]
            sg = [self.sb("sg", [128, 512], F32, st) for _ in range(3)]
            mt = [self.sb("mt", [128, 512], F32, st) for _ in range(3)]
            psA = [self.ps("psA", [128, 512], F32, st) for _ in range(3)]
            psB = [self.ps("psB", [128, 512], F32, st) for _ in range(3)]
            allh = [("hT", i) for i in range(len(TOKCH))]
            zoff = [O_SZ, O_MZ, O_DZ]
            for b in range(3):
                w = wb[b % 2]
                wtok = "wbF%d" % (b % 2)
                self.load_w("wz", W["w_in"], zoff[b], 512, 8, w, wtok, stage, "stgF")
                for c in range(4):
                    for ci, (t0, n) in enumerate(TOKCH):
                        i = (c * 5 + ci) % 2
                        self.dma(ych[i][:, 0:n], ybr_s[b][c * 128:(c + 1) * 128, t0:t0 + n], [("ybr", s, b)], ["ych%d" % i])
                        ps = psA[i]
                        for k in range(8):
                            self.mm(ps[:, 0:n], w[:, k, c * 128:(c + 1) * 128], hT[:, k, t0:t0 + n], k == 0, k == 7,
                                    [wtok, ("hT", ci)], ["psA%d" % i])
                        self.act(szt[i][:, 0:n], ps[:, 0:n], AF.Silu, ["psA%d" % i], ["szt%d" % i])
                        self.tt(yg[b][:, c, t0:t0 + n], ych[i][:, 0:n], szt[i][:, 0:n], ALU.mult,
                                ["ych%d" % i, "szt%d" % i], [("yg", b, ci)])
            wouts = [W["w_ssm_out"], W["w_ml_out"], W["w_da_out"]]
            for f in range(8):
                if f % 2 == 0:
                    wg, wo, wgt, wot = wb[0], wb[1], "wbF0", "wbF1"
                else:
                    wg, wo, wgt, wot = wb2[0], wb2[1], "wbG0", "wbG1"
                for b in range(3):
                    src = W["w_in"].rearrange("(k p) c -> p k c", p=128)[:, :, O_GL + b * 1024 + f * 128:O_GL + b * 1024 + (f + 1) * 128]
                    self.dma(stage[:, :, b * 128:(b + 1) * 128], src, [], ["stgF"], q="pool")
                self.cp(wg[:, :, 0:384], stage[:, :, 0:384], ["stgF"], [wgt], eng="pool")
                for b in range(3):
                    src = wouts[b].rearrange("(k p) c -> p k c", p=128)[:, :, f * 128:(f + 1) * 128]
                    self.dma(stage[:, 0:4, b * 128:(b + 1) * 128], src, [], ["stgF"], q="pool")
                self.cp(wo[:, 0:4, 0:384], stage[:, 0:4, 0:384], ["stgF"], [wot], eng="pool")
                for ci, (t0, n) in enumerate(TOKCH):
                    for b in range(3):
                        for k in range(8):
                            self.mm(psA[b][:, 0:n], wg[:, k, b * 128:(b + 1) * 128], hT[:, k, t0:t0 + n], k == 0, k == 7,
                                    [wgt, ("hT", ci)], ["psA%d" % b])
                        for k in range(4):
                            self.mm(psB[b][:, 0:n], wo[:, k, b * 128:(b + 1) * 128], yg[b][:, k, t0:t0 + n], k == 0, k == 3,
                                    [wot, ("yg", b, ci)], ["psB%d" % b])
                    for b in range(3):
                        self.act(sg[b][:, 0:n], psA[b][:, 0:n], AF.Sigmoid, ["psA%d" % b], ["sg%d" % b])
                        self.tt(mt[b][:, 0:n], sg[b][:, 0:n], psB[b][:, 0:n], ALU.mult, ["sg%d" % b, "psB%d" % b], ["mt%d" % b])
                    self.tt(mt[0][:, 0:n], mt[0][:, 0:n], mt[1][:, 0:n], ALU.add, ["mt0", "mt1"], ["mt0"])
                    self.tt(mT[:, f, t0:t0 + n], mt[0][:, 0:n], mt[2][:, 0:n], ALU.add, ["mt0", "mt2"], [("mT", ci)])
            wo_full = [wb[0], wb[1]]
            for half in range(2):
                self.load_w("wo", W["w_out"], half * 512, 512, 8, wo_full[half], "wbF%d" % half, stage, "stgF")
            for fo in range(8):
                w = wo_full[fo // 4]
                wtok = "wbF%d" % (fo // 4)
                for ci, (t0, n) in enumerate(TOKCH):
                    v = 2 if t0 == 0 else s
                    i = (fo * 5 + ci) % 2
                    ps = psA[i]
                    for k in range(8):
                        self.mm(ps[:, 0:n], w[:, k, (fo % 4) * 128:(fo % 4 + 1) * 128], mT[:, k, t0:t0 + n], k == 0, k == 7,
                                [wtok, ("mT", ci)], ["psA%d" % i])
                    self.dma(ych[i][:, 0:n], xT_s[fo * 128:(fo + 1) * 128, t0:t0 + n], [(xtok_in, s)], ["ych%d" % i])
                    self.stt(szt[i][:, 0:n], ps[:, 0:n], self.modT[:, 16 + fo, v:v + 1], ych[i][:, 0:n], ALU.mult, ALU.add,
                             ["psA%d" % i, "ych%d" % i, "modT"], ["szt%d" % i])
                    self.dma(xo_s[fo * 128:(fo + 1) * 128, t0:t0 + n], szt[i][:, 0:n], ["szt%d" % i], [(xtok_out, s)])
            S.barrier()


    def stage_attn(self, W, cin, ybr_c, s, li, with_ctx=True):
        S = self.S
        hT = self.hT
        lam_init = 0.8 - 0.6 * math.exp(-0.3 * li)
        allh = [("hT", i) for i in range(len(TOKCH))]
        def hts(i):
            t = i * 128
            for ci, (t0, n) in enumerate(TOKCH):
                if t0 <= t < t0 + n:
                    return ("hT", ci)
        with ExitStack() as st:
            qT0 = self.sb("qT0", [128, 4, LT], BF16, st)
            qT1 = self.sb("qT1", [128, 4, LT], BF16, st)
            qTm = [qT0, qT1]
            self.memset(qT0[64:128, :, :], 0.0, ["qz0"], eng="pool")
            self.memset(qT1[0:64, :, :], 0.0, ["qz1"], eng="pool")
            kT = self.sb("kT", [128, 4, LT], BF16, st)
            vaug = self.sb("vaug", [128, NT, 4, 129], BF16, st)
            gq = self.sb("gq", [128, 64], F32, st)
            gk = self.sb("gk", [128, 64], F32, st)
            gsub = self.sb("gsub", [128, 128], F32, st)
            lamt = self.sb("lamt", [128, 256], F32, st)
            lprod = self.sb("lprod", [128, 2, 64], F32, st)
            lsum = self.sb("lsum", [128, 2], F32, st)
            neglam = self.sb("neglam", [128, 1], F32, st)
            self.dma(gq[:], W["da_qnorm_g"].rearrange("(o d) -> o d", o=1).to_broadcast([128, 64]), [], ["gq"])
            self.dma(gk[:], W["da_knorm_g"].rearrange("(o d) -> o d", o=1).to_broadcast([128, 64]), [], ["gk"])
            self.dma(gsub[:], W["da_subln_g"].rearrange("(o d) -> o d", o=1).to_broadcast([128, 128]), [], ["gsub"])
            self.dma(lamt[:], W["da_lambda"].rearrange("(o a) d -> o (a d)", o=1).to_broadcast([128, 256]), [], ["lamt"])
            self.ts(gq[:], gq[:], 0.125, None, ALU.mult, None, ["gq"], ["gq"])
            lamc = self.sb("lamc", [128, 2], F32, st)
            self.dma(lamc[:], cin["lamc"][li], [], ["lamc"])
            self.ts(gsub[:], gsub[:], lamc[:, 1:2], None, ALU.mult, None, ["gsub", "lamc"], ["gsub"])
            lv = lamt[:].rearrange("p (a b d) -> p a b d", a=2, b=2)
            self.tt(lprod[:], lv[:, :, 0, :], lv[:, :, 1, :], ALU.mult, ["lamt"], ["lprod"])
            self.S.add("dve", lambda e: e.tensor_reduce(out=lsum[:], in_=lprod[:], axis=AX.X, op=ALU.add), ["lprod"], ["lsum"])
            self.act(lsum[:], lsum[:], AF.Exp, ["lsum"], ["lsum"])
            self.tt(neglam[:], lsum[:, 1:2], lsum[:, 0:1], ALU.subtract, ["lsum"], ["neglam"])
            self.ts(neglam[:], neglam[:], lamc[:, 0:1], None, ALU.add, None, ["neglam", "lamc"], ["neglam"])
            self.memset(vaug[:, :, :, 128:129], 1.0, ["vones"])
            with ExitStack() as st2:
                stage = self.sb("stgE", [128, 8, 512], F32, st2)
                wq = self.sb("wq", [128, 8, 512], BF16, st2)
                wk = self.sb("wk", [128, 8, 512], BF16, st2)
                wv = self.sb("wv", [128, 8, 512], BF16, st2)
                sqf = self.sb("sqf", [128, 512], F32, st2)
                xraw = self.sb("xraw", [128, 512], F32, st2)
                ss = self.sb("ss8", [128, 8], F32, st2)
                xn = self.sb("xn", [128, 512], F32, st2)
                t1 = self.sb("t1", [128, 512], F32, st2)
                t2 = self.sb("t2", [128, 512], F32, st2)
                xb = [self.sb("xb", [128, 512], BF16, st2) for _ in range(2)]
                ropeC = self.sb("ropeC", [128, 16, 64], F32, st2)
                ropeS = self.sb("ropeS", [128, 16, 64], F32, st2)
                psq = self.ps("psq", [128, 512], F32, st2)
                psk = self.ps("psk", [128, 512], F32, st2)
                psv = self.ps("psv", [128, 512], F32, st2)
                pst = [self.ps("pstE", [128, 512], BF16, st2) for _ in range(2)]
                self.dma(ropeC[:], cin["ropeC"], [], ["ropeC"])
                self.dma(ropeS[:], cin["ropeS"], [], ["ropeS"])
                self.load_w("wq", W["w_in"], O_DQ, 512, 8, wq, "wq", stage, "stgE")
                self.load_w("wk", W["w_in"], O_DK, 512, 8, wk, "wk", stage, "stgE")
                self.load_w("wv", W["w_in"], O_DV, 512, 8, wv, "wv", stage, "stgE")
                for i in range(NT):
                    tsl = slice(i * 128, (i + 1) * 128)
                    ht = hts(i)
                    for (w, wt, ps, pt) in ((wq, "wq", psq, "psq"), (wk, "wk", psk, "psk"), (wv, "wv", psv, "psv")):
                        for k in range(8):
                            self.mm(ps[:], hT[:, k, tsl], w[:, k, :], k == 0, k == 7, [ht, wt], [pt])
                    self.cp(vaug[:, i, :, 0:128], psv[:].rearrange("p (h d) -> p h d", h=4), ["psv"], [("vaug", i)], eng="act")
                    for qi, (ps, pt, g, gt, dst, dt) in enumerate(((psq, "psq", gq, "gq", None, "qT"), (psk, "psk", gk, "gk", kT, "kT"))):
                        self.cp(xraw[:], ps[:], [pt], ["xraw"], eng="act")
                        self.tt(sqf[:], xraw[:], xraw[:], ALU.mult, ["xraw"], ["sqf"])
                        self.S.add("dve", lambda e, : e.tensor_reduce(out=ss[:], in_=sqf[:].rearrange("p (g d) -> p g d", d=64),
                                                                     axis=AX.X, op=ALU.add), ["sqf"], ["ss8"])
                        self.rsqrt_le(ss[:], ss[:], 1.0 / 64, EPS, ["ss8"], ["ss8"])
                        self.tt(xn[:].rearrange("p (g d) -> p g d", d=64), xraw[:].rearrange("p (g d) -> p g d", d=64),
                                ss[:].unsqueeze(2).to_broadcast([128, 8, 64]), ALU.mult, ["xraw", "ss8"], ["xn"])
                        x_b = xb[qi]
                        xbt = "xb%d" % qi
                        if i >= 2:
                            self.tt(xn[:].rearrange("p (g d) -> p g d", d=64), xn[:].rearrange("p (g d) -> p g d", d=64),
                                    g[:].unsqueeze(1).to_broadcast([128, 8, 64]), ALU.mult, ["xn", gt], ["xn"])
                            lt = i - 2
                            self.tt(t1[:].rearrange("p (g d) -> p g d", d=64), xn[:].rearrange("p (g d) -> p g d", d=64),
                                    ropeC[:, lt, :].unsqueeze(1).to_broadcast([128, 8, 64]), ALU.mult, ["xn", "ropeC"], ["t1"])
                            xv = xn[:].rearrange("p (g r h d) -> p g r h d", g=8, r=2, h=2)
                            tv = t2[:].rearrange("p (g r h d) -> p g r h d", g=8, r=2, h=2)
                            sv = ropeS[:, lt, :].rearrange("p (r h d) -> p r h d", r=2, h=2)
                            self.tt(tv[:, :, :, 0, :], xv[:, :, :, 1, :], sv[:, :, 0, :].unsqueeze(1).to_broadcast([128, 8, 2, 16]),
                                    ALU.mult, ["xn", "ropeS"], ["t2"])
                            self.tt(tv[:, :, :, 1, :], xv[:, :, :, 0, :], sv[:, :, 1, :].unsqueeze(1).to_broadcast([128, 8, 2, 16]),
                                    ALU.mult, ["xn", "ropeS"], ["t2"])
                            self.tt(x_b[:], t1[:], t2[:], ALU.add, ["t1", "t2"], [xbt])
                        else:
                            self.tt(x_b[:].rearrange("p (g d) -> p g d", d=64), xn[:].rearrange("p (g d) -> p g d", d=64),
                                    g[:].unsqueeze(1).to_broadcast([128, 8, 64]), ALU.mult, ["xn", gt], [xbt])
                        pp = pst[qi]
                        ppt = "pstE%d" % qi
                        for h in range(4):
                            self.tr(pp[:, h * 128:(h + 1) * 128], x_b[:, h * 128:(h + 1) * 128], self.identB[:], [xbt, "identB"], [ppt])
                        if dst is None:
                            pv = pp[:].rearrange("p (h t) -> p h t", h=4)
                            self.cp(qT0[0:64, :, tsl], pv[0:64], [ppt], [("qT", i, 0)], eng="act")
                            self.cp(qT1[64:128, :, tsl], pv[64:128], [ppt], [("qT", i, 1)], eng="act")
                        else:
                            self.cp(dst[:, :, tsl], pp[:].rearrange("p (h t) -> p h t", h=4), [ppt], [(dt, i)], eng="act")
                S.barrier()
            with ExitStack() as st3:
                pT = [self.sb("pT", [128, NT, 512], BF16, st3) for _ in range(2)]
                o = [self.sb("oE", [128, 128], F32, st3) for _ in range(2)]
                osq = self.sb("osq", [128, 128], F32, st3)
                ob = [self.sb("obE", [128, 128], BF16, st3) for _ in range(2)]
                rec = self.sb("recE", [128, 4], F32, st3)
                yst = [self.sb("ystE", [128, 512], F32, st3) for _ in range(2)]
                pss = [self.ps("pssE", [128, 512], F32, st3) for _ in range(3)]
                acc = [self.ps("accE", [128, 129], F32, st3) for _ in range(2)]
                pso = [self.ps("psoE", [128, 512], BF16, st3) for _ in range(2)]
                qchunks = [(256 + 512 * j, 512, list(range(NT))) for j in range(4)]
                if with_ctx:
                    qchunks.append((0, 256, [0, 1]))
                cnt = 0
                ycnt = 0
                for h in range(4):
                    for (t0, n, keys) in qchunks:
                        qtoks0 = [[("qT", (t0 // 128) + j, m_) for j in range(n // 128)] + ["qz%d" % m_] for m_ in range(2)]
                        for m in range(2):
                            rows = slice(m * 64, (m + 1) * 64)
                            for kt in keys:
                                ps = pss[cnt % 3]
                                pstk = "pssE%d" % (cnt % 3)
                                cnt += 1
                                self.mm(ps[:, 0:n], kT[:, h, kt * 128:(kt + 1) * 128], qTm[m][:, h, t0:t0 + n], True, True,
                                        [("kT", kt)] + qtoks0[m], [pstk])
                                self.act(pT[m][:, kt, 0:n], ps[:, 0:n], AF.Exp, [pstk], [("pT", m, kt)])
                        ys = yst[ycnt % 2]
                        yt = "ystE%d" % (ycnt % 2)
                        pso_ = pso[ycnt % 2]
                        psot = "psoE%d" % (ycnt % 2)
                        ycnt += 1
                        for qs in range(n // 128):
                            oo = o[qs % 2]
                            ot = "oE%d" % (qs % 2)
                            for m in range(2):
                                a = acc[m]
                                at = "accE%d" % m
                                for j, kt in enumerate(keys):
                                    self.mm(a[:], pT[m][:, kt, qs * 128:(qs + 1) * 128], vaug[:, kt, h, :], j == 0, j == len(keys) - 1,
                                            [("pT", m, kt), ("vaug", kt), "vones"], [at])
                                self.S.add("dve", lambda e, a=a, m=m: e.reciprocal(out=rec[:, m:m + 1], in_=a[:, 128:129]), [at], [("rec", m)])
                                if m == 0:
                                    self.ts(oo[:], a[:, 0:128], rec[:, 0:1], None, ALU.mult, None, [at, ("rec", 0)], [ot])
                                else:
                                    self.tt(rec[:, 2:3], rec[:, 1:2], neglam[:], ALU.mult, [("rec", 1), "neglam"], [("rec", 2)])
                                    self.stt(oo[:], a[:, 0:128], rec[:, 2:3], oo[:], ALU.mult, ALU.add, [at, ("rec", 2), ot], [ot])
                            self.tt(osq[:], oo[:], oo[:], ALU.mult, [ot], ["osq"])
                            S.add("dve", lambda e, : e.tensor_reduce(out=rec[:, 3:4], in_=osq[:], axis=AX.X, op=ALU.add), ["osq"], [("rec", 3)])
                            self.rsqrt_le(rec[:, 3:4], rec[:, 3:4], 1.0 / 128, EPS, [("rec", 3)], [("rec", 3)])
                            obb = ob[qs % 2]
                            obt = "obE%d" % (qs % 2)
                            self.stt(obb[:], oo[:], rec[:, 3:4], gsub[:], ALU.mult, ALU.mult, [ot, ("rec", 3), "gsub"], [obt])
                            self.tr(pso_[:, qs * 128:(qs + 1) * 128], obb[:], self.identB[:], [obt, "identB"], [psot])
                        self.cp(ys[:, 0:n], pso_[:, 0:n], [psot], [yt], eng="act")
                        self.dma(ybr_c[h * 128:(h + 1) * 128, t0:t0 + n], ys[:, 0:n], [yt], [("ybr", s, 2)])
                S.barrier()
            S.barrier()


    def stage_mlstm(self, W, cin, ybr_b, s, li):
        S = self.S
        hT = self.hT
        def hts(i):
            t = i * 128
            for ci, (t0, n) in enumerate(TOKCH):
                if t0 <= t < t0 + n:
                    return ("hT", ci)
        order = [list(range(NT)), [1, 0] + list(range(NT - 1, 1, -1))]
        with ExitStack() as st:
            tokS = [self.sb("tokS", [128, NT, 12], F32, st) for _ in range(2)]
            decB = [self.sb("decB", [128, NT, 4], F32, st) for _ in range(2)]
            cw = self.sb("cw", [128, 8, 3], F32, st)
            cb = self.sb("cb", [128, 8], F32, st)
            gml = self.sb("gml", [128, 512], F32, st)
            id4 = self.identF[0:4, 0:4]
            stgH = [self.sb("stgD", [128, 8, 512], F32, st) for _ in range(2)]
            wbH = [self.sb("wbD", [128, 8, 512], BF16, st) for _ in range(2)]

            def load_head(hh):
                cols_ = [O_MQK + hh * 128, O_MQK + 512 + hh * 128, O_MV + hh * 128, O_MO + hh * 128]
                for j_, c0_ in enumerate(cols_):
                    self.dma(stgH[hh % 2][:, :, j_ * 128:(j_ + 1) * 128], W["w_in"].rearrange("(k p) c -> p k c", p=128)[:, :, c0_:c0_ + 128],
                             [], ["stgD%d" % (hh % 2)], q="pool")
                self.cp(wbH[hh % 2][:], stgH[hh % 2][:], ["stgD%d" % (hh % 2)], ["wbD%d" % (hh % 2)], eng="pool")
            load_head(0)
            for j in range(3):
                self.dma(cw[:, :, j], W["ml_conv_w"][j].rearrange("(f p) -> p f", p=128), [], ["cw"], slow=True)
            self.dma(cb[:], W["ml_conv_b"].rearrange("(f p) -> p f", p=128), [], ["cb"], slow=True)
            self.dma(gml[:], W["ml_norm_g"].rearrange("(o d) -> o d", o=1).to_broadcast([128, 512]), [], ["gml"])
            with ExitStack() as st2:
                T = [self.sb("gT", [4, LT], F32, st2) for _ in range(4)]
                ones4 = self.sb("ones4", [4, 1], F32, st2)
                gb = self.sb("gb", [4, 4], F32, st2)
                mend = self.sb("mend", [4, NT], F32, st2)
                dec = self.sb("dec", [4, NT], F32, st2)
                ddg = self.sb("ddg", [4, NT, 4], F32, st2)
                stage = self.sb("stgG", [128, 8, 16], F32, st2)
                wg = self.sb("wg", [128, 8, 16], BF16, st2)
                psg = [self.ps("psg", [4, 512], F32, st2) for _ in range(2)]
                pstk = self.ps("pstk", [128, NT, 12], F32, st2)
                psd = self.ps("psd", [128, NT * 4], F32, st2)
                self.memset(ones4[:], 1.0, ["ones4"])
                self.dma(gb[:], W["ml_gate_b"].rearrange("a h -> h a"), [], ["gb"], slow=True)
                self.dma(stage[:], W["w_in"].rearrange("(k p) c -> p k c", p=128)[:, :, O_MG:O_MG + 16], [], ["stgG"], q="pool")
                self.cp(wg[:], stage[:], ["stgG"], ["wg"], eng="pool")
                onesb = ones4[:].to_broadcast([4, LT])
                for d in range(2):
                    Ti, Tf, Tg, Tm = T
                    for typ, dst, dtok in ((2 * d, Ti, "gT0"), (2 * d + 1, Tf, "gT1")):
                        for ci, (t0, n) in enumerate(TOKCH):
                            ps = psg[ci % 2]
                            pt = "psg%d" % (ci % 2)
                            for k in range(8):
                                self.mm(ps[:, 0:n], wg[:, k, typ * 4:(typ + 1) * 4], hT[:, k, t0:t0 + n], k == 0, k == 7,
                                        ["wg", ("hT", ci)], [pt])
                            if d == 0:
                                o_ap = dst[:, t0:t0 + n]
                            else:
                                if t0 == 0:
                                    o_ap = dst[:, 255::-1] if True else None
                                else:
                                    hi = 2559 - t0
                                    lo = hi - n
                                    o_ap = dst[:, hi:lo:-1]
                            self.ts(o_ap, ps[:, 0:n], gb[:, typ:typ + 1], None, ALU.add, None, [pt, "gb"], [dtok])
                    self.act(Tf[:], Tf[:], AF.Sigmoid, ["gT1"], ["gT1"])
                    self.act(Tf[:], Tf[:], AF.Ln, ["gT1"], ["gT1"])
                    S.add("dve", lambda e, Tg=Tg, Tf=Tf: e.tensor_tensor_scan(out=Tg[:], data0=onesb, data1=Tf[:], initial=0.0,
                                                                             op0=ALU.mult, op1=ALU.add), ["gT1", "ones4"], ["gT2"])
                    self.tt(Ti[:], Ti[:], Tg[:], ALU.subtract, ["gT0", "gT2"], ["gT0"])
                    S.add("dve", lambda e, Tm=Tm, Ti=Ti: e.tensor_tensor_scan(out=Tm[:], data0=onesb, data1=Ti[:], initial=0.0,
                                                                             op0=ALU.mult, op1=ALU.max), ["gT0", "ones4"], ["gT3"])
                    self.cp(mend[:], Tm[:, 127::128], ["gT3"], ["mend"])
                    self.ts(dec[:, 0:1], mend[:, 0:1], -1.0, None, ALU.mult, None, ["mend"], ["dec"])
                    self.tt(dec[:, 1:NT], mend[:, 0:NT - 1], mend[:, 1:NT], ALU.subtract, ["mend"], ["dec"])
                    self.act(dec[:], dec[:], AF.Exp, ["dec"], ["dec"])
                    self.tt(Tf[:], Tg[:], Tm[:], ALU.add, ["gT2", "gT3"], ["gT1"])
                    self.act(Tf[:], Tf[:], AF.Exp, ["gT1"], ["gT1"], scale=-1.0)
                    mb = mend[:].unsqueeze(2).to_broadcast([4, NT, 128])
                    self.tt(Tg[:].rearrange("p (c j) -> p c j", j=128), Ti[:].rearrange("p (c j) -> p c j", j=128), mb, ALU.subtract,
                            ["gT0", "mend"], ["gT2"])
                    self.act(Tg[:], Tg[:], AF.Exp, ["gT2"], ["gT2"])
                    self.tt(Ti[:].rearrange("p (c j) -> p c j", j=128), mb, Tm[:].rearrange("p (c j) -> p c j", j=128), ALU.subtract,
                            ["gT3", "mend"], ["gT0"])
                    self.act(Ti[:], Ti[:], AF.Exp, ["gT0"], ["gT0"])
                    U, Rr, FL = Tg, Ti, Tf
                    ut, rt, ft = "gT2", "gT0", "gT1"
                    if d == 1:
                        def rev(dst, src, st_, dt_):
                            self.cp(dst[:, 0:256], src[:, 255::-1], [st_], [dt_])
                            self.cp(dst[:, 256:LT], src[:, LT - 1:255:-1], [st_], [dt_])
                        rev(Tm, U, "gT2", "gT3")
                        rev(Tg, Rr, "gT0", "gT2")
                        rev(Ti, FL, "gT1", "gT0")
                        U, Rr, FL = Tm, Tg, Ti
                        ut, rt, ft = "gT3", "gT2", "gT0"
                    for mc in range(NT):
                        for qi, (src, stok) in enumerate(((U, ut), (Rr, rt), (FL, ft))):
                            self.mm(pstk[:, mc, qi * 4:(qi + 1) * 4], src[:, mc * 128:(mc + 1) * 128], id4, True, True,
                                    [stok, "identF"], ["pstk"])
                    self.cp(tokS[d][:], pstk[:], ["pstk"], [("tokS", d)])
                    self.tt(ddg[:], dec[:].unsqueeze(2).to_broadcast([4, NT, 4]), id4.unsqueeze(1).to_broadcast([4, NT, 4]), ALU.mult,
                            ["dec", "identF"], ["ddg"])
                    self.mm(psd[:], self.onesF[0:4, :], ddg[:].rearrange("p c h -> p (c h)"), True, True, ["ddg", "onesF"], ["psd"])
                    self.cp(decB[d][:].rearrange("p c h -> p (c h)"), psd[:], ["psd"], [("decB", d)])
                S.barrier()
            for h in range(4):
                with ExitStack() as st3:
                    wb = wbH[h % 2]
                    wbt = "wbD%d" % (h % 2)
                    xr = self.sb("xr", [128, LT], F32, st3)
                    ac = self.sb("acD", [128, LT], F32, st3)
                    qh = self.sb("qh", [128, LT], BF16, st3)
                    kh = self.sb("kh", [128, LT], BF16, st3)
                    ktok = self.sb("ktok", [128, NT, 128], BF16, st3)
                    vh = self.sb("vh", [128, NT, 129], BF16, st3)
                    hacc = self.sb("hacc", [128, NT, 128], F32, st3)
                    hnum = [self.sb("hnum", [128, NT, 129], F32, st3) for _ in range(2)]
                    ep = self.sb("epD", [128, 2, NT], F32, st3)
                    hbt = self.sb("hbt", [128, NT, 128], BF16, st3)
                    Cstd = [self.sb("Cst", [128, 129], F32, st3) for _ in range(2)]
                    Cbfd = [self.sb("Cbf", [128, 129], BF16, st3) for _ in range(2)]
                    smd = [self.sb("smD", [128, 4], F32, st3) for _ in range(2)]
                    PT = [self.sb("PT", [128, 128], BF16, st3) for _ in range(2)]
                    Vs = [self.sb("Vs", [128, 129], BF16, st3) for _ in range(2)]
                    yst = [self.sb("ystD", [128, 512], F32, st3) for _ in range(2)]
                    psA = [self.ps("psDA", [128, 512], F32, st3) for _ in range(2)]
                    psT = self.ps("psDT", [128, 512], BF16, st3)
                    psS = [self.ps("psDS", [128, 128], F32, st3) for _ in range(2)]
                    psO = [self.ps("psDO", [128, 129], F32, st3) for _ in range(2)]
                    psC = self.ps("psDC", [128, 129], F32, st3)
                    if h + 1 < 4:
                        load_head(h + 1)
                    for j, (dst, dtok, f) in enumerate(((qh, "qh", h), (kh, "kh", 4 + h))):
                        for ci, (t0, n) in enumerate(TOKCH):
                            ps = psA[ci % 2]
                            pt = "psDA%d" % (ci % 2)
                            for k in range(8):
                                self.mm(ps[:, 0:n], wb[:, k, j * 128:(j + 1) * 128], hT[:, k, t0:t0 + n], k == 0, k == 7,
                                        [wbt, ("hT", ci)], [pt])
                            self.cp(xr[:, t0:t0 + n], ps[:, 0:n], [pt], ["xr"], eng="act")
                        self.ts(ac[:], xr[:], cw[:, f, 1:2], cb[:, f:f + 1], ALU.mult, ALU.add, ["xr", "cw", "cb"], ["acD"])
                        for (a0, a1) in ((0, 256), (256, LT)):
                            self.stt(ac[:, a0 + 1:a1], xr[:, a0:a1 - 1], cw[:, f, 0:1], ac[:, a0 + 1:a1], ALU.mult, ALU.add,
                                     ["xr", "cw", "acD"], ["acD"])
                            self.stt(ac[:, a0:a1 - 1], xr[:, a0 + 1:a1], cw[:, f, 2:3], ac[:, a0:a1 - 1], ALU.mult, ALU.add,
                                     ["xr", "cw", "acD"], ["acD"])
                        if j == 0:
                            self.act(dst[:], ac[:], AF.Silu, ["acD"], [dtok])
                        else:
                            self.act(ac[:], ac[:], AF.Silu, ["acD"], ["acD"])
                            self.ts(dst[:], ac[:], 128.0 ** -0.5, None, ALU.mult, None, ["acD"], [dtok])
                    for g0 in range(0, NT, 4):
                        nn = min(4, NT - g0)
                        for j in range(nn):
                            i = g0 + j
                            self.tr(psT[:, j * 128:(j + 1) * 128], kh[:, i * 128:(i + 1) * 128], self.identB[:], ["kh", "identB"], ["psDT"])
                        self.cp(ktok[:, g0:g0 + nn, :], psT[:, 0:nn * 128].rearrange("p (a b) -> p a b", b=128), ["psDT"], ["ktok"], eng="act")
                    self.memset(vh[:, :, 128:129], 1.0, ["vh1"])
                    for g0 in range(0, NT, 4):
                        nn = min(4, NT - g0)
                        ps = psA[(g0 // 4) % 2]
                        pt = "psDA%d" % ((g0 // 4) % 2)
                        for j in range(nn):
                            i = g0 + j
                            for k in range(8):
                                self.mm(ps[:, j * 128:(j + 1) * 128], hT[:, k, i * 128:(i + 1) * 128], wb[:, k, 256:384], k == 0, k == 7,
                                        [wbt, hts(i)], [pt])
                        self.cp(vh[:, g0:g0 + nn, 0:128], ps[:, 0:nn * 128].rearrange("p (a b) -> p a b", b=128), [pt], ["vh"], eng="act")
                    for d in range(2):
                        self.memset(Cstd[d][:], 0.0, ["Cst%d" % d])

                    def mstep(d, c):
                        mc = order[d][c]
                        mask = self.triL if d == 0 else self.triU
                        Cst, Cbf, PT_, Vs_, sm = Cstd[d], Cbfd[d], PT[d], Vs[d], smd[d]
                        pS, pO = psS[d], psO[d]
                        cst, cbf, ptt, vst, pst_, pot = "Cst%d" % d, "Cbf%d" % d, "PT%d" % d, "Vs%d" % d, "psDS%d" % d, "psDO%d" % d
                        tsl = slice(mc * 128, (mc + 1) * 128)
                        self.mm(pS[:], kh[:, tsl], qh[:, tsl], True, True, ["kh", "qh"], [pst_])
                        self.tt(PT_[:], pS[:], mask[:], ALU.mult, [pst_, "triL", "triU"], [ptt])
                        self.act(Vs_[:], vh[:, mc, :], AF.Identity, ["vh", "vh1", ("tokS", d)], [vst], scale=tokS[d][:, mc, h:h + 1])
                        self.ts(Cst[:], Cst[:], decB[d][:, c, h:h + 1], None, ALU.mult, None, [cst, ("decB", d)], [cst])
                        self.cp(Cbf[:], Cst[:], [cst], [cbf], eng="pool")
                        self.mm(pO[:], PT_[:], Vs_[:], True, False, [ptt, vst], [pot])
                        self.mm(pO[:], qh[:, tsl], Cbf[:], False, True, ["qh", cbf], [pot])
                        self.mm(psC[:], ktok[:, mc, :], Vs_[:], True, True, ["ktok", vst], ["psDC"])
                        self.tt(Cst[:], Cst[:], psC[:], ALU.add, [cst, "psDC"], [cst])
                        self.cp(hnum[d][:, mc, :], pO[:], [pot], [("hnum", d, mc)], eng="act")

                    for c in range(NT):
                        for d in range(2):
                            mstep(d, c)
                    sm = smd[0]
                    for d in range(2):
                        hall = [("hnum", d, i) for i in range(NT)]
                        r = tokS[d][:, :, 4 + h]
                        fl = tokS[d][:, :, 8 + h]
                        e0, e1 = ep[:, 0, :], ep[:, 1, :]
                        self.tt(e0, hnum[d][:, :, 128], r, ALU.mult, hall + [("tokS", d)], ["ep0"])
                        self.ts(e1, e0, -1.0, None, ALU.mult, None, ["ep0"], ["ep1"])
                        self.tt(e0, e0, e1, ALU.max, ["ep0", "ep1"], ["ep0"])
                        self.tt(e0, e0, fl, ALU.max, ["ep0", ("tokS", d)], ["ep0"])
                        S.add("dve", lambda e, e0=e0: e.reciprocal(out=e0, in_=e0), ["ep0"], ["ep0"])
                        self.tt(e0, e0, r, ALU.mult, ["ep0", ("tokS", d)], ["ep0"])
                        fb = e0.unsqueeze(2).to_broadcast([128, NT, 128])
                        if d == 0:
                            self.tt(hacc[:], hnum[0][:, :, 0:128], fb, ALU.mult, hall + ["ep0"], [("hacc", i) for i in range(NT)])
                        else:
                            self.tt(hnum[1][:, :, 0:128], hnum[1][:, :, 0:128], fb, ALU.mult, hall + ["ep0"], hall)
                            self.tt(hacc[:], hacc[:], hnum[1][:, :, 0:128], ALU.add, hall + [("hacc", i) for i in range(NT)],
                                    [("hacc", i) for i in range(NT)], eng="pool")
                    hall = [("hacc", i) for i in range(NT)]
                    sq3 = hnum[0][:, :, 0:128]
                    so3 = hnum[1][:, :, 0:128]
                    for i in range(NT):
                        ps = psA[i % 2]
                        pt = "psDA%d" % (i % 2)
                        for k in range(8):
                            self.mm(ps[:, 0:128], hT[:, k, i * 128:(i + 1) * 128], wb[:, k, 384:512], k == 0, k == 7, [wbt, hts(i)], [pt])
                        self.act(so3[:, i, :], ps[:, 0:128], AF.Sigmoid, [pt], [("so3", i)] + [("hnum", 1, j) for j in range(NT)])
                    self.tt(sq3, hacc[:], hacc[:], ALU.mult, hall, ["sq3"] + [("hnum", 0, j) for j in range(NT)])
                    S.add("dve", lambda e, sq3=sq3, ep=ep: e.tensor_reduce(out=ep[:, 0, :], in_=sq3, axis=AX.X, op=ALU.add), ["sq3"], ["ep0"])
                    self.rsqrt(ep[:, 0, :], ep[:, 0, :], 1.0 / 128, EPS, ["ep0"], ["ep0"])
                    self.tt(hacc[:], hacc[:], ep[:, 0, :].unsqueeze(2).to_broadcast([128, NT, 128]), ALU.mult, hall + ["ep0"], hall)
                    self.tt(hacc[:], hacc[:], gml[:, h * 128:(h + 1) * 128].unsqueeze(1).to_broadcast([128, NT, 128]), ALU.mult,
                            hall + ["gml"], hall, eng="pool")
                    self.tt(hbt[:], hacc[:], so3, ALU.mult, hall + [("so3", i) for i in range(NT)], ["hbt"])
                    for g0 in range(0, NT, 4):
                        nn = min(4, NT - g0)
                        ys = yst[(g0 // 4) % 2]
                        yt = "ystD%d" % ((g0 // 4) % 2)
                        for j in range(nn):
                            i = g0 + j
                            self.tr(psT[:, j * 128:(j + 1) * 128], hbt[:, i, :], self.identB[:], ["hbt", "identB"], ["psDT"])
                        self.cp(ys[:, 0:nn * 128], psT[:, 0:nn * 128], ["psDT"], [yt], eng="act")
                        self.dma(ybr_b[h * 128:(h + 1) * 128, g0 * 128:(g0 + nn) * 128], ys[:, 0:nn * 128], [yt], [("ybr", s, 1)])
                    S.barrier()
            S.barrier()


    def cmul(self, ore, oim, are, aim, bre, bim, ts4, rtoks, wtok, tk="cm", pool_one=False):
        t1, t2, t3, t4 = ts4
        k = [tk + "_t%d" % i for i in range(4)]
        self.tt(t2, aim, bim, ALU.mult, rtoks, [k[1]], eng="pool" if pool_one else "dve")
        self.tt(t1, are, bre, ALU.mult, rtoks, [k[0]])
        self.tt(t3, are, bim, ALU.mult, rtoks, [k[2]])
        self.tt(t4, aim, bre, ALU.mult, rtoks, [k[3]])
        self.tt(oim, t3, t4, ALU.add, [k[2], k[3]], [wtok + "_im"])
        self.tt(ore, t1, t2, ALU.subtract, [k[0], k[1]], [wtok + "_re"])

    def stage_s5(self, W, cin, ybr_a, s, li):
        S = self.S
        hT = self.hT
        order = [list(range(NT)), [1, 0] + list(range(NT - 1, 1, -1))]
        if self.dbg.get("s5_stop") == "none":
            return
        with ExitStack() as st:
            gel = self.sb("gel", [128, 4, LT], BF16, st)
            wsu = self.sb("wsu", [128, 8, 512], BF16, st)
            dsk = self.sb("dsk", [128, 4], F32, st)
            with ExitStack() as st0:
                stage = self.sb("stgC", [128, 8, 512], F32, st0)
                self.load_w("wsu", W["w_in"], O_SU, 512, 8, wsu, "wsu", stage, "stgC")
                S.barrier()
            self.dma(dsk[:], W["ssm_d"].rearrange("(c p) -> p c", p=128), [], ["dsk"], slow=True)
            for c in range(self.dbg.get("s5_nc", 4)):
                with ExitStack() as st2:
                    suT = self.sb("suT", [128, LT], BF16, st2)
                    yacc = self.sb("yacc", [128, LT], F32, st2)
                    sc = self.sb("s5sc", [128, 16, 4], F32, st2)
                    N = self.sb("s5N", [128, 2, 4, 128], F32, st2)
                    Nr = self.sb("s5Nr", [128, 2, 4, 128], F32, st2)
                    braw = self.sb("s5braw", [128, 2, 4, 16], F32, st2)
                    bbar = self.sb("s5bbar", [128, 2, 4, 16], F32, st2)
                    bt = self.sb("s5bt", [128, 2, 4, 16], F32, st2)
                    Zp = self.sb("s5Zp", [128, 2, 4, 128], F32, st2)
                    Yp = self.sb("s5Yp", [128, 2, 4, 128], F32, st2)
                    Pd = [self.sb("s5P", [128, 2, 4, 128], F32, st2) for _ in range(2)]
                    Eitd = [self.sb("s5Eit", [128, 1024], F32, st2) for _ in range(2)]
                    Bbdd = [self.sb("s5Bbd", [128, 1024], BF16, st2) for _ in range(2)]
                    Cbdd = [self.sb("s5Cbd", [128, 2, 4, 128], BF16, st2) for _ in range(2)]
                    Wbd = [self.sb("s5Wb", [128, 1024], BF16, st2) for _ in range(2)]
                    t13d = [self.sb("s5t13", [128, 1024], F32, st2) for _ in range(4)]
                    t24d = [self.sb("s5t24", [128, 1024], F32, st2) for _ in range(4)]
                    Psd = [self.sb("s5Ps", [128, 2, 4, 128], F32, st2) for _ in range(2)]
                    Eisd = [self.sb("s5Eis", [128, 1024], F32, st2) for _ in range(2)]
                    Zcd = [self.sb("s5Zc", [128, 1024], F32, st2) for _ in range(2)]
                    Xfd = [[self.sb("s5Xf", [128, 1024], F32, st2) for _ in range(2)] for _ in range(2)]
                    Xbd = [self.sb("s5Xb", [128, 1024], BF16, st2) for _ in range(2)]
                    psA = [self.ps("psCA", [128, 512], F32, st2) for _ in range(2)]
                    psZd = [[self.ps("psCZ", [128, 512], F32, st2) for _ in range(2)] for _ in range(2)]
                    psYd = [self.ps("psCY", [128, 128], F32, st2) for _ in range(2)]
                    t1, t2 = t13d[0], t24d[0]
                    for ci, (t0, n) in enumerate(TOKCH):
                        ps = psA[ci % 2]
                        pt = "psCA%d" % (ci % 2)
                        for k in range(8):
                            self.mm(ps[:, 0:n], wsu[:, k, c * 128:(c + 1) * 128], hT[:, k, t0:t0 + n], k == 0, k == 7,
                                    ["wsu", ("hT", ci)], [pt])
                        self.cp(suT[:, t0:t0 + n], ps[:, 0:n], [pt], ["suT"], eng="act")
                        self.ts(yacc[:, t0:t0 + n], ps[:, 0:n], dsk[:, c:c + 1], None, ALU.mult, None, [pt, "dsk"],
                                [("yacc", j) for j in range(t0 // 128, (t0 + n) // 128)])
                    for d in range(2):
                        P, Eit, Bbd, Cbd = Pd[d], Eitd[d], Bbdd[d], Cbdd[d]
                        ptk, etk, btk, ctk = "s5P%d" % d, "s5Eit%d" % d, "s5Bbd%d" % d, "s5Cbd%d" % d
                        psT = psZd[d]
                        psTt = ["psCZ%d%d" % (d, 0), "psCZ%d%d" % (d, 1)]
                        def col(i):
                            return sc[:, i, :]
                        LRE, LIM, DT, LDR, LDI, MAG, C8, S8, ABR, ABI, DEN, CR, CI, TA, TB, TC = [col(i) for i in range(16)]
                        gs = slice(8 * c, 8 * c + 8)
                        self.dma(LRE, W["ssm_lam_re"][d, gs].rearrange("(k g) p -> (g p) k", g=2), [], ["sc"], slow=True)
                        self.dma(LIM, W["ssm_lam_im"][d, gs].rearrange("(k g) p -> (g p) k", g=2), [], ["sc"], slow=True)
                        for g2 in range(2):
                            src = W["ssm_log_step"][d, gs].rearrange("(o k g) -> o g k", o=1, g=2)[:, g2, :]
                            self.dma(sc[g2 * 64:(g2 + 1) * 64, 2, :], src.to_broadcast([64, 4]), [], ["sc"], slow=True)
                        T_ = ["sc"]
                        self.ts(LRE, LRE, -1e-4, None, ALU.min, None, T_, T_)
                        self.act(DT, DT, AF.Exp, T_, T_)
                        self.tt(LDR, LRE, DT, ALU.mult, T_, T_)
                        self.tt(LDI, LIM, DT, ALU.mult, T_, T_)
                        self.act(MAG, LDR, AF.Exp, T_, T_)
                        self.act(S8, LDI, AF.Sin, T_, T_, scale=1.0 / 16)
                        self.act(TA, LDI, AF.Sin, T_, T_, scale=1.0 / 32)
                        self.tt(TA, TA, TA, ALU.mult, T_, T_)
                        self.ts(C8, TA, -2.0, 1.0, ALU.mult, ALU.add, T_, T_)
                        for _ in range(4):
                            self.tt(TA, C8, C8, ALU.mult, T_, T_)
                            self.tt(TB, S8, S8, ALU.mult, T_, T_)
                            self.tt(TC, C8, S8, ALU.mult, T_, T_)
                            self.tt(C8, TA, TB, ALU.subtract, T_, T_)
                            self.ts(S8, TC, 2.0, None, ALU.mult, None, T_, T_)
                        self.tt(ABR, MAG, C8, ALU.mult, T_, T_)
                        self.tt(ABI, MAG, S8, ALU.mult, T_, T_)
                        self.tt(TA, LRE, LRE, ALU.mult, T_, T_)
                        self.tt(TB, LIM, LIM, ALU.mult, T_, T_)
                        self.tt(DEN, TA, TB, ALU.add, T_, T_)
                        S.add("dve", lambda e, DEN=DEN: e.reciprocal(out=DEN, in_=DEN), T_, T_)
                        self.ts(TC, ABR, -1.0, None, ALU.add, None, T_, T_)
                        self.tt(TA, TC, LRE, ALU.mult, T_, T_)
                        self.tt(TB, ABI, LIM, ALU.mult, T_, T_)
                        self.tt(CR, TA, TB, ALU.add, T_, T_)
                        self.tt(CR, CR, DEN, ALU.mult, T_, T_)
                        self.tt(TA, ABI, LRE, ALU.mult, T_, T_)
                        self.tt(TB, TC, LIM, ALU.mult, T_, T_)
                        self.tt(CI, TA, TB, ALU.subtract, T_, T_)
                        self.tt(CI, CI, DEN, ALU.mult, T_, T_)
                        self.tt(TA, MAG, MAG, ALU.mult, T_, T_)
                        S.add("dve", lambda e, TA=TA: e.reciprocal(out=TA, in_=TA), T_, T_)
                        self.tt(LDR, ABR, TA, ALU.mult, T_, T_)
                        self.tt(LDI, ABI, TA, ALU.mult, T_, T_)
                        self.ts(LDI, LDI, -1.0, None, ALU.mult, None, T_, T_)
                        for (Tb, ar, ai, tk) in ((P, ABR, ABI, ptk), (N, LDR, LDI, "s5N")):
                            for kq in range(4):
                                self.cp(Tb[:, 0, kq, 0:1], ar[:, kq:kq + 1], T_, [tk])
                                self.cp(Tb[:, 1, kq, 0:1], ai[:, kq:kq + 1], T_, [tk])
                            L = 1
                            tv1 = t1[:, 0:512].rearrange("p (k j) -> p k j", j=128)
                            tv2 = t2[:, 0:512].rearrange("p (k j) -> p k j", j=128)
                            while L < 128:
                                mr = Tb[:, 0, :, L - 1:L].to_broadcast([128, 4, L])
                                mi = Tb[:, 1, :, L - 1:L].to_broadcast([128, 4, L])
                                sr = Tb[:, 0, :, 0:L]
                                si = Tb[:, 1, :, 0:L]
                                dr = Tb[:, 0, :, L:2 * L]
                                di = Tb[:, 1, :, L:2 * L]
                                self.tt(tv1[:, :, 0:L], si, mi, ALU.mult, [tk], ["s5t1"])
                                self.tt(tv2[:, :, 0:L], sr, mr, ALU.mult, [tk], ["s5t2"])
                                self.tt(dr, tv2[:, :, 0:L], tv1[:, :, 0:L], ALU.subtract, ["s5t1", "s5t2"], [tk])
                                self.tt(tv1[:, :, 0:L], si, mr, ALU.mult, [tk], ["s5t1"])
                                self.tt(tv2[:, :, 0:L], sr, mi, ALU.mult, [tk], ["s5t2"])
                                self.tt(di, tv2[:, :, 0:L], tv1[:, :, 0:L], ALU.add, ["s5t1", "s5t2"], [tk])
                                L *= 2
                        if d == 0:
                            Nsrc, ntk = N, "s5N"
                        else:
                            self.cp(Nr[:].rearrange("p a k j -> p (a k) j"), N[:].rearrange("p a k j -> p (a k) j")[:, :, ::-1], ["s5N"], ["s5Nr"])
                            Nsrc, ntk = Nr, "s5Nr"
                        for part in range(2):
                            for kq in range(4):
                                self.tr(psT[part][:, kq * 128:(kq + 1) * 128], Nsrc[:, part, kq, :], self.identF[:], [ntk, "identF"], [psTt[part]])
                            self.cp(Eit[:, part * 512:(part + 1) * 512], psT[part][:], [psTt[part]], [etk], eng="act")
                        self.ts(Eisd[d][:, 0:512], Eit[:, 512:1024], -1.0, None, ALU.mult, None, [etk], ["s5Eis%d" % d])
                        self.cp(Eisd[d][:, 512:1024], Eit[:, 0:512], [etk], ["s5Eis%d" % d], eng="pool")
                        self.ts(Psd[d][:, 0], P[:, 1], -1.0, None, ALU.mult, None, [ptk], ["s5Ps%d" % d])
                        self.cp(Psd[d][:, 1], P[:, 0], [ptk], ["s5Ps%d" % d], eng="pool")
                        self.dma(braw[:, 0], W["ssm_b_re"][d, gs].rearrange("(k g) p m -> (g p) k m", g=2), [], ["s5braw"], slow=True)
                        self.dma(braw[:, 1], W["ssm_b_im"][d, gs].rearrange("(k g) p m -> (g p) k m", g=2), [], ["s5braw"], slow=True)
                        crb = CR.unsqueeze(2).to_broadcast([128, 4, 16])
                        cib = CI.unsqueeze(2).to_broadcast([128, 4, 16])
                        self.tt(bbar[:, 0], braw[:, 0], crb, ALU.mult, ["s5braw", "sc"], ["s5bbar"])
                        self.tt(bt[:, 0], braw[:, 1], cib, ALU.mult, ["s5braw", "sc"], ["s5bt"])
                        self.tt(bbar[:, 0], bbar[:, 0], bt[:, 0], ALU.subtract, ["s5bbar", "s5bt"], ["s5bbar"])
                        self.tt(bbar[:, 1], braw[:, 1], crb, ALU.mult, ["s5braw", "sc"], ["s5bbar"])
                        self.tt(bt[:, 1], braw[:, 0], cib, ALU.mult, ["s5braw", "sc"], ["s5bt"])
                        self.tt(bbar[:, 1], bbar[:, 1], bt[:, 1], ALU.add, ["s5bbar", "s5bt"], ["s5bbar"])
                        self.memset(Zp[:], 0.0, ["s5Zp"])
                        self.memset(Yp[:], 0.0, ["s5Yp"], eng="pool")
                        for part in range(2):
                            for kq in range(4):
                                for g2 in range(2):
                                    rs_ = slice(g2 * 64, (g2 + 1) * 64)
                                    c0 = 32 * kq + 16 * g2
                                    self.cp(Zp[rs_, part, kq, c0:c0 + 16], bbar[rs_, part, kq, :], ["s5bbar"], ["s5Zp"])
                        for part in range(2):
                            for kq in range(4):
                                self.tr(psT[part][:, kq * 128:(kq + 1) * 128], Zp[:, part, kq, :], self.identF[:], ["s5Zp", "identF"], [psTt[part]])
                            self.cp(Bbd[:, part * 512:(part + 1) * 512], psT[part][:], [psTt[part]], [btk], eng="act")
                        for part, nm in enumerate(("ssm_c_re", "ssm_c_im")):
                            for kq in range(4):
                                for g2 in range(2):
                                    g = 8 * c + 2 * kq + g2
                                    r0 = 32 * kq + 16 * g2
                                    self.dma(Yp[r0:r0 + 16, part, kq, 64 * g2:64 * g2 + 64], W[nm][d, g], [], ["s5Yp"])
                        for part in range(2):
                            for kq in range(4):
                                self.tr(psT[part][:, kq * 128:(kq + 1) * 128], Yp[:, part, kq, :], self.identF[:], ["s5Yp", "identF"], [psTt[part]])
                            if part == 0:
                                self.cp(Cbd[:, 0].rearrange("p k j -> p (k j)"), psT[0][:], [psTt[0]], [ctk], eng="act")
                            else:
                                self.ts(Cbd[:, 1].rearrange("p k j -> p (k j)"), psT[1][:], -1.0, None, ALU.mult, None, [psTt[1]], [ctk])
                    def half1(d, ci_):
                        mc = order[d][ci_]
                        tsl = slice(mc * 128, (mc + 1) * 128)
                        Ei, Eis, Bbd, Wb = Eitd[d], Eisd[d], Bbdd[d], Wbd[d]
                        tri = self.triL if d == 0 else self.triU
                        for part in range(2):
                            self.mm(psA[part][:], suT[:, tsl], Bbd[:, part * 512:(part + 1) * 512], True, True, ["suT", "s5Bbd%d" % d], ["psCA%d" % part])
                        v2 = lambda a: a.rearrange("p (a b) -> p a b", a=2)
                        ta, tb = t13d[d], t24d[d]
                        ka, kb = "s5t13_%d" % d, "s5t24_%d" % d
                        self.tt(v2(ta[:]), psA[0][:].unsqueeze(1).to_broadcast([128, 2, 512]), v2(Ei[:]), ALU.mult, ["psCA0", "s5Eit%d" % d], [ka])
                        self.tt(v2(tb[:]), psA[1][:].unsqueeze(1).to_broadcast([128, 2, 512]), v2(Eis[:]), ALU.mult, ["psCA1", "s5Eis%d" % d], [kb])
                        self.tt(Wb[:], ta[:], tb[:], ALU.add, [ka, kb], ["s5Wb%d" % d])
                        for part in range(2):
                            for kq in range(4):
                                self.mm(psZd[d][part][:, kq * 128:(kq + 1) * 128], Wb[:, part * 512 + kq * 128:part * 512 + (kq + 1) * 128], tri[:],
                                        True, True, ["s5Wb%d" % d, "triL", "triU"], ["psCZ%d%d" % (d, part)])

                    v4 = lambda a: a.rearrange("p (a k j) -> p a k j", a=2, k=4)

                    def half2a(d, ci_):
                        P, Ps, Zc = Pd[d], Psd[d], Zcd[d]
                        jc = 127 if d == 0 else 0
                        Xp = Xfd[d][(ci_ + 1) % 2]
                        xpt = "s5Xf%d%d" % (d, (ci_ + 1) % 2)
                        zct = "s5Zc%d" % d
                        for part in range(2):
                            for kq in range(4):
                                o0 = part * 512 + kq * 128
                                self.act(Zc[:, o0:o0 + 128], psZd[d][part][:, kq * 128:(kq + 1) * 128], AF.Identity,
                                         ["psCZ%d%d" % (d, part), xpt], [zct], bias=Xp[:, o0 + jc:o0 + jc + 1])
                        if d == 0:
                            pa, pb = P[:], Ps[:]
                        else:
                            pa, pb = P[:, :, :, ::-1], Ps[:, :, :, ::-1]
                        zr = Zc[:, 0:512].rearrange("p (k j) -> p k j", j=128).unsqueeze(1).to_broadcast([128, 2, 4, 128])
                        zi = Zc[:, 512:1024].rearrange("p (k j) -> p k j", j=128).unsqueeze(1).to_broadcast([128, 2, 4, 128])
                        ta, tb = t13d[2 + d], t24d[2 + d]
                        ka, kb = "s5t13_%d" % (2 + d), "s5t24_%d" % (2 + d)
                        self.tt(v4(tb[:]), zi, pb, ALU.mult, [zct, "s5Ps%d" % d], [kb], eng="pool")
                        self.tt(v4(ta[:]), zr, pa, ALU.mult, [zct, "s5P%d" % d], [ka])

                    def half2b(d, ci_):
                        mc = order[d][ci_]
                        tsl = slice(mc * 128, (mc + 1) * 128)
                        Cbd, Xb = Cbdd[d], Xbd[d]
                        Xn = Xfd[d][ci_ % 2]
                        xnt = "s5Xf%d%d" % (d, ci_ % 2)
                        ta, tb = t13d[2 + d], t24d[2 + d]
                        ka, kb = "s5t13_%d" % (2 + d), "s5t24_%d" % (2 + d)
                        self.tt(Xn[:], ta[:], tb[:], ALU.add, [ka, kb], [xnt])
                        self.cp(Xb[:], Xn[:], [xnt], ["s5Xb%d" % d], eng="act")
                        n8 = 0
                        for part in range(2):
                            for kq in range(4):
                                o0 = part * 512 + kq * 128
                                self.mm(psYd[d][:], Cbd[:, part, kq, :], Xb[:, o0:o0 + 128], n8 == 0, n8 == 7, ["s5Cbd%d" % d, "s5Xb%d" % d], ["psCY%d" % d])
                                n8 += 1
                        self.tt(yacc[:, tsl], yacc[:, tsl], psYd[d][:], ALU.add, [("yacc", mc), "psCY%d" % d], [("yacc", mc)])

                    for d in range(2):
                        self.memset(Xfd[d][1][:], 0.0, ["s5Xf%d1" % d])
                    half1(0, 0)
                    half1(1, 0)
                    for ci_ in range(NT):
                        for d in range(2):
                            half2a(d, ci_)
                            if ci_ + 1 < NT:
                                half1(d, ci_ + 1)
                            half2b(d, ci_)
                    Zc = Zcd[0]
                    yall = [("yacc", j) for j in range(NT)]
                    for a0 in range(0, LT, 1024):
                        a1 = min(LT, a0 + 1024)
                        w_ = a1 - a0
                        self.tt(Zc[:, 0:w_], yacc[:, a0:a1], yacc[:, a0:a1], ALU.mult, yall, ["s5Zc0"])
                        self.ts(Zc[:, 0:w_], Zc[:, 0:w_], 0.044715, 1.0, ALU.mult, ALU.add, ["s5Zc0"], ["s5Zc0"])
                        self.tt(Zc[:, 0:w_], Zc[:, 0:w_], yacc[:, a0:a1], ALU.mult, ["s5Zc0"] + yall, ["s5Zc0"])
                        self.act(Zc[:, 0:w_], Zc[:, 0:w_], AF.Sigmoid, ["s5Zc0"], ["s5Zc0"], scale=1.5957691216057308)
                        self.tt(gel[:, c, a0:a1], Zc[:, 0:w_], yacc[:, a0:a1], ALU.mult, ["s5Zc0"] + yall, [("gel", c)])
                    S.barrier()
            if self.dbg.get("s5_noglu"):
                S.barrier()
                return
            with ExitStack() as st4:
                stage = self.sb("stgC", [128, 8, 512], F32, st4)
                wgl = [self.sb("wglu", [128, 4, 512], BF16, st4) for _ in range(2)]
                gbias = self.sb("gbias", [128, 8], F32, st4)
                sg = [self.sb("sgC", [128, 512], F32, st4) for _ in range(2)]
                yst = [self.sb("ystC", [128, 512], F32, st4) for _ in range(2)]
                psa = [self.ps("psGa", [128, 512], F32, st4) for _ in range(2)]
                psg = [self.ps("psGg", [128, 512], F32, st4) for _ in range(2)]
                self.dma(gbias[:], W["ssm_glu_b"].rearrange("(j p) -> p j", p=128), [], ["gbias"], slow=True)
                for half in range(2):
                    self.load_w("wglu", W["ssm_glu_w"], half * 512, 512, 4, wgl[half], "wglu%d" % half, stage, "stgC")
                cnt = 0
                for j in range(4):
                    for ci, (t0, n) in enumerate(TOKCH):
                        i2 = cnt % 2
                        cnt += 1
                        for k in range(4):
                            self.mm(psa[i2][:, 0:n], wgl[0][:, k, j * 128:(j + 1) * 128], gel[:, k, t0:t0 + n], k == 0, k == 3,
                                    ["wglu0", ("gel", k)], ["psGa%d" % i2])
                        for k in range(4):
                            self.mm(psg[i2][:, 0:n], wgl[1][:, k, j * 128:(j + 1) * 128], gel[:, k, t0:t0 + n], k == 0, k == 3,
                                    ["wglu1", ("gel", k)], ["psGg%d" % i2])
                        self.act(sg[i2][:, 0:n], psg[i2][:, 0:n], AF.Sigmoid, ["psGg%d" % i2, "gbias"], ["sgC%d" % i2], bias=gbias[:, 4 + j:5 + j])
                        self.stt(yst[i2][:, 0:n], psa[i2][:, 0:n], gbias[:, j:j + 1], sg[i2][:, 0:n], ALU.add, ALU.mult,
                                 ["psGa%d" % i2, "gbias", "sgC%d" % i2], ["ystC%d" % i2])
                        self.dma(ybr_a[j * 128:(j + 1) * 128, t0:t0 + n], yst[i2][:, 0:n], ["ystC%d" % i2], [("ybr", s, 0)])
                S.barrier()
            S.barrier()

CONST_SHAPES = {"ident": [128, 128], "ones": [128, 128], "tril": [128, 128], "triu": [128, 128],
                "ropeC": [128, 16, 64], "ropeS": [128, 16, 64], "lamc": [DEPTH, 128, 2]}
LAYER_W = [("norm_g", [D]), ("ada_w", [D, 3 * D]), ("ada_b", [3 * D]), ("w_in", [D, D_IN]),
           ("w_ssm_out", [512, D]), ("w_ml_out", [512, D]), ("w_da_out", [512, D]), ("w_out", [D, D]),
           ("da_qnorm_g", [64]), ("da_knorm_g", [64]), ("da_lambda", [4, 64]), ("da_subln_g", [128]),
           ("ml_conv_w", [3, 1024]), ("ml_conv_b", [1024]), ("ml_gate_b", [4, 4]), ("ml_norm_g", [512]),
           ("ssm_lam_re", [2, 32, 64]), ("ssm_lam_im", [2, 32, 64]), ("ssm_log_step", [2, 32]),
           ("ssm_b_re", [2, 32, 64, 16]), ("ssm_b_im", [2, 32, 64, 16]), ("ssm_c_re", [2, 32, 16, 64]),
           ("ssm_c_im", [2, 32, 16, 64]), ("ssm_d", [512]), ("ssm_glu_w", [512, 1024]), ("ssm_glu_b", [1024])]


def host_consts(li=0):
    i = np.arange(128)
    c = {
        "ident": np.eye(128, dtype=np.float32),
        "ones": np.ones((128, 128), np.float32),
        "tril": (i[:, None] <= i[None, :]).astype(np.float32),
        "triu": (i[:, None] >= i[None, :]).astype(np.float32),
    }
    t = np.arange(LL)
    row = (t // 64).astype(np.float32)
    col = (t % 64).astype(np.float32)
    half = 32
    inv = np.power(np.float32(10000.0), -np.arange(0, half, 2, dtype=np.float32) / np.float32(half)).astype(np.float32)
    ar = (row[:, None] * inv).astype(np.float32)
    ac = (col[:, None] * inv).astype(np.float32)
    cr, sr, cc, sc = np.cos(ar), np.sin(ar), np.cos(ac), np.sin(ac)
    C64 = np.concatenate([cr, cr, cc, cc], axis=1).astype(np.float32)
    S64 = np.concatenate([-sr, sr, -sc, sc], axis=1).astype(np.float32)
    c["ropeC"] = np.ascontiguousarray(C64.reshape(16, 128, 64).transpose(1, 0, 2))
    c["ropeS"] = np.ascontiguousarray(S64.reshape(16, 128, 64).transpose(1, 0, 2))
    lam = [0.8 - 0.6 * math.exp(-0.3 * l) for l in range(DEPTH)]
    c["lamc"] = np.stack([np.tile(np.array([[-v, 1.0 - v]], np.float32), (128, 1)) for v in lam])
    return c


def build_layer_program(li=0, branches=("a", "b", "c"), dump_ybr=False, do_merge=True, dbg=None):
    nc = bass.Bass("TRN2", target_bir_lowering=False)
    S = Sched()
    W = {}
    for nm, shp in LAYER_W:
        W[nm] = nc.dram_tensor(nm, shp, F32, kind="ExternalInput").ap()
    W["cvec"] = nc.dram_tensor("cvec", [3, D], F32, kind="ExternalInput").ap()
    cin = {nm: nc.dram_tensor(nm, shp, F32, kind="ExternalInput").ap() for nm, shp in CONST_SHAPES.items()}
    xT = nc.dram_tensor("xT", [NSEQ, D, LT], F32, kind="ExternalInput").ap()
    xo = nc.dram_tensor("xo", [NSEQ, D, LT], F32, kind="ExternalOutput").ap()
    ybr_in = None
    if len(branches) < 3:
        ybr_in = nc.dram_tensor("ybr", [NSEQ, 3, 512, LT], F32, kind="ExternalInput").ap()
    ybr_dev = nc.dram_tensor("ybr_dev", [NSEQ, 3, 512, LT], F32, kind="ExternalOutput" if dump_ybr else "Internal").ap()
    with ExitStack() as stack:
        B = LayerBuilder(nc, S, stack, dbg=dbg)
        B.setup_consts(cin)
        B.hT = B.sb("hT", [128, 8, LT], BF16)
        B.stage_mod(W)
        for s in range((dbg or {}).get("nseq", NSEQ)):
            B.stage_norm(xT[s], s)
            srcs = []
            for bi, b in enumerate("abc"):
                srcs.append(ybr_dev[s, bi] if b in branches else ybr_in[s, bi])
            if "a" in branches:
                B.stage_s5(W, cin, ybr_dev[s, 0], s, li)
            if "b" in branches:
                B.stage_mlstm(W, cin, ybr_dev[s, 1], s, li)
            if "c" in branches:
                B.stage_attn(W, cin, ybr_dev[s, 2], s, li)
            if do_merge:
                B.stage_merge(W, xT[s], xo[s], srcs, s)
        S.emit(nc, stack)
    return nc, S


_PROG = {}


def build_program(n_layers=DEPTH, layer0=0):
    nc = bass.Bass("TRN2", target_bir_lowering=False)
    S = Sched()
    Wall = {}
    for nm, shp in LAYER_W:
        Wall[nm] = nc.dram_tensor(nm, [DEPTH] + list(shp), F32, kind="ExternalInput").ap()
    cvec = nc.dram_tensor("cvec", [3, D], F32, kind="ExternalInput").ap()
    cin = {nm: nc.dram_tensor(nm, shp, F32, kind="ExternalInput").ap() for nm, shp in CONST_SHAPES.items()}
    xT = nc.dram_tensor("xT", [NSEQ, D, LT], F32, kind="ExternalInput").ap()
    xo = nc.dram_tensor("xo", [NSEQ, D, LT], F32, kind="ExternalOutput").ap()
    xs = [nc.dram_tensor("xs%d" % i, [NSEQ, D, LT], F32).ap() for i in range(2)]
    ybr_dev = nc.dram_tensor("ybr_dev", [NSEQ, 3, 512, LT], F32).ap()
    with ExitStack() as stack:
        B = LayerBuilder(nc, S, stack)
        B.setup_consts(cin)
        B.hT = B.sb("hT", [128, 8, LT], BF16)
        for j in range(n_layers):
            li = layer0 + j
            S.epoch = j
            W = {nm: Wall[nm][li] for nm, _ in LAYER_W}
            W["cvec"] = cvec
            x_in, tin = (xT, "xT") if j == 0 else (xs[(j - 1) % 2], "xs%d" % ((j - 1) % 2))
            x_out, tout = (xo, "xo") if j == n_layers - 1 else (xs[j % 2], "xs%d" % (j % 2))
            B.stage_mod(W)
            for s in range(NSEQ):
                B.stage_norm(x_in[s], s, tin)
                B.stage_s5(W, cin, ybr_dev[s, 0], s, li)
                B.stage_mlstm(W, cin, ybr_dev[s, 1], s, li)
                B.stage_attn(W, cin, ybr_dev[s, 2], s, li)
                B.stage_merge(W, x_in[s], x_out[s], [ybr_dev[s, b] for b in range(3)], s, tin, tout)
        S.emit(nc, stack)
    return nc, S


def _fm(ctx, lat):
    return np.ascontiguousarray(np.concatenate([ctx, lat], axis=1).transpose(0, 2, 1)).astype(np.float32)


def kernel(**inputs):
    x = np.asarray(inputs["x"], np.float32)
    c = np.asarray(inputs["c"], np.float32)
    ctx = np.asarray(inputs["ctx"], np.float32)
    c_ctx = np.asarray(inputs["c_ctx"], np.float32)
    ncores = 8
    if "p" not in _PROG:
        _PROG["p"] = build_program()
    nc, _ = _PROG["p"]
    consts = host_consts()
    wl = {nm: np.ascontiguousarray(np.asarray(inputs[nm], np.float32)) for nm, _ in LAYER_W}
    in_maps = []
    for i in range(ncores):
        m = {"xT": _fm(ctx[2 * i:2 * i + 2], x[2 * i:2 * i + 2]),
             "cvec": np.stack([c[2 * i], c[2 * i + 1], c_ctx]).astype(np.float32)}
        m.update(wl)
        m.update(consts)
        in_maps.append(m)
    res = run_bass_kernel_spmd(nc, in_maps, core_ids=list(range(ncores)))
    out = np.concatenate([np.ascontiguousarray(np.asarray(r["xo"], np.float32)[:, :, LC:].transpose(0, 2, 1))
                          for r in res.results], axis=0)
    return out.astype(np.float32)
```

```python
import math
from contextlib import ExitStack

import numpy as np
import concourse.bass as bass
import concourse.mybir as mybir
from concourse.bass_utils import run_bass_kernel_spmd

F32 = mybir.dt.float32
BF16 = mybir.dt.bfloat16
AF = mybir.ActivationFunctionType
ALU = mybir.AluOpType
AX = mybir.AxisListType

D = 1024
LC = 256
LL = 2048
LT = LC + LL
NT = LT // 128
DEPTH = 4
EPS = 1e-6
NSEQ = 2
O_SU, O_SZ, O_MQK, O_MV, O_MO, O_MZ, O_MG, O_DQ, O_DK, O_DV, O_DZ, O_GL = (
    0, 512, 1024, 2048, 2560, 3072, 3584, 3600, 4112, 4624, 5136, 5648)
D_IN = 8720
TOKCH = [(0, 256), (256, 512), (768, 512), (1280, 512), (1792, 512)]
TWO_PI = 2.0 * math.pi


class _Op:
    __slots__ = ("eng", "fn", "deps", "dma", "ms", "sem", "val", "idx", "ep")


class Sched:
    ENGS = ("pe", "act", "dve", "pool", "sp")
    R = 6

    def __init__(self):
        self.ops = []
        self.tw = {}
        self.tr = {}
        self.last = {e: None for e in self.ENGS}
        self.pending_barrier = {e: set() for e in self.ENGS}
        self.dma_hist = {e: [] for e in self.ENGS}
        self.epoch = 0

    def add(self, eng, fn, reads=(), writes=(), dma=False):
        op = _Op()
        op.eng, op.fn, op.dma, op.ms, op.sem, op.val = eng, fn, dma, False, None, None
        op.idx = len(self.ops)
        op.ep = self.epoch
        xs = [r for r in reads if isinstance(r, str) and (r.startswith("ps") or r.startswith("acc"))]
        if xs:
            writes = list(writes) + [x for x in xs if x not in writes]
        deps = set()
        for r in reads:
            w = self.tw.get(r)
            if w is not None:
                deps.add(w)
        for wt in writes:
            w = self.tw.get(wt)
            if w is not None:
                deps.add(w)
            for rr in self.tr.get(wt, ()):
                deps.add(rr)
        deps |= self.pending_barrier[eng]
        self.pending_barrier[eng] = set()
        keep = set()
        rset = None
        for d in deps:
            o = self.ops[d]
            if o.eng == eng and not o.dma and not dma:
                if eng == "pe":
                    continue
            keep.add(d)
        op.deps = keep
        for r in reads:
            self.tr.setdefault(r, []).append(op.idx)
        for wt in writes:
            self.tw[wt] = op.idx
            self.tr[wt] = []
        self.ops.append(op)
        self.last[eng] = op.idx
        if dma:
            self.dma_hist[eng].append(op.idx)
        return op.idx

    def barrier(self):
        s = set()
        for e in self.ENGS:
            if self.last[e] is not None:
                s.add(self.last[e])
            for i in self.dma_hist[e][-self.R:]:
                s.add(i)
        for e in self.ENGS:
            self.pending_barrier[e] |= s

    def emit(self, nc, stack):
        ops = self.ops
        for op in ops:
            for d in op.deps:
                ops[d].ms = True
        neps = max(op.ep for op in ops) + 1
        csem = {(e, ep): stack.enter_context(nc.semaphore("c_%s%d" % (e, ep))) for e in self.ENGS for ep in range(neps)}
        dsem = {e: [stack.enter_context(nc.semaphore("d_%s%d" % (e, i))) for i in range(self.R)]
                for e in ("sp", "pool", "act")}
        ccount = {(e, ep): 0 for e in self.ENGS for ep in range(neps)}
        dcount = {e: 0 for e in self.ENGS}
        per_eng = {e: [] for e in self.ENGS}
        for op in ops:
            if op.dma:
                i = dcount[op.eng]
                op.sem = dsem[op.eng][i % self.R]
                op.val = 16 * (i // self.R + 1)
                dcount[op.eng] += 1
            elif op.ms:
                ccount[(op.eng, op.ep)] += 1
                op.sem = csem[(op.eng, op.ep)]
                op.val = ccount[(op.eng, op.ep)]
            per_eng[op.eng].append(op)
        self.stats = {e: (len(per_eng[e]), sum(ccount[(e, ep)] for ep in range(neps)), dcount[e]) for e in self.ENGS}
        R = self.R

        def replay(ename, e):
            seen = {}
            ndma = 0
            for op in per_eng[ename]:
                waits = {}
                for d in op.deps:
                    o = ops[d]
                    k = o.sem
                    if waits.get(k, (None, 0))[1] < o.val:
                        waits[k] = (o.sem, o.val)
                if op.dma:
                    if ndma >= R:
                        k = op.sem
                        v = op.val - 16
                        if waits.get(k, (None, 0))[1] < v:
                            waits[k] = (op.sem, v)
                    ndma += 1
                for k, (sem, val) in waits.items():
                    if seen.get(k, 0) < val:
                        e.wait_ge(sem, val)
                        seen[k] = val
                ins = op.fn(e)
                if op.dma:
                    ins.then_inc(op.sem, 16)
                elif op.ms:
                    ins.then_inc(op.sem, 1)
            if ename in dsem:
                n = dcount[ename]
                for j in range(min(n, R)):
                    cnt = (n - 1 - j) // R + 1
                    e.wait_ge(dsem[ename][j], 16 * cnt)

        with nc.Block() as block:
            @block.tensor
            def _(e):
                replay("pe", e)

            @block.scalar
            def _(e):
                replay("act", e)

            @block.vector
            def _(e):
                replay("dve", e)

            @block.gpsimd
            def _(e):
                replay("pool", e)

            @block.sync
            def _(e):
                replay("sp", e)


class LayerBuilder:
    def __init__(self, nc, S, stack, dbg=None):
        self.nc, self.S, self.stack = nc, S, stack
        self.dbg = dbg or {}
        self.uid = 0
        self.wq = 0

    def sb(self, name, shape, dt, stack=None):
        self.uid += 1
        return (stack or self.stack).enter_context(self.nc.sbuf_tensor("%s_%d" % (name, self.uid), shape, dt))

    def ps(self, name, shape, dt=F32, stack=None):
        self.uid += 1
        full = 512 if dt == F32 else 1024
        t = (stack or self.stack).enter_context(self.nc.psum_tensor("%s_%d" % (name, self.uid), [128, full], dt))
        n = 1
        for d in shape[1:]:
            n *= d
        assert n <= full, (name, shape)
        v = t[0:shape[0], 0:n]
        if len(shape) == 3:
            v = v.rearrange("p (a b) -> p a b", b=shape[2])
        return v

    def dma(self, out, in_, reads, writes, q=None, slow=False):
        if q is None:
            q = "sp"
        if slow:
            fn = lambda e, o=out, i=in_: e.dma_start(out=o, in_=i, allow_slow_non_contiguous=True)
        else:
            fn = lambda e, o=out, i=in_: e.dma_start(out=o, in_=i)
        return self.S.add(q, fn, reads, writes, dma=True)

    def mm(self, out, lhsT, rhs, start, stop, reads, writes, skip=False):
        if skip:
            fn = lambda e: e.matmul(out, lhsT, rhs, start=start, stop=stop, skip_group_check=True)
        else:
            fn = lambda e: e.matmul(out, lhsT, rhs, start=start, stop=stop)
        return self.S.add("pe", fn, reads, writes)

    def tr(self, out, in_, ident, reads, writes):
        return self.S.add("pe", lambda e: e.transpose(out, in_, ident), reads, writes)

    def act(self, out, in_, func, reads, writes, bias=None, scale=None, accum_out=None):
        kw = {}
        if bias is not None:
            kw["bias"] = bias
        if scale is not None:
            kw["scale"] = scale
        if accum_out is not None:
            kw["accum_out"] = accum_out
        return self.S.add("act", lambda e: e.activation(out=out, in_=in_, func=func, **kw), reads, writes)

    def tt(self, out, in0, in1, op, reads, writes, eng="dve"):
        return self.S.add(eng, lambda e: e.tensor_tensor(out=out, in0=in0, in1=in1, op=op), reads, writes)

    def ts(self, out, in0, s1, s2, op0, op1, reads, writes, eng="dve"):
        if op1 is None:
            fn = lambda e: e.tensor_scalar(out=out, in0=in0, scalar1=s1, scalar2=None, op0=op0)
        else:
            fn = lambda e: e.tensor_scalar(out=out, in0=in0, scalar1=s1, scalar2=s2, op0=op0, op1=op1)
        return self.S.add(eng, fn, reads, writes)

    def stt(self, out, in0, scalar, in1, op0, op1, reads, writes):
        return self.S.add("dve", lambda e: e.scalar_tensor_tensor(out=out, in0=in0, scalar=scalar, in1=in1,
                                                                  op0=op0, op1=op1), reads, writes)

    def cp(self, out, in_, reads, writes, eng="dve"):
        if eng == "act":
            return self.S.add("act", lambda e: e.activation(out=out, in_=in_, func=AF.Copy), reads, writes)
        return self.S.add(eng, lambda e: e.tensor_copy(out=out, in_=in_), reads, writes)

    def rsqrt(self, out, in_, scale, eps, reads, writes):
        et = self.epsT[eps]
        np_ = out.shape[0]
        self.act(out, in_, AF.Sqrt, list(reads) + [("epsT", eps)], writes, bias=et[0:np_, :], scale=scale)
        return self.S.add("dve", lambda e: e.reciprocal(out=out, in_=out), writes, writes)

    def rsqrt_le(self, out, in_, scale, eps, reads, writes):
        et = self.epsT[eps]
        np_ = out.shape[0]
        self.act(out, in_, AF.Ln, list(reads) + [("epsT", eps)], writes, bias=et[0:np_, :], scale=scale)
        return self.act(out, out, AF.Exp, writes, writes, scale=-0.5)

    def memset(self, ap, val, writes, eng="dve"):
        return self.S.add(eng, lambda e: e.memset(ap, val), (), writes)

    def setup_consts(self, cin):
        S = self.S
        self.identF = self.sb("identF", [128, 128], F32)
        self.identB = self.sb("identB", [128, 128], BF16)
        self.onesF = self.sb("onesF", [128, 128], F32)
        self.triL = self.sb("triL", [128, 128], BF16)
        self.triU = self.sb("triU", [128, 128], BF16)
        self.dma(self.identF[:], cin["ident"], (), ["identF"])
        self.dma(self.onesF[:], cin["ones"], (), ["onesF"])
        tmp = self.sb("ctmp", [128, 256], F32)
        self.dma(tmp[:, 0:128], cin["tril"], (), ["ctmp0"])
        self.dma(tmp[:, 128:256], cin["triu"], (), ["ctmp1"])
        self.triLF = tmp[:, 0:128]
        self.triUF = tmp[:, 128:256]
        self.epsT = {}
        t = self.sb("epsT", [128, 1], F32)
        self.memset(t[:], EPS, [("epsT", EPS)])
        self.epsT[EPS] = t
        self.cp(self.identB[:], self.identF[:], ["identF"], ["identB"])
        self.cp(self.triL[:], tmp[:, 0:128], ["ctmp0"], ["triL"])
        self.cp(self.triU[:], tmp[:, 128:256], ["ctmp1"], ["triU"])

    def load_w(self, name, w_dram, c0, ncols, KT, dst, dst_tok, stage, stage_tok, q="pool", cast_eng="pool"):
        src = w_dram.rearrange("(k p) c -> p k c", p=128)[:, :, c0:c0 + ncols]
        self.dma(stage[:, 0:KT, 0:ncols], src, [], [stage_tok], q=q)
        self.cp(dst[:, 0:KT, 0:ncols], stage[:, 0:KT, 0:ncols], [stage_tok], [dst_tok], eng=cast_eng)


    def stage_mod(self, W):
        S, nc = self.S, self.nc
        self.modT = self.sb("modT", [128, 24, 3], F32)
        self.A1 = self.sb("A1", [128, 8, 3], F32)
        with ExitStack() as st:
            cS = self.sb("cS", [128, 8, 3], F32, st)
            adab = self.sb("adab", [128, 24], F32, st)
            normg = self.sb("normg", [128, 8], F32, st)
            stage = self.sb("stgA", [128, 8, 512], F32, st)
            stage2 = self.sb("stgA2", [128, 8, 512], F32, st)
            psM = self.ps("psM", [128, 72], F32, st)
            for v in range(3):
                self.dma(cS[:, :, v], W["cvec"][v].rearrange("(k p) -> p k", p=128), [], ["cS"], slow=True)
            self.dma(adab[:], W["ada_b"].rearrange("(j p) -> p j", p=128), [], ["adab"], slow=True)
            self.dma(normg[:], W["norm_g"].rearrange("(k p) -> p k", p=128), [], ["normg"], slow=True)
            self.act(cS[:], cS[:], AF.Silu, ["cS"], ["cS"])
            stgs = [(stage, "stgA"), (stage2, "stgA2")]
            for ch in range(6):
                stg, tok = stgs[ch % 2]
                src = W["ada_w"].rearrange("(k p) c -> p k c", p=128)[:, :, ch * 512:(ch + 1) * 512]
                self.dma(stg[:], src, [], [tok], q="pool")
                for j in range(4):
                    jj = ch * 4 + j
                    for k in range(8):
                        self.mm(psM[:, 3 * jj:3 * jj + 3], stg[:, k, j * 128:(j + 1) * 128], cS[:, k, :],
                                k == 0, k == 7, [tok, "cS"], ["psM"])
            self.tt(self.modT[:], psM[:].rearrange("p (j v) -> p j v", v=3),
                    adab[:].unsqueeze(2).to_broadcast([128, 24, 3]), ALU.add, ["psM", "adab"], ["modT"])
            self.stt(self.A1[:], self.modT[:, 8:16, :], 1.0, normg[:].unsqueeze(2).to_broadcast([128, 8, 3]),
                     ALU.add, ALU.mult, ["modT", "normg"], ["A1"])
            S.barrier()

    def stage_norm(self, xT_s, s, xtok_in="xin"):
        S = self.S
        with ExitStack() as st:
            xin = [self.sb("xin", [128, 8, 512], F32, st) for _ in range(2)]
            sqt = self.sb("sqt", [128, 8, 512], F32, st)
            rs = self.sb("rs", [128, 512], F32, st)
            tmpn = [self.sb("tmpn", [128, 512], F32, st) for _ in range(2)]
            pss = [self.ps("pss", [128, 512], F32, st) for _ in range(2)]
            for ci, (t0, n) in enumerate(TOKCH):
                v = 2 if t0 == 0 else s
                xi = xin[ci % 2]
                xtok = "xin%d" % (ci % 2)
                ps = pss[ci % 2]
                pstok = "pss%d" % (ci % 2)
                self.dma(xi[:, :, 0:n], xT_s.rearrange("(k p) t -> p k t", p=128)[:, :, t0:t0 + n], [(xtok_in, s)], [xtok])
                self.act(sqt[:, :, 0:n], xi[:, :, 0:n], AF.Square, [xtok], ["sqt"])
                for k in range(8):
                    self.mm(ps[:, 0:n], self.onesF[:], sqt[:, k, 0:n], k == 0, k == 7, ["sqt", "onesF"], [pstok])
                self.rsqrt(rs[:, 0:n], ps[:, 0:n], 1.0 / D, EPS, [pstok], ["rs"])
                for k in range(8):
                    tm = tmpn[k % 2]
                    ttok = "tmpn%d" % (k % 2)
                    self.tt(tm[:, 0:n], xi[:, k, 0:n], rs[:, 0:n], ALU.mult, [xtok, "rs"], [ttok])
                    self.act(self.hT[:, k, t0:t0 + n], tm[:, 0:n], AF.Identity, [ttok, "A1", "modT"], [("hT", ci)],
                             bias=self.modT[:, k, v:v + 1], scale=self.A1[:, k, v:v + 1])
            S.barrier()

    def stage_merge(self, W, xT_s, xo_s, ybr_s, s, xtok_in="xin", xtok_out="xout"):
        S = self.S
        hT = self.hT
        with ExitStack() as st:
            yg = [self.sb("yg%d" % b, [128, 4, LT], BF16, st) for b in range(3)]
            mT = self.sb("mT", [128, 8, LT], BF16, st)
            stage = self.sb("stgF", [128, 8, 512], F32, st)
            wb = [self.sb("wbF", [128, 8, 512], BF16, st) for _ in range(2)]
            wb2 = [self.sb("wbG", [128, 8, 384], BF16, st), self.sb("wbG", [128, 4, 384], BF16, st)]
            ych = [self.sb("ych", [128, 512], F32, st) for _ in range(2)]
            szt = [self.sb("szt", [128, 512], F32, st) for _ in range(2)]
            sg = [self.sb("sg", [128, 512], F32, st) for _ in range(3)]
            mt = [self.sb("mt", [128, 512], F32, st) for _ in range(3)]
            psA = [self.ps("psA", [128, 512], F32, st) for _ in range(3)]
            psB = [self.ps("psB", [128, 512], F32, st) for _ in range(3)]
            allh = [("hT", i) for i in range(len(TOKCH))]
            zoff = [O_SZ, O_MZ, O_DZ]
            for b in range(3):
                w = wb[b % 2]
                wtok = "wbF%d" % (b % 2)
                self.load_w("wz", W["w_in"], zoff[b], 512, 8, w, wtok, stage, "stgF")
                for c in range(4):
                    for ci, (t0, n) in enumerate(TOKCH):
                        i = (c * 5 + ci) % 2
                        self.dma(ych[i][:, 0:n], ybr_s[b][c * 128:(c + 1) * 128, t0:t0 + n], [("ybr", s, b)], ["ych%d" % i])
                        ps = psA[i]
                        for k in range(8):
                            self.mm(ps[:, 0:n], w[:, k, c * 128:(c + 1) * 128], hT[:, k, t0:t0 + n], k == 0, k == 7,
                                    [wtok, ("hT", ci)], ["psA%d" % i])
                        self.act(szt[i][:, 0:n], ps[:, 0:n], AF.Silu, ["psA%d" % i], ["szt%d" % i])
                        self.tt(yg[b][:, c, t0:t0 + n], ych[i][:, 0:n], szt[i][:, 0:n], ALU.mult,
                                ["ych%d" % i, "szt%d" % i], [("yg", b, ci)])
            wouts = [W["w_ssm_out"], W["w_ml_out"], W["w_da_out"]]
            for f in range(8):
                if f % 2 == 0:
                    wg, wo, wgt, wot = wb[0], wb[1], "wbF0", "wbF1"
                else:
                    wg, wo, wgt, wot = wb2[0], wb2[1], "wbG0", "wbG1"
                for b in range(3):
                    src = W["w_in"].rearrange("(k p) c -> p k c", p=128)[:, :, O_GL + b * 1024 + f * 128:O_GL + b * 1024 + (f + 1) * 128]
                    self.dma(stage[:, :, b * 128:(b + 1) * 128], src, [], ["stgF"], q="pool")
                self.cp(wg[:, :, 0:384], stage[:, :, 0:384], ["stgF"], [wgt], eng="pool")
                for b in range(3):
                    src = wouts[b].rearrange("(k p) c -> p k c", p=128)[:, :, f * 128:(f + 1) * 128]
                    self.dma(stage[:, 0:4, b * 128:(b + 1) * 128], src, [], ["stgF"], q="pool")
                self.cp(wo[:, 0:4, 0:384], stage[:, 0:4, 0:384], ["stgF"], [wot], eng="pool")
                for ci, (t0, n) in enumerate(TOKCH):
                    for b in range(3):
                        for k in range(8):
                            self.mm(psA[b][:, 0:n], wg[:, k, b * 128:(b + 1) * 128], hT[:, k, t0:t0 + n], k == 0, k == 7,
                                    [wgt, ("hT", ci)], ["psA%d" % b])
                        for k in range(4):
                            self.mm(psB[b][:, 0:n], wo[:, k, b * 128:(b + 1) * 128], yg[b][:, k, t0:t0 + n], k == 0, k == 3,
                                    [wot, ("yg", b, ci)], ["psB%d" % b])
                    for b in range(3):
                        self.act(sg[b][:, 0:n], psA[b][:, 0:n], AF.Sigmoid, ["psA%d" % b], ["sg%d" % b])
                        self.tt(mt[b][:, 0:n], sg[b][:, 0:n], psB[b][:, 0:n], ALU.mult, ["sg%d" % b, "psB%d" % b], ["mt%d" % b])
                    self.tt(mt[0][:, 0:n], mt[0][:, 0:n], mt[1][:, 0:n], ALU.add, ["mt0", "mt1"], ["mt0"])
                    self.tt(mT[:, f, t0:t0 + n], mt[0][:, 0:n], mt[2][:, 0:n], ALU.add, ["mt0", "mt2"], [("mT", ci)])
            wo_full = [wb[0], wb[1]]
            for half in range(2):
                self.load_w("wo", W["w_out"], half * 512, 512, 8, wo_full[half], "wbF%d" % half, stage, "stgF")
            for fo in range(8):
                w = wo_full[fo // 4]
                wtok = "wbF%d" % (fo // 4)
                for ci, (t0, n) in enumerate(TOKCH):
                    v = 2 if t0 == 0 else s
                    i = (fo * 5 + ci) % 2
                    ps = psA[i]
                    for k in range(8):
                        self.mm(ps[:, 0:n], w[:, k, (fo % 4) * 128:(fo % 4 + 1) * 128], mT[:, k, t0:t0 + n], k == 0, k == 7,
                                [wtok, ("mT", ci)], ["psA%d" % i])
                    self.dma(ych[i][:, 0:n], xT_s[fo * 128:(fo + 1) * 128, t0:t0 + n], [(xtok_in, s)], ["ych%d" % i])
                    self.stt(szt[i][:, 0:n], ps[:, 0:n], self.modT[:, 16 + fo, v:v + 1], ych[i][:, 0:n], ALU.mult, ALU.add,
                             ["psA%d" % i, "ych%d" % i, "modT"], ["szt%d" % i])
                    self.dma(xo_s[fo * 128:(fo + 1) * 128, t0:t0 + n], szt[i][:, 0:n], ["szt%d" % i], [(xtok_out, s)])
            S.barrier()


    def stage_attn(self, W, cin, ybr_c, s, li, with_ctx=True):
        S = self.S
        hT = self.hT
        lam_init = 0.8 - 0.6 * math.exp(-0.3 * li)
        allh = [("hT", i) for i in range(len(TOKCH))]
        def hts(i):
            t = i * 128
            for ci, (t0, n) in enumerate(TOKCH):
                if t0 <= t < t0 + n:
                    return ("hT", ci)
        with ExitStack() as st:
            qT0 = self.sb("qT0", [128, 4, LT], BF16, st)
            qT1 = self.sb("qT1", [128, 4, LT], BF16, st)
            qTm = [qT0, qT1]
            self.memset(qT0[64:128, :, :], 0.0, ["qz0"], eng="pool")
            self.memset(qT1[0:64, :, :], 0.0, ["qz1"], eng="pool")
            kT = self.sb("kT", [128, 4, LT], BF16, st)
            vaug = self.sb("vaug", [128, NT, 4, 129], BF16, st)
            gq = self.sb("gq", [128, 64], F32, st)
            gk = self.sb("gk", [128, 64], F32, st)
            gsub = self.sb("gsub", [128, 128], F32, st)
            lamt = self.sb("lamt", [128, 256], F32, st)
            lprod = self.sb("lprod", [128, 2, 64], F32, st)
            lsum = self.sb("lsum", [128, 2], F32, st)
            neglam = self.sb("neglam", [128, 1], F32, st)
            self.dma(gq[:], W["da_qnorm_g"].rearrange("(o d) -> o d", o=1).to_broadcast([128, 64]), [], ["gq"])
            self.dma(gk[:], W["da_knorm_g"].rearrange("(o d) -> o d", o=1).to_broadcast([128, 64]), [], ["gk"])
            self.dma(gsub[:], W["da_subln_g"].rearrange("(o d) -> o d", o=1).to_broadcast([128, 128]), [], ["gsub"])
            self.dma(lamt[:], W["da_lambda"].rearrange("(o a) d -> o (a d)", o=1).to_broadcast([128, 256]), [], ["lamt"])
            self.ts(gq[:], gq[:], 0.125, None, ALU.mult, None, ["gq"], ["gq"])
            lamc = self.sb("lamc", [128, 2], F32, st)
            self.dma(lamc[:], cin["lamc"][li], [], ["lamc"])
            self.ts(gsub[:], gsub[:], lamc[:, 1:2], None, ALU.mult, None, ["gsub", "lamc"], ["gsub"])
            lv = lamt[:].rearrange("p (a b d) -> p a b d", a=2, b=2)
            self.tt(lprod[:], lv[:, :, 0, :], lv[:, :, 1, :], ALU.mult, ["lamt"], ["lprod"])
            self.S.add("dve", lambda e: e.tensor_reduce(out=lsum[:], in_=lprod[:], axis=AX.X, op=ALU.add), ["lprod"], ["lsum"])
            self.act(lsum[:], lsum[:], AF.Exp, ["lsum"], ["lsum"])
            self.tt(neglam[:], lsum[:, 1:2], lsum[:, 0:1], ALU.subtract, ["lsum"], ["neglam"])
            self.ts(neglam[:], neglam[:], lamc[:, 0:1], None, ALU.add, None, ["neglam", "lamc"], ["neglam"])
            self.memset(vaug[:, :, :, 128:129], 1.0, ["vones"])
            with ExitStack() as st2:
                stage = self.sb("stgE", [128, 8, 512], F32, st2)
                wq = self.sb("wq", [128, 8, 512], BF16, st2)
                wk = self.sb("wk", [128, 8, 512], BF16, st2)
                wv = self.sb("wv", [128, 8, 512], BF16, st2)
                sqf = self.sb("sqf", [128, 512], F32, st2)
                xraw = self.sb("xraw", [128, 512], F32, st2)
                ss = self.sb("ss8", [128, 8], F32, st2)
                xn = self.sb("xn", [128, 512], F32, st2)
                t1 = self.sb("t1", [128, 512], F32, st2)
                t2 = self.sb("t2", [128, 512], F32, st2)
                xb = [self.sb("xb", [128, 512], BF16, st2) for _ in range(2)]
                ropeC = self.sb("ropeC", [128, 16, 64], F32, st2)
                ropeS = self.sb("ropeS", [128, 16, 64], F32, st2)
                psq = self.ps("psq", [128, 512], F32, st2)
                psk = self.ps("psk", [128, 512], F32, st2)
                psv = self.ps("psv", [128, 512], F32, st2)
                pst = [self.ps("pstE", [128, 512], BF16, st2) for _ in range(2)]
                self.dma(ropeC[:], cin["ropeC"], [], ["ropeC"])
                self.dma(ropeS[:], cin["ropeS"], [], ["ropeS"])
                self.load_w("wq", W["w_in"], O_DQ, 512, 8, wq, "wq", stage, "stgE")
                self.load_w("wk", W["w_in"], O_DK, 512, 8, wk, "wk", stage, "stgE")
                self.load_w("wv", W["w_in"], O_DV, 512, 8, wv, "wv", stage, "stgE")
                for i in range(NT):
                    tsl = slice(i * 128, (i + 1) * 128)
                    ht = hts(i)
                    for (w, wt, ps, pt) in ((wq, "wq", psq, "psq"), (wk, "wk", psk, "psk"), (wv, "wv", psv, "psv")):
                        for k in range(8):
                            self.mm(ps[:], hT[:, k, tsl], w[:, k, :], k == 0, k == 7, [ht, wt], [pt])
                    self.cp(vaug[:, i, :, 0:128], psv[:].rearrange("p (h d) -> p h d", h=4), ["psv"], [("vaug", i)], eng="act")
                    for qi, (ps, pt, g, gt, dst, dt) in enumerate(((psq, "psq", gq, "gq", None, "qT"), (psk, "psk", gk, "gk", kT, "kT"))):
                        self.cp(xraw[:], ps[:], [pt], ["xraw"], eng="act")
                        self.tt(sqf[:], xraw[:], xraw[:], ALU.mult, ["xraw"], ["sqf"])
                        self.S.add("dve", lambda e, : e.tensor_reduce(out=ss[:], in_=sqf[:].rearrange("p (g d) -> p g d", d=64),
                                                                     axis=AX.X, op=ALU.add), ["sqf"], ["ss8"])
                        self.rsqrt_le(ss[:], ss[:], 1.0 / 64, EPS, ["ss8"], ["ss8"])
                        self.tt(xn[:].rearrange("p (g d) -> p g d", d=64), xraw[:].rearrange("p (g d) -> p g d", d=64),
                                ss[:].unsqueeze(2).to_broadcast([128, 8, 64]), ALU.mult, ["xraw", "ss8"], ["xn"])
                        x_b = xb[qi]
                        xbt = "xb%d" % qi
                        if i >= 2:
                            self.tt(xn[:].rearrange("p (g d) -> p g d", d=64), xn[:].rearrange("p (g d) -> p g d", d=64),
                                    g[:].unsqueeze(1).to_broadcast([128, 8, 64]), ALU.mult, ["xn", gt], ["xn"])
                            lt = i - 2
                            self.tt(t1[:].rearrange("p (g d) -> p g d", d=64), xn[:].rearrange("p (g d) -> p g d", d=64),
                                    ropeC[:, lt, :].unsqueeze(1).to_broadcast([128, 8, 64]), ALU.mult, ["xn", "ropeC"], ["t1"])
                            xv = xn[:].rearrange("p (g r h d) -> p g r h d", g=8, r=2, h=2)
                            tv = t2[:].rearrange("p (g r h d) -> p g r h d", g=8, r=2, h=2)
                            sv = ropeS[:, lt, :].rearrange("p (r h d) -> p r h d", r=2, h=2)
                            self.tt(tv[:, :, :, 0, :], xv[:, :, :, 1, :], sv[:, :, 0, :].unsqueeze(1).to_broadcast([128, 8, 2, 16]),
                                    ALU.mult, ["xn", "ropeS"], ["t2"])
                            self.tt(tv[:, :, :, 1, :], xv[:, :, :, 0, :], sv[:, :, 1, :].unsqueeze(1).to_broadcast([128, 8, 2, 16]),
                                    ALU.mult, ["xn", "ropeS"], ["t2"])
                            self.tt(x_b[:], t1[:], t2[:], ALU.add, ["t1", "t2"], [xbt])
                        else:
                            self.tt(x_b[:].rearrange("p (g d) -> p g d", d=64), xn[:].rearrange("p (g d) -> p g d", d=64),
                                    g[:].unsqueeze(1).to_broadcast([128, 8, 64]), ALU.mult, ["xn", gt], [xbt])
                        pp = pst[qi]
                        ppt = "pstE%d" % qi
                        for h in range(4):
                            self.tr(pp[:, h * 128:(h + 1) * 128], x_b[:, h * 128:(h + 1) * 128], self.identB[:], [xbt, "identB"], [ppt])
                        if dst is None:
                            pv = pp[:].rearrange("p (h t) -> p h t", h=4)
                            self.cp(qT0[0:64, :, tsl], pv[0:64], [ppt], [("qT", i, 0)], eng="act")
                            self.cp(qT1[64:128, :, tsl], pv[64:128], [ppt], [("qT", i, 1)], eng="act")
                        else:
                            self.cp(dst[:, :, tsl], pp[:].rearrange("p (h t) -> p h t", h=4), [ppt], [(dt, i)], eng="act")
                S.barrier()
            with ExitStack() as st3:
                pT = [self.sb("pT", [128, NT, 512], BF16, st3) for _ in range(2)]
                o = [self.sb("oE", [128, 128], F32, st3) for _ in range(2)]
                osq = self.sb("osq", [128, 128], F32, st3)
                ob = [self.sb("obE", [128, 128], BF16, st3) for _ in range(2)]
                rec = self.sb("recE", [128, 8], F32, st3)
                yst = [self.sb("ystE", [128, 512], F32, st3) for _ in range(2)]
                pss = [self.ps("pssE", [128, 512], F32, st3) for _ in range(3)]
                acc = [self.ps("accE", [128, 129], F32, st3) for _ in range(2)]
                pso = [self.ps("psoE", [128, 512], BF16, st3) for _ in range(2)]
                qchunks = [(256 + 512 * j, 512, list(range(NT))) for j in range(4)]
                if with_ctx:
                    qchunks.append((0, 256, [0, 1]))
                items = [(h, t0, n, keys) for h in range(4) for (t0, n, keys) in qchunks]
                oo4 = [self.sb("oo4", [128, 4, 128], F32, st3) for _ in range(2)]
                cnt = [0]

                def Hhalf(it, m):
                    h, t0, n, keys = items[it]
                    qtk = [("qT", (t0 // 128) + j, m) for j in range(n // 128)] + ["qz%d" % m]
                    for kt in keys:
                        ps = pss[cnt[0] % 3]
                        pstk = "pssE%d" % (cnt[0] % 3)
                        cnt[0] += 1
                        self.mm(ps[:, 0:n], kT[:, h, kt * 128:(kt + 1) * 128], qTm[m][:, h, t0:t0 + n], True, True,
                                [("kT", kt)] + qtk, [pstk])
                        self.act(pT[m][:, kt, 0:n], ps[:, 0:n], AF.Exp, [pstk], [("pT", m, kt)])

                def Vhalf(it, m):
                    h, t0, n, keys = items[it]
                    o4 = oo4[it % 2]
                    ys = yst[it % 2]
                    yt = "ystE%d" % (it % 2)
                    pso_ = pso[it % 2]
                    psot = "psoE%d" % (it % 2)
                    for qs in range(n // 128):
                        a = acc[qs % 2]
                        at = "accE%d" % (qs % 2)
                        ot = ("oo4", it % 2, qs)
                        for j, kt in enumerate(keys):
                            self.mm(a[:], pT[m][:, kt, qs * 128:(qs + 1) * 128], vaug[:, kt, h, :], j == 0, j == len(keys) - 1,
                                    [("pT", m, kt), ("vaug", kt), "vones"], [at])
                        rk = ("rec", qs % 2, m)
                        rcol = rec[:, 4 * (qs % 2) + m:4 * (qs % 2) + m + 1]
                        self.S.add("dve", lambda e, a=a, rcol=rcol: e.reciprocal(out=rcol, in_=a[:, 128:129]), [at], [rk])
                        if m == 0:
                            self.ts(o4[:, qs, :], a[:, 0:128], rcol, None, ALU.mult, None, [at, rk], [ot])
                        else:
                            r2 = rec[:, 4 * (qs % 2) + 2:4 * (qs % 2) + 3]
                            r3 = rec[:, 4 * (qs % 2) + 3:4 * (qs % 2) + 4]
                            rk2, rk3 = ("rec", qs % 2, 2), ("rec", qs % 2, 3)
                            self.tt(r2, rcol, neglam[:], ALU.mult, [rk, "neglam"], [rk2])
                            self.stt(o4[:, qs, :], a[:, 0:128], r2, o4[:, qs, :], ALU.mult, ALU.add, [at, rk2, ot], [ot])
                            self.tt(osq[:], o4[:, qs, :], o4[:, qs, :], ALU.mult, [ot], ["osq"])
                            self.S.add("dve", lambda e, r3=r3: e.tensor_reduce(out=r3, in_=osq[:], axis=AX.X, op=ALU.add), ["osq"], [rk3])
                            self.rsqrt_le(r3, r3, 1.0 / 128, EPS, [rk3], [rk3])
                            obb = ob[qs % 2]
                            obt = "obE%d" % (qs % 2)
                            self.stt(obb[:], o4[:, qs, :], r3, gsub[:], ALU.mult, ALU.mult, [ot, rk3, "gsub"], [obt])
                            self.tr(pso_[:, qs * 128:(qs + 1) * 128], obb[:], self.identB[:], [obt, "identB"], [psot])
                    if m == 1:
                        self.cp(ys[:, 0:n], pso_[:, 0:n], [psot], [yt], eng="act")
                        self.dma(ybr_c[h * 128:(h + 1) * 128, t0:t0 + n], ys[:, 0:n], [yt], [("ybr", s, 2)])

                halves = [(it, m) for it in range(len(items)) for m in range(2)]
                for j in range(len(halves) + 2):
                    if j >= 2:
                        Vhalf(*halves[j - 2])
                    if j < len(halves):
                        Hhalf(*halves[j])
                S.barrier()
            S.barrier()


    def stage_mlstm(self, W, cin, ybr_b, s, li):
        S = self.S
        hT = self.hT
        def hts(i):
            t = i * 128
            for ci, (t0, n) in enumerate(TOKCH):
                if t0 <= t < t0 + n:
                    return ("hT", ci)
        order = [list(range(NT)), [1, 0] + list(range(NT - 1, 1, -1))]
        with ExitStack() as st:
            tokS = [self.sb("tokS", [128, NT, 12], F32, st) for _ in range(2)]
            decB = [self.sb("decB", [128, NT, 4], F32, st) for _ in range(2)]
            cw = self.sb("cw", [128, 8, 3], F32, st)
            cb = self.sb("cb", [128, 8], F32, st)
            gml = self.sb("gml", [128, 512], F32, st)
            id4 = self.identF[0:4, 0:4]
            stgH = [self.sb("stgD", [128, 8, 512], F32, st) for _ in range(2)]
            wbH = [self.sb("wbD", [128, 8, 512], BF16, st) for _ in range(2)]

            def load_head(hh):
                cols_ = [O_MQK + hh * 128, O_MQK + 512 + hh * 128, O_MV + hh * 128, O_MO + hh * 128]
                for j_, c0_ in enumerate(cols_):
                    self.dma(stgH[hh % 2][:, :, j_ * 128:(j_ + 1) * 128], W["w_in"].rearrange("(k p) c -> p k c", p=128)[:, :, c0_:c0_ + 128],
                             [], ["stgD%d" % (hh % 2)], q="pool")
                self.cp(wbH[hh % 2][:], stgH[hh % 2][:], ["stgD%d" % (hh % 2)], ["wbD%d" % (hh % 2)], eng="pool")
            load_head(0)
            for j in range(3):
                self.dma(cw[:, :, j], W["ml_conv_w"][j].rearrange("(f p) -> p f", p=128), [], ["cw"], slow=True)
            self.dma(cb[:], W["ml_conv_b"].rearrange("(f p) -> p f", p=128), [], ["cb"], slow=True)
            self.dma(gml[:], W["ml_norm_g"].rearrange("(o d) -> o d", o=1).to_broadcast([128, 512]), [], ["gml"])
            with ExitStack() as st2:
                T = [self.sb("gT", [4, LT], F32, st2) for _ in range(4)]
                ones4 = self.sb("ones4", [4, 1], F32, st2)
                gb = self.sb("gb", [4, 4], F32, st2)
                mend = self.sb("mend", [4, NT], F32, st2)
                dec = self.sb("dec", [4, NT], F32, st2)
                ddg = self.sb("ddg", [4, NT, 4], F32, st2)
                stage = self.sb("stgG", [128, 8, 16], F32, st2)
                wg = self.sb("wg", [128, 8, 16], BF16, st2)
                psg = [self.ps("psg", [4, 512], F32, st2) for _ in range(2)]
                pstk = self.ps("pstk", [128, NT, 12], F32, st2)
                psd = self.ps("psd", [128, NT * 4], F32, st2)
                self.memset(ones4[:], 1.0, ["ones4"])
                self.dma(gb[:], W["ml_gate_b"].rearrange("a h -> h a"), [], ["gb"], slow=True)
                self.dma(stage[:], W["w_in"].rearrange("(k p) c -> p k c", p=128)[:, :, O_MG:O_MG + 16], [], ["stgG"], q="pool")
                self.cp(wg[:], stage[:], ["stgG"], ["wg"], eng="pool")
                onesb = ones4[:].to_broadcast([4, LT])
                for d in range(2):
                    Ti, Tf, Tg, Tm = T
                    for typ, dst, dtok in ((2 * d, Ti, "gT0"), (2 * d + 1, Tf, "gT1")):
                        for ci, (t0, n) in enumerate(TOKCH):
                            ps = psg[ci % 2]
                            pt = "psg%d" % (ci % 2)
                            for k in range(8):
                                self.mm(ps[:, 0:n], wg[:, k, typ * 4:(typ + 1) * 4], hT[:, k, t0:t0 + n], k == 0, k == 7,
                                        ["wg", ("hT", ci)], [pt])
                            if d == 0:
                                o_ap = dst[:, t0:t0 + n]
                            else:
                                if t0 == 0:
                                    o_ap = dst[:, 255::-1] if True else None
                                else:
                                    hi = 2559 - t0
                                    lo = hi - n
                                    o_ap = dst[:, hi:lo:-1]
                            self.ts(o_ap, ps[:, 0:n], gb[:, typ:typ + 1], None, ALU.add, None, [pt, "gb"], [dtok])
                    self.act(Tf[:], Tf[:], AF.Sigmoid, ["gT1"], ["gT1"])
                    self.act(Tf[:], Tf[:], AF.Ln, ["gT1"], ["gT1"])
                    S.add("dve", lambda e, Tg=Tg, Tf=Tf: e.tensor_tensor_scan(out=Tg[:], data0=onesb, data1=Tf[:], initial=0.0,
                                                                             op0=ALU.mult, op1=ALU.add), ["gT1", "ones4"], ["gT2"])
                    self.tt(Ti[:], Ti[:], Tg[:], ALU.subtract, ["gT0", "gT2"], ["gT0"])
                    S.add("dve", lambda e, Tm=Tm, Ti=Ti: e.tensor_tensor_scan(out=Tm[:], data0=onesb, data1=Ti[:], initial=0.0,
                                                                             op0=ALU.mult, op1=ALU.max), ["gT0", "ones4"], ["gT3"])
                    self.cp(mend[:], Tm[:, 127::128], ["gT3"], ["mend"])
                    self.ts(dec[:, 0:1], mend[:, 0:1], -1.0, None, ALU.mult, None, ["mend"], ["dec"])
                    self.tt(dec[:, 1:NT], mend[:, 0:NT - 1], mend[:, 1:NT], ALU.subtract, ["mend"], ["dec"])
                    self.act(dec[:], dec[:], AF.Exp, ["dec"], ["dec"])
                    self.tt(Tf[:], Tg[:], Tm[:], ALU.add, ["gT2", "gT3"], ["gT1"])
                    self.act(Tf[:], Tf[:], AF.Exp, ["gT1"], ["gT1"], scale=-1.0)
                    mb = mend[:].unsqueeze(2).to_broadcast([4, NT, 128])
                    self.tt(Tg[:].rearrange("p (c j) -> p c j", j=128), Ti[:].rearrange("p (c j) -> p c j", j=128), mb, ALU.subtract,
                            ["gT0", "mend"], ["gT2"])
                    self.act(Tg[:], Tg[:], AF.Exp, ["gT2"], ["gT2"])
                    self.tt(Ti[:].rearrange("p (c j) -> p c j", j=128), mb, Tm[:].rearrange("p (c j) -> p c j", j=128), ALU.subtract,
                            ["gT3", "mend"], ["gT0"])
                    self.act(Ti[:], Ti[:], AF.Exp, ["gT0"], ["gT0"])
                    U, Rr, FL = Tg, Ti, Tf
                    ut, rt, ft = "gT2", "gT0", "gT1"
                    if d == 1:
                        def rev(dst, src, st_, dt_):
                            self.cp(dst[:, 0:256], src[:, 255::-1], [st_], [dt_])
                            self.cp(dst[:, 256:LT], src[:, LT - 1:255:-1], [st_], [dt_])
                        rev(Tm, U, "gT2", "gT3")
                        rev(Tg, Rr, "gT0", "gT2")
                        rev(Ti, FL, "gT1", "gT0")
                        U, Rr, FL = Tm, Tg, Ti
                        ut, rt, ft = "gT3", "gT2", "gT0"
                    for mc in range(NT):
                        for qi, (src, stok) in enumerate(((U, ut), (Rr, rt), (FL, ft))):
                            self.mm(pstk[:, mc, qi * 4:(qi + 1) * 4], src[:, mc * 128:(mc + 1) * 128], id4, True, True,
                                    [stok, "identF"], ["pstk"])
                    self.cp(tokS[d][:], pstk[:], ["pstk"], [("tokS", d)])
                    self.tt(ddg[:], dec[:].unsqueeze(2).to_broadcast([4, NT, 4]), id4.unsqueeze(1).to_broadcast([4, NT, 4]), ALU.mult,
                            ["dec", "identF"], ["ddg"])
                    self.mm(psd[:], self.onesF[0:4, :], ddg[:].rearrange("p c h -> p (c h)"), True, True, ["ddg", "onesF"], ["psd"])
                    self.cp(decB[d][:].rearrange("p c h -> p (c h)"), psd[:], ["psd"], [("decB", d)])
                S.barrier()
            for h in range(4):
                with ExitStack() as st3:
                    wb = wbH[h % 2]
                    wbt = "wbD%d" % (h % 2)
                    xr = self.sb("xr", [128, LT], F32, st3)
                    ac = self.sb("acD", [128, LT], F32, st3)
                    qh = self.sb("qh", [128, LT], BF16, st3)
                    kh = self.sb("kh", [128, LT], BF16, st3)
                    ktok = self.sb("ktok", [128, NT, 128], BF16, st3)
                    vh = self.sb("vh", [128, NT, 129], BF16, st3)
                    hacc = self.sb("hacc", [128, NT, 128], F32, st3)
                    hnum = [self.sb("hnum", [128, NT, 129], F32, st3) for _ in range(2)]
                    ep = self.sb("epD", [128, 2, NT], F32, st3)
                    hbt = self.sb("hbt", [128, NT, 128], BF16, st3)
                    Cstd = [self.sb("Cst", [128, 129], F32, st3) for _ in range(2)]
                    Cbfd = [self.sb("Cbf", [128, 129], BF16, st3) for _ in range(2)]
                    smd = [self.sb("smD", [128, 4], F32, st3) for _ in range(2)]
                    PT = [self.sb("PT", [128, 128], BF16, st3) for _ in range(2)]
                    Vs = [self.sb("Vs", [128, 129], BF16, st3) for _ in range(2)]
                    yst = [self.sb("ystD", [128, 512], F32, st3) for _ in range(2)]
                    psA = [self.ps("psDA", [128, 512], F32, st3) for _ in range(2)]
                    psT = self.ps("psDT", [128, 512], BF16, st3)
                    psS = [self.ps("psDS", [128, 128], F32, st3) for _ in range(2)]
                    psO = [self.ps("psDO", [128, 129], F32, st3) for _ in range(2)]
                    psC = self.ps("psDC", [128, 129], F32, st3)
                    if h + 1 < 4:
                        load_head(h + 1)
                    for j, (dst, dtok, f) in enumerate(((qh, "qh", h), (kh, "kh", 4 + h))):
                        for ci, (t0, n) in enumerate(TOKCH):
                            ps = psA[ci % 2]
                            pt = "psDA%d" % (ci % 2)
                            for k in range(8):
                                self.mm(ps[:, 0:n], wb[:, k, j * 128:(j + 1) * 128], hT[:, k, t0:t0 + n], k == 0, k == 7,
                                        [wbt, ("hT", ci)], [pt])
                            self.cp(xr[:, t0:t0 + n], ps[:, 0:n], [pt], ["xr"], eng="act")
                        self.ts(ac[:], xr[:], cw[:, f, 1:2], cb[:, f:f + 1], ALU.mult, ALU.add, ["xr", "cw", "cb"], ["acD"])
                        for (a0, a1) in ((0, 256), (256, LT)):
                            self.stt(ac[:, a0 + 1:a1], xr[:, a0:a1 - 1], cw[:, f, 0:1], ac[:, a0 + 1:a1], ALU.mult, ALU.add,
                                     ["xr", "cw", "acD"], ["acD"])
                            self.stt(ac[:, a0:a1 - 1], xr[:, a0 + 1:a1], cw[:, f, 2:3], ac[:, a0:a1 - 1], ALU.mult, ALU.add,
                                     ["xr", "cw", "acD"], ["acD"])
                        if j == 0:
                            self.act(dst[:], ac[:], AF.Silu, ["acD"], [dtok])
                        else:
                            self.act(ac[:], ac[:], AF.Silu, ["acD"], ["acD"])
                            self.ts(dst[:], ac[:], 128.0 ** -0.5, None, ALU.mult, None, ["acD"], [dtok])
                    for g0 in range(0, NT, 4):
                        nn = min(4, NT - g0)
                        for j in range(nn):
                            i = g0 + j
                            self.tr(psT[:, j * 128:(j + 1) * 128], kh[:, i * 128:(i + 1) * 128], self.identB[:], ["kh", "identB"], ["psDT"])
                        self.cp(ktok[:, g0:g0 + nn, :], psT[:, 0:nn * 128].rearrange("p (a b) -> p a b", b=128), ["psDT"], ["ktok"], eng="act")
                    self.memset(vh[:, :, 128:129], 1.0, ["vh1"])
                    for g0 in range(0, NT, 4):
                        nn = min(4, NT - g0)
                        ps = psA[(g0 // 4) % 2]
                        pt = "psDA%d" % ((g0 // 4) % 2)
                        for j in range(nn):
                            i = g0 + j
                            for k in range(8):
                                self.mm(ps[:, j * 128:(j + 1) * 128], hT[:, k, i * 128:(i + 1) * 128], wb[:, k, 256:384], k == 0, k == 7,
                                        [wbt, hts(i)], [pt])
                        self.cp(vh[:, g0:g0 + nn, 0:128], ps[:, 0:nn * 128].rearrange("p (a b) -> p a b", b=128), [pt], ["vh"], eng="act")
                    for d in range(2):
                        self.memset(Cstd[d][:], 0.0, ["Cst%d" % d])

                    def mstep(d, c):
                        mc = order[d][c]
                        mask = self.triL if d == 0 else self.triU
                        Cst, Cbf, PT_, Vs_, sm = Cstd[d], Cbfd[d], PT[d], Vs[d], smd[d]
                        pS, pO = psS[d], psO[d]
                        cst, cbf, ptt, vst, pst_, pot = "Cst%d" % d, "Cbf%d" % d, "PT%d" % d, "Vs%d" % d, "psDS%d" % d, "psDO%d" % d
                        tsl = slice(mc * 128, (mc + 1) * 128)
                        self.mm(pS[:], kh[:, tsl], qh[:, tsl], True, True, ["kh", "qh"], [pst_])
                        self.tt(PT_[:], pS[:], mask[:], ALU.mult, [pst_, "triL", "triU"], [ptt])
                        self.act(Vs_[:], vh[:, mc, :], AF.Identity, ["vh", "vh1", ("tokS", d)], [vst], scale=tokS[d][:, mc, h:h + 1])
                        self.ts(Cst[:], Cst[:], decB[d][:, c, h:h + 1], None, ALU.mult, None, [cst, ("decB", d)], [cst])
                        self.cp(Cbf[:], Cst[:], [cst], [cbf], eng="pool")
                        self.mm(pO[:], PT_[:], Vs_[:], True, False, [ptt, vst], [pot])
                        self.mm(pO[:], qh[:, tsl], Cbf[:], False, True, ["qh", cbf], [pot])
                        self.mm(psC[:], ktok[:, mc, :], Vs_[:], True, True, ["ktok", vst], ["psDC"])
                        self.tt(Cst[:], Cst[:], psC[:], ALU.add, [cst, "psDC"], [cst])
                        self.cp(hnum[d][:, mc, :], pO[:], [pot], [("hnum", d, mc)], eng="act")

                    for c in range(NT):
                        for d in range(2):
                            mstep(d, c)
                    sm = smd[0]
                    for d in range(2):
                        hall = [("hnum", d, i) for i in range(NT)]
                        r = tokS[d][:, :, 4 + h]
                        fl = tokS[d][:, :, 8 + h]
                        e0, e1 = ep[:, 0, :], ep[:, 1, :]
                        self.tt(e0, hnum[d][:, :, 128], r, ALU.mult, hall + [("tokS", d)], ["ep0"])
                        self.ts(e1, e0, -1.0, None, ALU.mult, None, ["ep0"], ["ep1"])
                        self.tt(e0, e0, e1, ALU.max, ["ep0", "ep1"], ["ep0"])
                        self.tt(e0, e0, fl, ALU.max, ["ep0", ("tokS", d)], ["ep0"])
                        S.add("dve", lambda e, e0=e0: e.reciprocal(out=e0, in_=e0), ["ep0"], ["ep0"])
                        self.tt(e0, e0, r, ALU.mult, ["ep0", ("tokS", d)], ["ep0"])
                        fb = e0.unsqueeze(2).to_broadcast([128, NT, 128])
                        if d == 0:
                            self.tt(hacc[:], hnum[0][:, :, 0:128], fb, ALU.mult, hall + ["ep0"], [("hacc", i) for i in range(NT)])
                        else:
                            self.tt(hnum[1][:, :, 0:128], hnum[1][:, :, 0:128], fb, ALU.mult, hall + ["ep0"], hall)
                            self.tt(hacc[:], hacc[:], hnum[1][:, :, 0:128], ALU.add, hall + [("hacc", i) for i in range(NT)],
                                    [("hacc", i) for i in range(NT)], eng="pool")
                    hall = [("hacc", i) for i in range(NT)]
                    sq3 = hnum[0][:, :, 0:128]
                    so3 = hnum[1][:, :, 0:128]
                    for i in range(NT):
                        ps = psA[i % 2]
                        pt = "psDA%d" % (i % 2)
                        for k in range(8):
                            self.mm(ps[:, 0:128], hT[:, k, i * 128:(i + 1) * 128], wb[:, k, 384:512], k == 0, k == 7, [wbt, hts(i)], [pt])
                        self.act(so3[:, i, :], ps[:, 0:128], AF.Sigmoid, [pt], [("so3", i)] + [("hnum", 1, j) for j in range(NT)])
                    self.tt(sq3, hacc[:], hacc[:], ALU.mult, hall, ["sq3"] + [("hnum", 0, j) for j in range(NT)])
                    S.add("dve", lambda e, sq3=sq3, ep=ep: e.tensor_reduce(out=ep[:, 0, :], in_=sq3, axis=AX.X, op=ALU.add), ["sq3"], ["ep0"])
                    self.rsqrt(ep[:, 0, :], ep[:, 0, :], 1.0 / 128, EPS, ["ep0"], ["ep0"])
                    self.tt(hacc[:], hacc[:], ep[:, 0, :].unsqueeze(2).to_broadcast([128, NT, 128]), ALU.mult, hall + ["ep0"], hall)
                    self.tt(hacc[:], hacc[:], gml[:, h * 128:(h + 1) * 128].unsqueeze(1).to_broadcast([128, NT, 128]), ALU.mult,
                            hall + ["gml"], hall, eng="pool")
                    self.tt(hbt[:], hacc[:], so3, ALU.mult, hall + [("so3", i) for i in range(NT)], ["hbt"])
                    for g0 in range(0, NT, 4):
                        nn = min(4, NT - g0)
                        ys = yst[(g0 // 4) % 2]
                        yt = "ystD%d" % ((g0 // 4) % 2)
                        for j in range(nn):
                            i = g0 + j
                            self.tr(psT[:, j * 128:(j + 1) * 128], hbt[:, i, :], self.identB[:], ["hbt", "identB"], ["psDT"])
                        self.cp(ys[:, 0:nn * 128], psT[:, 0:nn * 128], ["psDT"], [yt], eng="act")
                        self.dma(ybr_b[h * 128:(h + 1) * 128, g0 * 128:(g0 + nn) * 128], ys[:, 0:nn * 128], [yt], [("ybr", s, 1)])
                    S.barrier()
            S.barrier()


    def cmul(self, ore, oim, are, aim, bre, bim, ts4, rtoks, wtok, tk="cm", pool_one=False):
        t1, t2, t3, t4 = ts4
        k = [tk + "_t%d" % i for i in range(4)]
        self.tt(t2, aim, bim, ALU.mult, rtoks, [k[1]], eng="pool" if pool_one else "dve")
        self.tt(t1, are, bre, ALU.mult, rtoks, [k[0]])
        self.tt(t3, are, bim, ALU.mult, rtoks, [k[2]])
        self.tt(t4, aim, bre, ALU.mult, rtoks, [k[3]])
        self.tt(oim, t3, t4, ALU.add, [k[2], k[3]], [wtok + "_im"])
        self.tt(ore, t1, t2, ALU.subtract, [k[0], k[1]], [wtok + "_re"])

    def stage_s5(self, W, cin, ybr_a, s, li):
        S = self.S
        hT = self.hT
        order = [list(range(NT)), [1, 0] + list(range(NT - 1, 1, -1))]
        if self.dbg.get("s5_stop") == "none":
            return
        with ExitStack() as st:
            gel = self.sb("gel", [128, 4, LT], BF16, st)
            wsu = self.sb("wsu", [128, 8, 512], BF16, st)
            dsk = self.sb("dsk", [128, 4], F32, st)
            with ExitStack() as st0:
                stage = self.sb("stgC", [128, 8, 512], F32, st0)
                self.load_w("wsu", W["w_in"], O_SU, 512, 8, wsu, "wsu", stage, "stgC")
                S.barrier()
            self.dma(dsk[:], W["ssm_d"].rearrange("(c p) -> p c", p=128), [], ["dsk"], slow=True)
            for c in range(self.dbg.get("s5_nc", 4)):
                with ExitStack() as st2:
                    suT = self.sb("suT", [128, LT], BF16, st2)
                    yacc = self.sb("yacc", [128, LT], F32, st2)
                    sc = self.sb("s5sc", [128, 16, 4], F32, st2)
                    N = self.sb("s5N", [128, 2, 4, 128], F32, st2)
                    Nr = self.sb("s5Nr", [128, 2, 4, 128], F32, st2)
                    braw = self.sb("s5braw", [128, 2, 4, 16], F32, st2)
                    bbar = self.sb("s5bbar", [128, 2, 4, 16], F32, st2)
                    bt = self.sb("s5bt", [128, 2, 4, 16], F32, st2)
                    Zp = self.sb("s5Zp", [128, 2, 4, 128], F32, st2)
                    Yp = self.sb("s5Yp", [128, 2, 4, 128], F32, st2)
                    Pd = [self.sb("s5P", [128, 2, 4, 128], F32, st2) for _ in range(2)]
                    Eitd = [self.sb("s5Eit", [128, 1024], F32, st2) for _ in range(2)]
                    Bbdd = [self.sb("s5Bbd", [128, 1024], BF16, st2) for _ in range(2)]
                    Cbdd = [self.sb("s5Cbd", [128, 2, 4, 128], BF16, st2) for _ in range(2)]
                    Wbd = [self.sb("s5Wb", [128, 1024], BF16, st2) for _ in range(2)]
                    t13d = [self.sb("s5t13", [128, 1024], F32, st2) for _ in range(4)]
                    t24d = [self.sb("s5t24", [128, 1024], F32, st2) for _ in range(4)]
                    Psd = [self.sb("s5Ps", [128, 2, 4, 128], F32, st2) for _ in range(2)]
                    Eisd = [self.sb("s5Eis", [128, 1024], F32, st2) for _ in range(2)]
                    Zcd = [self.sb("s5Zc", [128, 1024], F32, st2) for _ in range(2)]
                    carry = [[self.sb("s5cy", [128, 8], F32, st2) for _ in range(2)] for _ in range(2)]
                    Xbd = [self.sb("s5Xb", [128, 1024], BF16, st2) for _ in range(2)]
                    psA = [self.ps("psCA", [128, 512], F32, st2) for _ in range(2)]
                    psZd = [[self.ps("psCZ", [128, 512], F32, st2) for _ in range(2)] for _ in range(2)]
                    psYd = [self.ps("psCY", [128, 128], F32, st2) for _ in range(2)]
                    t1, t2 = t13d[0], t24d[0]
                    for ci, (t0, n) in enumerate(TOKCH):
                        ps = psA[ci % 2]
                        pt = "psCA%d" % (ci % 2)
                        for k in range(8):
                            self.mm(ps[:, 0:n], wsu[:, k, c * 128:(c + 1) * 128], hT[:, k, t0:t0 + n], k == 0, k == 7,
                                    ["wsu", ("hT", ci)], [pt])
                        self.cp(suT[:, t0:t0 + n], ps[:, 0:n], [pt], ["suT"], eng="act")
                        self.ts(yacc[:, t0:t0 + n], ps[:, 0:n], dsk[:, c:c + 1], None, ALU.mult, None, [pt, "dsk"],
                                [("yacc", j) for j in range(t0 // 128, (t0 + n) // 128)])
                    for d in range(2):
                        P, Eit, Bbd, Cbd = Pd[d], Eitd[d], Bbdd[d], Cbdd[d]
                        ptk, etk, btk, ctk = "s5P%d" % d, "s5Eit%d" % d, "s5Bbd%d" % d, "s5Cbd%d" % d
                        psT = psZd[d]
                        psTt = ["psCZ%d%d" % (d, 0), "psCZ%d%d" % (d, 1)]
                        def col(i):
                            return sc[:, i, :]
                        LRE, LIM, DT, LDR, LDI, MAG, C8, S8, ABR, ABI, DEN, CR, CI, TA, TB, TC = [col(i) for i in range(16)]
                        gs = slice(8 * c, 8 * c + 8)
                        self.dma(LRE, W["ssm_lam_re"][d, gs].rearrange("(k g) p -> (g p) k", g=2), [], ["sc"], slow=True)
                        self.dma(LIM, W["ssm_lam_im"][d, gs].rearrange("(k g) p -> (g p) k", g=2), [], ["sc"], slow=True)
                        for g2 in range(2):
                            src = W["ssm_log_step"][d, gs].rearrange("(o k g) -> o g k", o=1, g=2)[:, g2, :]
                            self.dma(sc[g2 * 64:(g2 + 1) * 64, 2, :], src.to_broadcast([64, 4]), [], ["sc"], slow=True)
                        T_ = ["sc"]
                        self.ts(LRE, LRE, -1e-4, None, ALU.min, None, T_, T_)
                        self.act(DT, DT, AF.Exp, T_, T_)
                        self.tt(LDR, LRE, DT, ALU.mult, T_, T_)
                        self.tt(LDI, LIM, DT, ALU.mult, T_, T_)
                        self.act(MAG, LDR, AF.Exp, T_, T_)
                        self.act(S8, LDI, AF.Sin, T_, T_, scale=1.0 / 16)
                        self.act(TA, LDI, AF.Sin, T_, T_, scale=1.0 / 32)
                        self.tt(TA, TA, TA, ALU.mult, T_, T_)
                        self.ts(C8, TA, -2.0, 1.0, ALU.mult, ALU.add, T_, T_)
                        for _ in range(4):
                            self.tt(TA, C8, C8, ALU.mult, T_, T_)
                            self.tt(TB, S8, S8, ALU.mult, T_, T_)
                            self.tt(TC, C8, S8, ALU.mult, T_, T_)
                            self.tt(C8, TA, TB, ALU.subtract, T_, T_)
                            self.ts(S8, TC, 2.0, None, ALU.mult, None, T_, T_)
                        self.tt(ABR, MAG, C8, ALU.mult, T_, T_)
                        self.tt(ABI, MAG, S8, ALU.mult, T_, T_)
                        self.tt(TA, LRE, LRE, ALU.mult, T_, T_)
                        self.tt(TB, LIM, LIM, ALU.mult, T_, T_)
                        self.tt(DEN, TA, TB, ALU.add, T_, T_)
                        S.add("dve", lambda e, DEN=DEN: e.reciprocal(out=DEN, in_=DEN), T_, T_)
                        self.ts(TC, ABR, -1.0, None, ALU.add, None, T_, T_)
                        self.tt(TA, TC, LRE, ALU.mult, T_, T_)
                        self.tt(TB, ABI, LIM, ALU.mult, T_, T_)
                        self.tt(CR, TA, TB, ALU.add, T_, T_)
                        self.tt(CR, CR, DEN, ALU.mult, T_, T_)
                        self.tt(TA, ABI, LRE, ALU.mult, T_, T_)
                        self.tt(TB, TC, LIM, ALU.mult, T_, T_)
                        self.tt(CI, TA, TB, ALU.subtract, T_, T_)
                        self.tt(CI, CI, DEN, ALU.mult, T_, T_)
                        self.tt(TA, MAG, MAG, ALU.mult, T_, T_)
                        S.add("dve", lambda e, TA=TA: e.reciprocal(out=TA, in_=TA), T_, T_)
                        self.tt(LDR, ABR, TA, ALU.mult, T_, T_)
                        self.tt(LDI, ABI, TA, ALU.mult, T_, T_)
                        self.ts(LDI, LDI, -1.0, None, ALU.mult, None, T_, T_)
                        for (Tb, ar, ai, tk) in ((P, ABR, ABI, ptk), (N, LDR, LDI, "s5N")):
                            for kq in range(4):
                                self.cp(Tb[:, 0, kq, 0:1], ar[:, kq:kq + 1], T_, [tk])
                                self.cp(Tb[:, 1, kq, 0:1], ai[:, kq:kq + 1], T_, [tk])
                            L = 1
                            tv1 = t1[:, 0:512].rearrange("p (k j) -> p k j", j=128)
                            tv2 = t2[:, 0:512].rearrange("p (k j) -> p k j", j=128)
                            while L < 128:
                                mr = Tb[:, 0, :, L - 1:L].to_broadcast([128, 4, L])
                                mi = Tb[:, 1, :, L - 1:L].to_broadcast([128, 4, L])
                                sr = Tb[:, 0, :, 0:L]
                                si = Tb[:, 1, :, 0:L]
                                dr = Tb[:, 0, :, L:2 * L]
                                di = Tb[:, 1, :, L:2 * L]
                                self.tt(tv1[:, :, 0:L], si, mi, ALU.mult, [tk], ["s5t1"])
                                self.tt(tv2[:, :, 0:L], sr, mr, ALU.mult, [tk], ["s5t2"])
                                self.tt(dr, tv2[:, :, 0:L], tv1[:, :, 0:L], ALU.subtract, ["s5t1", "s5t2"], [tk])
                                self.tt(tv1[:, :, 0:L], si, mr, ALU.mult, [tk], ["s5t1"])
                                self.tt(tv2[:, :, 0:L], sr, mi, ALU.mult, [tk], ["s5t2"])
                                self.tt(di, tv2[:, :, 0:L], tv1[:, :, 0:L], ALU.add, ["s5t1", "s5t2"], [tk])
                                L *= 2
                        if d == 0:
                            Nsrc, ntk = N, "s5N"
                        else:
                            self.cp(Nr[:].rearrange("p a k j -> p (a k) j"), N[:].rearrange("p a k j -> p (a k) j")[:, :, ::-1], ["s5N"], ["s5Nr"])
                            Nsrc, ntk = Nr, "s5Nr"
                        for part in range(2):
                            for kq in range(4):
                                self.tr(psT[part][:, kq * 128:(kq + 1) * 128], Nsrc[:, part, kq, :], self.identF[:], [ntk, "identF"], [psTt[part]])
                            self.cp(Eit[:, part * 512:(part + 1) * 512], psT[part][:], [psTt[part]], [etk], eng="act")
                        self.ts(Eisd[d][:, 0:512], Eit[:, 512:1024], -1.0, None, ALU.mult, None, [etk], ["s5Eis%d" % d])
                        self.cp(Eisd[d][:, 512:1024], Eit[:, 0:512], [etk], ["s5Eis%d" % d], eng="pool")
                        self.ts(Psd[d][:, 0], P[:, 1], -1.0, None, ALU.mult, None, [ptk], ["s5Ps%d" % d])
                        self.cp(Psd[d][:, 1], P[:, 0], [ptk], ["s5Ps%d" % d], eng="pool")
                        self.dma(braw[:, 0], W["ssm_b_re"][d, gs].rearrange("(k g) p m -> (g p) k m", g=2), [], ["s5braw"], slow=True)
                        self.dma(braw[:, 1], W["ssm_b_im"][d, gs].rearrange("(k g) p m -> (g p) k m", g=2), [], ["s5braw"], slow=True)
                        crb = CR.unsqueeze(2).to_broadcast([128, 4, 16])
                        cib = CI.unsqueeze(2).to_broadcast([128, 4, 16])
                        self.tt(bbar[:, 0], braw[:, 0], crb, ALU.mult, ["s5braw", "sc"], ["s5bbar"])
                        self.tt(bt[:, 0], braw[:, 1], cib, ALU.mult, ["s5braw", "sc"], ["s5bt"])
                        self.tt(bbar[:, 0], bbar[:, 0], bt[:, 0], ALU.subtract, ["s5bbar", "s5bt"], ["s5bbar"])
                        self.tt(bbar[:, 1], braw[:, 1], crb, ALU.mult, ["s5braw", "sc"], ["s5bbar"])
                        self.tt(bt[:, 1], braw[:, 0], cib, ALU.mult, ["s5braw", "sc"], ["s5bt"])
                        self.tt(bbar[:, 1], bbar[:, 1], bt[:, 1], ALU.add, ["s5bbar", "s5bt"], ["s5bbar"])
                        self.memset(Zp[:], 0.0, ["s5Zp"])
                        self.memset(Yp[:], 0.0, ["s5Yp"], eng="pool")
                        for part in range(2):
                            for kq in range(4):
                                for g2 in range(2):
                                    rs_ = slice(g2 * 64, (g2 + 1) * 64)
                                    c0 = 32 * kq + 16 * g2
                                    self.cp(Zp[rs_, part, kq, c0:c0 + 16], bbar[rs_, part, kq, :], ["s5bbar"], ["s5Zp"])
                        for part in range(2):
                            for kq in range(4):
                                self.tr(psT[part][:, kq * 128:(kq + 1) * 128], Zp[:, part, kq, :], self.identF[:], ["s5Zp", "identF"], [psTt[part]])
                            self.cp(Bbd[:, part * 512:(part + 1) * 512], psT[part][:], [psTt[part]], [btk], eng="act")
                        for part, nm in enumerate(("ssm_c_re", "ssm_c_im")):
                            for kq in range(4):
                                for g2 in range(2):
                                    g = 8 * c + 2 * kq + g2
                                    r0 = 32 * kq + 16 * g2
                                    self.dma(Yp[r0:r0 + 16, part, kq, 64 * g2:64 * g2 + 64], W[nm][d, g], [], ["s5Yp"])
                        for part in range(2):
                            for kq in range(4):
                                self.tr(psT[part][:, kq * 128:(kq + 1) * 128], Yp[:, part, kq, :], self.identF[:], ["s5Yp", "identF"], [psTt[part]])
                            if part == 0:
                                self.cp(Cbd[:, 0].rearrange("p k j -> p (k j)"), psT[0][:], [psTt[0]], [ctk], eng="act")
                            else:
                                self.ts(Cbd[:, 1].rearrange("p k j -> p (k j)"), psT[1][:], -1.0, None, ALU.mult, None, [psTt[1]], [ctk])
                    def half1(d, ci_):
                        mc = order[d][ci_]
                        tsl = slice(mc * 128, (mc + 1) * 128)
                        Ei, Eis, Bbd, Wb = Eitd[d], Eisd[d], Bbdd[d], Wbd[d]
                        tri = self.triL if d == 0 else self.triU
                        for part in range(2):
                            self.mm(psA[part][:], suT[:, tsl], Bbd[:, part * 512:(part + 1) * 512], True, True, ["suT", "s5Bbd%d" % d], ["psCA%d" % part])
                        v2 = lambda a: a.rearrange("p (a b) -> p a b", a=2)
                        ta, tb = t13d[d], t24d[d]
                        ka, kb = "s5t13_%d" % d, "s5t24_%d" % d
                        self.tt(v2(ta[:]), psA[0][:].unsqueeze(1).to_broadcast([128, 2, 512]), v2(Ei[:]), ALU.mult, ["psCA0", "s5Eit%d" % d], [ka])
                        self.tt(v2(tb[:]), psA[1][:].unsqueeze(1).to_broadcast([128, 2, 512]), v2(Eis[:]), ALU.mult, ["psCA1", "s5Eis%d" % d], [kb])
                        self.tt(Wb[:], ta[:], tb[:], ALU.add, [ka, kb], ["s5Wb%d" % d])
                        for part in range(2):
                            for kq in range(4):
                                self.mm(psZd[d][part][:, kq * 128:(kq + 1) * 128], Wb[:, part * 512 + kq * 128:part * 512 + (kq + 1) * 128], tri[:],
                                        True, True, ["s5Wb%d" % d, "triL", "triU"], ["psCZ%d%d" % (d, part)])

                    v4 = lambda a: a.rearrange("p (a k j) -> p a k j", a=2, k=4)

                    def half2a(d, ci_):
                        P, Ps, Zc = Pd[d], Psd[d], Zcd[d]
                        cp_ = carry[d][(ci_ + 1) % 2]
                        cpt = "s5cy%d%d" % (d, (ci_ + 1) % 2)
                        zct = "s5Zc%d" % d
                        for part in range(2):
                            for kq in range(4):
                                o0 = part * 512 + kq * 128
                                self.act(Zc[:, o0:o0 + 128], psZd[d][part][:, kq * 128:(kq + 1) * 128], AF.Identity,
                                         ["psCZ%d%d" % (d, part), cpt], [zct], bias=cp_[:, part * 4 + kq:part * 4 + kq + 1])
                        if d == 0:
                            pa, pb = P[:], Ps[:]
                        else:
                            pa, pb = P[:, :, :, ::-1], Ps[:, :, :, ::-1]
                        zr = Zc[:, 0:512].rearrange("p (k j) -> p k j", j=128).unsqueeze(1).to_broadcast([128, 2, 4, 128])
                        zi = Zc[:, 512:1024].rearrange("p (k j) -> p k j", j=128).unsqueeze(1).to_broadcast([128, 2, 4, 128])
                        ta, tb = t13d[2 + d], t24d[2 + d]
                        ka, kb = "s5t13_%d" % (2 + d), "s5t24_%d" % (2 + d)
                        self.tt(v4(tb[:]), zi, pb, ALU.mult, [zct, "s5Ps%d" % d], [kb], eng="pool")
                        self.tt(v4(ta[:]), zr, pa, ALU.mult, [zct, "s5P%d" % d], [ka])

                    def half2b(d, ci_):
                        Cbd, Xb = Cbdd[d], Xbd[d]
                        jc = 127 if d == 0 else 0
                        cn_ = carry[d][ci_ % 2]
                        cnt_ = "s5cy%d%d" % (d, ci_ % 2)
                        ta, tb = t13d[2 + d], t24d[2 + d]
                        ka, kb = "s5t13_%d" % (2 + d), "s5t24_%d" % (2 + d)
                        c8 = lambda a: a.rearrange("p (q j) -> p q j", j=128)[:, :, jc]
                        self.tt(cn_[:], c8(ta[:]), c8(tb[:]), ALU.add, [ka, kb], [cnt_])
                        self.tt(Xb[:], ta[:], tb[:], ALU.add, [ka, kb], ["s5Xb%d" % d])
                        n8 = 0
                        for part in range(2):
                            for kq in range(4):
                                o0 = part * 512 + kq * 128
                                self.mm(psYd[d][:], Cbd[:, part, kq, :], Xb[:, o0:o0 + 128], n8 == 0, n8 == 7, ["s5Cbd%d" % d, "s5Xb%d" % d], ["psCY%d" % d])
                                n8 += 1

                    def yadd(d, ci_):
                        mc = order[d][ci_]
                        tsl = slice(mc * 128, (mc + 1) * 128)
                        self.tt(yacc[:, tsl], yacc[:, tsl], psYd[d][:], ALU.add, [("yacc", mc), "psCY%d" % d], [("yacc", mc)])

                    for d in range(2):
                        self.memset(carry[d][1][:], 0.0, ["s5cy%d1" % d])
                    half1(0, 0)
                    half1(1, 0)
                    pend = []
                    for ci_ in range(NT):
                        for d in range(2):
                            half2a(d, ci_)
                            if ci_ + 1 < NT:
                                half1(d, ci_ + 1)
                            while pend:
                                yadd(*pend.pop(0))
                            half2b(d, ci_)
                            pend.append((d, ci_))
                    while pend:
                        yadd(*pend.pop(0))
                    Zc = Zcd[0]
                    yall = [("yacc", j) for j in range(NT)]
                    for a0 in range(0, LT, 1024):
                        a1 = min(LT, a0 + 1024)
                        w_ = a1 - a0
                        self.tt(Zc[:, 0:w_], yacc[:, a0:a1], yacc[:, a0:a1], ALU.mult, yall, ["s5Zc0"])
                        self.ts(Zc[:, 0:w_], Zc[:, 0:w_], 0.044715, 1.0, ALU.mult, ALU.add, ["s5Zc0"], ["s5Zc0"])
                        self.tt(Zc[:, 0:w_], Zc[:, 0:w_], yacc[:, a0:a1], ALU.mult, ["s5Zc0"] + yall, ["s5Zc0"])
                        self.act(Zc[:, 0:w_], Zc[:, 0:w_], AF.Sigmoid, ["s5Zc0"], ["s5Zc0"], scale=1.5957691216057308)
                        self.tt(gel[:, c, a0:a1], Zc[:, 0:w_], yacc[:, a0:a1], ALU.mult, ["s5Zc0"] + yall, [("gel", c)])
                    S.barrier()
            if self.dbg.get("s5_noglu"):
                S.barrier()
                return
            with ExitStack() as st4:
                stage = self.sb("stgC", [128, 8, 512], F32, st4)
                wgl = [self.sb("wglu", [128, 4, 512], BF16, st4) for _ in range(2)]
                gbias = self.sb("gbias", [128, 8], F32, st4)
                sg = [self.sb("sgC", [128, 512], F32, st4) for _ in range(2)]
                yst = [self.sb("ystC", [128, 512], F32, st4) for _ in range(2)]
                psa = [self.ps("psGa", [128, 512], F32, st4) for _ in range(2)]
                psg = [self.ps("psGg", [128, 512], F32, st4) for _ in range(2)]
                self.dma(gbias[:], W["ssm_glu_b"].rearrange("(j p) -> p j", p=128), [], ["gbias"], slow=True)
                for half in range(2):
                    self.load_w("wglu", W["ssm_glu_w"], half * 512, 512, 4, wgl[half], "wglu%d" % half, stage, "stgC")
                cnt = 0
                for j in range(4):
                    for ci, (t0, n) in enumerate(TOKCH):
                        i2 = cnt % 2
                        cnt += 1
                        for k in range(4):
                            self.mm(psa[i2][:, 0:n], wgl[0][:, k, j * 128:(j + 1) * 128], gel[:, k, t0:t0 + n], k == 0, k == 3,
                                    ["wglu0", ("gel", k)], ["psGa%d" % i2])
                        for k in range(4):
                            self.mm(psg[i2][:, 0:n], wgl[1][:, k, j * 128:(j + 1) * 128], gel[:, k, t0:t0 + n], k == 0, k == 3,
                                    ["wglu1", ("gel", k)], ["psGg%d" % i2])
                        self.act(sg[i2][:, 0:n], psg[i2][:, 0:n], AF.Sigmoid, ["psGg%d" % i2, "gbias"], ["sgC%d" % i2], bias=gbias[:, 4 + j:5 + j])
                        self.stt(yst[i2][:, 0:n], psa[i2][:, 0:n], gbias[:, j:j + 1], sg[i2][:, 0:n], ALU.add, ALU.mult,
                                 ["psGa%d" % i2, "gbias", "sgC%d" % i2], ["ystC%d" % i2])
                        self.dma(ybr_a[j * 128:(j + 1) * 128, t0:t0 + n], yst[i2][:, 0:n], ["ystC%d" % i2], [("ybr", s, 0)])
                S.barrier()
            S.barrier()

CONST_SHAPES = {"ident": [128, 128], "ones": [128, 128], "tril": [128, 128], "triu": [128, 128],
                "ropeC": [128, 16, 64], "ropeS": [128, 16, 64], "lamc": [DEPTH, 128, 2]}
LAYER_W = [("norm_g", [D]), ("ada_w", [D, 3 * D]), ("ada_b", [3 * D]), ("w_in", [D, D_IN]),
           ("w_ssm_out", [512, D]), ("w_ml_out", [512, D]), ("w_da_out", [512, D]), ("w_out", [D, D]),
           ("da_qnorm_g", [64]), ("da_knorm_g", [64]), ("da_lambda", [4, 64]), ("da_subln_g", [128]),
           ("ml_conv_w", [3, 1024]), ("ml_conv_b", [1024]), ("ml_gate_b", [4, 4]), ("ml_norm_g", [512]),
           ("ssm_lam_re", [2, 32, 64]), ("ssm_lam_im", [2, 32, 64]), ("ssm_log_step", [2, 32]),
           ("ssm_b_re", [2, 32, 64, 16]), ("ssm_b_im", [2, 32, 64, 16]), ("ssm_c_re", [2, 32, 16, 64]),
           ("ssm_c_im", [2, 32, 16, 64]), ("ssm_d", [512]), ("ssm_glu_w", [512, 1024]), ("ssm_glu_b", [1024])]


def host_consts(li=0):
    i = np.arange(128)
    c = {
        "ident": np.eye(128, dtype=np.float32),
        "ones": np.ones((128, 128), np.float32),
        "tril": (i[:, None] <= i[None, :]).astype(np.float32),
        "triu": (i[:, None] >= i[None, :]).astype(np.float32),
    }
    t = np.arange(LL)
    row = (t // 64).astype(np.float32)
    col = (t % 64).astype(np.float32)
    half = 32
    inv = np.power(np.float32(10000.0), -np.arange(0, half, 2, dtype=np.float32) / np.float32(half)).astype(np.float32)
    ar = (row[:, None] * inv).astype(np.float32)
    ac = (col[:, None] * inv).astype(np.float32)
    cr, sr, cc, sc = np.cos(ar), np.sin(ar), np.cos(ac), np.sin(ac)
    C64 = np.concatenate([cr, cr, cc, cc], axis=1).astype(np.float32)
    S64 = np.concatenate([-sr, sr, -sc, sc], axis=1).astype(np.float32)
    c["ropeC"] = np.ascontiguousarray(C64.reshape(16, 128, 64).transpose(1, 0, 2))
    c["ropeS"] = np.ascontiguousarray(S64.reshape(16, 128, 64).transpose(1, 0, 2))
    lam = [0.8 - 0.6 * math.exp(-0.3 * l) for l in range(DEPTH)]
    c["lamc"] = np.stack([np.tile(np.array([[-v, 1.0 - v]], np.float32), (128, 1)) for v in lam])
    return c


def build_layer_program(li=0, branches=("a", "b", "c"), dump_ybr=False, do_merge=True, dbg=None):
    nc = bass.Bass("TRN2", target_bir_lowering=False)
    S = Sched()
    W = {}
    for nm, shp in LAYER_W:
        W[nm] = nc.dram_tensor(nm, shp, F32, kind="ExternalInput").ap()
    W["cvec"] = nc.dram_tensor("cvec", [3, D], F32, kind="ExternalInput").ap()
    cin = {nm: nc.dram_tensor(nm, shp, F32, kind="ExternalInput").ap() for nm, shp in CONST_SHAPES.items()}
    xT = nc.dram_tensor("xT", [NSEQ, D, LT], F32, kind="ExternalInput").ap()
    xo = nc.dram_tensor("xo", [NSEQ, D, LT], F32, kind="ExternalOutput").ap()
    ybr_in = None
    if len(branches) < 3:
        ybr_in = nc.dram_tensor("ybr", [NSEQ, 3, 512, LT], F32, kind="ExternalInput").ap()
    ybr_dev = nc.dram_tensor("ybr_dev", [NSEQ, 3, 512, LT], F32, kind="ExternalOutput" if dump_ybr else "Internal").ap()
    with ExitStack() as stack:
        B = LayerBuilder(nc, S, stack, dbg=dbg)
        B.setup_consts(cin)
        B.hT = B.sb("hT", [128, 8, LT], BF16)
        B.stage_mod(W)
        for s in range((dbg or {}).get("nseq", NSEQ)):
            B.stage_norm(xT[s], s)
            srcs = []
            for bi, b in enumerate("abc"):
                srcs.append(ybr_dev[s, bi] if b in branches else ybr_in[s, bi])
            if "a" in branches:
                B.stage_s5(W, cin, ybr_dev[s, 0], s, li)
            if "b" in branches:
                B.stage_mlstm(W, cin, ybr_dev[s, 1], s, li)
            if "c" in branches:
                B.stage_attn(W, cin, ybr_dev[s, 2], s, li)
            if do_merge:
                B.stage_merge(W, xT[s], xo[s], srcs, s)
        S.emit(nc, stack)
    return nc, S


_PROG = {}


def build_program(n_layers=DEPTH, layer0=0):
    nc = bass.Bass("TRN2", target_bir_lowering=False)
    S = Sched()
    Wall = {}
    for nm, shp in LAYER_W:
        Wall[nm] = nc.dram_tensor(nm, [DEPTH] + list(shp), F32, kind="ExternalInput").ap()
    cvec = nc.dram_tensor("cvec", [3, D], F32, kind="ExternalInput").ap()
    cin = {nm: nc.dram_tensor(nm, shp, F32, kind="ExternalInput").ap() for nm, shp in CONST_SHAPES.items()}
    xT = nc.dram_tensor("xT", [NSEQ, D, LT], F32, kind="ExternalInput").ap()
    xo = nc.dram_tensor("xo", [NSEQ, D, LT], F32, kind="ExternalOutput").ap()
    xs = [nc.dram_tensor("xs%d" % i, [NSEQ, D, LT], F32).ap() for i in range(2)]
    ybr_dev = nc.dram_tensor("ybr_dev", [NSEQ, 3, 512, LT], F32).ap()
    with ExitStack() as stack:
        B = LayerBuilder(nc, S, stack)
        B.setup_consts(cin)
        B.hT = B.sb("hT", [128, 8, LT], BF16)
        for j in range(n_layers):
            li = layer0 + j
            S.epoch = j
            W = {nm: Wall[nm][li] for nm, _ in LAYER_W}
            W["cvec"] = cvec
            x_in, tin = (xT, "xT") if j == 0 else (xs[(j - 1) % 2], "xs%d" % ((j - 1) % 2))
            x_out, tout = (xo, "xo") if j == n_layers - 1 else (xs[j % 2], "xs%d" % (j % 2))
            B.stage_mod(W)
            for s in range(NSEQ):
                B.stage_norm(x_in[s], s, tin)
                B.stage_s5(W, cin, ybr_dev[s, 0], s, li)
                B.stage_mlstm(W, cin, ybr_dev[s, 1], s, li)
                B.stage_attn(W, cin, ybr_dev[s, 2], s, li)
                B.stage_merge(W, x_in[s], x_out[s], [ybr_dev[s, b] for b in range(3)], s, tin, tout)
        S.emit(nc, stack)
    return nc, S


def _fm(ctx, lat):
    return np.ascontiguousarray(np.concatenate([ctx, lat], axis=1).transpose(0, 2, 1)).astype(np.float32)


def kernel(**inputs):
    x = np.asarray(inputs["x"], np.float32)
    c = np.asarray(inputs["c"], np.float32)
    ctx = np.asarray(inputs["ctx"], np.float32)
    c_ctx = np.asarray(inputs["c_ctx"], np.float32)
    ncores = 8
    if "p" not in _PROG:
        _PROG["p"] = build_program()
    nc, _ = _PROG["p"]
    consts = host_consts()
    wl = {nm: np.ascontiguousarray(np.asarray(inputs[nm], np.float32)) for nm, _ in LAYER_W}
    in_maps = []
    for i in range(ncores):
        m = {"xT": _fm(ctx[2 * i:2 * i + 2], x[2 * i:2 * i + 2]),
             "cvec": np.stack([c[2 * i], c[2 * i + 1], c_ctx]).astype(np.float32)}
        m.update(wl)
        m.update(consts)
        in_maps.append(m)
    res = run_bass_kernel_spmd(nc, in_maps, core_ids=list(range(ncores)))
    out = np.concatenate([np.ascontiguousarray(np.asarray(r["xo"], np.float32)[:, :, LC:].transpose(0, 2, 1))
                          for r in res.results], axis=0)
    return out.astype(np.float32)
```

```python
import math
from contextlib import ExitStack

import numpy as np
import concourse.bass as bass
import concourse.mybir as mybir
from concourse.bass_utils import run_bass_kernel_spmd

F32 = mybir.dt.float32
BF16 = mybir.dt.bfloat16
AF = mybir.ActivationFunctionType
ALU = mybir.AluOpType
AX = mybir.AxisListType

D = 1024
LC = 256
LL = 2048
LT = LC + LL
NT = LT // 128
DEPTH = 4
EPS = 1e-6
NSEQ = 2
O_SU, O_SZ, O_MQK, O_MV, O_MO, O_MZ, O_MG, O_DQ, O_DK, O_DV, O_DZ, O_GL = (
    0, 512, 1024, 2048, 2560, 3072, 3584, 3600, 4112, 4624, 5136, 5648)
D_IN = 8720
TOKCH = [(0, 256), (256, 512), (768, 512), (1280, 512), (1792, 512)]
TWO_PI = 2.0 * math.pi


class _Op:
    __slots__ = ("eng", "fn", "deps", "dma", "ms", "sem", "val", "idx", "ep")


class Sched:
    ENGS = ("pe", "act", "dve", "pool", "sp")
    R = 6

    def __init__(self):
        self.ops = []
        self.tw = {}
        self.tr = {}
        self.last = {e: None for e in self.ENGS}
        self.pending_barrier = {e: set() for e in self.ENGS}
        self.dma_hist = {e: [] for e in self.ENGS}
        self.epoch = 0

    def add(self, eng, fn, reads=(), writes=(), dma=False):
        op = _Op()
        op.eng, op.fn, op.dma, op.ms, op.sem, op.val = eng, fn, dma, False, None, None
        op.idx = len(self.ops)
        op.ep = self.epoch
        xs = [r for r in reads if isinstance(r, str) and (r.startswith("ps") or r.startswith("acc"))]
        if xs:
            writes = list(writes) + [x for x in xs if x not in writes]
        deps = set()
        for r in reads:
            w = self.tw.get(r)
            if w is not None:
                deps.add(w)
        for wt in writes:
            w = self.tw.get(wt)
            if w is not None:
                deps.add(w)
            for rr in self.tr.get(wt, ()):
                deps.add(rr)
        deps |= self.pending_barrier[eng]
        self.pending_barrier[eng] = set()
        keep = set()
        rset = None
        for d in deps:
            o = self.ops[d]
            if o.eng == eng and not o.dma and not dma:
                if eng == "pe":
                    continue
            keep.add(d)
        op.deps = keep
        for r in reads:
            self.tr.setdefault(r, []).append(op.idx)
        for wt in writes:
            self.tw[wt] = op.idx
            self.tr[wt] = []
        self.ops.append(op)
        self.last[eng] = op.idx
        if dma:
            self.dma_hist[eng].append(op.idx)
        return op.idx

    def barrier(self):
        s = set()
        for e in self.ENGS:
            if self.last[e] is not None:
                s.add(self.last[e])
            for i in self.dma_hist[e][-self.R:]:
                s.add(i)
        for e in self.ENGS:
            self.pending_barrier[e] |= s

    def emit(self, nc, stack):
        ops = self.ops
        for op in ops:
            for d in op.deps:
                ops[d].ms = True
        neps = max(op.ep for op in ops) + 1
        csem = {(e, ep): stack.enter_context(nc.semaphore("c_%s%d" % (e, ep))) for e in self.ENGS for ep in range(neps)}
        dsem = {e: [stack.enter_context(nc.semaphore("d_%s%d" % (e, i))) for i in range(self.R)]
                for e in ("sp", "pool", "act")}
        ccount = {(e, ep): 0 for e in self.ENGS for ep in range(neps)}
        dcount = {e: 0 for e in self.ENGS}
        per_eng = {e: [] for e in self.ENGS}
        for op in ops:
            if op.dma:
                i = dcount[op.eng]
                op.sem = dsem[op.eng][i % self.R]
                op.val = 16 * (i // self.R + 1)
                dcount[op.eng] += 1
            elif op.ms:
                ccount[(op.eng, op.ep)] += 1
                op.sem = csem[(op.eng, op.ep)]
                op.val = ccount[(op.eng, op.ep)]
            per_eng[op.eng].append(op)
        self.stats = {e: (len(per_eng[e]), sum(ccount[(e, ep)] for ep in range(neps)), dcount[e]) for e in self.ENGS}
        R = self.R

        def replay(ename, e):
            seen = {}
            ndma = 0
            for op in per_eng[ename]:
                waits = {}
                for d in op.deps:
                    o = ops[d]
                    k = o.sem
                    if waits.get(k, (None, 0))[1] < o.val:
                        waits[k] = (o.sem, o.val)
                if op.dma:
                    if ndma >= R:
                        k = op.sem
                        v = op.val - 16
                        if waits.get(k, (None, 0))[1] < v:
                            waits[k] = (op.sem, v)
                    ndma += 1
                for k, (sem, val) in waits.items():
                    if seen.get(k, 0) < val:
                        e.wait_ge(sem, val)
                        seen[k] = val
                ins = op.fn(e)
                if op.dma:
                    ins.then_inc(op.sem, 16)
                elif op.ms:
                    ins.then_inc(op.sem, 1)
            if ename in dsem:
                n = dcount[ename]
                for j in range(min(n, R)):
                    cnt = (n - 1 - j) // R + 1
                    e.wait_ge(dsem[ename][j], 16 * cnt)

        with nc.Block() as block:
            @block.tensor
            def _(e):
                replay("pe", e)

            @block.scalar
            def _(e):
                replay("act", e)

            @block.vector
            def _(e):
                replay("dve", e)

            @block.gpsimd
            def _(e):
                replay("pool", e)

            @block.sync
            def _(e):
                replay("sp", e)


class LayerBuilder:
    def __init__(self, nc, S, stack, dbg=None):
        self.nc, self.S, self.stack = nc, S, stack
        self.dbg = dbg or {}
        self.uid = 0
        self.wq = 0
        self.tabcache = None

    def sb(self, name, shape, dt, stack=None):
        self.uid += 1
        return (stack or self.stack).enter_context(self.nc.sbuf_tensor("%s_%d" % (name, self.uid), shape, dt))

    def ps(self, name, shape, dt=F32, stack=None):
        self.uid += 1
        full = 512 if dt == F32 else 1024
        t = (stack or self.stack).enter_context(self.nc.psum_tensor("%s_%d" % (name, self.uid), [128, full], dt))
        n = 1
        for d in shape[1:]:
            n *= d
        assert n <= full, (name, shape)
        v = t[0:shape[0], 0:n]
        if len(shape) == 3:
            v = v.rearrange("p (a b) -> p a b", b=shape[2])
        return v

    def dma(self, out, in_, reads, writes, q=None, slow=False):
        if q is None:
            q = "sp"
        if slow:
            fn = lambda e, o=out, i=in_: e.dma_start(out=o, in_=i, allow_slow_non_contiguous=True)
        else:
            fn = lambda e, o=out, i=in_: e.dma_start(out=o, in_=i)
        return self.S.add(q, fn, reads, writes, dma=True)

    def mm(self, out, lhsT, rhs, start, stop, reads, writes, skip=False):
        if skip:
            fn = lambda e: e.matmul(out, lhsT, rhs, start=start, stop=stop, skip_group_check=True)
        else:
            fn = lambda e: e.matmul(out, lhsT, rhs, start=start, stop=stop)
        return self.S.add("pe", fn, reads, writes)

    def tr(self, out, in_, ident, reads, writes):
        return self.S.add("pe", lambda e: e.transpose(out, in_, ident), reads, writes)

    def act(self, out, in_, func, reads, writes, bias=None, scale=None, accum_out=None):
        kw = {}
        if bias is not None:
            kw["bias"] = bias
        if scale is not None:
            kw["scale"] = scale
        if accum_out is not None:
            kw["accum_out"] = accum_out
        return self.S.add("act", lambda e: e.activation(out=out, in_=in_, func=func, **kw), reads, writes)

    def tt(self, out, in0, in1, op, reads, writes, eng="dve"):
        return self.S.add(eng, lambda e: e.tensor_tensor(out=out, in0=in0, in1=in1, op=op), reads, writes)

    def ts(self, out, in0, s1, s2, op0, op1, reads, writes, eng="dve"):
        if op1 is None:
            fn = lambda e: e.tensor_scalar(out=out, in0=in0, scalar1=s1, scalar2=None, op0=op0)
        else:
            fn = lambda e: e.tensor_scalar(out=out, in0=in0, scalar1=s1, scalar2=s2, op0=op0, op1=op1)
        return self.S.add(eng, fn, reads, writes)

    def stt(self, out, in0, scalar, in1, op0, op1, reads, writes):
        return self.S.add("dve", lambda e: e.scalar_tensor_tensor(out=out, in0=in0, scalar=scalar, in1=in1,
                                                                  op0=op0, op1=op1), reads, writes)

    def cp(self, out, in_, reads, writes, eng="dve"):
        if eng == "act":
            return self.S.add("act", lambda e: e.activation(out=out, in_=in_, func=AF.Copy), reads, writes)
        return self.S.add(eng, lambda e: e.tensor_copy(out=out, in_=in_), reads, writes)

    def rsqrt(self, out, in_, scale, eps, reads, writes):
        et = self.epsT[eps]
        np_ = out.shape[0]
        self.act(out, in_, AF.Sqrt, list(reads) + [("epsT", eps)], writes, bias=et[0:np_, :], scale=scale)
        return self.S.add("dve", lambda e: e.reciprocal(out=out, in_=out), writes, writes)

    def rsqrt_le(self, out, in_, scale, eps, reads, writes):
        et = self.epsT[eps]
        np_ = out.shape[0]
        self.act(out, in_, AF.Ln, list(reads) + [("epsT", eps)], writes, bias=et[0:np_, :], scale=scale)
        return self.act(out, out, AF.Exp, writes, writes, scale=-0.5)

    def memset(self, ap, val, writes, eng="dve"):
        return self.S.add(eng, lambda e: e.memset(ap, val), (), writes)

    def setup_consts(self, cin):
        S = self.S
        self.identF = self.sb("identF", [128, 128], F32)
        self.identB = self.sb("identB", [128, 128], BF16)
        self.onesF = self.sb("onesF", [128, 128], F32)
        self.triL = self.sb("triL", [128, 128], BF16)
        self.triU = self.sb("triU", [128, 128], BF16)
        self.dma(self.identF[:], cin["ident"], (), ["identF"])
        self.dma(self.onesF[:], cin["ones"], (), ["onesF"])
        tmp = self.sb("ctmp", [128, 256], F32)
        self.dma(tmp[:, 0:128], cin["tril"], (), ["ctmp0"])
        self.dma(tmp[:, 128:256], cin["triu"], (), ["ctmp1"])
        self.triLF = tmp[:, 0:128]
        self.triUF = tmp[:, 128:256]
        self.epsT = {}
        t = self.sb("epsT", [128, 1], F32)
        self.memset(t[:], EPS, [("epsT", EPS)])
        self.epsT[EPS] = t
        self.cp(self.identB[:], self.identF[:], ["identF"], ["identB"])
        self.cp(self.triL[:], tmp[:, 0:128], ["ctmp0"], ["triL"])
        self.cp(self.triU[:], tmp[:, 128:256], ["ctmp1"], ["triU"])

    def load_w(self, name, w_dram, c0, ncols, KT, dst, dst_tok, stage, stage_tok, q="pool", cast_eng="pool"):
        src = w_dram.rearrange("(k p) c -> p k c", p=128)[:, :, c0:c0 + ncols]
        self.dma(stage[:, 0:KT, 0:ncols], src, [], [stage_tok], q=q)
        self.cp(dst[:, 0:KT, 0:ncols], stage[:, 0:KT, 0:ncols], [stage_tok], [dst_tok], eng=cast_eng)


    def stage_mod(self, W):
        S, nc = self.S, self.nc
        self.modT = self.sb("modT", [128, 24, 3], F32)
        self.A1 = self.sb("A1", [128, 8, 3], F32)
        with ExitStack() as st:
            cS = self.sb("cS", [128, 8, 3], F32, st)
            adab = self.sb("adab", [128, 24], F32, st)
            normg = self.sb("normg", [128, 8], F32, st)
            stage = self.sb("stgA", [128, 8, 512], F32, st)
            stage2 = self.sb("stgA2", [128, 8, 512], F32, st)
            psM = self.ps("psM", [128, 72], F32, st)
            for v in range(3):
                self.dma(cS[:, :, v], W["cvec"][v].rearrange("(k p) -> p k", p=128), [], ["cS"], slow=True)
            self.dma(adab[:], W["ada_b"].rearrange("(j p) -> p j", p=128), [], ["adab"], slow=True)
            self.dma(normg[:], W["norm_g"].rearrange("(k p) -> p k", p=128), [], ["normg"], slow=True)
            self.act(cS[:], cS[:], AF.Silu, ["cS"], ["cS"])
            stgs = [(stage, "stgA"), (stage2, "stgA2")]
            for ch in range(6):
                stg, tok = stgs[ch % 2]
                src = W["ada_w"].rearrange("(k p) c -> p k c", p=128)[:, :, ch * 512:(ch + 1) * 512]
                self.dma(stg[:], src, [], [tok], q="pool")
                for j in range(4):
                    jj = ch * 4 + j
                    for k in range(8):
                        self.mm(psM[:, 3 * jj:3 * jj + 3], stg[:, k, j * 128:(j + 1) * 128], cS[:, k, :],
                                k == 0, k == 7, [tok, "cS"], ["psM"])
            self.tt(self.modT[:], psM[:].rearrange("p (j v) -> p j v", v=3),
                    adab[:].unsqueeze(2).to_broadcast([128, 24, 3]), ALU.add, ["psM", "adab"], ["modT"])
            self.stt(self.A1[:], self.modT[:, 8:16, :], 1.0, normg[:].unsqueeze(2).to_broadcast([128, 8, 3]),
                     ALU.add, ALU.mult, ["modT", "normg"], ["A1"])
            S.barrier()

    def stage_norm(self, xT_s, s, xtok_in="xin"):
        S = self.S
        with ExitStack() as st:
            xin = [self.sb("xin", [128, 8, 512], F32, st) for _ in range(2)]
            sqt = self.sb("sqt", [128, 8, 512], F32, st)
            rs = self.sb("rs", [128, 512], F32, st)
            tmpn = [self.sb("tmpn", [128, 512], F32, st) for _ in range(2)]
            pss = [self.ps("pss", [128, 512], F32, st) for _ in range(2)]
            for ci, (t0, n) in enumerate(TOKCH):
                v = 2 if t0 == 0 else s
                xi = xin[ci % 2]
                xtok = "xin%d" % (ci % 2)
                ps = pss[ci % 2]
                pstok = "pss%d" % (ci % 2)
                self.dma(xi[:, :, 0:n], xT_s.rearrange("(k p) t -> p k t", p=128)[:, :, t0:t0 + n], [(xtok_in, s)], [xtok])
                self.act(sqt[:, :, 0:n], xi[:, :, 0:n], AF.Square, [xtok], ["sqt"])
                for k in range(8):
                    self.mm(ps[:, 0:n], self.onesF[:], sqt[:, k, 0:n], k == 0, k == 7, ["sqt", "onesF"], [pstok])
                self.rsqrt(rs[:, 0:n], ps[:, 0:n], 1.0 / D, EPS, [pstok], ["rs"])
                for k in range(8):
                    tm = tmpn[k % 2]
                    ttok = "tmpn%d" % (k % 2)
                    self.tt(tm[:, 0:n], xi[:, k, 0:n], rs[:, 0:n], ALU.mult, [xtok, "rs"], [ttok])
                    self.act(self.hT[:, k, t0:t0 + n], tm[:, 0:n], AF.Identity, [ttok, "A1", "modT"], [("hT", ci)],
                             bias=self.modT[:, k, v:v + 1], scale=self.A1[:, k, v:v + 1])
            S.barrier()

    def stage_merge(self, W, xT_s, xo_s, ybr_s, s, xtok_in="xin", xtok_out="xout"):
        S = self.S
        hT = self.hT
        with ExitStack() as st:
            yg = [self.sb("yg%d" % b, [128, 4, LT], BF16, st) for b in range(3)]
            mT = self.sb("mT", [128, 8, LT], BF16, st)
            stage = self.sb("stgF", [128, 8, 512], F32, st)
            wb = [self.sb("wbF", [128, 8, 512], BF16, st) for _ in range(2)]
            wb2 = [self.sb("wbG", [128, 8, 384], BF16, st), self.sb("wbG", [128, 4, 384], BF16, st)]
            ych = [self.sb("ych", [128, 512], F32, st) for _ in range(2)]
            szt = [self.sb("szt", [128, 512], F32, st) for _ in range(2)]
            sg = [self.sb("sg", [128, 512], F32, st) for _ in range(3)]
            mt = [self.sb("mt", [128, 512], F32, st) for _ in range(3)]
            psA = [self.ps("psA", [128, 512], F32, st) for _ in range(3)]
            psB = [self.ps("psB", [128, 512], F32, st) for _ in range(3)]
            allh = [("hT", i) for i in range(len(TOKCH))]
            zoff = [O_SZ, O_MZ, O_DZ]
            for b in range(3):
                w = wb[b % 2]
                wtok = "wbF%d" % (b % 2)
                self.load_w("wz", W["w_in"], zoff[b], 512, 8, w, wtok, stage, "stgF")
                for c in range(4):
                    for ci, (t0, n) in enumerate(TOKCH):
                        i = (c * 5 + ci) % 2
                        self.dma(ych[i][:, 0:n], ybr_s[b][c * 128:(c + 1) * 128, t0:t0 + n], [("ybr", s, b)], ["ych%d" % i])
                        ps = psA[i]
                        for k in range(8):
                            self.mm(ps[:, 0:n], w[:, k, c * 128:(c + 1) * 128], hT[:, k, t0:t0 + n], k == 0, k == 7,
                                    [wtok, ("hT", ci)], ["psA%d" % i])
                        self.act(szt[i][:, 0:n], ps[:, 0:n], AF.Silu, ["psA%d" % i], ["szt%d" % i])
                        self.tt(yg[b][:, c, t0:t0 + n], ych[i][:, 0:n], szt[i][:, 0:n], ALU.mult,
                                ["ych%d" % i, "szt%d" % i], [("yg", b, ci)])
            wouts = [W["w_ssm_out"], W["w_ml_out"], W["w_da_out"]]
            for f in range(8):
                if f % 2 == 0:
                    wg, wo, wgt, wot = wb[0], wb[1], "wbF0", "wbF1"
                else:
                    wg, wo, wgt, wot = wb2[0], wb2[1], "wbG0", "wbG1"
                for b in range(3):
                    src = W["w_in"].rearrange("(k p) c -> p k c", p=128)[:, :, O_GL + b * 1024 + f * 128:O_GL + b * 1024 + (f + 1) * 128]
                    self.dma(stage[:, :, b * 128:(b + 1) * 128], src, [], ["stgF"], q="pool")
                self.cp(wg[:, :, 0:384], stage[:, :, 0:384], ["stgF"], [wgt], eng="pool")
                for b in range(3):
                    src = wouts[b].rearrange("(k p) c -> p k c", p=128)[:, :, f * 128:(f + 1) * 128]
                    self.dma(stage[:, 0:4, b * 128:(b + 1) * 128], src, [], ["stgF"], q="pool")
                self.cp(wo[:, 0:4, 0:384], stage[:, 0:4, 0:384], ["stgF"], [wot], eng="pool")
                for ci, (t0, n) in enumerate(TOKCH):
                    for b in range(3):
                        for k in range(8):
                            self.mm(psA[b][:, 0:n], wg[:, k, b * 128:(b + 1) * 128], hT[:, k, t0:t0 + n], k == 0, k == 7,
                                    [wgt, ("hT", ci)], ["psA%d" % b])
                        for k in range(4):
                            self.mm(psB[b][:, 0:n], wo[:, k, b * 128:(b + 1) * 128], yg[b][:, k, t0:t0 + n], k == 0, k == 3,
                                    [wot, ("yg", b, ci)], ["psB%d" % b])
                    for b in range(3):
                        self.act(sg[b][:, 0:n], psA[b][:, 0:n], AF.Sigmoid, ["psA%d" % b], ["sg%d" % b])
                        self.tt(mt[b][:, 0:n], sg[b][:, 0:n], psB[b][:, 0:n], ALU.mult, ["sg%d" % b, "psB%d" % b], ["mt%d" % b])
                    self.tt(mt[0][:, 0:n], mt[0][:, 0:n], mt[1][:, 0:n], ALU.add, ["mt0", "mt1"], ["mt0"])
                    self.tt(mT[:, f, t0:t0 + n], mt[0][:, 0:n], mt[2][:, 0:n], ALU.add, ["mt0", "mt2"], [("mT", ci)])
            wo_full = [wb[0], wb[1]]
            for half in range(2):
                self.load_w("wo", W["w_out"], half * 512, 512, 8, wo_full[half], "wbF%d" % half, stage, "stgF")
            for fo in range(8):
                w = wo_full[fo // 4]
                wtok = "wbF%d" % (fo // 4)
                for ci, (t0, n) in enumerate(TOKCH):
                    v = 2 if t0 == 0 else s
                    i = (fo * 5 + ci) % 2
                    ps = psA[i]
                    for k in range(8):
                        self.mm(ps[:, 0:n], w[:, k, (fo % 4) * 128:(fo % 4 + 1) * 128], mT[:, k, t0:t0 + n], k == 0, k == 7,
                                [wtok, ("mT", ci)], ["psA%d" % i])
                    self.dma(ych[i][:, 0:n], xT_s[fo * 128:(fo + 1) * 128, t0:t0 + n], [(xtok_in, s)], ["ych%d" % i])
                    self.stt(szt[i][:, 0:n], ps[:, 0:n], self.modT[:, 16 + fo, v:v + 1], ych[i][:, 0:n], ALU.mult, ALU.add,
                             ["psA%d" % i, "ych%d" % i, "modT"], ["szt%d" % i])
                    self.dma(xo_s[fo * 128:(fo + 1) * 128, t0:t0 + n], szt[i][:, 0:n], ["szt%d" % i], [(xtok_out, s)])
            S.barrier()


    def stage_attn(self, W, cin, ybr_c, s, li, with_ctx=True):
        S = self.S
        hT = self.hT
        lam_init = 0.8 - 0.6 * math.exp(-0.3 * li)
        allh = [("hT", i) for i in range(len(TOKCH))]
        def hts(i):
            t = i * 128
            for ci, (t0, n) in enumerate(TOKCH):
                if t0 <= t < t0 + n:
                    return ("hT", ci)
        with ExitStack() as st:
            qT0 = self.sb("qT0", [128, 4, LT], BF16, st)
            qT1 = self.sb("qT1", [128, 4, LT], BF16, st)
            qTm = [qT0, qT1]
            self.memset(qT0[64:128, :, :], 0.0, ["qz0"], eng="pool")
            self.memset(qT1[0:64, :, :], 0.0, ["qz1"], eng="pool")
            kT = self.sb("kT", [128, 4, LT], BF16, st)
            vaug = self.sb("vaug", [128, NT, 4, 129], BF16, st)
            gq = self.sb("gq", [128, 64], F32, st)
            gk = self.sb("gk", [128, 64], F32, st)
            gsub = self.sb("gsub", [128, 128], F32, st)
            lamt = self.sb("lamt", [128, 256], F32, st)
            lprod = self.sb("lprod", [128, 2, 64], F32, st)
            lsum = self.sb("lsum", [128, 2], F32, st)
            neglam = self.sb("neglam", [128, 1], F32, st)
            self.dma(gq[:], W["da_qnorm_g"].rearrange("(o d) -> o d", o=1).to_broadcast([128, 64]), [], ["gq"])
            self.dma(gk[:], W["da_knorm_g"].rearrange("(o d) -> o d", o=1).to_broadcast([128, 64]), [], ["gk"])
            self.dma(gsub[:], W["da_subln_g"].rearrange("(o d) -> o d", o=1).to_broadcast([128, 128]), [], ["gsub"])
            self.dma(lamt[:], W["da_lambda"].rearrange("(o a) d -> o (a d)", o=1).to_broadcast([128, 256]), [], ["lamt"])
            self.ts(gq[:], gq[:], 0.125, None, ALU.mult, None, ["gq"], ["gq"])
            lamc = self.sb("lamc", [128, 2], F32, st)
            self.dma(lamc[:], cin["lamc"][li], [], ["lamc"])
            self.ts(gsub[:], gsub[:], lamc[:, 1:2], None, ALU.mult, None, ["gsub", "lamc"], ["gsub"])
            lv = lamt[:].rearrange("p (a b d) -> p a b d", a=2, b=2)
            self.tt(lprod[:], lv[:, :, 0, :], lv[:, :, 1, :], ALU.mult, ["lamt"], ["lprod"])
            self.S.add("dve", lambda e: e.tensor_reduce(out=lsum[:], in_=lprod[:], axis=AX.X, op=ALU.add), ["lprod"], ["lsum"])
            self.act(lsum[:], lsum[:], AF.Exp, ["lsum"], ["lsum"])
            self.tt(neglam[:], lsum[:, 1:2], lsum[:, 0:1], ALU.subtract, ["lsum"], ["neglam"])
            self.ts(neglam[:], neglam[:], lamc[:, 0:1], None, ALU.add, None, ["neglam", "lamc"], ["neglam"])
            self.memset(vaug[:, :, :, 128:129], 1.0, ["vones"])
            with ExitStack() as st2:
                stage = self.sb("stgE", [128, 8, 512], F32, st2)
                wq = self.sb("wq", [128, 8, 512], BF16, st2)
                wk = self.sb("wk", [128, 8, 512], BF16, st2)
                wv = self.sb("wv", [128, 8, 512], BF16, st2)
                sqf = self.sb("sqf", [128, 512], F32, st2)
                xraw = self.sb("xraw", [128, 512], F32, st2)
                ss = self.sb("ss8", [128, 8], F32, st2)
                xn = self.sb("xn", [128, 512], F32, st2)
                t1 = self.sb("t1", [128, 512], F32, st2)
                t2 = self.sb("t2", [128, 512], F32, st2)
                xb = [self.sb("xb", [128, 512], BF16, st2) for _ in range(2)]
                ropeC = self.sb("ropeC", [128, 16, 64], F32, st2)
                ropeS = self.sb("ropeS", [128, 16, 64], F32, st2)
                psq = self.ps("psq", [128, 512], F32, st2)
                psk = self.ps("psk", [128, 512], F32, st2)
                psv = self.ps("psv", [128, 512], F32, st2)
                pst = [self.ps("pstE", [128, 512], BF16, st2) for _ in range(2)]
                self.dma(ropeC[:], cin["ropeC"], [], ["ropeC"])
                self.dma(ropeS[:], cin["ropeS"], [], ["ropeS"])
                self.load_w("wq", W["w_in"], O_DQ, 512, 8, wq, "wq", stage, "stgE")
                self.load_w("wk", W["w_in"], O_DK, 512, 8, wk, "wk", stage, "stgE")
                self.load_w("wv", W["w_in"], O_DV, 512, 8, wv, "wv", stage, "stgE")
                for i in range(NT):
                    tsl = slice(i * 128, (i + 1) * 128)
                    ht = hts(i)
                    for (w, wt, ps, pt) in ((wq, "wq", psq, "psq"), (wk, "wk", psk, "psk"), (wv, "wv", psv, "psv")):
                        for k in range(8):
                            self.mm(ps[:], hT[:, k, tsl], w[:, k, :], k == 0, k == 7, [ht, wt], [pt])
                    self.cp(vaug[:, i, :, 0:128], psv[:].rearrange("p (h d) -> p h d", h=4), ["psv"], [("vaug", i)], eng="act")
                    for qi, (ps, pt, g, gt, dst, dt) in enumerate(((psq, "psq", gq, "gq", None, "qT"), (psk, "psk", gk, "gk", kT, "kT"))):
                        self.cp(xraw[:], ps[:], [pt], ["xraw"], eng="act")
                        self.tt(sqf[:], xraw[:], xraw[:], ALU.mult, ["xraw"], ["sqf"])
                        self.S.add("dve", lambda e, : e.tensor_reduce(out=ss[:], in_=sqf[:].rearrange("p (g d) -> p g d", d=64),
                                                                     axis=AX.X, op=ALU.add), ["sqf"], ["ss8"])
                        self.rsqrt_le(ss[:], ss[:], 1.0 / 64, EPS, ["ss8"], ["ss8"])
                        self.tt(xn[:].rearrange("p (g d) -> p g d", d=64), xraw[:].rearrange("p (g d) -> p g d", d=64),
                                ss[:].unsqueeze(2).to_broadcast([128, 8, 64]), ALU.mult, ["xraw", "ss8"], ["xn"])
                        x_b = xb[qi]
                        xbt = "xb%d" % qi
                        if i >= 2:
                            self.tt(xn[:].rearrange("p (g d) -> p g d", d=64), xn[:].rearrange("p (g d) -> p g d", d=64),
                                    g[:].unsqueeze(1).to_broadcast([128, 8, 64]), ALU.mult, ["xn", gt], ["xn"])
                            lt = i - 2
                            self.tt(t1[:].rearrange("p (g d) -> p g d", d=64), xn[:].rearrange("p (g d) -> p g d", d=64),
                                    ropeC[:, lt, :].unsqueeze(1).to_broadcast([128, 8, 64]), ALU.mult, ["xn", "ropeC"], ["t1"])
                            xv = xn[:].rearrange("p (g r h d) -> p g r h d", g=8, r=2, h=2)
                            tv = t2[:].rearrange("p (g r h d) -> p g r h d", g=8, r=2, h=2)
                            sv = ropeS[:, lt, :].rearrange("p (r h d) -> p r h d", r=2, h=2)
                            self.tt(tv[:, :, :, 0, :], xv[:, :, :, 1, :], sv[:, :, 0, :].unsqueeze(1).to_broadcast([128, 8, 2, 16]),
                                    ALU.mult, ["xn", "ropeS"], ["t2"])
                            self.tt(tv[:, :, :, 1, :], xv[:, :, :, 0, :], sv[:, :, 1, :].unsqueeze(1).to_broadcast([128, 8, 2, 16]),
                                    ALU.mult, ["xn", "ropeS"], ["t2"])
                            self.tt(x_b[:], t1[:], t2[:], ALU.add, ["t1", "t2"], [xbt])
                        else:
                            self.tt(x_b[:].rearrange("p (g d) -> p g d", d=64), xn[:].rearrange("p (g d) -> p g d", d=64),
                                    g[:].unsqueeze(1).to_broadcast([128, 8, 64]), ALU.mult, ["xn", gt], [xbt])
                        pp = pst[qi]
                        ppt = "pstE%d" % qi
                        for h in range(4):
                            self.tr(pp[:, h * 128:(h + 1) * 128], x_b[:, h * 128:(h + 1) * 128], self.identB[:], [xbt, "identB"], [ppt])
                        if dst is None:
                            pv = pp[:].rearrange("p (h t) -> p h t", h=4)
                            self.cp(qT0[0:64, :, tsl], pv[0:64], [ppt], [("qT", i, 0)], eng="act")
                            self.cp(qT1[64:128, :, tsl], pv[64:128], [ppt], [("qT", i, 1)], eng="act")
                        else:
                            self.cp(dst[:, :, tsl], pp[:].rearrange("p (h t) -> p h t", h=4), [ppt], [(dt, i)], eng="act")
                S.barrier()
            with ExitStack() as st3:
                pT = [self.sb("pT", [128, NT, 512], BF16, st3) for _ in range(2)]
                o = [self.sb("oE", [128, 128], F32, st3) for _ in range(2)]
                osq = self.sb("osq", [128, 128], F32, st3)
                ob = [self.sb("obE", [128, 128], BF16, st3) for _ in range(2)]
                rec = self.sb("recE", [128, 8], F32, st3)
                yst = [self.sb("ystE", [128, 512], F32, st3) for _ in range(2)]
                pss = [self.ps("pssE", [128, 512], F32, st3) for _ in range(3)]
                acc = [self.ps("accE", [128, 129], F32, st3) for _ in range(2)]
                pso = [self.ps("psoE", [128, 512], BF16, st3) for _ in range(2)]
                qchunks = [(256 + 512 * j, 512, list(range(NT))) for j in range(4)]
                if with_ctx:
                    qchunks.append((0, 256, [0, 1]))
                items = [(h, t0, n, keys) for h in range(4) for (t0, n, keys) in qchunks]
                oo4 = [self.sb("oo4", [128, 4, 128], F32, st3) for _ in range(2)]
                cnt = [0]

                def Hhalf(it, m):
                    h, t0, n, keys = items[it]
                    qtk = [("qT", (t0 // 128) + j, m) for j in range(n // 128)] + ["qz%d" % m]
                    for kt in keys:
                        ps = pss[cnt[0] % 3]
                        pstk = "pssE%d" % (cnt[0] % 3)
                        cnt[0] += 1
                        self.mm(ps[:, 0:n], kT[:, h, kt * 128:(kt + 1) * 128], qTm[m][:, h, t0:t0 + n], True, True,
                                [("kT", kt)] + qtk, [pstk])
                        self.act(pT[m][:, kt, 0:n], ps[:, 0:n], AF.Exp, [pstk], [("pT", m, kt)])

                def Vhalf(it, m):
                    h, t0, n, keys = items[it]
                    o4 = oo4[it % 2]
                    ys = yst[it % 2]
                    yt = "ystE%d" % (it % 2)
                    pso_ = pso[it % 2]
                    psot = "psoE%d" % (it % 2)
                    for qs in range(n // 128):
                        a = acc[qs % 2]
                        at = "accE%d" % (qs % 2)
                        ot = ("oo4", it % 2, qs)
                        for j, kt in enumerate(keys):
                            self.mm(a[:], pT[m][:, kt, qs * 128:(qs + 1) * 128], vaug[:, kt, h, :], j == 0, j == len(keys) - 1,
                                    [("pT", m, kt), ("vaug", kt), "vones"], [at])
                        rk = ("rec", qs % 2, m)
                        rcol = rec[:, 4 * (qs % 2) + m:4 * (qs % 2) + m + 1]
                        self.S.add("dve", lambda e, a=a, rcol=rcol: e.reciprocal(out=rcol, in_=a[:, 128:129]), [at], [rk])
                        if m == 0:
                            self.ts(o4[:, qs, :], a[:, 0:128], rcol, None, ALU.mult, None, [at, rk], [ot])
                        else:
                            r2 = rec[:, 4 * (qs % 2) + 2:4 * (qs % 2) + 3]
                            r3 = rec[:, 4 * (qs % 2) + 3:4 * (qs % 2) + 4]
                            rk2, rk3 = ("rec", qs % 2, 2), ("rec", qs % 2, 3)
                            self.tt(r2, rcol, neglam[:], ALU.mult, [rk, "neglam"], [rk2])
                            self.stt(o4[:, qs, :], a[:, 0:128], r2, o4[:, qs, :], ALU.mult, ALU.add, [at, rk2, ot], [ot])
                            self.tt(osq[:], o4[:, qs, :], o4[:, qs, :], ALU.mult, [ot], ["osq"])
                            self.S.add("dve", lambda e, r3=r3: e.tensor_reduce(out=r3, in_=osq[:], axis=AX.X, op=ALU.add), ["osq"], [rk3])
                            self.rsqrt_le(r3, r3, 1.0 / 128, EPS, [rk3], [rk3])
                            obb = ob[qs % 2]
                            obt = "obE%d" % (qs % 2)
                            self.stt(obb[:], o4[:, qs, :], r3, gsub[:], ALU.mult, ALU.mult, [ot, rk3, "gsub"], [obt])
                            self.tr(pso_[:, qs * 128:(qs + 1) * 128], obb[:], self.identB[:], [obt, "identB"], [psot])
                    if m == 1:
                        self.cp(ys[:, 0:n], pso_[:, 0:n], [psot], [yt], eng="act")
                        self.dma(ybr_c[h * 128:(h + 1) * 128, t0:t0 + n], ys[:, 0:n], [yt], [("ybr", s, 2)])

                halves = [(it, m) for it in range(len(items)) for m in range(2)]
                for j in range(len(halves) + 2):
                    if j >= 2:
                        Vhalf(*halves[j - 2])
                    if j < len(halves):
                        Hhalf(*halves[j])
                S.barrier()
            S.barrier()


    def stage_mlstm(self, W, cin, ybr_b, s, li):
        S = self.S
        hT = self.hT
        def hts(i):
            t = i * 128
            for ci, (t0, n) in enumerate(TOKCH):
                if t0 <= t < t0 + n:
                    return ("hT", ci)
        order = [list(range(NT)), [1, 0] + list(range(NT - 1, 1, -1))]
        with ExitStack() as st:
            tokS = [self.sb("tokS", [128, NT, 12], F32, st) for _ in range(2)]
            decB = [self.sb("decB", [128, NT, 4], F32, st) for _ in range(2)]
            cw = self.sb("cw", [128, 8, 3], F32, st)
            cb = self.sb("cb", [128, 8], F32, st)
            gml = self.sb("gml", [128, 512], F32, st)
            id4 = self.identF[0:4, 0:4]
            stgH = [self.sb("stgD", [128, 8, 512], F32, st) for _ in range(2)]
            wbH = [self.sb("wbD", [128, 8, 512], BF16, st) for _ in range(2)]

            def load_head(hh):
                cols_ = [O_MQK + hh * 128, O_MQK + 512 + hh * 128, O_MV + hh * 128, O_MO + hh * 128]
                for j_, c0_ in enumerate(cols_):
                    self.dma(stgH[hh % 2][:, :, j_ * 128:(j_ + 1) * 128], W["w_in"].rearrange("(k p) c -> p k c", p=128)[:, :, c0_:c0_ + 128],
                             [], ["stgD%d" % (hh % 2)], q="pool")
                self.cp(wbH[hh % 2][:], stgH[hh % 2][:], ["stgD%d" % (hh % 2)], ["wbD%d" % (hh % 2)], eng="pool")
            load_head(0)
            for j in range(3):
                self.dma(cw[:, :, j], W["ml_conv_w"][j].rearrange("(f p) -> p f", p=128), [], ["cw"], slow=True)
            self.dma(cb[:], W["ml_conv_b"].rearrange("(f p) -> p f", p=128), [], ["cb"], slow=True)
            self.dma(gml[:], W["ml_norm_g"].rearrange("(o d) -> o d", o=1).to_broadcast([128, 512]), [], ["gml"])
            with ExitStack() as st2:
                T = [self.sb("gT", [4, LT], F32, st2) for _ in range(4)]
                ones4 = self.sb("ones4", [4, 1], F32, st2)
                gb = self.sb("gb", [4, 4], F32, st2)
                mend = self.sb("mend", [4, NT], F32, st2)
                dec = self.sb("dec", [4, NT], F32, st2)
                ddg = self.sb("ddg", [4, NT, 4], F32, st2)
                stage = self.sb("stgG", [128, 8, 16], F32, st2)
                wg = self.sb("wg", [128, 8, 16], BF16, st2)
                psg = [self.ps("psg", [4, 512], F32, st2) for _ in range(2)]
                pstk = self.ps("pstk", [128, NT, 12], F32, st2)
                psd = self.ps("psd", [128, NT * 4], F32, st2)
                self.memset(ones4[:], 1.0, ["ones4"])
                self.dma(gb[:], W["ml_gate_b"].rearrange("a h -> h a"), [], ["gb"], slow=True)
                self.dma(stage[:], W["w_in"].rearrange("(k p) c -> p k c", p=128)[:, :, O_MG:O_MG + 16], [], ["stgG"], q="pool")
                self.cp(wg[:], stage[:], ["stgG"], ["wg"], eng="pool")
                onesb = ones4[:].to_broadcast([4, LT])
                for d in range(2):
                    Ti, Tf, Tg, Tm = T
                    for typ, dst, dtok in ((2 * d, Ti, "gT0"), (2 * d + 1, Tf, "gT1")):
                        for ci, (t0, n) in enumerate(TOKCH):
                            ps = psg[ci % 2]
                            pt = "psg%d" % (ci % 2)
                            for k in range(8):
                                self.mm(ps[:, 0:n], wg[:, k, typ * 4:(typ + 1) * 4], hT[:, k, t0:t0 + n], k == 0, k == 7,
                                        ["wg", ("hT", ci)], [pt])
                            if d == 0:
                                o_ap = dst[:, t0:t0 + n]
                            else:
                                if t0 == 0:
                                    o_ap = dst[:, 255::-1] if True else None
                                else:
                                    hi = 2559 - t0
                                    lo = hi - n
                                    o_ap = dst[:, hi:lo:-1]
                            self.ts(o_ap, ps[:, 0:n], gb[:, typ:typ + 1], None, ALU.add, None, [pt, "gb"], [dtok])
                    self.act(Tf[:], Tf[:], AF.Sigmoid, ["gT1"], ["gT1"])
                    self.act(Tf[:], Tf[:], AF.Ln, ["gT1"], ["gT1"])
                    S.add("dve", lambda e, Tg=Tg, Tf=Tf: e.tensor_tensor_scan(out=Tg[:], data0=onesb, data1=Tf[:], initial=0.0,
                                                                             op0=ALU.mult, op1=ALU.add), ["gT1", "ones4"], ["gT2"])
                    self.tt(Ti[:], Ti[:], Tg[:], ALU.subtract, ["gT0", "gT2"], ["gT0"])
                    S.add("dve", lambda e, Tm=Tm, Ti=Ti: e.tensor_tensor_scan(out=Tm[:], data0=onesb, data1=Ti[:], initial=0.0,
                                                                             op0=ALU.mult, op1=ALU.max), ["gT0", "ones4"], ["gT3"])
                    self.cp(mend[:], Tm[:, 127::128], ["gT3"], ["mend"])
                    self.ts(dec[:, 0:1], mend[:, 0:1], -1.0, None, ALU.mult, None, ["mend"], ["dec"])
                    self.tt(dec[:, 1:NT], mend[:, 0:NT - 1], mend[:, 1:NT], ALU.subtract, ["mend"], ["dec"])
                    self.act(dec[:], dec[:], AF.Exp, ["dec"], ["dec"])
                    self.tt(Tf[:], Tg[:], Tm[:], ALU.add, ["gT2", "gT3"], ["gT1"])
                    self.act(Tf[:], Tf[:], AF.Exp, ["gT1"], ["gT1"], scale=-1.0)
                    mb = mend[:].unsqueeze(2).to_broadcast([4, NT, 128])
                    self.tt(Tg[:].rearrange("p (c j) -> p c j", j=128), Ti[:].rearrange("p (c j) -> p c j", j=128), mb, ALU.subtract,
                            ["gT0", "mend"], ["gT2"])
                    self.act(Tg[:], Tg[:], AF.Exp, ["gT2"], ["gT2"])
                    self.tt(Ti[:].rearrange("p (c j) -> p c j", j=128), mb, Tm[:].rearrange("p (c j) -> p c j", j=128), ALU.subtract,
                            ["gT3", "mend"], ["gT0"])
                    self.act(Ti[:], Ti[:], AF.Exp, ["gT0"], ["gT0"])
                    U, Rr, FL = Tg, Ti, Tf
                    ut, rt, ft = "gT2", "gT0", "gT1"
                    if d == 1:
                        def rev(dst, src, st_, dt_):
                            self.cp(dst[:, 0:256], src[:, 255::-1], [st_], [dt_])
                            self.cp(dst[:, 256:LT], src[:, LT - 1:255:-1], [st_], [dt_])
                        rev(Tm, U, "gT2", "gT3")
                        rev(Tg, Rr, "gT0", "gT2")
                        rev(Ti, FL, "gT1", "gT0")
                        U, Rr, FL = Tm, Tg, Ti
                        ut, rt, ft = "gT3", "gT2", "gT0"
                    for mc in range(NT):
                        for qi, (src, stok) in enumerate(((U, ut), (Rr, rt), (FL, ft))):
                            self.mm(pstk[:, mc, qi * 4:(qi + 1) * 4], src[:, mc * 128:(mc + 1) * 128], id4, True, True,
                                    [stok, "identF"], ["pstk"])
                    self.cp(tokS[d][:], pstk[:], ["pstk"], [("tokS", d)])
                    self.tt(ddg[:], dec[:].unsqueeze(2).to_broadcast([4, NT, 4]), id4.unsqueeze(1).to_broadcast([4, NT, 4]), ALU.mult,
                            ["dec", "identF"], ["ddg"])
                    self.mm(psd[:], self.onesF[0:4, :], ddg[:].rearrange("p c h -> p (c h)"), True, True, ["ddg", "onesF"], ["psd"])
                    self.cp(decB[d][:].rearrange("p c h -> p (c h)"), psd[:], ["psd"], [("decB", d)])
                S.barrier()
            for h in range(4):
                with ExitStack() as st3:
                    wb = wbH[h % 2]
                    wbt = "wbD%d" % (h % 2)
                    xr = self.sb("xr", [128, LT], F32, st3)
                    ac = self.sb("acD", [128, LT], F32, st3)
                    qh = self.sb("qh", [128, LT], BF16, st3)
                    kh = self.sb("kh", [128, LT], BF16, st3)
                    ktok = self.sb("ktok", [128, NT, 128], BF16, st3)
                    vh = self.sb("vh", [128, NT, 129], BF16, st3)
                    hacc = self.sb("hacc", [128, NT, 128], F32, st3)
                    hnum = [self.sb("hnum", [128, NT, 129], F32, st3) for _ in range(2)]
                    ep = self.sb("epD", [128, 2, NT], F32, st3)
                    hbt = self.sb("hbt", [128, NT, 128], BF16, st3)
                    Cstd = [self.sb("Cst", [128, 129], F32, st3) for _ in range(2)]
                    Cbfd = [self.sb("Cbf", [128, 129], BF16, st3) for _ in range(2)]
                    smd = [self.sb("smD", [128, 4], F32, st3) for _ in range(2)]
                    PT = [self.sb("PT", [128, 128], BF16, st3) for _ in range(2)]
                    Vs = [self.sb("Vs", [128, 129], BF16, st3) for _ in range(2)]
                    yst = [self.sb("ystD", [128, 512], F32, st3) for _ in range(2)]
                    psA = [self.ps("psDA", [128, 512], F32, st3) for _ in range(2)]
                    psT = self.ps("psDT", [128, 512], BF16, st3)
                    psS = [self.ps("psDS", [128, 128], F32, st3) for _ in range(2)]
                    psO = [self.ps("psDO", [128, 129], F32, st3) for _ in range(2)]
                    psC = self.ps("psDC", [128, 129], F32, st3)
                    if h + 1 < 4:
                        load_head(h + 1)
                    for j, (dst, dtok, f) in enumerate(((qh, "qh", h), (kh, "kh", 4 + h))):
                        for ci, (t0, n) in enumerate(TOKCH):
                            ps = psA[ci % 2]
                            pt = "psDA%d" % (ci % 2)
                            for k in range(8):
                                self.mm(ps[:, 0:n], wb[:, k, j * 128:(j + 1) * 128], hT[:, k, t0:t0 + n], k == 0, k == 7,
                                        [wbt, ("hT", ci)], [pt])
                            self.cp(xr[:, t0:t0 + n], ps[:, 0:n], [pt], ["xr"], eng="act")
                        self.ts(ac[:], xr[:], cw[:, f, 1:2], cb[:, f:f + 1], ALU.mult, ALU.add, ["xr", "cw", "cb"], ["acD"])
                        for (a0, a1) in ((0, 256), (256, LT)):
                            self.stt(ac[:, a0 + 1:a1], xr[:, a0:a1 - 1], cw[:, f, 0:1], ac[:, a0 + 1:a1], ALU.mult, ALU.add,
                                     ["xr", "cw", "acD"], ["acD"])
                            self.stt(ac[:, a0:a1 - 1], xr[:, a0 + 1:a1], cw[:, f, 2:3], ac[:, a0:a1 - 1], ALU.mult, ALU.add,
                                     ["xr", "cw", "acD"], ["acD"])
                        if j == 0:
                            self.act(dst[:], ac[:], AF.Silu, ["acD"], [dtok])
                        else:
                            self.act(ac[:], ac[:], AF.Silu, ["acD"], ["acD"])
                            self.ts(dst[:], ac[:], 128.0 ** -0.5, None, ALU.mult, None, ["acD"], [dtok])
                    for g0 in range(0, NT, 4):
                        nn = min(4, NT - g0)
                        for j in range(nn):
                            i = g0 + j
                            self.tr(psT[:, j * 128:(j + 1) * 128], kh[:, i * 128:(i + 1) * 128], self.identB[:], ["kh", "identB"], ["psDT"])
                        self.cp(ktok[:, g0:g0 + nn, :], psT[:, 0:nn * 128].rearrange("p (a b) -> p a b", b=128), ["psDT"], ["ktok"], eng="act")
                    self.memset(vh[:, :, 128:129], 1.0, ["vh1"])
                    for g0 in range(0, NT, 4):
                        nn = min(4, NT - g0)
                        ps = psA[(g0 // 4) % 2]
                        pt = "psDA%d" % ((g0 // 4) % 2)
                        for j in range(nn):
                            i = g0 + j
                            for k in range(8):
                                self.mm(ps[:, j * 128:(j + 1) * 128], hT[:, k, i * 128:(i + 1) * 128], wb[:, k, 256:384], k == 0, k == 7,
                                        [wbt, hts(i)], [pt])
                        self.cp(vh[:, g0:g0 + nn, 0:128], ps[:, 0:nn * 128].rearrange("p (a b) -> p a b", b=128), [pt], ["vh"], eng="act")
                    for d in range(2):
                        self.memset(Cstd[d][:], 0.0, ["Cst%d" % d])

                    def mstep(d, c):
                        mc = order[d][c]
                        mask = self.triL if d == 0 else self.triU
                        Cst, Cbf, PT_, Vs_, sm = Cstd[d], Cbfd[d], PT[d], Vs[d], smd[d]
                        pS, pO = psS[d], psO[d]
                        cst, cbf, ptt, vst, pst_, pot = "Cst%d" % d, "Cbf%d" % d, "PT%d" % d, "Vs%d" % d, "psDS%d" % d, "psDO%d" % d
                        tsl = slice(mc * 128, (mc + 1) * 128)
                        self.mm(pS[:], kh[:, tsl], qh[:, tsl], True, True, ["kh", "qh"], [pst_])
                        self.tt(PT_[:], pS[:], mask[:], ALU.mult, [pst_, "triL", "triU"], [ptt])
                        self.act(Vs_[:], vh[:, mc, :], AF.Identity, ["vh", "vh1", ("tokS", d)], [vst], scale=tokS[d][:, mc, h:h + 1])
                        self.ts(Cst[:], Cst[:], decB[d][:, c, h:h + 1], None, ALU.mult, None, [cst, ("decB", d)], [cst])
                        self.cp(Cbf[:], Cst[:], [cst], [cbf], eng="pool")
                        self.mm(pO[:], PT_[:], Vs_[:], True, False, [ptt, vst], [pot])
                        self.mm(pO[:], qh[:, tsl], Cbf[:], False, True, ["qh", cbf], [pot])
                        self.mm(psC[:], ktok[:, mc, :], Vs_[:], True, True, ["ktok", vst], ["psDC"])
                        self.tt(Cst[:], Cst[:], psC[:], ALU.add, [cst, "psDC"], [cst])
                        self.cp(hnum[d][:, mc, :], pO[:], [pot], [("hnum", d, mc)], eng="act")

                    for c in range(NT):
                        for d in range(2):
                            mstep(d, c)
                    sm = smd[0]
                    for d in range(2):
                        hall = [("hnum", d, i) for i in range(NT)]
                        r = tokS[d][:, :, 4 + h]
                        fl = tokS[d][:, :, 8 + h]
                        e0, e1 = ep[:, 0, :], ep[:, 1, :]
                        self.tt(e0, hnum[d][:, :, 128], r, ALU.mult, hall + [("tokS", d)], ["ep0"])
                        self.ts(e1, e0, -1.0, None, ALU.mult, None, ["ep0"], ["ep1"])
                        self.tt(e0, e0, e1, ALU.max, ["ep0", "ep1"], ["ep0"])
                        self.tt(e0, e0, fl, ALU.max, ["ep0", ("tokS", d)], ["ep0"])
                        S.add("dve", lambda e, e0=e0: e.reciprocal(out=e0, in_=e0), ["ep0"], ["ep0"])
                        self.tt(e0, e0, r, ALU.mult, ["ep0", ("tokS", d)], ["ep0"])
                        fb = e0.unsqueeze(2).to_broadcast([128, NT, 128])
                        if d == 0:
                            self.tt(hacc[:], hnum[0][:, :, 0:128], fb, ALU.mult, hall + ["ep0"], [("hacc", i) for i in range(NT)])
                        else:
                            self.tt(hnum[1][:, :, 0:128], hnum[1][:, :, 0:128], fb, ALU.mult, hall + ["ep0"], hall)
                            self.tt(hacc[:], hacc[:], hnum[1][:, :, 0:128], ALU.add, hall + [("hacc", i) for i in range(NT)],
                                    [("hacc", i) for i in range(NT)], eng="pool")
                    hall = [("hacc", i) for i in range(NT)]
                    sq3 = hnum[0][:, :, 0:128]
                    so3 = hnum[1][:, :, 0:128]
                    for i in range(NT):
                        ps = psA[i % 2]
                        pt = "psDA%d" % (i % 2)
                        for k in range(8):
                            self.mm(ps[:, 0:128], hT[:, k, i * 128:(i + 1) * 128], wb[:, k, 384:512], k == 0, k == 7, [wbt, hts(i)], [pt])
                        self.act(so3[:, i, :], ps[:, 0:128], AF.Sigmoid, [pt], [("so3", i)] + [("hnum", 1, j) for j in range(NT)])
                    self.tt(sq3, hacc[:], hacc[:], ALU.mult, hall, ["sq3"] + [("hnum", 0, j) for j in range(NT)])
                    S.add("dve", lambda e, sq3=sq3, ep=ep: e.tensor_reduce(out=ep[:, 0, :], in_=sq3, axis=AX.X, op=ALU.add), ["sq3"], ["ep0"])
                    self.rsqrt(ep[:, 0, :], ep[:, 0, :], 1.0 / 128, EPS, ["ep0"], ["ep0"])
                    self.tt(hacc[:], hacc[:], ep[:, 0, :].unsqueeze(2).to_broadcast([128, NT, 128]), ALU.mult, hall + ["ep0"], hall)
                    self.tt(hacc[:], hacc[:], gml[:, h * 128:(h + 1) * 128].unsqueeze(1).to_broadcast([128, NT, 128]), ALU.mult,
                            hall + ["gml"], hall, eng="pool")
                    self.tt(hbt[:], hacc[:], so3, ALU.mult, hall + [("so3", i) for i in range(NT)], ["hbt"])
                    for g0 in range(0, NT, 4):
                        nn = min(4, NT - g0)
                        ys = yst[(g0 // 4) % 2]
                        yt = "ystD%d" % ((g0 // 4) % 2)
                        for j in range(nn):
                            i = g0 + j
                            self.tr(psT[:, j * 128:(j + 1) * 128], hbt[:, i, :], self.identB[:], ["hbt", "identB"], ["psDT"])
                        self.cp(ys[:, 0:nn * 128], psT[:, 0:nn * 128], ["psDT"], [yt], eng="act")
                        self.dma(ybr_b[h * 128:(h + 1) * 128, g0 * 128:(g0 + nn) * 128], ys[:, 0:nn * 128], [yt], [("ybr", s, 1)])
                    S.barrier()
            S.barrier()


    def cmul(self, ore, oim, are, aim, bre, bim, ts4, rtoks, wtok, tk="cm", pool_one=False):
        t1, t2, t3, t4 = ts4
        k = [tk + "_t%d" % i for i in range(4)]
        self.tt(t2, aim, bim, ALU.mult, rtoks, [k[1]], eng="pool" if pool_one else "dve")
        self.tt(t1, are, bre, ALU.mult, rtoks, [k[0]])
        self.tt(t3, are, bim, ALU.mult, rtoks, [k[2]])
        self.tt(t4, aim, bre, ALU.mult, rtoks, [k[3]])
        self.tt(oim, t3, t4, ALU.add, [k[2], k[3]], [wtok + "_im"])
        self.tt(ore, t1, t2, ALU.subtract, [k[0], k[1]], [wtok + "_re"])

    def stage_s5(self, W, cin, ybr_a, s, li):
        S = self.S
        hT = self.hT
        order = [list(range(NT)), [1, 0] + list(range(NT - 1, 1, -1))]
        if self.dbg.get("s5_stop") == "none":
            return
        with ExitStack() as st:
            gel = self.sb("gel", [128, 4, LT], BF16, st)
            wsu = self.sb("wsu", [128, 8, 512], BF16, st)
            dsk = self.sb("dsk", [128, 4], F32, st)
            with ExitStack() as st0:
                stage = self.sb("stgC", [128, 8, 512], F32, st0)
                self.load_w("wsu", W["w_in"], O_SU, 512, 8, wsu, "wsu", stage, "stgC")
                S.barrier()
            self.dma(dsk[:], W["ssm_d"].rearrange("(c p) -> p c", p=128), [], ["dsk"], slow=True)
            for c in range(self.dbg.get("s5_nc", 4)):
                with ExitStack() as st2:
                    suT = self.sb("suT", [128, LT], BF16, st2)
                    yacc = self.sb("yacc", [128, LT], F32, st2)
                    sc = self.sb("s5sc", [128, 16, 4], F32, st2)
                    N = self.sb("s5N", [128, 2, 4, 128], F32, st2)
                    Nr = self.sb("s5Nr", [128, 2, 4, 128], F32, st2)
                    braw = self.sb("s5braw", [128, 2, 4, 16], F32, st2)
                    bbar = self.sb("s5bbar", [128, 2, 4, 16], F32, st2)
                    bt = self.sb("s5bt", [128, 2, 4, 16], F32, st2)
                    Zp = self.sb("s5Zp", [128, 2, 4, 128], F32, st2)
                    Yp = self.sb("s5Yp", [128, 2, 4, 128], F32, st2)
                    Pd = [self.sb("s5P", [128, 2, 4, 128], F32, st2) for _ in range(2)]
                    Eitd = [self.sb("s5Eit", [128, 1024], F32, st2) for _ in range(2)]
                    Bbdd = [self.sb("s5Bbd", [128, 1024], BF16, st2) for _ in range(2)]
                    Cbdd = [self.sb("s5Cbd", [128, 2, 4, 128], BF16, st2) for _ in range(2)]
                    wA = [self.sb("s5wA", [128, 1024], BF16, st2) for _ in range(2)]
                    wB = [self.sb("s5wB", [128, 1024], BF16, st2) for _ in range(2)]
                    xA = [self.sb("s5xA", [128, 1024], BF16, st2) for _ in range(2)]
                    xB = [self.sb("s5xB", [128, 1024], BF16, st2) for _ in range(2)]
                    cu = [[self.sb("s5cu", [128, 8], F32, st2) for _ in range(2)] for _ in range(2)]
                    t13d = [self.sb("s5t13", [128, 1024], F32, st2) for _ in range(1)]
                    t24d = [self.sb("s5t24", [128, 1024], F32, st2) for _ in range(1)]
                    Psd = [self.sb("s5Ps", [128, 2, 4, 128], F32, st2) for _ in range(2)]
                    Eisd = [self.sb("s5Eis", [128, 1024], F32, st2) for _ in range(2)]
                    Zcd = [self.sb("s5Zc", [128, 1024], F32, st2) for _ in range(2)]
                    carry = [[self.sb("s5cy", [128, 8], F32, st2) for _ in range(2)] for _ in range(2)]
                    psA = [self.ps("psCA", [128, 512], F32, st2) for _ in range(2)]
                    psZd = [[self.ps("psCZ", [128, 512], F32, st2) for _ in range(2)] for _ in range(2)]
                    psYd = [self.ps("psCY", [128, 128], F32, st2) for _ in range(2)]
                    t1, t2 = t13d[0], t24d[0]
                    for ci, (t0, n) in enumerate(TOKCH):
                        ps = psA[ci % 2]
                        pt = "psCA%d" % (ci % 2)
                        for k in range(8):
                            self.mm(ps[:, 0:n], wsu[:, k, c * 128:(c + 1) * 128], hT[:, k, t0:t0 + n], k == 0, k == 7,
                                    ["wsu", ("hT", ci)], [pt])
                        self.cp(suT[:, t0:t0 + n], ps[:, 0:n], [pt], ["suT"], eng="act")
                        self.ts(yacc[:, t0:t0 + n], ps[:, 0:n], dsk[:, c:c + 1], None, ALU.mult, None, [pt, "dsk"],
                                [("yacc", j) for j in range(t0 // 128, (t0 + n) // 128)])
                    for d in range(2):
                        P, Eit, Bbd, Cbd = Pd[d], Eitd[d], Bbdd[d], Cbdd[d]
                        ptk, etk, btk, ctk = "s5P%d" % d, "s5Eit%d" % d, "s5Bbd%d" % d, "s5Cbd%d" % d
                        psT = psZd[d]
                        psTt = ["psCZ%d%d" % (d, 0), "psCZ%d%d" % (d, 1)]
                        tc = self.tabcache
                        if tc is not None and s == 1:
                            self.dma(P[:].rearrange("p a k j -> p (a k j)"), tc["P"][c, d], [("tabc", c, d)], [ptk])
                            self.dma(Eit[:], tc["E"][c, d], [("tabc", c, d)], [etk])
                            self.dma(Bbd[:], tc["B"][c, d], [("tabc", c, d)], [btk])
                            self.dma(Cbd[:].rearrange("p a k j -> p (a k j)"), tc["C"][c, d], [("tabc", c, d)], [ctk])
                            self.ts(Eisd[d][:, 0:512], Eit[:, 512:1024], -1.0, None, ALU.mult, None, [etk], ["s5Eis%d" % d])
                            self.cp(Eisd[d][:, 512:1024], Eit[:, 0:512], [etk], ["s5Eis%d" % d], eng="pool")
                            self.ts(Psd[d][:, 0], P[:, 1], -1.0, None, ALU.mult, None, [ptk], ["s5Ps%d" % d])
                            self.cp(Psd[d][:, 1], P[:, 0], [ptk], ["s5Ps%d" % d], eng="pool")
                            continue
                        def col(i):
                            return sc[:, i, :]
                        LRE, LIM, DT, LDR, LDI, MAG, C8, S8, ABR, ABI, DEN, CR, CI, TA, TB, TC = [col(i) for i in range(16)]
                        gs = slice(8 * c, 8 * c + 8)
                        self.dma(LRE, W["ssm_lam_re"][d, gs].rearrange("(k g) p -> (g p) k", g=2), [], ["sc"], slow=True)
                        self.dma(LIM, W["ssm_lam_im"][d, gs].rearrange("(k g) p -> (g p) k", g=2), [], ["sc"], slow=True)
                        for g2 in range(2):
                            src = W["ssm_log_step"][d, gs].rearrange("(o k g) -> o g k", o=1, g=2)[:, g2, :]
                            self.dma(sc[g2 * 64:(g2 + 1) * 64, 2, :], src.to_broadcast([64, 4]), [], ["sc"], slow=True)
                        T_ = ["sc"]
                        self.ts(LRE, LRE, -1e-4, None, ALU.min, None, T_, T_)
                        self.act(DT, DT, AF.Exp, T_, T_)
                        self.tt(LDR, LRE, DT, ALU.mult, T_, T_)
                        self.tt(LDI, LIM, DT, ALU.mult, T_, T_)
                        self.act(MAG, LDR, AF.Exp, T_, T_)
                        self.act(S8, LDI, AF.Sin, T_, T_, scale=1.0 / 16)
                        self.act(TA, LDI, AF.Sin, T_, T_, scale=1.0 / 32)
                        self.tt(TA, TA, TA, ALU.mult, T_, T_)
                        self.ts(C8, TA, -2.0, 1.0, ALU.mult, ALU.add, T_, T_)
                        for _ in range(4):
                            self.tt(TA, C8, C8, ALU.mult, T_, T_)
                            self.tt(TB, S8, S8, ALU.mult, T_, T_)
                            self.tt(TC, C8, S8, ALU.mult, T_, T_)
                            self.tt(C8, TA, TB, ALU.subtract, T_, T_)
                            self.ts(S8, TC, 2.0, None, ALU.mult, None, T_, T_)
                        self.tt(ABR, MAG, C8, ALU.mult, T_, T_)
                        self.tt(ABI, MAG, S8, ALU.mult, T_, T_)
                        self.tt(TA, LRE, LRE, ALU.mult, T_, T_)
                        self.tt(TB, LIM, LIM, ALU.mult, T_, T_)
                        self.tt(DEN, TA, TB, ALU.add, T_, T_)
                        S.add("dve", lambda e, DEN=DEN: e.reciprocal(out=DEN, in_=DEN), T_, T_)
                        self.ts(TC, ABR, -1.0, None, ALU.add, None, T_, T_)
                        self.tt(TA, TC, LRE, ALU.mult, T_, T_)
                        self.tt(TB, ABI, LIM, ALU.mult, T_, T_)
                        self.tt(CR, TA, TB, ALU.add, T_, T_)
                        self.tt(CR, CR, DEN, ALU.mult, T_, T_)
                        self.tt(TA, ABI, LRE, ALU.mult, T_, T_)
                        self.tt(TB, TC, LIM, ALU.mult, T_, T_)
                        self.tt(CI, TA, TB, ALU.subtract, T_, T_)
                        self.tt(CI, CI, DEN, ALU.mult, T_, T_)
                        self.tt(TA, MAG, MAG, ALU.mult, T_, T_)
                        S.add("dve", lambda e, TA=TA: e.reciprocal(out=TA, in_=TA), T_, T_)
                        self.tt(LDR, ABR, TA, ALU.mult, T_, T_)
                        self.tt(LDI, ABI, TA, ALU.mult, T_, T_)
                        self.ts(LDI, LDI, -1.0, None, ALU.mult, None, T_, T_)
                        for (Tb, ar, ai, tk) in ((P, ABR, ABI, ptk), (N, LDR, LDI, "s5N")):
                            for kq in range(4):
                                self.cp(Tb[:, 0, kq, 0:1], ar[:, kq:kq + 1], T_, [tk])
                                self.cp(Tb[:, 1, kq, 0:1], ai[:, kq:kq + 1], T_, [tk])
                            L = 1
                            tv1 = t1[:, 0:512].rearrange("p (k j) -> p k j", j=128)
                            tv2 = t2[:, 0:512].rearrange("p (k j) -> p k j", j=128)
                            while L < 128:
                                mr = Tb[:, 0, :, L - 1:L].to_broadcast([128, 4, L])
                                mi = Tb[:, 1, :, L - 1:L].to_broadcast([128, 4, L])
                                sr = Tb[:, 0, :, 0:L]
                                si = Tb[:, 1, :, 0:L]
                                dr = Tb[:, 0, :, L:2 * L]
                                di = Tb[:, 1, :, L:2 * L]
                                self.tt(tv1[:, :, 0:L], si, mi, ALU.mult, [tk], ["s5t1"])
                                self.tt(tv2[:, :, 0:L], sr, mr, ALU.mult, [tk], ["s5t2"])
                                self.tt(dr, tv2[:, :, 0:L], tv1[:, :, 0:L], ALU.subtract, ["s5t1", "s5t2"], [tk])
                                self.tt(tv1[:, :, 0:L], si, mr, ALU.mult, [tk], ["s5t1"])
                                self.tt(tv2[:, :, 0:L], sr, mi, ALU.mult, [tk], ["s5t2"])
                                self.tt(di, tv2[:, :, 0:L], tv1[:, :, 0:L], ALU.add, ["s5t1", "s5t2"], [tk])
                                L *= 2
                        if d == 0:
                            Nsrc, ntk = N, "s5N"
                        else:
                            self.cp(Nr[:].rearrange("p a k j -> p (a k) j"), N[:].rearrange("p a k j -> p (a k) j")[:, :, ::-1], ["s5N"], ["s5Nr"])
                            Nsrc, ntk = Nr, "s5Nr"
                        for part in range(2):
                            for kq in range(4):
                                self.tr(psT[part][:, kq * 128:(kq + 1) * 128], Nsrc[:, part, kq, :], self.identF[:], [ntk, "identF"], [psTt[part]])
                            self.cp(Eit[:, part * 512:(part + 1) * 512], psT[part][:], [psTt[part]], [etk], eng="act")
                        self.ts(Eisd[d][:, 0:512], Eit[:, 512:1024], -1.0, None, ALU.mult, None, [etk], ["s5Eis%d" % d])
                        self.cp(Eisd[d][:, 512:1024], Eit[:, 0:512], [etk], ["s5Eis%d" % d], eng="pool")
                        self.ts(Psd[d][:, 0], P[:, 1], -1.0, None, ALU.mult, None, [ptk], ["s5Ps%d" % d])
                        self.cp(Psd[d][:, 1], P[:, 0], [ptk], ["s5Ps%d" % d], eng="pool")
                        self.dma(braw[:, 0], W["ssm_b_re"][d, gs].rearrange("(k g) p m -> (g p) k m", g=2), [], ["s5braw"], slow=True)
                        self.dma(braw[:, 1], W["ssm_b_im"][d, gs].rearrange("(k g) p m -> (g p) k m", g=2), [], ["s5braw"], slow=True)
                        crb = CR.unsqueeze(2).to_broadcast([128, 4, 16])
                        cib = CI.unsqueeze(2).to_broadcast([128, 4, 16])
                        self.tt(bbar[:, 0], braw[:, 0], crb, ALU.mult, ["s5braw", "sc"], ["s5bbar"])
                        self.tt(bt[:, 0], braw[:, 1], cib, ALU.mult, ["s5braw", "sc"], ["s5bt"])
                        self.tt(bbar[:, 0], bbar[:, 0], bt[:, 0], ALU.subtract, ["s5bbar", "s5bt"], ["s5bbar"])
                        self.tt(bbar[:, 1], braw[:, 1], crb, ALU.mult, ["s5braw", "sc"], ["s5bbar"])
                        self.tt(bt[:, 1], braw[:, 0], cib, ALU.mult, ["s5braw", "sc"], ["s5bt"])
                        self.tt(bbar[:, 1], bbar[:, 1], bt[:, 1], ALU.add, ["s5bbar", "s5bt"], ["s5bbar"])
                        self.memset(Zp[:], 0.0, ["s5Zp"])
                        self.memset(Yp[:], 0.0, ["s5Yp"], eng="pool")
                        for part in range(2):
                            for kq in range(4):
                                for g2 in range(2):
                                    rs_ = slice(g2 * 64, (g2 + 1) * 64)
                                    c0 = 32 * kq + 16 * g2
                                    self.cp(Zp[rs_, part, kq, c0:c0 + 16], bbar[rs_, part, kq, :], ["s5bbar"], ["s5Zp"])
                        for part in range(2):
                            for kq in range(4):
                                self.tr(psT[part][:, kq * 128:(kq + 1) * 128], Zp[:, part, kq, :], self.identF[:], ["s5Zp", "identF"], [psTt[part]])
                            self.cp(Bbd[:, part * 512:(part + 1) * 512], psT[part][:], [psTt[part]], [btk], eng="act")
                        for part, nm in enumerate(("ssm_c_re", "ssm_c_im")):
                            for kq in range(4):
                                for g2 in range(2):
                                    g = 8 * c + 2 * kq + g2
                                    r0 = 32 * kq + 16 * g2
                                    self.dma(Yp[r0:r0 + 16, part, kq, 64 * g2:64 * g2 + 64], W[nm][d, g], [], ["s5Yp"])
                        for part in range(2):
                            for kq in range(4):
                                self.tr(psT[part][:, kq * 128:(kq + 1) * 128], Yp[:, part, kq, :], self.identF[:], ["s5Yp", "identF"], [psTt[part]])
                            if part == 0:
                                self.cp(Cbd[:, 0].rearrange("p k j -> p (k j)"), psT[0][:], [psTt[0]], [ctk], eng="act")
                            else:
                                self.ts(Cbd[:, 1].rearrange("p k j -> p (k j)"), psT[1][:], -1.0, None, ALU.mult, None, [psTt[1]], [ctk])
                    if self.tabcache is not None and s == 0:
                        tc = self.tabcache
                        for d in range(2):
                            self.dma(tc["P"][c, d], Pd[d][:].rearrange("p a k j -> p (a k j)"), ["s5P%d" % d], [("tabc", c, d)])
                            self.dma(tc["E"][c, d], Eitd[d][:], ["s5Eit%d" % d], [("tabc", c, d)])
                            self.dma(tc["B"][c, d], Bbdd[d][:], ["s5Bbd%d" % d], [("tabc", c, d)])
                            self.dma(tc["C"][c, d], Cbdd[d][:].rearrange("p a k j -> p (a k j)"), ["s5Cbd%d" % d], [("tabc", c, d)])
                    def half1(d, ci_):
                        mc = order[d][ci_]
                        tsl = slice(mc * 128, (mc + 1) * 128)
                        Ei, Eis, Bbd = Eitd[d], Eisd[d], Bbdd[d]
                        tri = self.triL if d == 0 else self.triU
                        for part in range(2):
                            self.mm(psA[part][:], suT[:, tsl], Bbd[:, part * 512:(part + 1) * 512], True, True, ["suT", "s5Bbd%d" % d], ["psCA%d" % part])
                        v2 = lambda a: a.rearrange("p (a b) -> p a b", a=2)
                        ta, tb = wA[d], wB[d]
                        ka, kb = "s5wA%d" % d, "s5wB%d" % d
                        self.tt(v2(ta[:]), psA[0][:].unsqueeze(1).to_broadcast([128, 2, 512]), v2(Ei[:]), ALU.mult, ["psCA0", "s5Eit%d" % d], [ka])
                        self.tt(v2(tb[:]), psA[1][:].unsqueeze(1).to_broadcast([128, 2, 512]), v2(Eis[:]), ALU.mult, ["psCA1", "s5Eis%d" % d], [kb])
                        for part in range(2):
                            for kq in range(4):
                                cs = slice(part * 512 + kq * 128, part * 512 + (kq + 1) * 128)
                                self.mm(psZd[d][part][:, kq * 128:(kq + 1) * 128], ta[:, cs], tri[:], True, False, [ka, "triL", "triU"], ["psCZ%d%d" % (d, part)])
                                self.mm(psZd[d][part][:, kq * 128:(kq + 1) * 128], tb[:, cs], tri[:], False, True, [kb, "triL", "triU"], ["psCZ%d%d" % (d, part)])

                    v4 = lambda a: a.rearrange("p (a k j) -> p a k j", a=2, k=4)

                    def half2a(d, ci_):
                        P, Ps, Zc = Pd[d], Psd[d], Zcd[d]
                        cp_ = carry[d][(ci_ + 1) % 2]
                        cpt = "s5cy%d%d" % (d, (ci_ + 1) % 2)
                        cn_ = carry[d][ci_ % 2]
                        cnt_ = "s5cy%d%d" % (d, ci_ % 2)
                        jc = 127 if d == 0 else 0
                        zct = "s5Zc%d" % d
                        for part in range(2):
                            for kq in range(4):
                                o0 = part * 512 + kq * 128
                                self.act(Zc[:, o0:o0 + 128], psZd[d][part][:, kq * 128:(kq + 1) * 128], AF.Identity,
                                         ["psCZ%d%d" % (d, part), cpt], [zct], bias=cp_[:, part * 4 + kq:part * 4 + kq + 1])
                        zrc = Zc[:, 0:512].rearrange("p (k j) -> p k j", j=128)[:, :, jc].unsqueeze(1).to_broadcast([128, 2, 4])
                        zic = Zc[:, 512:1024].rearrange("p (k j) -> p k j", j=128)[:, :, jc].unsqueeze(1).to_broadcast([128, 2, 4])
                        u1, u2 = cu[d][0], cu[d][1]
                        c3 = lambda a: a.rearrange("p (a k) -> p a k", a=2)
                        self.tt(c3(u1[:]), zrc, P[:, :, :, 127], ALU.mult, [zct, "s5P%d" % d], ["s5u1%d" % d])
                        self.tt(c3(u2[:]), zic, Ps[:, :, :, 127], ALU.mult, [zct, "s5Ps%d" % d], ["s5u2%d" % d])
                        self.tt(cn_[:], u1[:], u2[:], ALU.add, ["s5u1%d" % d, "s5u2%d" % d], [cnt_])
                        if d == 0:
                            pa, pb = P[:], Ps[:]
                        else:
                            pa, pb = P[:, :, :, ::-1], Ps[:, :, :, ::-1]
                        zr = Zc[:, 0:512].rearrange("p (k j) -> p k j", j=128).unsqueeze(1).to_broadcast([128, 2, 4, 128])
                        zi = Zc[:, 512:1024].rearrange("p (k j) -> p k j", j=128).unsqueeze(1).to_broadcast([128, 2, 4, 128])
                        self.tt(v4(xB[d][:]), zi, pb, ALU.mult, [zct, "s5Ps%d" % d], ["s5xB%d" % d], eng="pool")
                        self.tt(v4(xA[d][:]), zr, pa, ALU.mult, [zct, "s5P%d" % d], ["s5xA%d" % d])

                    def half2b(d, ci_):
                        Cbd = Cbdd[d]
                        n8 = 0
                        for (xt, xk) in ((xA[d], "s5xA%d" % d), (xB[d], "s5xB%d" % d)):
                            for part in range(2):
                                for kq in range(4):
                                    o0 = part * 512 + kq * 128
                                    self.mm(psYd[d][:], Cbd[:, part, kq, :], xt[:, o0:o0 + 128], n8 == 0, n8 == 15, ["s5Cbd%d" % d, xk], ["psCY%d" % d])
                                    n8 += 1

                    def yadd(d, ci_):
                        mc = order[d][ci_]
                        tsl = slice(mc * 128, (mc + 1) * 128)
                        self.tt(yacc[:, tsl], yacc[:, tsl], psYd[d][:], ALU.add, [("yacc", mc), "psCY%d" % d], [("yacc", mc)])

                    for d in range(2):
                        self.memset(carry[d][1][:], 0.0, ["s5cy%d1" % d])
                    half1(0, 0)
                    half1(1, 0)
                    pend = []
                    for ci_ in range(NT):
                        for d in range(2):
                            half2a(d, ci_)
                            if ci_ + 1 < NT:
                                half1(d, ci_ + 1)
                            while pend:
                                yadd(*pend.pop(0))
                            half2b(d, ci_)
                            pend.append((d, ci_))
                    while pend:
                        yadd(*pend.pop(0))
                    Zc = Zcd[0]
                    yall = [("yacc", j) for j in range(NT)]
                    for a0 in range(0, LT, 1024):
                        a1 = min(LT, a0 + 1024)
                        w_ = a1 - a0
                        self.tt(Zc[:, 0:w_], yacc[:, a0:a1], yacc[:, a0:a1], ALU.mult, yall, ["s5Zc0"])
                        self.ts(Zc[:, 0:w_], Zc[:, 0:w_], 0.044715, 1.0, ALU.mult, ALU.add, ["s5Zc0"], ["s5Zc0"])
                        self.tt(Zc[:, 0:w_], Zc[:, 0:w_], yacc[:, a0:a1], ALU.mult, ["s5Zc0"] + yall, ["s5Zc0"])
                        self.act(Zc[:, 0:w_], Zc[:, 0:w_], AF.Sigmoid, ["s5Zc0"], ["s5Zc0"], scale=1.5957691216057308)
                        self.tt(gel[:, c, a0:a1], Zc[:, 0:w_], yacc[:, a0:a1], ALU.mult, ["s5Zc0"] + yall, [("gel", c)])
                    S.barrier()
            if self.dbg.get("s5_noglu"):
                S.barrier()
                return
            with ExitStack() as st4:
                stage = self.sb("stgC", [128, 8, 512], F32, st4)
                wgl = [self.sb("wglu", [128, 4, 512], BF16, st4) for _ in range(2)]
                gbias = self.sb("gbias", [128, 8], F32, st4)
                sg = [self.sb("sgC", [128, 512], F32, st4) for _ in range(2)]
                yst = [self.sb("ystC", [128, 512], F32, st4) for _ in range(2)]
                psa = [self.ps("psGa", [128, 512], F32, st4) for _ in range(2)]
                psg = [self.ps("psGg", [128, 512], F32, st4) for _ in range(2)]
                self.dma(gbias[:], W["ssm_glu_b"].rearrange("(j p) -> p j", p=128), [], ["gbias"], slow=True)
                for half in range(2):
                    self.load_w("wglu", W["ssm_glu_w"], half * 512, 512, 4, wgl[half], "wglu%d" % half, stage, "stgC")
                cnt = 0
                for j in range(4):
                    for ci, (t0, n) in enumerate(TOKCH):
                        i2 = cnt % 2
                        cnt += 1
                        for k in range(4):
                            self.mm(psa[i2][:, 0:n], wgl[0][:, k, j * 128:(j + 1) * 128], gel[:, k, t0:t0 + n], k == 0, k == 3,
                                    ["wglu0", ("gel", k)], ["psGa%d" % i2])
                        for k in range(4):
                            self.mm(psg[i2][:, 0:n], wgl[1][:, k, j * 128:(j + 1) * 128], gel[:, k, t0:t0 + n], k == 0, k == 3,
                                    ["wglu1", ("gel", k)], ["psGg%d" % i2])
                        self.act(sg[i2][:, 0:n], psg[i2][:, 0:n], AF.Sigmoid, ["psGg%d" % i2, "gbias"], ["sgC%d" % i2], bias=gbias[:, 4 + j:5 + j])
                        self.stt(yst[i2][:, 0:n], psa[i2][:, 0:n], gbias[:, j:j + 1], sg[i2][:, 0:n], ALU.add, ALU.mult,
                                 ["psGa%d" % i2, "gbias", "sgC%d" % i2], ["ystC%d" % i2])
                        self.dma(ybr_a[j * 128:(j + 1) * 128, t0:t0 + n], yst[i2][:, 0:n], ["ystC%d" % i2], [("ybr", s, 0)])
                S.barrier()
            S.barrier()

CONST_SHAPES = {"ident": [128, 128], "ones": [128, 128], "tril": [128, 128], "triu": [128, 128],
                "ropeC": [128, 16, 64], "ropeS": [128, 16, 64], "lamc": [DEPTH, 128, 2]}
LAYER_W = [("norm_g", [D]), ("ada_w", [D, 3 * D]), ("ada_b", [3 * D]), ("w_in", [D, D_IN]),
           ("w_ssm_out", [512, D]), ("w_ml_out", [512, D]), ("w_da_out", [512, D]), ("w_out", [D, D]),
           ("da_qnorm_g", [64]), ("da_knorm_g", [64]), ("da_lambda", [4, 64]), ("da_subln_g", [128]),
           ("ml_conv_w", [3, 1024]), ("ml_conv_b", [1024]), ("ml_gate_b", [4, 4]), ("ml_norm_g", [512]),
           ("ssm_lam_re", [2, 32, 64]), ("ssm_lam_im", [2, 32, 64]), ("ssm_log_step", [2, 32]),
           ("ssm_b_re", [2, 32, 64, 16]), ("ssm_b_im", [2, 32, 64, 16]), ("ssm_c_re", [2, 32, 16, 64]),
           ("ssm_c_im", [2, 32, 16, 64]), ("ssm_d", [512]), ("ssm_glu_w", [512, 1024]), ("ssm_glu_b", [1024])]


def host_consts(li=0):
    i = np.arange(128)
    c = {
        "ident": np.eye(128, dtype=np.float32),
        "ones": np.ones((128, 128), np.float32),
        "tril": (i[:, None] <= i[None, :]).astype(np.float32),
        "triu": (i[:, None] >= i[None, :]).astype(np.float32),
    }
    t = np.arange(LL)
    row = (t // 64).astype(np.float32)
    col = (t % 64).astype(np.float32)
    half = 32
    inv = np.power(np.float32(10000.0), -np.arange(0, half, 2, dtype=np.float32) / np.float32(half)).astype(np.float32)
    ar = (row[:, None] * inv).astype(np.float32)
    ac = (col[:, None] * inv).astype(np.float32)
    cr, sr, cc, sc = np.cos(ar), np.sin(ar), np.cos(ac), np.sin(ac)
    C64 = np.concatenate([cr, cr, cc, cc], axis=1).astype(np.float32)
    S64 = np.concatenate([-sr, sr, -sc, sc], axis=1).astype(np.float32)
    c["ropeC"] = np.ascontiguousarray(C64.reshape(16, 128, 64).transpose(1, 0, 2))
    c["ropeS"] = np.ascontiguousarray(S64.reshape(16, 128, 64).transpose(1, 0, 2))
    lam = [0.8 - 0.6 * math.exp(-0.3 * l) for l in range(DEPTH)]
    c["lamc"] = np.stack([np.tile(np.array([[-v, 1.0 - v]], np.float32), (128, 1)) for v in lam])
    return c


def build_layer_program(li=0, branches=("a", "b", "c"), dump_ybr=False, do_merge=True, dbg=None):
    nc = bass.Bass("TRN2", target_bir_lowering=False)
    S = Sched()
    W = {}
    for nm, shp in LAYER_W:
        W[nm] = nc.dram_tensor(nm, shp, F32, kind="ExternalInput").ap()
    W["cvec"] = nc.dram_tensor("cvec", [3, D], F32, kind="ExternalInput").ap()
    cin = {nm: nc.dram_tensor(nm, shp, F32, kind="ExternalInput").ap() for nm, shp in CONST_SHAPES.items()}
    xT = nc.dram_tensor("xT", [NSEQ, D, LT], F32, kind="ExternalInput").ap()
    xo = nc.dram_tensor("xo", [NSEQ, D, LT], F32, kind="ExternalOutput").ap()
    ybr_in = None
    if len(branches) < 3:
        ybr_in = nc.dram_tensor("ybr", [NSEQ, 3, 512, LT], F32, kind="ExternalInput").ap()
    ybr_dev = nc.dram_tensor("ybr_dev", [NSEQ, 3, 512, LT], F32, kind="ExternalOutput" if dump_ybr else "Internal").ap()
    with ExitStack() as stack:
        B = LayerBuilder(nc, S, stack, dbg=dbg)
        if (dbg or {}).get("tabcache"):
            B.tabcache = {"P": nc.dram_tensor("tabP", [4, 2, 128, 1024], F32).ap(), "E": nc.dram_tensor("tabE", [4, 2, 128, 1024], F32).ap(),
                          "B": nc.dram_tensor("tabB", [4, 2, 128, 1024], BF16).ap(), "C": nc.dram_tensor("tabC", [4, 2, 128, 1024], BF16).ap()}
        B.setup_consts(cin)
        B.hT = B.sb("hT", [128, 8, LT], BF16)
        B.stage_mod(W)
        for s in range((dbg or {}).get("nseq", NSEQ)):
            B.stage_norm(xT[s], s)
            srcs = []
            for bi, b in enumerate("abc"):
                srcs.append(ybr_dev[s, bi] if b in branches else ybr_in[s, bi])
            if "a" in branches:
                B.stage_s5(W, cin, ybr_dev[s, 0], s, li)
            if "b" in branches:
                B.stage_mlstm(W, cin, ybr_dev[s, 1], s, li)
            if "c" in branches:
                B.stage_attn(W, cin, ybr_dev[s, 2], s, li)
            if do_merge:
                B.stage_merge(W, xT[s], xo[s], srcs, s)
        S.emit(nc, stack)
    return nc, S


_PROG = {}


def build_program(n_layers=DEPTH, layer0=0):
    nc = bass.Bass("TRN2", target_bir_lowering=False)
    S = Sched()
    Wall = {}
    for nm, shp in LAYER_W:
        Wall[nm] = nc.dram_tensor(nm, [DEPTH] + list(shp), F32, kind="ExternalInput").ap()
    cvec = nc.dram_tensor("cvec", [3, D], F32, kind="ExternalInput").ap()
    cin = {nm: nc.dram_tensor(nm, shp, F32, kind="ExternalInput").ap() for nm, shp in CONST_SHAPES.items()}
    xT = nc.dram_tensor("xT", [NSEQ, D, LT], F32, kind="ExternalInput").ap()
    xo = nc.dram_tensor("xo", [NSEQ, D, LT], F32, kind="ExternalOutput").ap()
    xs = [nc.dram_tensor("xs%d" % i, [NSEQ, D, LT], F32).ap() for i in range(2)]
    ybr_dev = nc.dram_tensor("ybr_dev", [NSEQ, 3, 512, LT], F32).ap()
    tabc = {"P": nc.dram_tensor("tabP", [4, 2, 128, 1024], F32).ap(), "E": nc.dram_tensor("tabE", [4, 2, 128, 1024], F32).ap(),
            "B": nc.dram_tensor("tabB", [4, 2, 128, 1024], BF16).ap(), "C": nc.dram_tensor("tabC", [4, 2, 128, 1024], BF16).ap()}
    with ExitStack() as stack:
        B = LayerBuilder(nc, S, stack)
        B.tabcache = tabc
        B.setup_consts(cin)
        B.hT = B.sb("hT", [128, 8, LT], BF16)
        for j in range(n_layers):
            li = layer0 + j
            S.epoch = j
            W = {nm: Wall[nm][li] for nm, _ in LAYER_W}
            W["cvec"] = cvec
            x_in, tin = (xT, "xT") if j == 0 else (xs[(j - 1) % 2], "xs%d" % ((j - 1) % 2))
            x_out, tout = (xo, "xo") if j == n_layers - 1 else (xs[j % 2], "xs%d" % (j % 2))
            B.stage_mod(W)
            for s in range(NSEQ):
                B.stage_norm(x_in[s], s, tin)
                B.stage_s5(W, cin, ybr_dev[s, 0], s, li)
                B.stage_mlstm(W, cin, ybr_dev[s, 1], s, li)
                B.stage_attn(W, cin, ybr_dev[s, 2], s, li)
                B.stage_merge(W, x_in[s], x_out[s], [ybr_dev[s, b] for b in range(3)], s, tin, tout)
        S.emit(nc, stack)
    return nc, S


def _fm(ctx, lat):
    return np.ascontiguousarray(np.concatenate([ctx, lat], axis=1).transpose(0, 2, 1)).astype(np.float32)


def kernel(**inputs):
    x = np.asarray(inputs["x"], np.float32)
    c = np.asarray(inputs["c"], np.float32)
    ctx = np.asarray(inputs["ctx"], np.float32)
    c_ctx = np.asarray(inputs["c_ctx"], np.float32)
    ncores = 8
    if "p" not in _PROG:
        _PROG["p"] = build_program()
    nc, _ = _PROG["p"]
    consts = host_consts()
    wl = {nm: np.ascontiguousarray(np.asarray(inputs[nm], np.float32)) for nm, _ in LAYER_W}
    in_maps = []
    for i in range(ncores):
        m = {"xT": _fm(ctx[2 * i:2 * i + 2], x[2 * i:2 * i + 2]),
             "cvec": np.stack([c[2 * i], c[2 * i + 1], c_ctx]).astype(np.float32)}
        m.update(wl)
        m.update(consts)
        in_maps.append(m)
    res = run_bass_kernel_spmd(nc, in_maps, core_ids=list(range(ncores)))
    out = np.concatenate([np.ascontiguousarray(np.asarray(r["xo"], np.float32)[:, :, LC:].transpose(0, 2, 1))
                          for r in res.results], axis=0)
    return out.astype(np.float32)
```

```python
import math
from contextlib import ExitStack

import numpy as np
import concourse.bass as bass
import concourse.mybir as mybir
from concourse.bass_utils import run_bass_kernel_spmd

F32 = mybir.dt.float32
BF16 = mybir.dt.bfloat16
AF = mybir.ActivationFunctionType
ALU = mybir.AluOpType
AX = mybir.AxisListType

D = 1024
LC = 256
LL = 2048
LT = LC + LL
NT = LT // 128
DEPTH = 4
EPS = 1e-6
NSEQ = 2
O_SU, O_SZ, O_MQK, O_MV, O_MO, O_MZ, O_MG, O_DQ, O_DK, O_DV, O_DZ, O_GL = (
    0, 512, 1024, 2048, 2560, 3072, 3584, 3600, 4112, 4624, 5136, 5648)
D_IN = 8720
TOKCH = [(0, 256), (256, 512), (768, 512), (1280, 512), (1792, 512)]
TWO_PI = 2.0 * math.pi


class _Op:
    __slots__ = ("eng", "fn", "deps", "dma", "ms", "sem", "val", "idx", "ep")


class Sched:
    ENGS = ("pe", "act", "dve", "pool", "sp")
    R = 6

    def __init__(self):
        self.ops = []
        self.tw = {}
        self.tr = {}
        self.last = {e: None for e in self.ENGS}
        self.pending_barrier = {e: set() for e in self.ENGS}
        self.dma_hist = {e: [] for e in self.ENGS}
        self.epoch = 0

    def add(self, eng, fn, reads=(), writes=(), dma=False):
        op = _Op()
        op.eng, op.fn, op.dma, op.ms, op.sem, op.val = eng, fn, dma, False, None, None
        op.idx = len(self.ops)
        op.ep = self.epoch
        xs = [r for r in reads if isinstance(r, str) and (r.startswith("ps") or r.startswith("acc"))]
        if xs:
            writes = list(writes) + [x for x in xs if x not in writes]
        deps = set()
        for r in reads:
            w = self.tw.get(r)
            if w is not None:
                deps.add(w)
        for wt in writes:
            w = self.tw.get(wt)
            if w is not None:
                deps.add(w)
            for rr in self.tr.get(wt, ()):
                deps.add(rr)
        deps |= self.pending_barrier[eng]
        self.pending_barrier[eng] = set()
        keep = set()
        rset = None
        for d in deps:
            o = self.ops[d]
            if o.eng == eng and not o.dma and not dma:
                if eng == "pe":
                    continue
            keep.add(d)
        op.deps = keep
        for r in reads:
            self.tr.setdefault(r, []).append(op.idx)
        for wt in writes:
            self.tw[wt] = op.idx
            self.tr[wt] = []
        self.ops.append(op)
        self.last[eng] = op.idx
        if dma:
            self.dma_hist[eng].append(op.idx)
        return op.idx

    def barrier(self):
        s = set()
        for e in self.ENGS:
            if self.last[e] is not None:
                s.add(self.last[e])
            for i in self.dma_hist[e][-self.R:]:
                s.add(i)
        for e in self.ENGS:
            self.pending_barrier[e] |= s

    def emit(self, nc, stack):
        ops = self.ops
        for op in ops:
            for d in op.deps:
                ops[d].ms = True
        neps = max(op.ep for op in ops) + 1
        csem = {(e, ep): stack.enter_context(nc.semaphore("c_%s%d" % (e, ep))) for e in self.ENGS for ep in range(neps)}
        dsem = {e: [stack.enter_context(nc.semaphore("d_%s%d" % (e, i))) for i in range(self.R)]
                for e in ("sp", "pool", "act")}
        ccount = {(e, ep): 0 for e in self.ENGS for ep in range(neps)}
        dcount = {e: 0 for e in self.ENGS}
        per_eng = {e: [] for e in self.ENGS}
        for op in ops:
            if op.dma:
                i = dcount[op.eng]
                op.sem = dsem[op.eng][i % self.R]
                op.val = 16 * (i // self.R + 1)
                dcount[op.eng] += 1
            elif op.ms:
                ccount[(op.eng, op.ep)] += 1
                op.sem = csem[(op.eng, op.ep)]
                op.val = ccount[(op.eng, op.ep)]
            per_eng[op.eng].append(op)
        self.stats = {e: (len(per_eng[e]), sum(ccount[(e, ep)] for ep in range(neps)), dcount[e]) for e in self.ENGS}
        R = self.R

        def replay(ename, e):
            seen = {}
            ndma = 0
            for op in per_eng[ename]:
                waits = {}
                for d in op.deps:
                    o = ops[d]
                    k = o.sem
                    if waits.get(k, (None, 0))[1] < o.val:
                        waits[k] = (o.sem, o.val)
                if op.dma:
                    if ndma >= R:
                        k = op.sem
                        v = op.val - 16
                        if waits.get(k, (None, 0))[1] < v:
                            waits[k] = (op.sem, v)
                    ndma += 1
                for k, (sem, val) in waits.items():
                    if seen.get(k, 0) < val:
                        e.wait_ge(sem, val)
                        seen[k] = val
                ins = op.fn(e)
                if op.dma:
                    ins.then_inc(op.sem, 16)
                elif op.ms:
                    ins.then_inc(op.sem, 1)
            if ename in dsem:
                n = dcount[ename]
                for j in range(min(n, R)):
                    cnt = (n - 1 - j) // R + 1
                    e.wait_ge(dsem[ename][j], 16 * cnt)

        with nc.Block() as block:
            @block.tensor
            def _(e):
                replay("pe", e)

            @block.scalar
            def _(e):
                replay("act", e)

            @block.vector
            def _(e):
                replay("dve", e)

            @block.gpsimd
            def _(e):
                replay("pool", e)

            @block.sync
            def _(e):
                replay("sp", e)


class LayerBuilder:
    def __init__(self, nc, S, stack, dbg=None):
        self.nc, self.S, self.stack = nc, S, stack
        self.dbg = dbg or {}
        self.uid = 0
        self.wq = 0
        self.tabcache = None

    def sb(self, name, shape, dt, stack=None):
        self.uid += 1
        return (stack or self.stack).enter_context(self.nc.sbuf_tensor("%s_%d" % (name, self.uid), shape, dt))

    def ps(self, name, shape, dt=F32, stack=None):
        self.uid += 1
        full = 512 if dt == F32 else 1024
        t = (stack or self.stack).enter_context(self.nc.psum_tensor("%s_%d" % (name, self.uid), [128, full], dt))
        n = 1
        for d in shape[1:]:
            n *= d
        assert n <= full, (name, shape)
        v = t[0:shape[0], 0:n]
        if len(shape) == 3:
            v = v.rearrange("p (a b) -> p a b", b=shape[2])
        return v

    def dma(self, out, in_, reads, writes, q=None, slow=False):
        if q is None:
            q = "sp"
        if slow:
            fn = lambda e, o=out, i=in_: e.dma_start(out=o, in_=i, allow_slow_non_contiguous=True)
        else:
            fn = lambda e, o=out, i=in_: e.dma_start(out=o, in_=i)
        return self.S.add(q, fn, reads, writes, dma=True)

    def mm(self, out, lhsT, rhs, start, stop, reads, writes, skip=False):
        if skip:
            fn = lambda e: e.matmul(out, lhsT, rhs, start=start, stop=stop, skip_group_check=True)
        else:
            fn = lambda e: e.matmul(out, lhsT, rhs, start=start, stop=stop)
        return self.S.add("pe", fn, reads, writes)

    def tr(self, out, in_, ident, reads, writes):
        return self.S.add("pe", lambda e: e.transpose(out, in_, ident), reads, writes)

    def act(self, out, in_, func, reads, writes, bias=None, scale=None, accum_out=None):
        kw = {}
        if bias is not None:
            kw["bias"] = bias
        if scale is not None:
            kw["scale"] = scale
        if accum_out is not None:
            kw["accum_out"] = accum_out
        return self.S.add("act", lambda e: e.activation(out=out, in_=in_, func=func, **kw), reads, writes)

    def tt(self, out, in0, in1, op, reads, writes, eng="dve"):
        return self.S.add(eng, lambda e: e.tensor_tensor(out=out, in0=in0, in1=in1, op=op), reads, writes)

    def ts(self, out, in0, s1, s2, op0, op1, reads, writes, eng="dve"):
        if op1 is None:
            fn = lambda e: e.tensor_scalar(out=out, in0=in0, scalar1=s1, scalar2=None, op0=op0)
        else:
            fn = lambda e: e.tensor_scalar(out=out, in0=in0, scalar1=s1, scalar2=s2, op0=op0, op1=op1)
        return self.S.add(eng, fn, reads, writes)

    def stt(self, out, in0, scalar, in1, op0, op1, reads, writes):
        return self.S.add("dve", lambda e: e.scalar_tensor_tensor(out=out, in0=in0, scalar=scalar, in1=in1,
                                                                  op0=op0, op1=op1), reads, writes)

    def cp(self, out, in_, reads, writes, eng="dve"):
        if eng == "act":
            return self.S.add("act", lambda e: e.activation(out=out, in_=in_, func=AF.Copy), reads, writes)
        return self.S.add(eng, lambda e: e.tensor_copy(out=out, in_=in_), reads, writes)

    def rsqrt(self, out, in_, scale, eps, reads, writes):
        et = self.epsT[eps]
        np_ = out.shape[0]
        self.act(out, in_, AF.Sqrt, list(reads) + [("epsT", eps)], writes, bias=et[0:np_, :], scale=scale)
        return self.S.add("dve", lambda e: e.reciprocal(out=out, in_=out), writes, writes)

    def rsqrt_le(self, out, in_, scale, eps, reads, writes):
        et = self.epsT[eps]
        np_ = out.shape[0]
        self.act(out, in_, AF.Ln, list(reads) + [("epsT", eps)], writes, bias=et[0:np_, :], scale=scale)
        return self.act(out, out, AF.Exp, writes, writes, scale=-0.5)

    def memset(self, ap, val, writes, eng="dve"):
        return self.S.add(eng, lambda e: e.memset(ap, val), (), writes)

    def setup_consts(self, cin):
        S = self.S
        self.identF = self.sb("identF", [128, 128], F32)
        self.identB = self.sb("identB", [128, 128], BF16)
        self.onesF = self.sb("onesF", [128, 128], F32)
        self.triL = self.sb("triL", [128, 128], BF16)
        self.triU = self.sb("triU", [128, 128], BF16)
        self.dma(self.identF[:], cin["ident"], (), ["identF"])
        self.dma(self.onesF[:], cin["ones"], (), ["onesF"])
        tmp = self.sb("ctmp", [128, 256], F32)
        self.dma(tmp[:, 0:128], cin["tril"], (), ["ctmp0"])
        self.dma(tmp[:, 128:256], cin["triu"], (), ["ctmp1"])
        self.triLF = tmp[:, 0:128]
        self.triUF = tmp[:, 128:256]
        self.epsT = {}
        t = self.sb("epsT", [128, 1], F32)
        self.memset(t[:], EPS, [("epsT", EPS)])
        self.epsT[EPS] = t
        self.cp(self.identB[:], self.identF[:], ["identF"], ["identB"])
        self.cp(self.triL[:], tmp[:, 0:128], ["ctmp0"], ["triL"])
        self.cp(self.triU[:], tmp[:, 128:256], ["ctmp1"], ["triU"])

    def load_w(self, name, w_dram, c0, ncols, KT, dst, dst_tok, stage, stage_tok, q="pool", cast_eng="pool"):
        src = w_dram.rearrange("(k p) c -> p k c", p=128)[:, :, c0:c0 + ncols]
        self.dma(stage[:, 0:KT, 0:ncols], src, [], [stage_tok], q=q)
        self.cp(dst[:, 0:KT, 0:ncols], stage[:, 0:KT, 0:ncols], [stage_tok], [dst_tok], eng=cast_eng)


    def stage_mod(self, W):
        S, nc = self.S, self.nc
        self.modT = self.sb("modT", [128, 24, 3], F32)
        self.A1 = self.sb("A1", [128, 8, 3], F32)
        with ExitStack() as st:
            cS = self.sb("cS", [128, 8, 3], F32, st)
            adab = self.sb("adab", [128, 24], F32, st)
            normg = self.sb("normg", [128, 8], F32, st)
            stage = self.sb("stgA", [128, 8, 512], F32, st)
            stage2 = self.sb("stgA2", [128, 8, 512], F32, st)
            psM = self.ps("psM", [128, 72], F32, st)
            for v in range(3):
                self.dma(cS[:, :, v], W["cvec"][v].rearrange("(k p) -> p k", p=128), [], ["cS"], slow=True)
            self.dma(adab[:], W["ada_b"].rearrange("(j p) -> p j", p=128), [], ["adab"], slow=True)
            self.dma(normg[:], W["norm_g"].rearrange("(k p) -> p k", p=128), [], ["normg"], slow=True)
            self.act(cS[:], cS[:], AF.Silu, ["cS"], ["cS"])
            stgs = [(stage, "stgA"), (stage2, "stgA2")]
            for ch in range(6):
                stg, tok = stgs[ch % 2]
                src = W["ada_w"].rearrange("(k p) c -> p k c", p=128)[:, :, ch * 512:(ch + 1) * 512]
                self.dma(stg[:], src, [], [tok], q="pool")
                for j in range(4):
                    jj = ch * 4 + j
                    for k in range(8):
                        self.mm(psM[:, 3 * jj:3 * jj + 3], stg[:, k, j * 128:(j + 1) * 128], cS[:, k, :],
                                k == 0, k == 7, [tok, "cS"], ["psM"])
            self.tt(self.modT[:], psM[:].rearrange("p (j v) -> p j v", v=3),
                    adab[:].unsqueeze(2).to_broadcast([128, 24, 3]), ALU.add, ["psM", "adab"], ["modT"])
            self.stt(self.A1[:], self.modT[:, 8:16, :], 1.0, normg[:].unsqueeze(2).to_broadcast([128, 8, 3]),
                     ALU.add, ALU.mult, ["modT", "normg"], ["A1"])
            S.barrier()

    def stage_norm(self, xT_s, s, xtok_in="xin"):
        S = self.S
        with ExitStack() as st:
            xin = [self.sb("xin", [128, 8, 512], F32, st) for _ in range(2)]
            sqt = self.sb("sqt", [128, 8, 512], F32, st)
            rs = self.sb("rs", [128, 512], F32, st)
            tmpn = [self.sb("tmpn", [128, 512], F32, st) for _ in range(2)]
            pss = [self.ps("pss", [128, 512], F32, st) for _ in range(2)]
            for ci, (t0, n) in enumerate(TOKCH):
                v = 2 if t0 == 0 else s
                xi = xin[ci % 2]
                xtok = "xin%d" % (ci % 2)
                ps = pss[ci % 2]
                pstok = "pss%d" % (ci % 2)
                self.dma(xi[:, :, 0:n], xT_s.rearrange("(k p) t -> p k t", p=128)[:, :, t0:t0 + n], [(xtok_in, s)], [xtok])
                self.act(sqt[:, :, 0:n], xi[:, :, 0:n], AF.Square, [xtok], ["sqt"])
                for k in range(8):
                    self.mm(ps[:, 0:n], self.onesF[:], sqt[:, k, 0:n], k == 0, k == 7, ["sqt", "onesF"], [pstok])
                self.rsqrt(rs[:, 0:n], ps[:, 0:n], 1.0 / D, EPS, [pstok], ["rs"])
                for k in range(8):
                    tm = tmpn[k % 2]
                    ttok = "tmpn%d" % (k % 2)
                    self.tt(tm[:, 0:n], xi[:, k, 0:n], rs[:, 0:n], ALU.mult, [xtok, "rs"], [ttok])
                    self.act(self.hT[:, k, t0:t0 + n], tm[:, 0:n], AF.Identity, [ttok, "A1", "modT"], [("hT", ci)],
                             bias=self.modT[:, k, v:v + 1], scale=self.A1[:, k, v:v + 1])
            S.barrier()

    def stage_merge(self, W, xT_s, xo_s, ybr_s, s, xtok_in="xin", xtok_out="xout"):
        S = self.S
        hT = self.hT
        with ExitStack() as st:
            yg = [self.sb("yg%d" % b, [128, 4, LT], BF16, st) for b in range(3)]
            mT = self.sb("mT", [128, 8, LT], BF16, st)
            stage = self.sb("stgF", [128, 8, 512], F32, st)
            wb = [self.sb("wbF", [128, 8, 512], BF16, st) for _ in range(2)]
            wb2 = [self.sb("wbG", [128, 8, 384], BF16, st), self.sb("wbG", [128, 4, 384], BF16, st)]
            ych = [self.sb("ych", [128, 512], F32, st) for _ in range(2)]
            szt = [self.sb("szt", [128, 512], F32, st) for _ in range(2)]
            sg = [self.sb("sg", [128, 512], F32, st) for _ in range(3)]
            mt = [self.sb("mt", [128, 512], F32, st) for _ in range(3)]
            psA = [self.ps("psA", [128, 512], F32, st) for _ in range(3)]
            psB = [self.ps("psB", [128, 512], F32, st) for _ in range(3)]
            allh = [("hT", i) for i in range(len(TOKCH))]
            zoff = [O_SZ, O_MZ, O_DZ]
            for b in range(3):
                w = wb[b % 2]
                wtok = "wbF%d" % (b % 2)
                self.load_w("wz", W["w_in"], zoff[b], 512, 8, w, wtok, stage, "stgF")
                for c in range(4):
                    for ci, (t0, n) in enumerate(TOKCH):
                        i = (c * 5 + ci) % 2
                        self.dma(ych[i][:, 0:n], ybr_s[b][c * 128:(c + 1) * 128, t0:t0 + n], [("ybr", s, b)], ["ych%d" % i])
                        ps = psA[i]
                        for k in range(8):
                            self.mm(ps[:, 0:n], w[:, k, c * 128:(c + 1) * 128], hT[:, k, t0:t0 + n], k == 0, k == 7,
                                    [wtok, ("hT", ci)], ["psA%d" % i])
                        self.act(szt[i][:, 0:n], ps[:, 0:n], AF.Silu, ["psA%d" % i], ["szt%d" % i])
                        self.tt(yg[b][:, c, t0:t0 + n], ych[i][:, 0:n], szt[i][:, 0:n], ALU.mult,
                                ["ych%d" % i, "szt%d" % i], [("yg", b, ci)])
            wouts = [W["w_ssm_out"], W["w_ml_out"], W["w_da_out"]]
            for f in range(8):
                if f % 2 == 0:
                    wg, wo, wgt, wot = wb[0], wb[1], "wbF0", "wbF1"
                else:
                    wg, wo, wgt, wot = wb2[0], wb2[1], "wbG0", "wbG1"
                for b in range(3):
                    src = W["w_in"].rearrange("(k p) c -> p k c", p=128)[:, :, O_GL + b * 1024 + f * 128:O_GL + b * 1024 + (f + 1) * 128]
                    self.dma(stage[:, :, b * 128:(b + 1) * 128], src, [], ["stgF"], q="pool")
                self.cp(wg[:, :, 0:384], stage[:, :, 0:384], ["stgF"], [wgt], eng="pool")
                for b in range(3):
                    src = wouts[b].rearrange("(k p) c -> p k c", p=128)[:, :, f * 128:(f + 1) * 128]
                    self.dma(stage[:, 0:4, b * 128:(b + 1) * 128], src, [], ["stgF"], q="pool")
                self.cp(wo[:, 0:4, 0:384], stage[:, 0:4, 0:384], ["stgF"], [wot], eng="pool")
                for ci, (t0, n) in enumerate(TOKCH):
                    for b in range(3):
                        for k in range(8):
                            self.mm(psA[b][:, 0:n], wg[:, k, b * 128:(b + 1) * 128], hT[:, k, t0:t0 + n], k == 0, k == 7,
                                    [wgt, ("hT", ci)], ["psA%d" % b])
                        for k in range(4):
                            self.mm(psB[b][:, 0:n], wo[:, k, b * 128:(b + 1) * 128], yg[b][:, k, t0:t0 + n], k == 0, k == 3,
                                    [wot, ("yg", b, ci)], ["psB%d" % b])
                    for b in range(3):
                        self.act(sg[b][:, 0:n], psA[b][:, 0:n], AF.Sigmoid, ["psA%d" % b], ["sg%d" % b])
                        self.tt(mt[b][:, 0:n], sg[b][:, 0:n], psB[b][:, 0:n], ALU.mult, ["sg%d" % b, "psB%d" % b], ["mt%d" % b])
                    self.tt(mt[0][:, 0:n], mt[0][:, 0:n], mt[1][:, 0:n], ALU.add, ["mt0", "mt1"], ["mt0"])
                    self.tt(mT[:, f, t0:t0 + n], mt[0][:, 0:n], mt[2][:, 0:n], ALU.add, ["mt0", "mt2"], [("mT", ci)])
            wo_full = [wb[0], wb[1]]
            for half in range(2):
                self.load_w("wo", W["w_out"], half * 512, 512, 8, wo_full[half], "wbF%d" % half, stage, "stgF")
            for fo in range(8):
                w = wo_full[fo // 4]
                wtok = "wbF%d" % (fo // 4)
                for ci, (t0, n) in enumerate(TOKCH):
                    v = 2 if t0 == 0 else s
                    i = (fo * 5 + ci) % 2
                    ps = psA[i]
                    for k in range(8):
                        self.mm(ps[:, 0:n], w[:, k, (fo % 4) * 128:(fo % 4 + 1) * 128], mT[:, k, t0:t0 + n], k == 0, k == 7,
                                [wtok, ("mT", ci)], ["psA%d" % i])
                    self.dma(ych[i][:, 0:n], xT_s[fo * 128:(fo + 1) * 128, t0:t0 + n], [(xtok_in, s)], ["ych%d" % i])
                    self.stt(szt[i][:, 0:n], ps[:, 0:n], self.modT[:, 16 + fo, v:v + 1], ych[i][:, 0:n], ALU.mult, ALU.add,
                             ["psA%d" % i, "ych%d" % i, "modT"], ["szt%d" % i])
                    self.dma(xo_s[fo * 128:(fo + 1) * 128, t0:t0 + n], szt[i][:, 0:n], ["szt%d" % i], [(xtok_out, s)])
            S.barrier()


    def stage_attn(self, W, cin, ybr_c, s, li, with_ctx=True):
        S = self.S
        hT = self.hT
        lam_init = 0.8 - 0.6 * math.exp(-0.3 * li)
        allh = [("hT", i) for i in range(len(TOKCH))]
        def hts(i):
            t = i * 128
            for ci, (t0, n) in enumerate(TOKCH):
                if t0 <= t < t0 + n:
                    return ("hT", ci)
        with ExitStack() as st:
            qT0 = self.sb("qT0", [128, 4, LT], BF16, st)
            qT1 = self.sb("qT1", [128, 4, LT], BF16, st)
            qTm = [qT0, qT1]
            self.memset(qT0[64:128, :, :], 0.0, ["qz0"], eng="pool")
            self.memset(qT1[0:64, :, :], 0.0, ["qz1"], eng="pool")
            kT = self.sb("kT", [128, 4, LT], BF16, st)
            vaug = self.sb("vaug", [128, NT, 4, 129], BF16, st)
            gq = self.sb("gq", [128, 64], F32, st)
            gk = self.sb("gk", [128, 64], F32, st)
            gsub = self.sb("gsub", [128, 128], F32, st)
            lamt = self.sb("lamt", [128, 256], F32, st)
            lprod = self.sb("lprod", [128, 2, 64], F32, st)
            lsum = self.sb("lsum", [128, 2], F32, st)
            neglam = self.sb("neglam", [128, 1], F32, st)
            self.dma(gq[:], W["da_qnorm_g"].rearrange("(o d) -> o d", o=1).to_broadcast([128, 64]), [], ["gq"])
            self.dma(gk[:], W["da_knorm_g"].rearrange("(o d) -> o d", o=1).to_broadcast([128, 64]), [], ["gk"])
            self.dma(gsub[:], W["da_subln_g"].rearrange("(o d) -> o d", o=1).to_broadcast([128, 128]), [], ["gsub"])
            self.dma(lamt[:], W["da_lambda"].rearrange("(o a) d -> o (a d)", o=1).to_broadcast([128, 256]), [], ["lamt"])
            self.ts(gq[:], gq[:], 0.125, None, ALU.mult, None, ["gq"], ["gq"])
            lamc = self.sb("lamc", [128, 2], F32, st)
            self.dma(lamc[:], cin["lamc"][li], [], ["lamc"])
            self.ts(gsub[:], gsub[:], lamc[:, 1:2], None, ALU.mult, None, ["gsub", "lamc"], ["gsub"])
            lv = lamt[:].rearrange("p (a b d) -> p a b d", a=2, b=2)
            self.tt(lprod[:], lv[:, :, 0, :], lv[:, :, 1, :], ALU.mult, ["lamt"], ["lprod"])
            self.S.add("dve", lambda e: e.tensor_reduce(out=lsum[:], in_=lprod[:], axis=AX.X, op=ALU.add), ["lprod"], ["lsum"])
            self.act(lsum[:], lsum[:], AF.Exp, ["lsum"], ["lsum"])
            self.tt(neglam[:], lsum[:, 1:2], lsum[:, 0:1], ALU.subtract, ["lsum"], ["neglam"])
            self.ts(neglam[:], neglam[:], lamc[:, 0:1], None, ALU.add, None, ["neglam", "lamc"], ["neglam"])
            self.memset(vaug[:, :, :, 128:129], 1.0, ["vones"])
            with ExitStack() as st2:
                stage = self.sb("stgE", [128, 8, 512], F32, st2)
                wq = self.sb("wq", [128, 8, 512], BF16, st2)
                wk = self.sb("wk", [128, 8, 512], BF16, st2)
                wv = self.sb("wv", [128, 8, 512], BF16, st2)
                sqf = self.sb("sqf", [128, 512], F32, st2)
                xraw = self.sb("xraw", [128, 512], F32, st2)
                ss = self.sb("ss8", [128, 8], F32, st2)
                xn = self.sb("xn", [128, 512], F32, st2)
                t1 = self.sb("t1", [128, 512], F32, st2)
                t2 = self.sb("t2", [128, 512], F32, st2)
                xb = [self.sb("xb", [128, 512], BF16, st2) for _ in range(2)]
                ropeC = self.sb("ropeC", [128, 16, 64], F32, st2)
                ropeS = self.sb("ropeS", [128, 16, 64], F32, st2)
                psq = self.ps("psq", [128, 512], F32, st2)
                psk = self.ps("psk", [128, 512], F32, st2)
                psv = self.ps("psv", [128, 512], F32, st2)
                pst = [self.ps("pstE", [128, 512], BF16, st2) for _ in range(2)]
                self.dma(ropeC[:], cin["ropeC"], [], ["ropeC"])
                self.dma(ropeS[:], cin["ropeS"], [], ["ropeS"])
                self.load_w("wq", W["w_in"], O_DQ, 512, 8, wq, "wq", stage, "stgE")
                self.load_w("wk", W["w_in"], O_DK, 512, 8, wk, "wk", stage, "stgE")
                self.load_w("wv", W["w_in"], O_DV, 512, 8, wv, "wv", stage, "stgE")
                for i in range(NT):
                    tsl = slice(i * 128, (i + 1) * 128)
                    ht = hts(i)
                    for (w, wt, ps, pt) in ((wq, "wq", psq, "psq"), (wk, "wk", psk, "psk"), (wv, "wv", psv, "psv")):
                        for k in range(8):
                            self.mm(ps[:], hT[:, k, tsl], w[:, k, :], k == 0, k == 7, [ht, wt], [pt])
                    self.cp(vaug[:, i, :, 0:128], psv[:].rearrange("p (h d) -> p h d", h=4), ["psv"], [("vaug", i)], eng="act")
                    for qi, (ps, pt, g, gt, dst, dt) in enumerate(((psq, "psq", gq, "gq", None, "qT"), (psk, "psk", gk, "gk", kT, "kT"))):
                        self.cp(xraw[:], ps[:], [pt], ["xraw"], eng="act")
                        self.tt(sqf[:], xraw[:], xraw[:], ALU.mult, ["xraw"], ["sqf"])
                        self.S.add("dve", lambda e, : e.tensor_reduce(out=ss[:], in_=sqf[:].rearrange("p (g d) -> p g d", d=64),
                                                                     axis=AX.X, op=ALU.add), ["sqf"], ["ss8"])
                        self.rsqrt_le(ss[:], ss[:], 1.0 / 64, EPS, ["ss8"], ["ss8"])
                        self.tt(xn[:].rearrange("p (g d) -> p g d", d=64), xraw[:].rearrange("p (g d) -> p g d", d=64),
                                ss[:].unsqueeze(2).to_broadcast([128, 8, 64]), ALU.mult, ["xraw", "ss8"], ["xn"])
                        x_b = xb[qi]
                        xbt = "xb%d" % qi
                        if i >= 2:
                            self.tt(xn[:].rearrange("p (g d) -> p g d", d=64), xn[:].rearrange("p (g d) -> p g d", d=64),
                                    g[:].unsqueeze(1).to_broadcast([128, 8, 64]), ALU.mult, ["xn", gt], ["xn"])
                            lt = i - 2
                            self.tt(t1[:].rearrange("p (g d) -> p g d", d=64), xn[:].rearrange("p (g d) -> p g d", d=64),
                                    ropeC[:, lt, :].unsqueeze(1).to_broadcast([128, 8, 64]), ALU.mult, ["xn", "ropeC"], ["t1"])
                            xv = xn[:].rearrange("p (g r h d) -> p g r h d", g=8, r=2, h=2)
                            tv = t2[:].rearrange("p (g r h d) -> p g r h d", g=8, r=2, h=2)
                            sv = ropeS[:, lt, :].rearrange("p (r h d) -> p r h d", r=2, h=2)
                            self.tt(tv[:, :, :, 0, :], xv[:, :, :, 1, :], sv[:, :, 0, :].unsqueeze(1).to_broadcast([128, 8, 2, 16]),
                                    ALU.mult, ["xn", "ropeS"], ["t2"])
                            self.tt(tv[:, :, :, 1, :], xv[:, :, :, 0, :], sv[:, :, 1, :].unsqueeze(1).to_broadcast([128, 8, 2, 16]),
                                    ALU.mult, ["xn", "ropeS"], ["t2"])
                            self.tt(x_b[:], t1[:], t2[:], ALU.add, ["t1", "t2"], [xbt])
                        else:
                            self.tt(x_b[:].rearrange("p (g d) -> p g d", d=64), xn[:].rearrange("p (g d) -> p g d", d=64),
                                    g[:].unsqueeze(1).to_broadcast([128, 8, 64]), ALU.mult, ["xn", gt], [xbt])
                        pp = pst[qi]
                        ppt = "pstE%d" % qi
                        for h in range(4):
                            self.tr(pp[:, h * 128:(h + 1) * 128], x_b[:, h * 128:(h + 1) * 128], self.identB[:], [xbt, "identB"], [ppt])
                        if dst is None:
                            pv = pp[:].rearrange("p (h t) -> p h t", h=4)
                            self.cp(qT0[0:64, :, tsl], pv[0:64], [ppt], [("qT", i, 0)], eng="act")
                            self.cp(qT1[64:128, :, tsl], pv[64:128], [ppt], [("qT", i, 1)], eng="act")
                        else:
                            self.cp(dst[:, :, tsl], pp[:].rearrange("p (h t) -> p h t", h=4), [ppt], [(dt, i)], eng="act")
                S.barrier()
            with ExitStack() as st3:
                pT = [self.sb("pT", [128, NT, 512], BF16, st3) for _ in range(2)]
                o = [self.sb("oE", [128, 128], F32, st3) for _ in range(2)]
                osq = self.sb("osq", [128, 128], F32, st3)
                ob = [self.sb("obE", [128, 128], BF16, st3) for _ in range(2)]
                rec = self.sb("recE", [128, 8], F32, st3)
                yst = [self.sb("ystE", [128, 512], F32, st3) for _ in range(2)]
                pss = [self.ps("pssE", [128, 512], F32, st3) for _ in range(3)]
                acc = [self.ps("accE", [128, 129], F32, st3) for _ in range(2)]
                pso = [self.ps("psoE", [128, 512], BF16, st3) for _ in range(2)]
                qchunks = [(256 + 512 * j, 512, list(range(NT))) for j in range(4)]
                if with_ctx:
                    qchunks.append((0, 256, [0, 1]))
                items = [(h, t0, n, keys) for h in range(4) for (t0, n, keys) in qchunks]
                oo4 = [self.sb("oo4", [128, 4, 128], F32, st3) for _ in range(2)]
                cnt = [0]

                def Hhalf(it, m):
                    h, t0, n, keys = items[it]
                    qtk = [("qT", (t0 // 128) + j, m) for j in range(n // 128)] + ["qz%d" % m]
                    for kt in keys:
                        ps = pss[cnt[0] % 3]
                        pstk = "pssE%d" % (cnt[0] % 3)
                        cnt[0] += 1
                        self.mm(ps[:, 0:n], kT[:, h, kt * 128:(kt + 1) * 128], qTm[m][:, h, t0:t0 + n], True, True,
                                [("kT", kt)] + qtk, [pstk])
                        self.act(pT[m][:, kt, 0:n], ps[:, 0:n], AF.Exp, [pstk], [("pT", m, kt)])

                def Vhalf(it, m):
                    h, t0, n, keys = items[it]
                    o4 = oo4[it % 2]
                    ys = yst[it % 2]
                    yt = "ystE%d" % (it % 2)
                    pso_ = pso[it % 2]
                    psot = "psoE%d" % (it % 2)
                    for qs in range(n // 128):
                        a = acc[qs % 2]
                        at = "accE%d" % (qs % 2)
                        ot = ("oo4", it % 2, qs)
                        for j, kt in enumerate(keys):
                            self.mm(a[:], pT[m][:, kt, qs * 128:(qs + 1) * 128], vaug[:, kt, h, :], j == 0, j == len(keys) - 1,
                                    [("pT", m, kt), ("vaug", kt), "vones"], [at])
                        rk = ("rec", qs % 2, m)
                        rcol = rec[:, 4 * (qs % 2) + m:4 * (qs % 2) + m + 1]
                        self.S.add("dve", lambda e, a=a, rcol=rcol: e.reciprocal(out=rcol, in_=a[:, 128:129]), [at], [rk])
                        if m == 0:
                            self.ts(o4[:, qs, :], a[:, 0:128], rcol, None, ALU.mult, None, [at, rk], [ot])
                        else:
                            r2 = rec[:, 4 * (qs % 2) + 2:4 * (qs % 2) + 3]
                            r3 = rec[:, 4 * (qs % 2) + 3:4 * (qs % 2) + 4]
                            rk2, rk3 = ("rec", qs % 2, 2), ("rec", qs % 2, 3)
                            self.tt(r2, rcol, neglam[:], ALU.mult, [rk, "neglam"], [rk2])
                            self.stt(o4[:, qs, :], a[:, 0:128], r2, o4[:, qs, :], ALU.mult, ALU.add, [at, rk2, ot], [ot])
                            self.tt(osq[:], o4[:, qs, :], o4[:, qs, :], ALU.mult, [ot], ["osq"])
                            self.S.add("dve", lambda e, r3=r3: e.tensor_reduce(out=r3, in_=osq[:], axis=AX.X, op=ALU.add), ["osq"], [rk3])
                            self.rsqrt_le(r3, r3, 1.0 / 128, EPS, [rk3], [rk3])
                            obb = ob[qs % 2]
                            obt = "obE%d" % (qs % 2)
                            self.stt(obb[:], o4[:, qs, :], r3, gsub[:], ALU.mult, ALU.mult, [ot, rk3, "gsub"], [obt])
                            self.tr(pso_[:, qs * 128:(qs + 1) * 128], obb[:], self.identB[:], [obt, "identB"], [psot])
                    if m == 1:
                        self.cp(ys[:, 0:n], pso_[:, 0:n], [psot], [yt], eng="act")
                        self.dma(ybr_c[h * 128:(h + 1) * 128, t0:t0 + n], ys[:, 0:n], [yt], [("ybr", s, 2)])

                halves = [(it, m) for it in range(len(items)) for m in range(2)]
                for j in range(len(halves) + 2):
                    if j >= 2:
                        Vhalf(*halves[j - 2])
                    if j < len(halves):
                        Hhalf(*halves[j])
                S.barrier()
            S.barrier()


    def stage_mlstm(self, W, cin, ybr_b, s, li):
        S = self.S
        hT = self.hT
        def hts(i):
            t = i * 128
            for ci, (t0, n) in enumerate(TOKCH):
                if t0 <= t < t0 + n:
                    return ("hT", ci)
        order = [list(range(NT)), [1, 0] + list(range(NT - 1, 1, -1))]
        with ExitStack() as st:
            tokS = [self.sb("tokS", [128, NT, 12], F32, st) for _ in range(2)]
            decB = [self.sb("decB", [128, NT, 4], F32, st) for _ in range(2)]
            cw = self.sb("cw", [128, 8, 3], F32, st)
            cb = self.sb("cb", [128, 8], F32, st)
            gml = self.sb("gml", [128, 512], F32, st)
            id4 = self.identF[0:4, 0:4]
            stgH = [self.sb("stgD", [128, 8, 512], F32, st) for _ in range(2)]
            wbH = [self.sb("wbD", [128, 8, 512], BF16, st) for _ in range(2)]

            def load_head(hh):
                cols_ = [O_MQK + hh * 128, O_MQK + 512 + hh * 128, O_MV + hh * 128, O_MO + hh * 128]
                for j_, c0_ in enumerate(cols_):
                    self.dma(stgH[hh % 2][:, :, j_ * 128:(j_ + 1) * 128], W["w_in"].rearrange("(k p) c -> p k c", p=128)[:, :, c0_:c0_ + 128],
                             [], ["stgD%d" % (hh % 2)], q="pool")
                self.cp(wbH[hh % 2][:], stgH[hh % 2][:], ["stgD%d" % (hh % 2)], ["wbD%d" % (hh % 2)], eng="pool")
            load_head(0)
            for j in range(3):
                self.dma(cw[:, :, j], W["ml_conv_w"][j].rearrange("(f p) -> p f", p=128), [], ["cw"], slow=True)
            self.dma(cb[:], W["ml_conv_b"].rearrange("(f p) -> p f", p=128), [], ["cb"], slow=True)
            self.dma(gml[:], W["ml_norm_g"].rearrange("(o d) -> o d", o=1).to_broadcast([128, 512]), [], ["gml"])
            with ExitStack() as st2:
                ones4 = self.sb("ones4", [4, 1], F32, st2)
                gb = self.sb("gb", [4, 4], F32, st2)
                stage = self.sb("stgG", [128, 8, 16], F32, st2)
                wg = self.sb("wg", [128, 8, 16], BF16, st2)
                Td = [[self.sb("gT", [4, LT], F32, st2) for _ in range(4)] for _ in range(2)]
                mendd = [self.sb("mend", [4, NT], F32, st2) for _ in range(2)]
                decd = [self.sb("dec", [4, NT], F32, st2) for _ in range(2)]
                ddgd = [self.sb("ddg", [4, NT, 4], F32, st2) for _ in range(2)]
                psgd = [[self.ps("psg", [4, 512], F32, st2) for _ in range(2)] for _ in range(2)]
                pstkd = [self.ps("pstk", [128, NT, 12], F32, st2) for _ in range(2)]
                psdd = [self.ps("psd", [128, NT * 4], F32, st2) for _ in range(2)]
                self.memset(ones4[:], 1.0, ["ones4"])
                self.dma(gb[:], W["ml_gate_b"].rearrange("a h -> h a"), [], ["gb"], slow=True)
                self.dma(stage[:], W["w_in"].rearrange("(k p) c -> p k c", p=128)[:, :, O_MG:O_MG + 16], [], ["stgG"], q="pool")
                self.cp(wg[:], stage[:], ["stgG"], ["wg"], eng="pool")
                onesb = ones4[:].to_broadcast([4, LT])

                def gate_dir(d):
                    Ti, Tf, Tg, Tm = Td[d]
                    g0, g1, g2, g3 = ["gT%d_%d" % (i, d) for i in range(4)]
                    mend, dec, ddg, pstk, psd = mendd[d], decd[d], ddgd[d], pstkd[d], psdd[d]
                    mt_, dt_, ddt, pkt, pdt = "mend%d" % d, "dec%d" % d, "ddg%d" % d, "pstk%d" % d, "psd%d" % d
                    for typ, dst, dtok in ((2 * d, Ti, g0), (2 * d + 1, Tf, g1)):
                        for ci, (t0, n) in enumerate(TOKCH):
                            ps = psgd[d][ci % 2]
                            pt = "psg%d_%d" % (d, ci % 2)
                            for k in range(8):
                                self.mm(ps[:, 0:n], wg[:, k, typ * 4:(typ + 1) * 4], hT[:, k, t0:t0 + n], k == 0, k == 7,
                                        ["wg", ("hT", ci)], [pt])
                            if d == 0:
                                o_ap = dst[:, t0:t0 + n]
                            elif t0 == 0:
                                o_ap = dst[:, 255::-1]
                            else:
                                hi = 2559 - t0
                                o_ap = dst[:, hi:hi - n:-1]
                            self.ts(o_ap, ps[:, 0:n], gb[:, typ:typ + 1], None, ALU.add, None, [pt, "gb"], [dtok])
                            yield
                    self.act(Tf[:], Tf[:], AF.Sigmoid, [g1], [g1]); yield
                    self.act(Tf[:], Tf[:], AF.Ln, [g1], [g1]); yield
                    S.add("dve", lambda e: e.tensor_tensor_scan(out=Tg[:], data0=onesb, data1=Tf[:], initial=0.0,
                                                               op0=ALU.mult, op1=ALU.add), [g1, "ones4"], [g2]); yield
                    self.tt(Ti[:], Ti[:], Tg[:], ALU.subtract, [g0, g2], [g0]); yield
                    S.add("dve", lambda e: e.tensor_tensor_scan(out=Tm[:], data0=onesb, data1=Ti[:], initial=0.0,
                                                               op0=ALU.mult, op1=ALU.max), [g0, "ones4"], [g3]); yield
                    self.cp(mend[:], Tm[:, 127::128], [g3], [mt_])
                    self.ts(dec[:, 0:1], mend[:, 0:1], -1.0, None, ALU.mult, None, [mt_], [dt_])
                    self.tt(dec[:, 1:NT], mend[:, 0:NT - 1], mend[:, 1:NT], ALU.subtract, [mt_], [dt_])
                    self.act(dec[:], dec[:], AF.Exp, [dt_], [dt_]); yield
                    self.tt(Tf[:], Tg[:], Tm[:], ALU.add, [g2, g3], [g1]); yield
                    self.act(Tf[:], Tf[:], AF.Exp, [g1], [g1], scale=-1.0); yield
                    mb = mend[:].unsqueeze(2).to_broadcast([4, NT, 128])
                    self.tt(Tg[:].rearrange("p (c j) -> p c j", j=128), Ti[:].rearrange("p (c j) -> p c j", j=128), mb, ALU.subtract,
                            [g0, mt_], [g2]); yield
                    self.act(Tg[:], Tg[:], AF.Exp, [g2], [g2]); yield
                    self.tt(Ti[:].rearrange("p (c j) -> p c j", j=128), mb, Tm[:].rearrange("p (c j) -> p c j", j=128), ALU.subtract,
                            [g3, mt_], [g0]); yield
                    self.act(Ti[:], Ti[:], AF.Exp, [g0], [g0]); yield
                    U, Rr, FL = Tg, Ti, Tf
                    ut, rt, ft = g2, g0, g1
                    if d == 1:
                        def rev(dst, src, st_, dt2):
                            self.cp(dst[:, 0:256], src[:, 255::-1], [st_], [dt2])
                            self.cp(dst[:, 256:LT], src[:, LT - 1:255:-1], [st_], [dt2])
                        rev(Tm, U, g2, g3); yield
                        rev(Tg, Rr, g0, g2); yield
                        rev(Ti, FL, g1, g0); yield
                        U, Rr, FL = Tm, Tg, Ti
                        ut, rt, ft = g3, g2, g0
                    for mc in range(NT):
                        for qi, (src, stok) in enumerate(((U, ut), (Rr, rt), (FL, ft))):
                            self.mm(pstk[:, mc, qi * 4:(qi + 1) * 4], src[:, mc * 128:(mc + 1) * 128], id4, True, True,
                                    [stok, "identF"], [pkt])
                        if mc % 6 == 5:
                            yield
                    self.cp(tokS[d][:], pstk[:], [pkt], [("tokS", d)])
                    self.tt(ddg[:], dec[:].unsqueeze(2).to_broadcast([4, NT, 4]), id4.unsqueeze(1).to_broadcast([4, NT, 4]), ALU.mult,
                            [dt_, "identF"], [ddt])
                    self.mm(psd[:], self.onesF[0:4, :], ddg[:].rearrange("p c h -> p (c h)"), True, True, [ddt, "onesF"], [pdt])
                    self.cp(decB[d][:].rearrange("p c h -> p (c h)"), psd[:], [pdt], [("decB", d)])
                    yield

                gens = [gate_dir(0), gate_dir(1)]
                while gens:
                    for g in list(gens):
                        try:
                            next(g)
                        except StopIteration:
                            gens.remove(g)
                S.barrier()
            for h in range(4):
                with ExitStack() as st3:
                    wb = wbH[h % 2]
                    wbt = "wbD%d" % (h % 2)
                    xr = self.sb("xr", [128, LT], F32, st3)
                    ac = self.sb("acD", [128, LT], F32, st3)
                    qh = self.sb("qh", [128, LT], BF16, st3)
                    kh = self.sb("kh", [128, LT], BF16, st3)
                    ktok = self.sb("ktok", [128, NT, 128], BF16, st3)
                    vh = self.sb("vh", [128, NT, 129], BF16, st3)
                    hacc = self.sb("hacc", [128, NT, 128], F32, st3)
                    hnum = [self.sb("hnum", [128, NT, 129], F32, st3) for _ in range(2)]
                    ep = self.sb("epD", [128, 2, NT], F32, st3)
                    hbt = self.sb("hbt", [128, NT, 128], BF16, st3)
                    Cstd = [self.sb("Cst", [128, 129], F32, st3) for _ in range(2)]
                    Cbfd = [self.sb("Cbf", [128, 129], BF16, st3) for _ in range(2)]
                    smd = [self.sb("smD", [128, 4], F32, st3) for _ in range(2)]
                    PT = [self.sb("PT", [128, 128], BF16, st3) for _ in range(2)]
                    Vs = [self.sb("Vs", [128, 129], BF16, st3) for _ in range(2)]
                    yst = [self.sb("ystD", [128, 512], F32, st3) for _ in range(2)]
                    psA = [self.ps("psDA", [128, 512], F32, st3) for _ in range(2)]
                    psT = self.ps("psDT", [128, 512], BF16, st3)
                    psS = [self.ps("psDS", [128, 128], F32, st3) for _ in range(2)]
                    psO = [self.ps("psDO", [128, 129], F32, st3) for _ in range(2)]
                    psC = self.ps("psDC", [128, 129], F32, st3)
                    if h + 1 < 4:
                        load_head(h + 1)
                    for j, (dst, dtok, f) in enumerate(((qh, "qh", h), (kh, "kh", 4 + h))):
                        for ci, (t0, n) in enumerate(TOKCH):
                            ps = psA[ci % 2]
                            pt = "psDA%d" % (ci % 2)
                            for k in range(8):
                                self.mm(ps[:, 0:n], wb[:, k, j * 128:(j + 1) * 128], hT[:, k, t0:t0 + n], k == 0, k == 7,
                                        [wbt, ("hT", ci)], [pt])
                            self.cp(xr[:, t0:t0 + n], ps[:, 0:n], [pt], ["xr"], eng="act")
                        self.ts(ac[:], xr[:], cw[:, f, 1:2], cb[:, f:f + 1], ALU.mult, ALU.add, ["xr", "cw", "cb"], ["acD"])
                        for (a0, a1) in ((0, 256), (256, LT)):
                            self.stt(ac[:, a0 + 1:a1], xr[:, a0:a1 - 1], cw[:, f, 0:1], ac[:, a0 + 1:a1], ALU.mult, ALU.add,
                                     ["xr", "cw", "acD"], ["acD"])
                            self.stt(ac[:, a0:a1 - 1], xr[:, a0 + 1:a1], cw[:, f, 2:3], ac[:, a0:a1 - 1], ALU.mult, ALU.add,
                                     ["xr", "cw", "acD"], ["acD"])
                        if j == 0:
                            self.act(dst[:], ac[:], AF.Silu, ["acD"], [dtok])
                        else:
                            self.act(ac[:], ac[:], AF.Silu, ["acD"], ["acD"])
                            self.ts(dst[:], ac[:], 128.0 ** -0.5, None, ALU.mult, None, ["acD"], [dtok])
                    for g0 in range(0, NT, 4):
                        nn = min(4, NT - g0)
                        for j in range(nn):
                            i = g0 + j
                            self.tr(psT[:, j * 128:(j + 1) * 128], kh[:, i * 128:(i + 1) * 128], self.identB[:], ["kh", "identB"], ["psDT"])
                        self.cp(ktok[:, g0:g0 + nn, :], psT[:, 0:nn * 128].rearrange("p (a b) -> p a b", b=128), ["psDT"], ["ktok"], eng="act")
                    self.memset(vh[:, :, 128:129], 1.0, ["vh1"])
                    for g0 in range(0, NT, 4):
                        nn = min(4, NT - g0)
                        ps = psA[(g0 // 4) % 2]
                        pt = "psDA%d" % ((g0 // 4) % 2)
                        for j in range(nn):
                            i = g0 + j
                            for k in range(8):
                                self.mm(ps[:, j * 128:(j + 1) * 128], hT[:, k, i * 128:(i + 1) * 128], wb[:, k, 256:384], k == 0, k == 7,
                                        [wbt, hts(i)], [pt])
                        self.cp(vh[:, g0:g0 + nn, 0:128], ps[:, 0:nn * 128].rearrange("p (a b) -> p a b", b=128), [pt], ["vh"], eng="act")
                    for d in range(2):
                        self.memset(Cstd[d][:], 0.0, ["Cst%d" % d])

                    def mstep(d, c):
                        mc = order[d][c]
                        mask = self.triL if d == 0 else self.triU
                        Cst, Cbf, PT_, Vs_, sm = Cstd[d], Cbfd[d], PT[d], Vs[d], smd[d]
                        pS, pO = psS[d], psO[d]
                        cst, cbf, ptt, vst, pst_, pot = "Cst%d" % d, "Cbf%d" % d, "PT%d" % d, "Vs%d" % d, "psDS%d" % d, "psDO%d" % d
                        tsl = slice(mc * 128, (mc + 1) * 128)
                        self.mm(pS[:], kh[:, tsl], qh[:, tsl], True, True, ["kh", "qh"], [pst_])
                        self.tt(PT_[:], pS[:], mask[:], ALU.mult, [pst_, "triL", "triU"], [ptt])
                        self.act(Vs_[:], vh[:, mc, :], AF.Identity, ["vh", "vh1", ("tokS", d)], [vst], scale=tokS[d][:, mc, h:h + 1])
                        self.ts(Cst[:], Cst[:], decB[d][:, c, h:h + 1], None, ALU.mult, None, [cst, ("decB", d)], [cst])
                        self.cp(Cbf[:], Cst[:], [cst], [cbf], eng="pool")
                        self.mm(pO[:], PT_[:], Vs_[:], True, False, [ptt, vst], [pot])
                        self.mm(pO[:], qh[:, tsl], Cbf[:], False, True, ["qh", cbf], [pot])
                        self.mm(psC[:], ktok[:, mc, :], Vs_[:], True, True, ["ktok", vst], ["psDC"])
                        self.tt(Cst[:], Cst[:], psC[:], ALU.add, [cst, "psDC"], [cst])
                        self.cp(hnum[d][:, mc, :], pO[:], [pot], [("hnum", d, mc)], eng="act")

                    for c in range(NT):
                        for d in range(2):
                            mstep(d, c)
                    sm = smd[0]
                    for d in range(2):
                        hall = [("hnum", d, i) for i in range(NT)]
                        r = tokS[d][:, :, 4 + h]
                        fl = tokS[d][:, :, 8 + h]
                        e0, e1 = ep[:, 0, :], ep[:, 1, :]
                        self.tt(e0, hnum[d][:, :, 128], r, ALU.mult, hall + [("tokS", d)], ["ep0"])
                        self.ts(e1, e0, -1.0, None, ALU.mult, None, ["ep0"], ["ep1"])
                        self.tt(e0, e0, e1, ALU.max, ["ep0", "ep1"], ["ep0"])
                        self.tt(e0, e0, fl, ALU.max, ["ep0", ("tokS", d)], ["ep0"])
                        S.add("dve", lambda e, e0=e0: e.reciprocal(out=e0, in_=e0), ["ep0"], ["ep0"])
                        self.tt(e0, e0, r, ALU.mult, ["ep0", ("tokS", d)], ["ep0"])
                        fb = e0.unsqueeze(2).to_broadcast([128, NT, 128])
                        if d == 0:
                            self.tt(hacc[:], hnum[0][:, :, 0:128], fb, ALU.mult, hall + ["ep0"], [("hacc", i) for i in range(NT)])
                        else:
                            self.tt(hnum[1][:, :, 0:128], hnum[1][:, :, 0:128], fb, ALU.mult, hall + ["ep0"], hall)
                            self.tt(hacc[:], hacc[:], hnum[1][:, :, 0:128], ALU.add, hall + [("hacc", i) for i in range(NT)],
                                    [("hacc", i) for i in range(NT)], eng="pool")
                    hall = [("hacc", i) for i in range(NT)]
                    sq3 = hnum[0][:, :, 0:128]
                    so3 = hnum[1][:, :, 0:128]
                    for i in range(NT):
                        ps = psA[i % 2]
                        pt = "psDA%d" % (i % 2)
                        for k in range(8):
                            self.mm(ps[:, 0:128], hT[:, k, i * 128:(i + 1) * 128], wb[:, k, 384:512], k == 0, k == 7, [wbt, hts(i)], [pt])
                        self.act(so3[:, i, :], ps[:, 0:128], AF.Sigmoid, [pt], [("so3", i)] + [("hnum", 1, j) for j in range(NT)])
                    self.tt(sq3, hacc[:], hacc[:], ALU.mult, hall, ["sq3"] + [("hnum", 0, j) for j in range(NT)])
                    S.add("dve", lambda e, sq3=sq3, ep=ep: e.tensor_reduce(out=ep[:, 0, :], in_=sq3, axis=AX.X, op=ALU.add), ["sq3"], ["ep0"])
                    self.rsqrt(ep[:, 0, :], ep[:, 0, :], 1.0 / 128, EPS, ["ep0"], ["ep0"])
                    self.tt(hacc[:], hacc[:], ep[:, 0, :].unsqueeze(2).to_broadcast([128, NT, 128]), ALU.mult, hall + ["ep0"], hall)
                    self.tt(hacc[:], hacc[:], gml[:, h * 128:(h + 1) * 128].unsqueeze(1).to_broadcast([128, NT, 128]), ALU.mult,
                            hall + ["gml"], hall, eng="pool")
                    self.tt(hbt[:], hacc[:], so3, ALU.mult, hall + [("so3", i) for i in range(NT)], ["hbt"])
                    for g0 in range(0, NT, 4):
                        nn = min(4, NT - g0)
                        ys = yst[(g0 // 4) % 2]
                        yt = "ystD%d" % ((g0 // 4) % 2)
                        for j in range(nn):
                            i = g0 + j
                            self.tr(psT[:, j * 128:(j + 1) * 128], hbt[:, i, :], self.identB[:], ["hbt", "identB"], ["psDT"])
                        self.cp(ys[:, 0:nn * 128], psT[:, 0:nn * 128], ["psDT"], [yt], eng="act")
                        self.dma(ybr_b[h * 128:(h + 1) * 128, g0 * 128:(g0 + nn) * 128], ys[:, 0:nn * 128], [yt], [("ybr", s, 1)])
                    S.barrier()
            S.barrier()


    def cmul(self, ore, oim, are, aim, bre, bim, ts4, rtoks, wtok, tk="cm", pool_one=False):
        t1, t2, t3, t4 = ts4
        k = [tk + "_t%d" % i for i in range(4)]
        self.tt(t2, aim, bim, ALU.mult, rtoks, [k[1]], eng="pool" if pool_one else "dve")
        self.tt(t1, are, bre, ALU.mult, rtoks, [k[0]])
        self.tt(t3, are, bim, ALU.mult, rtoks, [k[2]])
        self.tt(t4, aim, bre, ALU.mult, rtoks, [k[3]])
        self.tt(oim, t3, t4, ALU.add, [k[2], k[3]], [wtok + "_im"])
        self.tt(ore, t1, t2, ALU.subtract, [k[0], k[1]], [wtok + "_re"])

    def stage_s5(self, W, cin, ybr_a, s, li):
        S = self.S
        hT = self.hT
        order = [list(range(NT)), [1, 0] + list(range(NT - 1, 1, -1))]
        if self.dbg.get("s5_stop") == "none":
            return
        with ExitStack() as st:
            gel = self.sb("gel", [128, 4, LT], BF16, st)
            wsu = self.sb("wsu", [128, 8, 512], BF16, st)
            dsk = self.sb("dsk", [128, 4], F32, st)
            with ExitStack() as st0:
                stage = self.sb("stgC", [128, 8, 512], F32, st0)
                self.load_w("wsu", W["w_in"], O_SU, 512, 8, wsu, "wsu", stage, "stgC")
                S.barrier()
            self.dma(dsk[:], W["ssm_d"].rearrange("(c p) -> p c", p=128), [], ["dsk"], slow=True)
            for c in range(self.dbg.get("s5_nc", 4)):
                with ExitStack() as st2:
                    suT = self.sb("suT", [128, LT], BF16, st2)
                    yacc = self.sb("yacc", [128, LT], F32, st2)
                    sc = self.sb("s5sc", [128, 16, 4], F32, st2)
                    N = self.sb("s5N", [128, 2, 4, 128], F32, st2)
                    Nr = self.sb("s5Nr", [128, 2, 4, 128], F32, st2)
                    braw = self.sb("s5braw", [128, 2, 4, 16], F32, st2)
                    bbar = self.sb("s5bbar", [128, 2, 4, 16], F32, st2)
                    bt = self.sb("s5bt", [128, 2, 4, 16], F32, st2)
                    Zp = self.sb("s5Zp", [128, 2, 4, 128], F32, st2)
                    Yp = self.sb("s5Yp", [128, 2, 4, 128], F32, st2)
                    Pd = [self.sb("s5P", [128, 2, 4, 128], F32, st2) for _ in range(2)]
                    Eitd = [self.sb("s5Eit", [128, 1024], F32, st2) for _ in range(2)]
                    Bbdd = [self.sb("s5Bbd", [128, 1024], BF16, st2) for _ in range(2)]
                    Cbdd = [self.sb("s5Cbd", [128, 2, 4, 128], BF16, st2) for _ in range(2)]
                    wA = [self.sb("s5wA", [128, 1024], BF16, st2) for _ in range(2)]
                    wB = [self.sb("s5wB", [128, 1024], BF16, st2) for _ in range(2)]
                    xA = [self.sb("s5xA", [128, 1024], BF16, st2) for _ in range(2)]
                    xB = [self.sb("s5xB", [128, 1024], BF16, st2) for _ in range(2)]
                    cu = [[self.sb("s5cu", [128, 8], F32, st2) for _ in range(2)] for _ in range(2)]
                    t13d = [self.sb("s5t13", [128, 1024], F32, st2) for _ in range(1)]
                    t24d = [self.sb("s5t24", [128, 1024], F32, st2) for _ in range(1)]
                    Psd = [self.sb("s5Ps", [128, 2, 4, 128], F32, st2) for _ in range(2)]
                    Eisd = [self.sb("s5Eis", [128, 1024], F32, st2) for _ in range(2)]
                    Zcd = [self.sb("s5Zc", [128, 1024], F32, st2) for _ in range(2)]
                    carry = [[self.sb("s5cy", [128, 8], F32, st2) for _ in range(2)] for _ in range(2)]
                    psA = [self.ps("psCA", [128, 512], F32, st2) for _ in range(2)]
                    psZd = [[self.ps("psCZ", [128, 512], F32, st2) for _ in range(2)] for _ in range(2)]
                    psYd = [self.ps("psCY", [128, 128], F32, st2) for _ in range(2)]
                    t1, t2 = t13d[0], t24d[0]
                    for ci, (t0, n) in enumerate(TOKCH):
                        ps = psA[ci % 2]
                        pt = "psCA%d" % (ci % 2)
                        for k in range(8):
                            self.mm(ps[:, 0:n], wsu[:, k, c * 128:(c + 1) * 128], hT[:, k, t0:t0 + n], k == 0, k == 7,
                                    ["wsu", ("hT", ci)], [pt])
                        self.cp(suT[:, t0:t0 + n], ps[:, 0:n], [pt], ["suT"], eng="act")
                        self.ts(yacc[:, t0:t0 + n], ps[:, 0:n], dsk[:, c:c + 1], None, ALU.mult, None, [pt, "dsk"],
                                [("yacc", j) for j in range(t0 // 128, (t0 + n) // 128)])
                    for d in range(2):
                        P, Eit, Bbd, Cbd = Pd[d], Eitd[d], Bbdd[d], Cbdd[d]
                        ptk, etk, btk, ctk = "s5P%d" % d, "s5Eit%d" % d, "s5Bbd%d" % d, "s5Cbd%d" % d
                        psT = psZd[d]
                        psTt = ["psCZ%d%d" % (d, 0), "psCZ%d%d" % (d, 1)]
                        tc = self.tabcache
                        if tc is not None and s == 1:
                            self.dma(P[:].rearrange("p a k j -> p (a k j)"), tc["P"][c, d], [("tabc", c, d)], [ptk])
                            self.dma(Eit[:], tc["E"][c, d], [("tabc", c, d)], [etk])
                            self.dma(Bbd[:], tc["B"][c, d], [("tabc", c, d)], [btk])
                            self.dma(Cbd[:].rearrange("p a k j -> p (a k j)"), tc["C"][c, d], [("tabc", c, d)], [ctk])
                            self.ts(Eisd[d][:, 0:512], Eit[:, 512:1024], -1.0, None, ALU.mult, None, [etk], ["s5Eis%d" % d])
                            self.cp(Eisd[d][:, 512:1024], Eit[:, 0:512], [etk], ["s5Eis%d" % d], eng="pool")
                            self.ts(Psd[d][:, 0], P[:, 1], -1.0, None, ALU.mult, None, [ptk], ["s5Ps%d" % d])
                            self.cp(Psd[d][:, 1], P[:, 0], [ptk], ["s5Ps%d" % d], eng="pool")
                            continue
                        def col(i):
                            return sc[:, i, :]
                        LRE, LIM, DT, LDR, LDI, MAG, C8, S8, ABR, ABI, DEN, CR, CI, TA, TB, TC = [col(i) for i in range(16)]
                        gs = slice(8 * c, 8 * c + 8)
                        self.dma(LRE, W["ssm_lam_re"][d, gs].rearrange("(k g) p -> (g p) k", g=2), [], ["sc"], slow=True)
                        self.dma(LIM, W["ssm_lam_im"][d, gs].rearrange("(k g) p -> (g p) k", g=2), [], ["sc"], slow=True)
                        for g2 in range(2):
                            src = W["ssm_log_step"][d, gs].rearrange("(o k g) -> o g k", o=1, g=2)[:, g2, :]
                            self.dma(sc[g2 * 64:(g2 + 1) * 64, 2, :], src.to_broadcast([64, 4]), [], ["sc"], slow=True)
                        T_ = ["sc"]
                        self.ts(LRE, LRE, -1e-4, None, ALU.min, None, T_, T_)
                        self.act(DT, DT, AF.Exp, T_, T_)
                        self.tt(LDR, LRE, DT, ALU.mult, T_, T_)
                        self.tt(LDI, LIM, DT, ALU.mult, T_, T_)
                        self.act(MAG, LDR, AF.Exp, T_, T_)
                        self.act(S8, LDI, AF.Sin, T_, T_, scale=1.0 / 16)
                        self.act(TA, LDI, AF.Sin, T_, T_, scale=1.0 / 32)
                        self.tt(TA, TA, TA, ALU.mult, T_, T_)
                        self.ts(C8, TA, -2.0, 1.0, ALU.mult, ALU.add, T_, T_)
                        for _ in range(4):
                            self.tt(TA, C8, C8, ALU.mult, T_, T_)
                            self.tt(TB, S8, S8, ALU.mult, T_, T_)
                            self.tt(TC, C8, S8, ALU.mult, T_, T_)
                            self.tt(C8, TA, TB, ALU.subtract, T_, T_)
                            self.ts(S8, TC, 2.0, None, ALU.mult, None, T_, T_)
                        self.tt(ABR, MAG, C8, ALU.mult, T_, T_)
                        self.tt(ABI, MAG, S8, ALU.mult, T_, T_)
                        self.tt(TA, LRE, LRE, ALU.mult, T_, T_)
                        self.tt(TB, LIM, LIM, ALU.mult, T_, T_)
                        self.tt(DEN, TA, TB, ALU.add, T_, T_)
                        S.add("dve", lambda e, DEN=DEN: e.reciprocal(out=DEN, in_=DEN), T_, T_)
                        self.ts(TC, ABR, -1.0, None, ALU.add, None, T_, T_)
                        self.tt(TA, TC, LRE, ALU.mult, T_, T_)
                        self.tt(TB, ABI, LIM, ALU.mult, T_, T_)
                        self.tt(CR, TA, TB, ALU.add, T_, T_)
                        self.tt(CR, CR, DEN, ALU.mult, T_, T_)
                        self.tt(TA, ABI, LRE, ALU.mult, T_, T_)
                        self.tt(TB, TC, LIM, ALU.mult, T_, T_)
                        self.tt(CI, TA, TB, ALU.subtract, T_, T_)
                        self.tt(CI, CI, DEN, ALU.mult, T_, T_)
                        self.tt(TA, MAG, MAG, ALU.mult, T_, T_)
                        S.add("dve", lambda e, TA=TA: e.reciprocal(out=TA, in_=TA), T_, T_)
                        self.tt(LDR, ABR, TA, ALU.mult, T_, T_)
                        self.tt(LDI, ABI, TA, ALU.mult, T_, T_)
                        self.ts(LDI, LDI, -1.0, None, ALU.mult, None, T_, T_)
                        for (Tb, ar, ai, tk) in ((P, ABR, ABI, ptk), (N, LDR, LDI, "s5N")):
                            for kq in range(4):
                                self.cp(Tb[:, 0, kq, 0:1], ar[:, kq:kq + 1], T_, [tk])
                                self.cp(Tb[:, 1, kq, 0:1], ai[:, kq:kq + 1], T_, [tk])
                            L = 1
                            tv1 = t1[:, 0:512].rearrange("p (k j) -> p k j", j=128)
                            tv2 = t2[:, 0:512].rearrange("p (k j) -> p k j", j=128)
                            while L < 128:
                                mr = Tb[:, 0, :, L - 1:L].to_broadcast([128, 4, L])
                                mi = Tb[:, 1, :, L - 1:L].to_broadcast([128, 4, L])
                                sr = Tb[:, 0, :, 0:L]
                                si = Tb[:, 1, :, 0:L]
                                dr = Tb[:, 0, :, L:2 * L]
                                di = Tb[:, 1, :, L:2 * L]
                                self.tt(tv1[:, :, 0:L], si, mi, ALU.mult, [tk], ["s5t1"])
                                self.tt(tv2[:, :, 0:L], sr, mr, ALU.mult, [tk], ["s5t2"])
                                self.tt(dr, tv2[:, :, 0:L], tv1[:, :, 0:L], ALU.subtract, ["s5t1", "s5t2"], [tk])
                                self.tt(tv1[:, :, 0:L], si, mr, ALU.mult, [tk], ["s5t1"])
                                self.tt(tv2[:, :, 0:L], sr, mi, ALU.mult, [tk], ["s5t2"])
                                self.tt(di, tv2[:, :, 0:L], tv1[:, :, 0:L], ALU.add, ["s5t1", "s5t2"], [tk])
                                L *= 2
                        if d == 0:
                            Nsrc, ntk = N, "s5N"
                        else:
                            self.cp(Nr[:].rearrange("p a k j -> p (a k) j"), N[:].rearrange("p a k j -> p (a k) j")[:, :, ::-1], ["s5N"], ["s5Nr"])
                            Nsrc, ntk = Nr, "s5Nr"
                        for part in range(2):
                            for kq in range(4):
                                self.tr(psT[part][:, kq * 128:(kq + 1) * 128], Nsrc[:, part, kq, :], self.identF[:], [ntk, "identF"], [psTt[part]])
                            self.cp(Eit[:, part * 512:(part + 1) * 512], psT[part][:], [psTt[part]], [etk], eng="act")
                        self.ts(Eisd[d][:, 0:512], Eit[:, 512:1024], -1.0, None, ALU.mult, None, [etk], ["s5Eis%d" % d])
                        self.cp(Eisd[d][:, 512:1024], Eit[:, 0:512], [etk], ["s5Eis%d" % d], eng="pool")
                        self.ts(Psd[d][:, 0], P[:, 1], -1.0, None, ALU.mult, None, [ptk], ["s5Ps%d" % d])
                        self.cp(Psd[d][:, 1], P[:, 0], [ptk], ["s5Ps%d" % d], eng="pool")
                        self.dma(braw[:, 0], W["ssm_b_re"][d, gs].rearrange("(k g) p m -> (g p) k m", g=2), [], ["s5braw"], slow=True)
                        self.dma(braw[:, 1], W["ssm_b_im"][d, gs].rearrange("(k g) p m -> (g p) k m", g=2), [], ["s5braw"], slow=True)
                        crb = CR.unsqueeze(2).to_broadcast([128, 4, 16])
                        cib = CI.unsqueeze(2).to_broadcast([128, 4, 16])
                        self.tt(bbar[:, 0], braw[:, 0], crb, ALU.mult, ["s5braw", "sc"], ["s5bbar"])
                        self.tt(bt[:, 0], braw[:, 1], cib, ALU.mult, ["s5braw", "sc"], ["s5bt"])
                        self.tt(bbar[:, 0], bbar[:, 0], bt[:, 0], ALU.subtract, ["s5bbar", "s5bt"], ["s5bbar"])
                        self.tt(bbar[:, 1], braw[:, 1], crb, ALU.mult, ["s5braw", "sc"], ["s5bbar"])
                        self.tt(bt[:, 1], braw[:, 0], cib, ALU.mult, ["s5braw", "sc"], ["s5bt"])
                        self.tt(bbar[:, 1], bbar[:, 1], bt[:, 1], ALU.add, ["s5bbar", "s5bt"], ["s5bbar"])
                        self.memset(Zp[:], 0.0, ["s5Zp"])
                        self.memset(Yp[:], 0.0, ["s5Yp"], eng="pool")
                        for part in range(2):
                            for kq in range(4):
                                for g2 in range(2):
                                    rs_ = slice(g2 * 64, (g2 + 1) * 64)
                                    c0 = 32 * kq + 16 * g2
                                    self.cp(Zp[rs_, part, kq, c0:c0 + 16], bbar[rs_, part, kq, :], ["s5bbar"], ["s5Zp"])
                        for part in range(2):
                            for kq in range(4):
                                self.tr(psT[part][:, kq * 128:(kq + 1) * 128], Zp[:, part, kq, :], self.identF[:], ["s5Zp", "identF"], [psTt[part]])
                            self.cp(Bbd[:, part * 512:(part + 1) * 512], psT[part][:], [psTt[part]], [btk], eng="act")
                        for part, nm in enumerate(("ssm_c_re", "ssm_c_im")):
                            for kq in range(4):
                                for g2 in range(2):
                                    g = 8 * c + 2 * kq + g2
                                    r0 = 32 * kq + 16 * g2
                                    self.dma(Yp[r0:r0 + 16, part, kq, 64 * g2:64 * g2 + 64], W[nm][d, g], [], ["s5Yp"])
                        for part in range(2):
                            for kq in range(4):
                                self.tr(psT[part][:, kq * 128:(kq + 1) * 128], Yp[:, part, kq, :], self.identF[:], ["s5Yp", "identF"], [psTt[part]])
                            if part == 0:
                                self.cp(Cbd[:, 0].rearrange("p k j -> p (k j)"), psT[0][:], [psTt[0]], [ctk], eng="act")
                            else:
                                self.ts(Cbd[:, 1].rearrange("p k j -> p (k j)"), psT[1][:], -1.0, None, ALU.mult, None, [psTt[1]], [ctk])
                    if self.tabcache is not None and s == 0:
                        tc = self.tabcache
                        for d in range(2):
                            self.dma(tc["P"][c, d], Pd[d][:].rearrange("p a k j -> p (a k j)"), ["s5P%d" % d], [("tabc", c, d)])
                            self.dma(tc["E"][c, d], Eitd[d][:], ["s5Eit%d" % d], [("tabc", c, d)])
                            self.dma(tc["B"][c, d], Bbdd[d][:], ["s5Bbd%d" % d], [("tabc", c, d)])
                            self.dma(tc["C"][c, d], Cbdd[d][:].rearrange("p a k j -> p (a k j)"), ["s5Cbd%d" % d], [("tabc", c, d)])
                    def half1(d, ci_):
                        mc = order[d][ci_]
                        tsl = slice(mc * 128, (mc + 1) * 128)
                        Ei, Eis, Bbd = Eitd[d], Eisd[d], Bbdd[d]
                        tri = self.triL if d == 0 else self.triU
                        for part in range(2):
                            self.mm(psA[part][:], suT[:, tsl], Bbd[:, part * 512:(part + 1) * 512], True, True, ["suT", "s5Bbd%d" % d], ["psCA%d" % part])
                        v2 = lambda a: a.rearrange("p (a b) -> p a b", a=2)
                        ta, tb = wA[d], wB[d]
                        ka, kb = "s5wA%d" % d, "s5wB%d" % d
                        self.tt(v2(ta[:]), psA[0][:].unsqueeze(1).to_broadcast([128, 2, 512]), v2(Ei[:]), ALU.mult, ["psCA0", "s5Eit%d" % d], [ka])
                        self.tt(v2(tb[:]), psA[1][:].unsqueeze(1).to_broadcast([128, 2, 512]), v2(Eis[:]), ALU.mult, ["psCA1", "s5Eis%d" % d], [kb])
                        for part in range(2):
                            for kq in range(4):
                                cs = slice(part * 512 + kq * 128, part * 512 + (kq + 1) * 128)
                                self.mm(psZd[d][part][:, kq * 128:(kq + 1) * 128], ta[:, cs], tri[:], True, False, [ka, "triL", "triU"], ["psCZ%d%d" % (d, part)])
                                self.mm(psZd[d][part][:, kq * 128:(kq + 1) * 128], tb[:, cs], tri[:], False, True, [kb, "triL", "triU"], ["psCZ%d%d" % (d, part)])

                    v4 = lambda a: a.rearrange("p (a k j) -> p a k j", a=2, k=4)

                    def half2a(d, ci_):
                        P, Ps, Zc = Pd[d], Psd[d], Zcd[d]
                        cp_ = carry[d][(ci_ + 1) % 2]
                        cpt = "s5cy%d%d" % (d, (ci_ + 1) % 2)
                        cn_ = carry[d][ci_ % 2]
                        cnt_ = "s5cy%d%d" % (d, ci_ % 2)
                        jc = 127 if d == 0 else 0
                        zct = "s5Zc%d" % d
                        for part in range(2):
                            for kq in range(4):
                                o0 = part * 512 + kq * 128
                                self.act(Zc[:, o0:o0 + 128], psZd[d][part][:, kq * 128:(kq + 1) * 128], AF.Identity,
                                         ["psCZ%d%d" % (d, part), cpt], [zct], bias=cp_[:, part * 4 + kq:part * 4 + kq + 1])
                        zrc = Zc[:, 0:512].rearrange("p (k j) -> p k j", j=128)[:, :, jc].unsqueeze(1).to_broadcast([128, 2, 4])
                        zic = Zc[:, 512:1024].rearrange("p (k j) -> p k j", j=128)[:, :, jc].unsqueeze(1).to_broadcast([128, 2, 4])
                        u1, u2 = cu[d][0], cu[d][1]
                        c3 = lambda a: a.rearrange("p (a k) -> p a k", a=2)
                        self.tt(c3(u1[:]), zrc, P[:, :, :, 127], ALU.mult, [zct, "s5P%d" % d], ["s5u1%d" % d])
                        self.tt(c3(u2[:]), zic, Ps[:, :, :, 127], ALU.mult, [zct, "s5Ps%d" % d], ["s5u2%d" % d])
                        self.tt(cn_[:], u1[:], u2[:], ALU.add, ["s5u1%d" % d, "s5u2%d" % d], [cnt_])
                        if d == 0:
                            pa, pb = P[:], Ps[:]
                        else:
                            pa, pb = P[:, :, :, ::-1], Ps[:, :, :, ::-1]
                        zr = Zc[:, 0:512].rearrange("p (k j) -> p k j", j=128).unsqueeze(1).to_broadcast([128, 2, 4, 128])
                        zi = Zc[:, 512:1024].rearrange("p (k j) -> p k j", j=128).unsqueeze(1).to_broadcast([128, 2, 4, 128])
                        self.tt(v4(xB[d][:]), zi, pb, ALU.mult, [zct, "s5Ps%d" % d], ["s5xB%d" % d], eng="pool")
                        self.tt(v4(xA[d][:]), zr, pa, ALU.mult, [zct, "s5P%d" % d], ["s5xA%d" % d])

                    def half2b(d, ci_):
                        Cbd = Cbdd[d]
                        n8 = 0
                        for (xt, xk) in ((xA[d], "s5xA%d" % d), (xB[d], "s5xB%d" % d)):
                            for part in range(2):
                                for kq in range(4):
                                    o0 = part * 512 + kq * 128
                                    self.mm(psYd[d][:], Cbd[:, part, kq, :], xt[:, o0:o0 + 128], n8 == 0, n8 == 15, ["s5Cbd%d" % d, xk], ["psCY%d" % d])
                                    n8 += 1

                    def yadd(d, ci_):
                        mc = order[d][ci_]
                        tsl = slice(mc * 128, (mc + 1) * 128)
                        self.tt(yacc[:, tsl], yacc[:, tsl], psYd[d][:], ALU.add, [("yacc", mc), "psCY%d" % d], [("yacc", mc)])

                    for d in range(2):
                        self.memset(carry[d][1][:], 0.0, ["s5cy%d1" % d])
                    half1(0, 0)
                    half1(1, 0)
                    pend = []
                    for ci_ in range(NT):
                        for d in range(2):
                            half2a(d, ci_)
                            if ci_ + 1 < NT:
                                half1(d, ci_ + 1)
                            while pend:
                                yadd(*pend.pop(0))
                            half2b(d, ci_)
                            pend.append((d, ci_))
                    while pend:
                        yadd(*pend.pop(0))
                    Zc = Zcd[0]
                    yall = [("yacc", j) for j in range(NT)]
                    for a0 in range(0, LT, 1024):
                        a1 = min(LT, a0 + 1024)
                        w_ = a1 - a0
                        self.tt(Zc[:, 0:w_], yacc[:, a0:a1], yacc[:, a0:a1], ALU.mult, yall, ["s5Zc0"])
                        self.ts(Zc[:, 0:w_], Zc[:, 0:w_], 0.044715, 1.0, ALU.mult, ALU.add, ["s5Zc0"], ["s5Zc0"])
                        self.tt(Zc[:, 0:w_], Zc[:, 0:w_], yacc[:, a0:a1], ALU.mult, ["s5Zc0"] + yall, ["s5Zc0"])
                        self.act(Zc[:, 0:w_], Zc[:, 0:w_], AF.Sigmoid, ["s5Zc0"], ["s5Zc0"], scale=1.5957691216057308)
                        self.tt(gel[:, c, a0:a1], Zc[:, 0:w_], yacc[:, a0:a1], ALU.mult, ["s5Zc0"] + yall, [("gel", c)])
                    S.barrier()
            if self.dbg.get("s5_noglu"):
                S.barrier()
                return
            with ExitStack() as st4:
                stage = self.sb("stgC", [128, 8, 512], F32, st4)
                wgl = [self.sb("wglu", [128, 4, 512], BF16, st4) for _ in range(2)]
                gbias = self.sb("gbias", [128, 8], F32, st4)
                sg = [self.sb("sgC", [128, 512], F32, st4) for _ in range(2)]
                yst = [self.sb("ystC", [128, 512], F32, st4) for _ in range(2)]
                psa = [self.ps("psGa", [128, 512], F32, st4) for _ in range(2)]
                psg = [self.ps("psGg", [128, 512], F32, st4) for _ in range(2)]
                self.dma(gbias[:], W["ssm_glu_b"].rearrange("(j p) -> p j", p=128), [], ["gbias"], slow=True)
                for half in range(2):
                    self.load_w("wglu", W["ssm_glu_w"], half * 512, 512, 4, wgl[half], "wglu%d" % half, stage, "stgC")
                cnt = 0
                for j in range(4):
                    for ci, (t0, n) in enumerate(TOKCH):
                        i2 = cnt % 2
                        cnt += 1
                        for k in range(4):
                            self.mm(psa[i2][:, 0:n], wgl[0][:, k, j * 128:(j + 1) * 128], gel[:, k, t0:t0 + n], k == 0, k == 3,
                                    ["wglu0", ("gel", k)], ["psGa%d" % i2])
                        for k in range(4):
                            self.mm(psg[i2][:, 0:n], wgl[1][:, k, j * 128:(j + 1) * 128], gel[:, k, t0:t0 + n], k == 0, k == 3,
                                    ["wglu1", ("gel", k)], ["psGg%d" % i2])
                        self.act(sg[i2][:, 0:n], psg[i2][:, 0:n], AF.Sigmoid, ["psGg%d" % i2, "gbias"], ["sgC%d" % i2], bias=gbias[:, 4 + j:5 + j])
                        self.stt(yst[i2][:, 0:n], psa[i2][:, 0:n], gbias[:, j:j + 1], sg[i2][:, 0:n], ALU.add, ALU.mult,
                                 ["psGa%d" % i2, "gbias", "sgC%d" % i2], ["ystC%d" % i2])
                        self.dma(ybr_a[j * 128:(j + 1) * 128, t0:t0 + n], yst[i2][:, 0:n], ["ystC%d" % i2], [("ybr", s, 0)])
                S.barrier()
            S.barrier()

CONST_SHAPES = {"ident": [128, 128], "ones": [128, 128], "tril": [128, 128], "triu": [128, 128],
                "ropeC": [128, 16, 64], "ropeS": [128, 16, 64], "lamc": [DEPTH, 128, 2]}
LAYER_W = [("norm_g", [D]), ("ada_w", [D, 3 * D]), ("ada_b", [3 * D]), ("w_in", [D, D_IN]),
           ("w_ssm_out", [512, D]), ("w_ml_out", [512, D]), ("w_da_out", [512, D]), ("w_out", [D, D]),
           ("da_qnorm_g", [64]), ("da_knorm_g", [64]), ("da_lambda", [4, 64]), ("da_subln_g", [128]),
           ("ml_conv_w", [3, 1024]), ("ml_conv_b", [1024]), ("ml_gate_b", [4, 4]), ("ml_norm_g", [512]),
           ("ssm_lam_re", [2, 32, 64]), ("ssm_lam_im", [2, 32, 64]), ("ssm_log_step", [2, 32]),
           ("ssm_b_re", [2, 32, 64, 16]), ("ssm_b_im", [2, 32, 64, 16]), ("ssm_c_re", [2, 32, 16, 64]),
           ("ssm_c_im", [2, 32, 16, 64]), ("ssm_d", [512]), ("ssm_glu_w", [512, 1024]), ("ssm_glu_b", [1024])]


def host_consts(li=0):
    i = np.arange(128)
    c = {
        "ident": np.eye(128, dtype=np.float32),
        "ones": np.ones((128, 128), np.float32),
        "tril": (i[:, None] <= i[None, :]).astype(np.float32),
        "triu": (i[:, None] >= i[None, :]).astype(np.float32),
    }
    t = np.arange(LL)
    row = (t // 64).astype(np.float32)
    col = (t % 64).astype(np.float32)
    half = 32
    inv = np.power(np.float32(10000.0), -np.arange(0, half, 2, dtype=np.float32) / np.float32(half)).astype(np.float32)
    ar = (row[:, None] * inv).astype(np.float32)
    ac = (col[:, None] * inv).astype(np.float32)
    cr, sr, cc, sc = np.cos(ar), np.sin(ar), np.cos(ac), np.sin(ac)
    C64 = np.concatenate([cr, cr, cc, cc], axis=1).astype(np.float32)
    S64 = np.concatenate([-sr, sr, -sc, sc], axis=1).astype(np.float32)
    c["ropeC"] = np.ascontiguousarray(C64.reshape(16, 128, 64).transpose(1, 0, 2))
    c["ropeS"] = np.ascontiguousarray(S64.reshape(16, 128, 64).transpose(1, 0, 2))
    lam = [0.8 - 0.6 * math.exp(-0.3 * l) for l in range(DEPTH)]
    c["lamc"] = np.stack([np.tile(np.array([[-v, 1.0 - v]], np.float32), (128, 1)) for v in lam])
    return c


def build_layer_program(li=0, branches=("a", "b", "c"), dump_ybr=False, do_merge=True, dbg=None):
    nc = bass.Bass("TRN2", target_bir_lowering=False)
    S = Sched()
    W = {}
    for nm, shp in LAYER_W:
        W[nm] = nc.dram_tensor(nm, shp, F32, kind="ExternalInput").ap()
    W["cvec"] = nc.dram_tensor("cvec", [3, D], F32, kind="ExternalInput").ap()
    cin = {nm: nc.dram_tensor(nm, shp, F32, kind="ExternalInput").ap() for nm, shp in CONST_SHAPES.items()}
    xT = nc.dram_tensor("xT", [NSEQ, D, LT], F32, kind="ExternalInput").ap()
    xo = nc.dram_tensor("xo", [NSEQ, D, LT], F32, kind="ExternalOutput").ap()
    ybr_in = None
    if len(branches) < 3:
        ybr_in = nc.dram_tensor("ybr", [NSEQ, 3, 512, LT], F32, kind="ExternalInput").ap()
    ybr_dev = nc.dram_tensor("ybr_dev", [NSEQ, 3, 512, LT], F32, kind="ExternalOutput" if dump_ybr else "Internal").ap()
    with ExitStack() as stack:
        B = LayerBuilder(nc, S, stack, dbg=dbg)
        if (dbg or {}).get("tabcache"):
            B.tabcache = {"P": nc.dram_tensor("tabP", [4, 2, 128, 1024], F32).ap(), "E": nc.dram_tensor("tabE", [4, 2, 128, 1024], F32).ap(),
                          "B": nc.dram_tensor("tabB", [4, 2, 128, 1024], BF16).ap(), "C": nc.dram_tensor("tabC", [4, 2, 128, 1024], BF16).ap()}
        B.setup_consts(cin)
        B.hT = B.sb("hT", [128, 8, LT], BF16)
        B.stage_mod(W)
        for s in range((dbg or {}).get("nseq", NSEQ)):
            B.stage_norm(xT[s], s)
            srcs = []
            for bi, b in enumerate("abc"):
                srcs.append(ybr_dev[s, bi] if b in branches else ybr_in[s, bi])
            if "a" in branches:
                B.stage_s5(W, cin, ybr_dev[s, 0], s, li)
            if "b" in branches:
                B.stage_mlstm(W, cin, ybr_dev[s, 1], s, li)
            if "c" in branches:
                B.stage_attn(W, cin, ybr_dev[s, 2], s, li)
            if do_merge:
                B.stage_merge(W, xT[s], xo[s], srcs, s)
        S.emit(nc, stack)
    return nc, S


_PROG = {}


def build_program(n_layers=DEPTH, layer0=0):
    nc = bass.Bass("TRN2", target_bir_lowering=False)
    S = Sched()
    Wall = {}
    for nm, shp in LAYER_W:
        Wall[nm] = nc.dram_tensor(nm, [DEPTH] + list(shp), F32, kind="ExternalInput").ap()
    cvec = nc.dram_tensor("cvec", [3, D], F32, kind="ExternalInput").ap()
    cin = {nm: nc.dram_tensor(nm, shp, F32, kind="ExternalInput").ap() for nm, shp in CONST_SHAPES.items()}
    xT = nc.dram_tensor("xT", [NSEQ, D, LT], F32, kind="ExternalInput").ap()
    xo = nc.dram_tensor("xo", [NSEQ, D, LT], F32, kind="ExternalOutput").ap()
    xs = [nc.dram_tensor("xs%d" % i, [NSEQ, D, LT], F32).ap() for i in range(2)]
    ybr_dev = nc.dram_tensor("ybr_dev", [NSEQ, 3, 512, LT], F32).ap()
    tabc = {"P": nc.dram_tensor("tabP", [4, 2, 128, 1024], F32).ap(), "E": nc.dram_tensor("tabE", [4, 2, 128, 1024], F32).ap(),
            "B": nc.dram_tensor("tabB", [4, 2, 128, 1024], BF16).ap(), "C": nc.dram_tensor("tabC", [4, 2, 128, 1024], BF16).ap()}
    with ExitStack() as stack:
        B = LayerBuilder(nc, S, stack)
        B.tabcache = tabc
        B.setup_consts(cin)
        B.hT = B.sb("hT", [128, 8, LT], BF16)
        for j in range(n_layers):
            li = layer0 + j
            S.epoch = j
            W = {nm: Wall[nm][li] for nm, _ in LAYER_W}
            W["cvec"] = cvec
            x_in, tin = (xT, "xT") if j == 0 else (xs[(j - 1) % 2], "xs%d" % ((j - 1) % 2))
            x_out, tout = (xo, "xo") if j == n_layers - 1 else (xs[j % 2], "xs%d" % (j % 2))
            B.stage_mod(W)
            for s in range(NSEQ):
                B.stage_norm(x_in[s], s, tin)
                B.stage_s5(W, cin, ybr_dev[s, 0], s, li)
                B.stage_mlstm(W, cin, ybr_dev[s, 1], s, li)
                B.stage_attn(W, cin, ybr_dev[s, 2], s, li)
                B.stage_merge(W, x_in[s], x_out[s], [ybr_dev[s, b] for b in range(3)], s, tin, tout)
        S.emit(nc, stack)
    return nc, S


def _fm(ctx, lat):
    return np.ascontiguousarray(np.concatenate([ctx, lat], axis=1).transpose(0, 2, 1)).astype(np.float32)


def kernel(**inputs):
    x = np.asarray(inputs["x"], np.float32)
    c = np.asarray(inputs["c"], np.float32)
    ctx = np.asarray(inputs["ctx"], np.float32)
    c_ctx = np.asarray(inputs["c_ctx"], np.float32)
    ncores = 8
    if "p" not in _PROG:
        _PROG["p"] = build_program()
    nc, _ = _PROG["p"]
    consts = host_consts()
    wl = {nm: np.ascontiguousarray(np.asarray(inputs[nm], np.float32)) for nm, _ in LAYER_W}
    in_maps = []
    for i in range(ncores):
        m = {"xT": _fm(ctx[2 * i:2 * i + 2], x[2 * i:2 * i + 2]),
             "cvec": np.stack([c[2 * i], c[2 * i + 1], c_ctx]).astype(np.float32)}
        m.update(wl)
        m.update(consts)
        in_maps.append(m)
    res = run_bass_kernel_spmd(nc, in_maps, core_ids=list(range(ncores)))
    out = np.concatenate([np.ascontiguousarray(np.asarray(r["xo"], np.float32)[:, :, LC:].transpose(0, 2, 1))
                          for r in res.results], axis=0)
    return out.astype(np.float32)
```

```python
import math
from contextlib import ExitStack

import numpy as np
import concourse.bass as bass
import concourse.mybir as mybir
from concourse.bass_utils import run_bass_kernel_spmd

F32 = mybir.dt.float32
BF16 = mybir.dt.bfloat16
AF = mybir.ActivationFunctionType
ALU = mybir.AluOpType
AX = mybir.AxisListType

D = 1024
LC = 256
LL = 2048
LT = LC + LL
NT = LT // 128
DEPTH = 4
EPS = 1e-6
NSEQ = 2
O_SU, O_SZ, O_MQK, O_MV, O_MO, O_MZ, O_MG, O_DQ, O_DK, O_DV, O_DZ, O_GL = (
    0, 512, 1024, 2048, 2560, 3072, 3584, 3600, 4112, 4624, 5136, 5648)
D_IN = 8720
TOKCH = [(0, 256), (256, 512), (768, 512), (1280, 512), (1792, 512)]
TWO_PI = 2.0 * math.pi


class _Op:
    __slots__ = ("eng", "fn", "deps", "dma", "ms", "sem", "val", "idx", "ep")


class Sched:
    ENGS = ("pe", "act", "dve", "pool", "sp")
    R = 6

    def __init__(self):
        self.ops = []
        self.tw = {}
        self.tr = {}
        self.last = {e: None for e in self.ENGS}
        self.pending_barrier = {e: set() for e in self.ENGS}
        self.dma_hist = {e: [] for e in self.ENGS}
        self.epoch = 0

    def add(self, eng, fn, reads=(), writes=(), dma=False):
        op = _Op()
        op.eng, op.fn, op.dma, op.ms, op.sem, op.val = eng, fn, dma, False, None, None
        op.idx = len(self.ops)
        op.ep = self.epoch
        xs = [r for r in reads if isinstance(r, str) and (r.startswith("ps") or r.startswith("acc"))]
        if xs:
            writes = list(writes) + [x for x in xs if x not in writes]
        deps = set()
        for r in reads:
            w = self.tw.get(r)
            if w is not None:
                deps.add(w)
        for wt in writes:
            w = self.tw.get(wt)
            if w is not None:
                deps.add(w)
            for rr in self.tr.get(wt, ()):
                deps.add(rr)
        deps |= self.pending_barrier[eng]
        self.pending_barrier[eng] = set()
        keep = set()
        rset = None
        for d in deps:
            o = self.ops[d]
            if o.eng == eng and not o.dma and not dma:
                if eng == "pe":
                    continue
            keep.add(d)
        op.deps = keep
        for r in reads:
            self.tr.setdefault(r, []).append(op.idx)
        for wt in writes:
            self.tw[wt] = op.idx
            self.tr[wt] = []
        self.ops.append(op)
        self.last[eng] = op.idx
        if dma:
            self.dma_hist[eng].append(op.idx)
        return op.idx

    def barrier(self):
        s = set()
        for e in self.ENGS:
            if self.last[e] is not None:
                s.add(self.last[e])
            for i in self.dma_hist[e][-self.R:]:
                s.add(i)
        for e in self.ENGS:
            self.pending_barrier[e] |= s

    def emit(self, nc, stack):
        ops = self.ops
        for op in ops:
            for d in op.deps:
                ops[d].ms = True
        neps = max(op.ep for op in ops) + 1
        csem = {(e, ep): stack.enter_context(nc.semaphore("c_%s%d" % (e, ep))) for e in self.ENGS for ep in range(neps)}
        dsem = {e: [stack.enter_context(nc.semaphore("d_%s%d" % (e, i))) for i in range(self.R)]
                for e in ("sp", "pool", "act")}
        ccount = {(e, ep): 0 for e in self.ENGS for ep in range(neps)}
        dcount = {e: 0 for e in self.ENGS}
        per_eng = {e: [] for e in self.ENGS}
        for op in ops:
            if op.dma:
                i = dcount[op.eng]
                op.sem = dsem[op.eng][i % self.R]
                op.val = 16 * (i // self.R + 1)
                dcount[op.eng] += 1
            elif op.ms:
                ccount[(op.eng, op.ep)] += 1
                op.sem = csem[(op.eng, op.ep)]
                op.val = ccount[(op.eng, op.ep)]
            per_eng[op.eng].append(op)
        self.stats = {e: (len(per_eng[e]), sum(ccount[(e, ep)] for ep in range(neps)), dcount[e]) for e in self.ENGS}
        R = self.R

        def replay(ename, e):
            seen = {}
            ndma = 0
            for op in per_eng[ename]:
                waits = {}
                for d in op.deps:
                    o = ops[d]
                    k = o.sem
                    if waits.get(k, (None, 0))[1] < o.val:
                        waits[k] = (o.sem, o.val)
                if op.dma:
                    if ndma >= R:
                        k = op.sem
                        v = op.val - 16
                        if waits.get(k, (None, 0))[1] < v:
                            waits[k] = (op.sem, v)
                    ndma += 1
                for k, (sem, val) in waits.items():
                    if seen.get(k, 0) < val:
                        e.wait_ge(sem, val)
                        seen[k] = val
                ins = op.fn(e)
                if op.dma:
                    ins.then_inc(op.sem, 16)
                elif op.ms:
                    ins.then_inc(op.sem, 1)
            if ename in dsem:
                n = dcount[ename]
                for j in range(min(n, R)):
                    cnt = (n - 1 - j) // R + 1
                    e.wait_ge(dsem[ename][j], 16 * cnt)

        with nc.Block() as block:
            @block.tensor
            def _(e):
                replay("pe", e)

            @block.scalar
            def _(e):
                replay("act", e)

            @block.vector
            def _(e):
                replay("dve", e)

            @block.gpsimd
            def _(e):
                replay("pool", e)

            @block.sync
            def _(e):
                replay("sp", e)


class LayerBuilder:
    def __init__(self, nc, S, stack, dbg=None):
        self.nc, self.S, self.stack = nc, S, stack
        self.dbg = dbg or {}
        self.uid = 0
        self.wq = 0
        self.tabcache = None

    def sb(self, name, shape, dt, stack=None):
        self.uid += 1
        return (stack or self.stack).enter_context(self.nc.sbuf_tensor("%s_%d" % (name, self.uid), shape, dt))

    def ps(self, name, shape, dt=F32, stack=None):
        self.uid += 1
        full = 512 if dt == F32 else 1024
        t = (stack or self.stack).enter_context(self.nc.psum_tensor("%s_%d" % (name, self.uid), [128, full], dt))
        n = 1
        for d in shape[1:]:
            n *= d
        assert n <= full, (name, shape)
        v = t[0:shape[0], 0:n]
        if len(shape) == 3:
            v = v.rearrange("p (a b) -> p a b", b=shape[2])
        return v

    def dma(self, out, in_, reads, writes, q=None, slow=False):
        if q is None:
            q = "sp"
        if slow:
            fn = lambda e, o=out, i=in_: e.dma_start(out=o, in_=i, allow_slow_non_contiguous=True)
        else:
            fn = lambda e, o=out, i=in_: e.dma_start(out=o, in_=i)
        return self.S.add(q, fn, reads, writes, dma=True)

    def mm(self, out, lhsT, rhs, start, stop, reads, writes, skip=False):
        if skip:
            fn = lambda e: e.matmul(out, lhsT, rhs, start=start, stop=stop, skip_group_check=True)
        else:
            fn = lambda e: e.matmul(out, lhsT, rhs, start=start, stop=stop)
        return self.S.add("pe", fn, reads, writes)

    def tr(self, out, in_, ident, reads, writes):
        return self.S.add("pe", lambda e: e.transpose(out, in_, ident), reads, writes)

    def act(self, out, in_, func, reads, writes, bias=None, scale=None, accum_out=None):
        kw = {}
        if bias is not None:
            kw["bias"] = bias
        if scale is not None:
            kw["scale"] = scale
        if accum_out is not None:
            kw["accum_out"] = accum_out
        return self.S.add("act", lambda e: e.activation(out=out, in_=in_, func=func, **kw), reads, writes)

    def tt(self, out, in0, in1, op, reads, writes, eng="dve"):
        return self.S.add(eng, lambda e: e.tensor_tensor(out=out, in0=in0, in1=in1, op=op), reads, writes)

    def ts(self, out, in0, s1, s2, op0, op1, reads, writes, eng="dve"):
        if op1 is None:
            fn = lambda e: e.tensor_scalar(out=out, in0=in0, scalar1=s1, scalar2=None, op0=op0)
        else:
            fn = lambda e: e.tensor_scalar(out=out, in0=in0, scalar1=s1, scalar2=s2, op0=op0, op1=op1)
        return self.S.add(eng, fn, reads, writes)

    def stt(self, out, in0, scalar, in1, op0, op1, reads, writes):
        return self.S.add("dve", lambda e: e.scalar_tensor_tensor(out=out, in0=in0, scalar=scalar, in1=in1,
                                                                  op0=op0, op1=op1), reads, writes)

    def cp(self, out, in_, reads, writes, eng="dve"):
        if eng == "act":
            return self.S.add("act", lambda e: e.activation(out=out, in_=in_, func=AF.Copy), reads, writes)
        return self.S.add(eng, lambda e: e.tensor_copy(out=out, in_=in_), reads, writes)

    def rsqrt(self, out, in_, scale, eps, reads, writes):
        et = self.epsT[eps]
        np_ = out.shape[0]
        self.act(out, in_, AF.Sqrt, list(reads) + [("epsT", eps)], writes, bias=et[0:np_, :], scale=scale)
        return self.S.add("dve", lambda e: e.reciprocal(out=out, in_=out), writes, writes)

    def rsqrt_le(self, out, in_, scale, eps, reads, writes):
        et = self.epsT[eps]
        np_ = out.shape[0]
        self.act(out, in_, AF.Ln, list(reads) + [("epsT", eps)], writes, bias=et[0:np_, :], scale=scale)
        return self.act(out, out, AF.Exp, writes, writes, scale=-0.5)

    def memset(self, ap, val, writes, eng="dve"):
        return self.S.add(eng, lambda e: e.memset(ap, val), (), writes)

    def setup_consts(self, cin):
        S = self.S
        self.identF = self.sb("identF", [128, 128], F32)
        self.identB = self.sb("identB", [128, 128], BF16)
        self.onesF = self.sb("onesF", [128, 128], F32)
        self.triL = self.sb("triL", [128, 128], BF16)
        self.triU = self.sb("triU", [128, 128], BF16)
        self.dma(self.identF[:], cin["ident"], (), ["identF"])
        self.dma(self.onesF[:], cin["ones"], (), ["onesF"])
        tmp = self.sb("ctmp", [128, 256], F32)
        self.dma(tmp[:, 0:128], cin["tril"], (), ["ctmp0"])
        self.dma(tmp[:, 128:256], cin["triu"], (), ["ctmp1"])
        self.triLF = tmp[:, 0:128]
        self.triUF = tmp[:, 128:256]
        self.epsT = {}
        t = self.sb("epsT", [128, 1], F32)
        self.memset(t[:], EPS, [("epsT", EPS)])
        self.epsT[EPS] = t
        self.cp(self.identB[:], self.identF[:], ["identF"], ["identB"])
        self.cp(self.triL[:], tmp[:, 0:128], ["ctmp0"], ["triL"])
        self.cp(self.triU[:], tmp[:, 128:256], ["ctmp1"], ["triU"])

    def load_w(self, name, w_dram, c0, ncols, KT, dst, dst_tok, stage, stage_tok, q="pool", cast_eng="pool"):
        src = w_dram.rearrange("(k p) c -> p k c", p=128)[:, :, c0:c0 + ncols]
        self.dma(stage[:, 0:KT, 0:ncols], src, [], [stage_tok], q=q)
        self.cp(dst[:, 0:KT, 0:ncols], stage[:, 0:KT, 0:ncols], [stage_tok], [dst_tok], eng=cast_eng)


    def stage_mod(self, W):
        S, nc = self.S, self.nc
        self.modT = self.sb("modT", [128, 24, 3], F32)
        self.A1 = self.sb("A1", [128, 8, 3], F32)
        with ExitStack() as st:
            cS = self.sb("cS", [128, 8, 3], F32, st)
            adab = self.sb("adab", [128, 24], F32, st)
            normg = self.sb("normg", [128, 8], F32, st)
            stage = self.sb("stgA", [128, 8, 512], F32, st)
            stage2 = self.sb("stgA2", [128, 8, 512], F32, st)
            psM = self.ps("psM", [128, 72], F32, st)
            for v in range(3):
                self.dma(cS[:, :, v], W["cvec"][v].rearrange("(k p) -> p k", p=128), [], ["cS"], slow=True)
            self.dma(adab[:], W["ada_b"].rearrange("(j p) -> p j", p=128), [], ["adab"], slow=True)
            self.dma(normg[:], W["norm_g"].rearrange("(k p) -> p k", p=128), [], ["normg"], slow=True)
            self.act(cS[:], cS[:], AF.Silu, ["cS"], ["cS"])
            stgs = [(stage, "stgA"), (stage2, "stgA2")]
            for ch in range(6):
                stg, tok = stgs[ch % 2]
                src = W["ada_w"].rearrange("(k p) c -> p k c", p=128)[:, :, ch * 512:(ch + 1) * 512]
                self.dma(stg[:], src, [], [tok], q="pool")
                for j in range(4):
                    jj = ch * 4 + j
                    for k in range(8):
                        self.mm(psM[:, 3 * jj:3 * jj + 3], stg[:, k, j * 128:(j + 1) * 128], cS[:, k, :],
                                k == 0, k == 7, [tok, "cS"], ["psM"])
            self.tt(self.modT[:], psM[:].rearrange("p (j v) -> p j v", v=3),
                    adab[:].unsqueeze(2).to_broadcast([128, 24, 3]), ALU.add, ["psM", "adab"], ["modT"])
            self.stt(self.A1[:], self.modT[:, 8:16, :], 1.0, normg[:].unsqueeze(2).to_broadcast([128, 8, 3]),
                     ALU.add, ALU.mult, ["modT", "normg"], ["A1"])
            S.barrier()

    def stage_norm(self, xT_s, s, xtok_in="xin"):
        S = self.S
        with ExitStack() as st:
            xin = [self.sb("xin", [128, 8, 512], F32, st) for _ in range(2)]
            sqt = self.sb("sqt", [128, 8, 512], F32, st)
            rs = self.sb("rs", [128, 512], F32, st)
            tmpn = [self.sb("tmpn", [128, 512], F32, st) for _ in range(2)]
            pss = [self.ps("pss", [128, 512], F32, st) for _ in range(2)]
            for ci, (t0, n) in enumerate(TOKCH):
                v = 2 if t0 == 0 else s
                xi = xin[ci % 2]
                xtok = "xin%d" % (ci % 2)
                ps = pss[ci % 2]
                pstok = "pss%d" % (ci % 2)
                self.dma(xi[:, :, 0:n], xT_s.rearrange("(k p) t -> p k t", p=128)[:, :, t0:t0 + n], [(xtok_in, s)], [xtok])
                self.act(sqt[:, :, 0:n], xi[:, :, 0:n], AF.Square, [xtok], ["sqt"])
                for k in range(8):
                    self.mm(ps[:, 0:n], self.onesF[:], sqt[:, k, 0:n], k == 0, k == 7, ["sqt", "onesF"], [pstok])
                self.rsqrt(rs[:, 0:n], ps[:, 0:n], 1.0 / D, EPS, [pstok], ["rs"])
                for k in range(8):
                    tm = tmpn[k % 2]
                    ttok = "tmpn%d" % (k % 2)
                    self.tt(tm[:, 0:n], xi[:, k, 0:n], rs[:, 0:n], ALU.mult, [xtok, "rs"], [ttok])
                    self.act(self.hT[:, k, t0:t0 + n], tm[:, 0:n], AF.Identity, [ttok, "A1", "modT"], [("hT", ci)],
                             bias=self.modT[:, k, v:v + 1], scale=self.A1[:, k, v:v + 1])
            S.barrier()

    def stage_merge(self, W, xT_s, xo_s, ybr_s, s, xtok_in="xin", xtok_out="xout"):
        S = self.S
        hT = self.hT
        with ExitStack() as st:
            yg = [self.sb("yg%d" % b, [128, 4, LT], BF16, st) for b in range(3)]
            mT = self.sb("mT", [128, 8, LT], BF16, st)
            stage = self.sb("stgF", [128, 8, 512], F32, st)
            wb = [self.sb("wbF", [128, 8, 512], BF16, st) for _ in range(2)]
            wb2 = [self.sb("wbG", [128, 8, 384], BF16, st), self.sb("wbG", [128, 4, 384], BF16, st)]
            ych = [self.sb("ych", [128, 512], F32, st) for _ in range(2)]
            szt = [self.sb("szt", [128, 512], F32, st) for _ in range(2)]
            sg = [self.sb("sg", [128, 512], F32, st) for _ in range(3)]
            mt = [self.sb("mt", [128, 512], F32, st) for _ in range(3)]
            psA = [self.ps("psA", [128, 512], F32, st) for _ in range(3)]
            psB = [self.ps("psB", [128, 512], F32, st) for _ in range(3)]
            allh = [("hT", i) for i in range(len(TOKCH))]
            zoff = [O_SZ, O_MZ, O_DZ]
            for b in range(3):
                w = wb[b % 2]
                wtok = "wbF%d" % (b % 2)
                self.load_w("wz", W["w_in"], zoff[b], 512, 8, w, wtok, stage, "stgF")
                for c in range(4):
                    for ci, (t0, n) in enumerate(TOKCH):
                        i = (c * 5 + ci) % 2
                        self.dma(ych[i][:, 0:n], ybr_s[b][c * 128:(c + 1) * 128, t0:t0 + n], [("ybr", s, b)], ["ych%d" % i])
                        ps = psA[i]
                        for k in range(8):
                            self.mm(ps[:, 0:n], w[:, k, c * 128:(c + 1) * 128], hT[:, k, t0:t0 + n], k == 0, k == 7,
                                    [wtok, ("hT", ci)], ["psA%d" % i])
                        self.act(szt[i][:, 0:n], ps[:, 0:n], AF.Silu, ["psA%d" % i], ["szt%d" % i])
                        self.tt(yg[b][:, c, t0:t0 + n], ych[i][:, 0:n], szt[i][:, 0:n], ALU.mult,
                                ["ych%d" % i, "szt%d" % i], [("yg", b, ci)])
            wouts = [W["w_ssm_out"], W["w_ml_out"], W["w_da_out"]]
            for f in range(8):
                if f % 2 == 0:
                    wg, wo, wgt, wot = wb[0], wb[1], "wbF0", "wbF1"
                else:
                    wg, wo, wgt, wot = wb2[0], wb2[1], "wbG0", "wbG1"
                for b in range(3):
                    src = W["w_in"].rearrange("(k p) c -> p k c", p=128)[:, :, O_GL + b * 1024 + f * 128:O_GL + b * 1024 + (f + 1) * 128]
                    self.dma(stage[:, :, b * 128:(b + 1) * 128], src, [], ["stgF"], q="pool")
                self.cp(wg[:, :, 0:384], stage[:, :, 0:384], ["stgF"], [wgt], eng="pool")
                for b in range(3):
                    src = wouts[b].rearrange("(k p) c -> p k c", p=128)[:, :, f * 128:(f + 1) * 128]
                    self.dma(stage[:, 0:4, b * 128:(b + 1) * 128], src, [], ["stgF"], q="pool")
                self.cp(wo[:, 0:4, 0:384], stage[:, 0:4, 0:384], ["stgF"], [wot], eng="pool")
                for ci, (t0, n) in enumerate(TOKCH):
                    for b in range(3):
                        for k in range(8):
                            self.mm(psA[b][:, 0:n], wg[:, k, b * 128:(b + 1) * 128], hT[:, k, t0:t0 + n], k == 0, k == 7,
                                    [wgt, ("hT", ci)], ["psA%d" % b])
                        for k in range(4):
                            self.mm(psB[b][:, 0:n], wo[:, k, b * 128:(b + 1) * 128], yg[b][:, k, t0:t0 + n], k == 0, k == 3,
                                    [wot, ("yg", b, ci)], ["psB%d" % b])
                    for b in range(3):
                        self.act(sg[b][:, 0:n], psA[b][:, 0:n], AF.Sigmoid, ["psA%d" % b], ["sg%d" % b])
                        self.tt(mt[b][:, 0:n], sg[b][:, 0:n], psB[b][:, 0:n], ALU.mult, ["sg%d" % b, "psB%d" % b], ["mt%d" % b])
                    self.tt(mt[0][:, 0:n], mt[0][:, 0:n], mt[1][:, 0:n], ALU.add, ["mt0", "mt1"], ["mt0"])
                    self.tt(mT[:, f, t0:t0 + n], mt[0][:, 0:n], mt[2][:, 0:n], ALU.add, ["mt0", "mt2"], [("mT", ci)])
            wo_full = [wb[0], wb[1]]
            for half in range(2):
                self.load_w("wo", W["w_out"], half * 512, 512, 8, wo_full[half], "wbF%d" % half, stage, "stgF")
            for fo in range(8):
                w = wo_full[fo // 4]
                wtok = "wbF%d" % (fo // 4)
                for ci, (t0, n) in enumerate(TOKCH):
                    v = 2 if t0 == 0 else s
                    i = (fo * 5 + ci) % 2
                    ps = psA[i]
                    for k in range(8):
                        self.mm(ps[:, 0:n], w[:, k, (fo % 4) * 128:(fo % 4 + 1) * 128], mT[:, k, t0:t0 + n], k == 0, k == 7,
                                [wtok, ("mT", ci)], ["psA%d" % i])
                    self.dma(ych[i][:, 0:n], xT_s[fo * 128:(fo + 1) * 128, t0:t0 + n], [(xtok_in, s)], ["ych%d" % i])
                    self.stt(szt[i][:, 0:n], ps[:, 0:n], self.modT[:, 16 + fo, v:v + 1], ych[i][:, 0:n], ALU.mult, ALU.add,
                             ["psA%d" % i, "ych%d" % i, "modT"], ["szt%d" % i])
                    self.dma(xo_s[fo * 128:(fo + 1) * 128, t0:t0 + n], szt[i][:, 0:n], ["szt%d" % i], [(xtok_out, s)])
            S.barrier()


    def stage_attn(self, W, cin, ybr_c, s, li, with_ctx=True):
        S = self.S
        hT = self.hT
        lam_init = 0.8 - 0.6 * math.exp(-0.3 * li)
        allh = [("hT", i) for i in range(len(TOKCH))]
        def hts(i):
            t = i * 128
            for ci, (t0, n) in enumerate(TOKCH):
                if t0 <= t < t0 + n:
                    return ("hT", ci)
        with ExitStack() as st:
            qT0 = self.sb("qT0", [128, 4, LT], BF16, st)
            qT1 = self.sb("qT1", [128, 4, LT], BF16, st)
            qTm = [qT0, qT1]
            self.memset(qT0[64:128, :, :], 0.0, ["qz0"], eng="pool")
            self.memset(qT1[0:64, :, :], 0.0, ["qz1"], eng="pool")
            kT = self.sb("kT", [128, 4, LT], BF16, st)
            vaug = self.sb("vaug", [128, NT, 4, 129], BF16, st)
            gq = self.sb("gq", [128, 64], F32, st)
            gk = self.sb("gk", [128, 64], F32, st)
            gsub = self.sb("gsub", [128, 128], F32, st)
            lamt = self.sb("lamt", [128, 256], F32, st)
            lprod = self.sb("lprod", [128, 2, 64], F32, st)
            lsum = self.sb("lsum", [128, 2], F32, st)
            neglam = self.sb("neglam", [128, 1], F32, st)
            self.dma(gq[:], W["da_qnorm_g"].rearrange("(o d) -> o d", o=1).to_broadcast([128, 64]), [], ["gq"])
            self.dma(gk[:], W["da_knorm_g"].rearrange("(o d) -> o d", o=1).to_broadcast([128, 64]), [], ["gk"])
            self.dma(gsub[:], W["da_subln_g"].rearrange("(o d) -> o d", o=1).to_broadcast([128, 128]), [], ["gsub"])
            self.dma(lamt[:], W["da_lambda"].rearrange("(o a) d -> o (a d)", o=1).to_broadcast([128, 256]), [], ["lamt"])
            self.ts(gq[:], gq[:], 0.125, None, ALU.mult, None, ["gq"], ["gq"])
            lamc = self.sb("lamc", [128, 2], F32, st)
            self.dma(lamc[:], cin["lamc"][li], [], ["lamc"])
            self.ts(gsub[:], gsub[:], lamc[:, 1:2], None, ALU.mult, None, ["gsub", "lamc"], ["gsub"])
            lv = lamt[:].rearrange("p (a b d) -> p a b d", a=2, b=2)
            self.tt(lprod[:], lv[:, :, 0, :], lv[:, :, 1, :], ALU.mult, ["lamt"], ["lprod"])
            self.S.add("dve", lambda e: e.tensor_reduce(out=lsum[:], in_=lprod[:], axis=AX.X, op=ALU.add), ["lprod"], ["lsum"])
            self.act(lsum[:], lsum[:], AF.Exp, ["lsum"], ["lsum"])
            self.tt(neglam[:], lsum[:, 1:2], lsum[:, 0:1], ALU.subtract, ["lsum"], ["neglam"])
            self.ts(neglam[:], neglam[:], lamc[:, 0:1], None, ALU.add, None, ["neglam", "lamc"], ["neglam"])
            self.memset(vaug[:, :, :, 128:129], 1.0, ["vones"])
            with ExitStack() as st2:
                stage = self.sb("stgE", [128, 8, 512], F32, st2)
                wq = self.sb("wq", [128, 8, 512], BF16, st2)
                wk = self.sb("wk", [128, 8, 512], BF16, st2)
                wv = self.sb("wv", [128, 8, 512], BF16, st2)
                sqfd = [self.sb("sqf", [128, 512], F32, st2) for _ in range(2)]
                xrawd = [self.sb("xraw", [128, 512], F32, st2) for _ in range(2)]
                ssd = [self.sb("ss8", [128, 8], F32, st2) for _ in range(2)]
                xnd = [self.sb("xn", [128, 512], F32, st2) for _ in range(2)]
                t1q = [self.sb("t1", [128, 512], F32, st2) for _ in range(2)]
                t2q = [self.sb("t2", [128, 512], F32, st2) for _ in range(2)]
                xb = [self.sb("xb", [128, 512], BF16, st2) for _ in range(2)]
                ropeC = self.sb("ropeC", [128, 16, 64], F32, st2)
                ropeS = self.sb("ropeS", [128, 16, 64], F32, st2)
                psq = self.ps("psq", [128, 512], F32, st2)
                psk = self.ps("psk", [128, 512], F32, st2)
                psv = self.ps("psv", [128, 512], F32, st2)
                pst = [self.ps("pstE", [128, 512], BF16, st2) for _ in range(2)]
                self.dma(ropeC[:], cin["ropeC"], [], ["ropeC"])
                self.dma(ropeS[:], cin["ropeS"], [], ["ropeS"])
                self.load_w("wq", W["w_in"], O_DQ, 512, 8, wq, "wq", stage, "stgE")
                self.load_w("wk", W["w_in"], O_DK, 512, 8, wk, "wk", stage, "stgE")
                self.load_w("wv", W["w_in"], O_DV, 512, 8, wv, "wv", stage, "stgE")
                for i in range(NT):
                    tsl = slice(i * 128, (i + 1) * 128)
                    ht = hts(i)
                    for (w, wt, ps, pt) in ((wq, "wq", psq, "psq"), (wk, "wk", psk, "psk"), (wv, "wv", psv, "psv")):
                        for k in range(8):
                            self.mm(ps[:], hT[:, k, tsl], w[:, k, :], k == 0, k == 7, [ht, wt], [pt])
                    self.cp(vaug[:, i, :, 0:128], psv[:].rearrange("p (h d) -> p h d", h=4), ["psv"], [("vaug", i)], eng="act")
                    def qk_chain(qi, ps, pt, g, gt, dst, dt, i=i, tsl=tsl):
                        sqf_, xraw_, ss_, xn_, t1_, t2_ = sqfd[qi], xrawd[qi], ssd[qi], xnd[qi], t1q[qi], t2q[qi]
                        k_ = lambda nm: "%s%d" % (nm, qi)
                        self.cp(xraw_[:], ps[:], [pt], [k_("xraw")], eng="act"); yield
                        self.tt(sqf_[:], xraw_[:], xraw_[:], ALU.mult, [k_("xraw")], [k_("sqf")]); yield
                        self.S.add("dve", lambda e: e.tensor_reduce(out=ss_[:], in_=sqf_[:].rearrange("p (g d) -> p g d", d=64),
                                                                  axis=AX.X, op=ALU.add), [k_("sqf")], [k_("ss8")]); yield
                        et = self.epsT[EPS]
                        self.act(ss_[:], ss_[:], AF.Ln, [k_("ss8"), ("epsT", EPS)], [k_("ss8")], bias=et[:, :], scale=1.0 / 64); yield
                        self.act(ss_[:], ss_[:], AF.Exp, [k_("ss8")], [k_("ss8")], scale=-0.5); yield
                        self.tt(xn_[:].rearrange("p (g d) -> p g d", d=64), xraw_[:].rearrange("p (g d) -> p g d", d=64),
                                ss_[:].unsqueeze(2).to_broadcast([128, 8, 64]), ALU.mult, [k_("xraw"), k_("ss8")], [k_("xn")]); yield
                        x_b = xb[qi]
                        xbt = "xb%d" % qi
                        if i >= 2:
                            self.tt(xn_[:].rearrange("p (g d) -> p g d", d=64), xn_[:].rearrange("p (g d) -> p g d", d=64),
                                    g[:].unsqueeze(1).to_broadcast([128, 8, 64]), ALU.mult, [k_("xn"), gt], [k_("xn")]); yield
                            lt = i - 2
                            self.tt(t1_[:].rearrange("p (g d) -> p g d", d=64), xn_[:].rearrange("p (g d) -> p g d", d=64),
                                    ropeC[:, lt, :].unsqueeze(1).to_broadcast([128, 8, 64]), ALU.mult, [k_("xn"), "ropeC"], [k_("t1")]); yield
                            xv = xn_[:].rearrange("p (g r h d) -> p g r h d", g=8, r=2, h=2)
                            tv = t2_[:].rearrange("p (g r h d) -> p g r h d", g=8, r=2, h=2)
                            sv = ropeS[:, lt, :].rearrange("p (r h d) -> p r h d", r=2, h=2)
                            self.tt(tv[:, :, :, 0, :], xv[:, :, :, 1, :], sv[:, :, 0, :].unsqueeze(1).to_broadcast([128, 8, 2, 16]),
                                    ALU.mult, [k_("xn"), "ropeS"], [k_("t2")]); yield
                            self.tt(tv[:, :, :, 1, :], xv[:, :, :, 0, :], sv[:, :, 1, :].unsqueeze(1).to_broadcast([128, 8, 2, 16]),
                                    ALU.mult, [k_("xn"), "ropeS"], [k_("t2")]); yield
                            self.tt(x_b[:], t1_[:], t2_[:], ALU.add, [k_("t1"), k_("t2")], [xbt]); yield
                        else:
                            self.tt(x_b[:].rearrange("p (g d) -> p g d", d=64), xn_[:].rearrange("p (g d) -> p g d", d=64),
                                    g[:].unsqueeze(1).to_broadcast([128, 8, 64]), ALU.mult, [k_("xn"), gt], [xbt]); yield
                        pp = pst[qi]
                        ppt = "pstE%d" % qi
                        for h in range(4):
                            self.tr(pp[:, h * 128:(h + 1) * 128], x_b[:, h * 128:(h + 1) * 128], self.identB[:], [xbt, "identB"], [ppt])
                        yield
                        if dst is None:
                            pv = pp[:].rearrange("p (h t) -> p h t", h=4)
                            self.cp(qT0[0:64, :, tsl], pv[0:64], [ppt], [("qT", i, 0)], eng="act")
                            self.cp(qT1[64:128, :, tsl], pv[64:128], [ppt], [("qT", i, 1)], eng="act")
                        else:
                            self.cp(dst[:, :, tsl], pp[:].rearrange("p (h t) -> p h t", h=4), [ppt], [(dt, i)], eng="act")
                        yield

                    gens = [qk_chain(0, psq, "psq", gq, "gq", None, "qT"), qk_chain(1, psk, "psk", gk, "gk", kT, "kT")]
                    while gens:
                        for g_ in list(gens):
                            try:
                                next(g_)
                            except StopIteration:
                                gens.remove(g_)
                S.barrier()
            with ExitStack() as st3:
                pT = [self.sb("pT", [128, NT, 512], BF16, st3) for _ in range(2)]
                o = [self.sb("oE", [128, 128], F32, st3) for _ in range(2)]
                osq = self.sb("osq", [128, 128], F32, st3)
                ob = [self.sb("obE", [128, 128], BF16, st3) for _ in range(2)]
                rec = self.sb("recE", [128, 8], F32, st3)
                yst = [self.sb("ystE", [128, 512], F32, st3) for _ in range(2)]
                pss = [self.ps("pssE", [128, 512], F32, st3) for _ in range(3)]
                acc = [self.ps("accE", [128, 129], F32, st3) for _ in range(2)]
                pso = [self.ps("psoE", [128, 512], BF16, st3) for _ in range(2)]
                qchunks = [(256 + 512 * j, 512, list(range(NT))) for j in range(4)]
                if with_ctx:
                    qchunks.append((0, 256, [0, 1]))
                items = [(h, t0, n, keys) for h in range(4) for (t0, n, keys) in qchunks]
                oo4 = [self.sb("oo4", [128, 4, 128], F32, st3) for _ in range(2)]
                cnt = [0]

                def Hhalf(it, m):
                    h, t0, n, keys = items[it]
                    qtk = [("qT", (t0 // 128) + j, m) for j in range(n // 128)] + ["qz%d" % m]
                    for kt in keys:
                        ps = pss[cnt[0] % 3]
                        pstk = "pssE%d" % (cnt[0] % 3)
                        cnt[0] += 1
                        self.mm(ps[:, 0:n], kT[:, h, kt * 128:(kt + 1) * 128], qTm[m][:, h, t0:t0 + n], True, True,
                                [("kT", kt)] + qtk, [pstk])
                        self.act(pT[m][:, kt, 0:n], ps[:, 0:n], AF.Exp, [pstk], [("pT", m, kt)])

                def Vhalf(it, m):
                    h, t0, n, keys = items[it]
                    o4 = oo4[it % 2]
                    ys = yst[it % 2]
                    yt = "ystE%d" % (it % 2)
                    pso_ = pso[it % 2]
                    psot = "psoE%d" % (it % 2)
                    for qs in range(n // 128):
                        a = acc[qs % 2]
                        at = "accE%d" % (qs % 2)
                        ot = ("oo4", it % 2, qs)
                        for j, kt in enumerate(keys):
                            self.mm(a[:], pT[m][:, kt, qs * 128:(qs + 1) * 128], vaug[:, kt, h, :], j == 0, j == len(keys) - 1,
                                    [("pT", m, kt), ("vaug", kt), "vones"], [at])
                        rk = ("rec", qs % 2, m)
                        rcol = rec[:, 4 * (qs % 2) + m:4 * (qs % 2) + m + 1]
                        self.S.add("dve", lambda e, a=a, rcol=rcol: e.reciprocal(out=rcol, in_=a[:, 128:129]), [at], [rk])
                        if m == 0:
                            self.ts(o4[:, qs, :], a[:, 0:128], rcol, None, ALU.mult, None, [at, rk], [ot])
                        else:
                            r2 = rec[:, 4 * (qs % 2) + 2:4 * (qs % 2) + 3]
                            r3 = rec[:, 4 * (qs % 2) + 3:4 * (qs % 2) + 4]
                            rk2, rk3 = ("rec", qs % 2, 2), ("rec", qs % 2, 3)
                            self.tt(r2, rcol, neglam[:], ALU.mult, [rk, "neglam"], [rk2])
                            self.stt(o4[:, qs, :], a[:, 0:128], r2, o4[:, qs, :], ALU.mult, ALU.add, [at, rk2, ot], [ot])
                            self.tt(osq[:], o4[:, qs, :], o4[:, qs, :], ALU.mult, [ot], ["osq"])
                            self.S.add("dve", lambda e, r3=r3: e.tensor_reduce(out=r3, in_=osq[:], axis=AX.X, op=ALU.add), ["osq"], [rk3])
                            self.rsqrt_le(r3, r3, 1.0 / 128, EPS, [rk3], [rk3])
                            obb = ob[qs % 2]
                            obt = "obE%d" % (qs % 2)
                            self.stt(obb[:], o4[:, qs, :], r3, gsub[:], ALU.mult, ALU.mult, [ot, rk3, "gsub"], [obt])
                            self.tr(pso_[:, qs * 128:(qs + 1) * 128], obb[:], self.identB[:], [obt, "identB"], [psot])
                    if m == 1:
                        self.cp(ys[:, 0:n], pso_[:, 0:n], [psot], [yt], eng="act")
                        self.dma(ybr_c[h * 128:(h + 1) * 128, t0:t0 + n], ys[:, 0:n], [yt], [("ybr", s, 2)])

                halves = [(it, m) for it in range(len(items)) for m in range(2)]
                for j in range(len(halves) + 2):
                    if j >= 2:
                        Vhalf(*halves[j - 2])
                    if j < len(halves):
                        Hhalf(*halves[j])
                S.barrier()
            S.barrier()


    def stage_mlstm(self, W, cin, ybr_b, s, li):
        S = self.S
        hT = self.hT
        def hts(i):
            t = i * 128
            for ci, (t0, n) in enumerate(TOKCH):
                if t0 <= t < t0 + n:
                    return ("hT", ci)
        order = [list(range(NT)), [1, 0] + list(range(NT - 1, 1, -1))]
        with ExitStack() as st:
            tokS = [self.sb("tokS", [128, NT, 12], F32, st) for _ in range(2)]
            decB = [self.sb("decB", [128, NT, 4], F32, st) for _ in range(2)]
            cw = self.sb("cw", [128, 8, 3], F32, st)
            cb = self.sb("cb", [128, 8], F32, st)
            gml = self.sb("gml", [128, 512], F32, st)
            id4 = self.identF[0:4, 0:4]
            stgH = [self.sb("stgD", [128, 8, 512], F32, st) for _ in range(2)]
            wbH = [self.sb("wbD", [128, 8, 512], BF16, st) for _ in range(2)]

            def load_head(hh):
                cols_ = [O_MQK + hh * 128, O_MQK + 512 + hh * 128, O_MV + hh * 128, O_MO + hh * 128]
                for j_, c0_ in enumerate(cols_):
                    self.dma(stgH[hh % 2][:, :, j_ * 128:(j_ + 1) * 128], W["w_in"].rearrange("(k p) c -> p k c", p=128)[:, :, c0_:c0_ + 128],
                             [], ["stgD%d" % (hh % 2)], q="pool")
                self.cp(wbH[hh % 2][:], stgH[hh % 2][:], ["stgD%d" % (hh % 2)], ["wbD%d" % (hh % 2)], eng="pool")
            load_head(0)
            for j in range(3):
                self.dma(cw[:, :, j], W["ml_conv_w"][j].rearrange("(f p) -> p f", p=128), [], ["cw"], slow=True)
            self.dma(cb[:], W["ml_conv_b"].rearrange("(f p) -> p f", p=128), [], ["cb"], slow=True)
            self.dma(gml[:], W["ml_norm_g"].rearrange("(o d) -> o d", o=1).to_broadcast([128, 512]), [], ["gml"])
            with ExitStack() as st2:
                ones4 = self.sb("ones4", [4, 1], F32, st2)
                gb = self.sb("gb", [4, 4], F32, st2)
                stage = self.sb("stgG", [128, 8, 16], F32, st2)
                wg = self.sb("wg", [128, 8, 16], BF16, st2)
                Td = [[self.sb("gT", [4, LT], F32, st2) for _ in range(4)] for _ in range(2)]
                mendd = [self.sb("mend", [4, NT], F32, st2) for _ in range(2)]
                decd = [self.sb("dec", [4, NT], F32, st2) for _ in range(2)]
                ddgd = [self.sb("ddg", [4, NT, 4], F32, st2) for _ in range(2)]
                psgd = [[self.ps("psg", [4, 512], F32, st2) for _ in range(2)] for _ in range(2)]
                pstkd = [self.ps("pstk", [128, NT, 12], F32, st2) for _ in range(2)]
                psdd = [self.ps("psd", [128, NT * 4], F32, st2) for _ in range(2)]
                self.memset(ones4[:], 1.0, ["ones4"])
                self.dma(gb[:], W["ml_gate_b"].rearrange("a h -> h a"), [], ["gb"], slow=True)
                self.dma(stage[:], W["w_in"].rearrange("(k p) c -> p k c", p=128)[:, :, O_MG:O_MG + 16], [], ["stgG"], q="pool")
                self.cp(wg[:], stage[:], ["stgG"], ["wg"], eng="pool")
                onesb = ones4[:].to_broadcast([4, LT])

                def gate_dir(d):
                    Ti, Tf, Tg, Tm = Td[d]
                    g0, g1, g2, g3 = ["gT%d_%d" % (i, d) for i in range(4)]
                    mend, dec, ddg, pstk, psd = mendd[d], decd[d], ddgd[d], pstkd[d], psdd[d]
                    mt_, dt_, ddt, pkt, pdt = "mend%d" % d, "dec%d" % d, "ddg%d" % d, "pstk%d" % d, "psd%d" % d
                    for typ, dst, dtok in ((2 * d, Ti, g0), (2 * d + 1, Tf, g1)):
                        for ci, (t0, n) in enumerate(TOKCH):
                            ps = psgd[d][ci % 2]
                            pt = "psg%d_%d" % (d, ci % 2)
                            for k in range(8):
                                self.mm(ps[:, 0:n], wg[:, k, typ * 4:(typ + 1) * 4], hT[:, k, t0:t0 + n], k == 0, k == 7,
                                        ["wg", ("hT", ci)], [pt])
                            if d == 0:
                                o_ap = dst[:, t0:t0 + n]
                            elif t0 == 0:
                                o_ap = dst[:, 255::-1]
                            else:
                                hi = 2559 - t0
                                o_ap = dst[:, hi:hi - n:-1]
                            self.ts(o_ap, ps[:, 0:n], gb[:, typ:typ + 1], None, ALU.add, None, [pt, "gb"], [dtok])
                            yield
                    self.act(Tf[:], Tf[:], AF.Sigmoid, [g1], [g1]); yield
                    self.act(Tf[:], Tf[:], AF.Ln, [g1], [g1]); yield
                    S.add("dve", lambda e: e.tensor_tensor_scan(out=Tg[:], data0=onesb, data1=Tf[:], initial=0.0,
                                                               op0=ALU.mult, op1=ALU.add), [g1, "ones4"], [g2]); yield
                    self.tt(Ti[:], Ti[:], Tg[:], ALU.subtract, [g0, g2], [g0]); yield
                    S.add("dve", lambda e: e.tensor_tensor_scan(out=Tm[:], data0=onesb, data1=Ti[:], initial=0.0,
                                                               op0=ALU.mult, op1=ALU.max), [g0, "ones4"], [g3]); yield
                    self.cp(mend[:], Tm[:, 127::128], [g3], [mt_])
                    self.ts(dec[:, 0:1], mend[:, 0:1], -1.0, None, ALU.mult, None, [mt_], [dt_])
                    self.tt(dec[:, 1:NT], mend[:, 0:NT - 1], mend[:, 1:NT], ALU.subtract, [mt_], [dt_])
                    self.act(dec[:], dec[:], AF.Exp, [dt_], [dt_]); yield
                    self.tt(Tf[:], Tg[:], Tm[:], ALU.add, [g2, g3], [g1]); yield
                    self.act(Tf[:], Tf[:], AF.Exp, [g1], [g1], scale=-1.0); yield
                    mb = mend[:].unsqueeze(2).to_broadcast([4, NT, 128])
                    self.tt(Tg[:].rearrange("p (c j) -> p c j", j=128), Ti[:].rearrange("p (c j) -> p c j", j=128), mb, ALU.subtract,
                            [g0, mt_], [g2]); yield
                    self.act(Tg[:], Tg[:], AF.Exp, [g2], [g2]); yield
                    self.tt(Ti[:].rearrange("p (c j) -> p c j", j=128), mb, Tm[:].rearrange("p (c j) -> p c j", j=128), ALU.subtract,
                            [g3, mt_], [g0]); yield
                    self.act(Ti[:], Ti[:], AF.Exp, [g0], [g0]); yield
                    U, Rr, FL = Tg, Ti, Tf
                    ut, rt, ft = g2, g0, g1
                    if d == 1:
                        def rev(dst, src, st_, dt2):
                            self.cp(dst[:, 0:256], src[:, 255::-1], [st_], [dt2])
                            self.cp(dst[:, 256:LT], src[:, LT - 1:255:-1], [st_], [dt2])
                        rev(Tm, U, g2, g3); yield
                        rev(Tg, Rr, g0, g2); yield
                        rev(Ti, FL, g1, g0); yield
                        U, Rr, FL = Tm, Tg, Ti
                        ut, rt, ft = g3, g2, g0
                    for mc in range(NT):
                        for qi, (src, stok) in enumerate(((U, ut), (Rr, rt), (FL, ft))):
                            self.mm(pstk[:, mc, qi * 4:(qi + 1) * 4], src[:, mc * 128:(mc + 1) * 128], id4, True, True,
                                    [stok, "identF"], [pkt])
                        if mc % 6 == 5:
                            yield
                    self.cp(tokS[d][:], pstk[:], [pkt], [("tokS", d)])
                    self.tt(ddg[:], dec[:].unsqueeze(2).to_broadcast([4, NT, 4]), id4.unsqueeze(1).to_broadcast([4, NT, 4]), ALU.mult,
                            [dt_, "identF"], [ddt])
                    self.mm(psd[:], self.onesF[0:4, :], ddg[:].rearrange("p c h -> p (c h)"), True, True, [ddt, "onesF"], [pdt])
                    self.cp(decB[d][:].rearrange("p c h -> p (c h)"), psd[:], [pdt], [("decB", d)])
                    yield

                gens = [gate_dir(0), gate_dir(1)]
                while gens:
                    for g in list(gens):
                        try:
                            next(g)
                        except StopIteration:
                            gens.remove(g)
                S.barrier()
            for h in range(4):
                with ExitStack() as st3:
                    wb = wbH[h % 2]
                    wbt = "wbD%d" % (h % 2)
                    xr = self.sb("xr", [128, LT], F32, st3)
                    ac = self.sb("acD", [128, LT], F32, st3)
                    qh = self.sb("qh", [128, LT], BF16, st3)
                    kh = self.sb("kh", [128, LT], BF16, st3)
                    ktok = self.sb("ktok", [128, NT, 128], BF16, st3)
                    vh = self.sb("vh", [128, NT, 129], BF16, st3)
                    hacc = self.sb("hacc", [128, NT, 128], F32, st3)
                    hnum = [self.sb("hnum", [128, NT, 129], F32, st3) for _ in range(2)]
                    ep = self.sb("epD", [128, 2, NT], F32, st3)
                    hbt = self.sb("hbt", [128, NT, 128], BF16, st3)
                    Cstd = [self.sb("Cst", [128, 129], F32, st3) for _ in range(2)]
                    Cbfd = [self.sb("Cbf", [128, 129], BF16, st3) for _ in range(2)]
                    smd = [self.sb("smD", [128, 4], F32, st3) for _ in range(2)]
                    PT = [self.sb("PT", [128, 128], BF16, st3) for _ in range(2)]
                    Vs = [self.sb("Vs", [128, 129], BF16, st3) for _ in range(2)]
                    yst = [self.sb("ystD", [128, 512], F32, st3) for _ in range(2)]
                    psA = [self.ps("psDA", [128, 512], F32, st3) for _ in range(2)]
                    psT = self.ps("psDT", [128, 512], BF16, st3)
                    psS = [self.ps("psDS", [128, 128], F32, st3) for _ in range(2)]
                    psO = [self.ps("psDO", [128, 129], F32, st3) for _ in range(2)]
                    psC = self.ps("psDC", [128, 129], F32, st3)
                    if h + 1 < 4:
                        load_head(h + 1)
                    for j, (dst, dtok, f) in enumerate(((qh, "qh", h), (kh, "kh", 4 + h))):
                        for ci, (t0, n) in enumerate(TOKCH):
                            ps = psA[ci % 2]
                            pt = "psDA%d" % (ci % 2)
                            for k in range(8):
                                self.mm(ps[:, 0:n], wb[:, k, j * 128:(j + 1) * 128], hT[:, k, t0:t0 + n], k == 0, k == 7,
                                        [wbt, ("hT", ci)], [pt])
                            self.cp(xr[:, t0:t0 + n], ps[:, 0:n], [pt], ["xr"], eng="act")
                        self.ts(ac[:], xr[:], cw[:, f, 1:2], cb[:, f:f + 1], ALU.mult, ALU.add, ["xr", "cw", "cb"], ["acD"])
                        for (a0, a1) in ((0, 256), (256, LT)):
                            self.stt(ac[:, a0 + 1:a1], xr[:, a0:a1 - 1], cw[:, f, 0:1], ac[:, a0 + 1:a1], ALU.mult, ALU.add,
                                     ["xr", "cw", "acD"], ["acD"])
                            self.stt(ac[:, a0:a1 - 1], xr[:, a0 + 1:a1], cw[:, f, 2:3], ac[:, a0:a1 - 1], ALU.mult, ALU.add,
                                     ["xr", "cw", "acD"], ["acD"])
                        if j == 0:
                            self.act(dst[:], ac[:], AF.Silu, ["acD"], [dtok])
                        else:
                            self.act(ac[:], ac[:], AF.Silu, ["acD"], ["acD"])
                            self.ts(dst[:], ac[:], 128.0 ** -0.5, None, ALU.mult, None, ["acD"], [dtok])
                    for g0 in range(0, NT, 4):
                        nn = min(4, NT - g0)
                        for j in range(nn):
                            i = g0 + j
                            self.tr(psT[:, j * 128:(j + 1) * 128], kh[:, i * 128:(i + 1) * 128], self.identB[:], ["kh", "identB"], ["psDT"])
                        self.cp(ktok[:, g0:g0 + nn, :], psT[:, 0:nn * 128].rearrange("p (a b) -> p a b", b=128), ["psDT"], ["ktok"], eng="act")
                    self.memset(vh[:, :, 128:129], 1.0, ["vh1"])
                    for g0 in range(0, NT, 4):
                        nn = min(4, NT - g0)
                        ps = psA[(g0 // 4) % 2]
                        pt = "psDA%d" % ((g0 // 4) % 2)
                        for j in range(nn):
                            i = g0 + j
                            for k in range(8):
                                self.mm(ps[:, j * 128:(j + 1) * 128], hT[:, k, i * 128:(i + 1) * 128], wb[:, k, 256:384], k == 0, k == 7,
                                        [wbt, hts(i)], [pt])
                        self.cp(vh[:, g0:g0 + nn, 0:128], ps[:, 0:nn * 128].rearrange("p (a b) -> p a b", b=128), [pt], ["vh"], eng="act")
                    for d in range(2):
                        self.memset(Cstd[d][:], 0.0, ["Cst%d" % d])

                    def mstep(d, c):
                        mc = order[d][c]
                        mask = self.triL if d == 0 else self.triU
                        Cst, Cbf, PT_, Vs_, sm = Cstd[d], Cbfd[d], PT[d], Vs[d], smd[d]
                        pS, pO = psS[d], psO[d]
                        cst, cbf, ptt, vst, pst_, pot = "Cst%d" % d, "Cbf%d" % d, "PT%d" % d, "Vs%d" % d, "psDS%d" % d, "psDO%d" % d
                        tsl = slice(mc * 128, (mc + 1) * 128)
                        self.mm(pS[:], kh[:, tsl], qh[:, tsl], True, True, ["kh", "qh"], [pst_])
                        self.tt(PT_[:], pS[:], mask[:], ALU.mult, [pst_, "triL", "triU"], [ptt])
                        self.act(Vs_[:], vh[:, mc, :], AF.Identity, ["vh", "vh1", ("tokS", d)], [vst], scale=tokS[d][:, mc, h:h + 1])
                        self.ts(Cst[:], Cst[:], decB[d][:, c, h:h + 1], None, ALU.mult, None, [cst, ("decB", d)], [cst])
                        self.cp(Cbf[:], Cst[:], [cst], [cbf])
                        self.mm(pO[:], PT_[:], Vs_[:], True, False, [ptt, vst], [pot])
                        self.mm(pO[:], qh[:, tsl], Cbf[:], False, True, ["qh", cbf], [pot])
                        self.mm(psC[:], ktok[:, mc, :], Vs_[:], True, True, ["ktok", vst], ["psDC"])
                        self.tt(Cst[:], Cst[:], psC[:], ALU.add, [cst, "psDC"], [cst])
                        self.cp(hnum[d][:, mc, :], pO[:], [pot], [("hnum", d, mc)], eng="act")

                    for c in range(NT):
                        for d in range(2):
                            mstep(d, c)
                    sm = smd[0]
                    for d in range(2):
                        hall = [("hnum", d, i) for i in range(NT)]
                        r = tokS[d][:, :, 4 + h]
                        fl = tokS[d][:, :, 8 + h]
                        e0, e1 = ep[:, 0, :], ep[:, 1, :]
                        self.tt(e0, hnum[d][:, :, 128], r, ALU.mult, hall + [("tokS", d)], ["ep0"])
                        self.ts(e1, e0, -1.0, None, ALU.mult, None, ["ep0"], ["ep1"])
                        self.tt(e0, e0, e1, ALU.max, ["ep0", "ep1"], ["ep0"])
                        self.tt(e0, e0, fl, ALU.max, ["ep0", ("tokS", d)], ["ep0"])
                        S.add("dve", lambda e, e0=e0: e.reciprocal(out=e0, in_=e0), ["ep0"], ["ep0"])
                        self.tt(e0, e0, r, ALU.mult, ["ep0", ("tokS", d)], ["ep0"])
                        fb = e0.unsqueeze(2).to_broadcast([128, NT, 128])
                        if d == 0:
                            self.tt(hacc[:], hnum[0][:, :, 0:128], fb, ALU.mult, hall + ["ep0"], [("hacc", i) for i in range(NT)])
                        else:
                            self.tt(hnum[1][:, :, 0:128], hnum[1][:, :, 0:128], fb, ALU.mult, hall + ["ep0"], hall)
                            self.tt(hacc[:], hacc[:], hnum[1][:, :, 0:128], ALU.add, hall + [("hacc", i) for i in range(NT)],
                                    [("hacc", i) for i in range(NT)], eng="pool")
                    hall = [("hacc", i) for i in range(NT)]
                    sq3 = hnum[0][:, :, 0:128]
                    so3 = hnum[1][:, :, 0:128]
                    for i in range(NT):
                        ps = psA[i % 2]
                        pt = "psDA%d" % (i % 2)
                        for k in range(8):
                            self.mm(ps[:, 0:128], hT[:, k, i * 128:(i + 1) * 128], wb[:, k, 384:512], k == 0, k == 7, [wbt, hts(i)], [pt])
                        self.act(so3[:, i, :], ps[:, 0:128], AF.Sigmoid, [pt], [("so3", i)] + [("hnum", 1, j) for j in range(NT)])
                    self.tt(sq3, hacc[:], hacc[:], ALU.mult, hall, ["sq3"] + [("hnum", 0, j) for j in range(NT)])
                    S.add("dve", lambda e, sq3=sq3, ep=ep: e.tensor_reduce(out=ep[:, 0, :], in_=sq3, axis=AX.X, op=ALU.add), ["sq3"], ["ep0"])
                    self.rsqrt(ep[:, 0, :], ep[:, 0, :], 1.0 / 128, EPS, ["ep0"], ["ep0"])
                    self.tt(hacc[:], hacc[:], ep[:, 0, :].unsqueeze(2).to_broadcast([128, NT, 128]), ALU.mult, hall + ["ep0"], hall)
                    self.tt(hacc[:], hacc[:], gml[:, h * 128:(h + 1) * 128].unsqueeze(1).to_broadcast([128, NT, 128]), ALU.mult,
                            hall + ["gml"], hall, eng="pool")
                    self.tt(hbt[:], hacc[:], so3, ALU.mult, hall + [("so3", i) for i in range(NT)], ["hbt"])
                    for g0 in range(0, NT, 4):
                        nn = min(4, NT - g0)
                        ys = yst[(g0 // 4) % 2]
                        yt = "ystD%d" % ((g0 // 4) % 2)
                        for j in range(nn):
                            i = g0 + j
                            self.tr(psT[:, j * 128:(j + 1) * 128], hbt[:, i, :], self.identB[:], ["hbt", "identB"], ["psDT"])
                        self.cp(ys[:, 0:nn * 128], psT[:, 0:nn * 128], ["psDT"], [yt], eng="act")
                        self.dma(ybr_b[h * 128:(h + 1) * 128, g0 * 128:(g0 + nn) * 128], ys[:, 0:nn * 128], [yt], [("ybr", s, 1)])
                    S.barrier()
            S.barrier()


    def cmul(self, ore, oim, are, aim, bre, bim, ts4, rtoks, wtok, tk="cm", pool_one=False):
        t1, t2, t3, t4 = ts4
        k = [tk + "_t%d" % i for i in range(4)]
        self.tt(t2, aim, bim, ALU.mult, rtoks, [k[1]], eng="pool" if pool_one else "dve")
        self.tt(t1, are, bre, ALU.mult, rtoks, [k[0]])
        self.tt(t3, are, bim, ALU.mult, rtoks, [k[2]])
        self.tt(t4, aim, bre, ALU.mult, rtoks, [k[3]])
        self.tt(oim, t3, t4, ALU.add, [k[2], k[3]], [wtok + "_im"])
        self.tt(ore, t1, t2, ALU.subtract, [k[0], k[1]], [wtok + "_re"])

    def stage_s5(self, W, cin, ybr_a, s, li):
        S = self.S
        hT = self.hT
        order = [list(range(NT)), [1, 0] + list(range(NT - 1, 1, -1))]
        if self.dbg.get("s5_stop") == "none":
            return
        with ExitStack() as st:
            gel = self.sb("gel", [128, 4, LT], BF16, st)
            wsu = self.sb("wsu", [128, 8, 512], BF16, st)
            dsk = self.sb("dsk", [128, 4], F32, st)
            with ExitStack() as st0:
                stage = self.sb("stgC", [128, 8, 512], F32, st0)
                self.load_w("wsu", W["w_in"], O_SU, 512, 8, wsu, "wsu", stage, "stgC")
                S.barrier()
            self.dma(dsk[:], W["ssm_d"].rearrange("(c p) -> p c", p=128), [], ["dsk"], slow=True)
            for c in range(self.dbg.get("s5_nc", 4)):
                with ExitStack() as st2:
                    suT = self.sb("suT", [128, LT], BF16, st2)
                    yacc = self.sb("yacc", [128, LT], F32, st2)
                    sc = self.sb("s5sc", [128, 16, 4], F32, st2)
                    N = self.sb("s5N", [128, 2, 4, 128], F32, st2)
                    Nr = self.sb("s5Nr", [128, 2, 4, 128], F32, st2)
                    braw = self.sb("s5braw", [128, 2, 4, 16], F32, st2)
                    bbar = self.sb("s5bbar", [128, 2, 4, 16], F32, st2)
                    bt = self.sb("s5bt", [128, 2, 4, 16], F32, st2)
                    Zp = self.sb("s5Zp", [128, 2, 4, 128], F32, st2)
                    Yp = self.sb("s5Yp", [128, 2, 4, 128], F32, st2)
                    Pd = [self.sb("s5P", [128, 2, 4, 128], F32, st2) for _ in range(2)]
                    Eitd = [self.sb("s5Eit", [128, 1024], F32, st2) for _ in range(2)]
                    Bbdd = [self.sb("s5Bbd", [128, 1024], BF16, st2) for _ in range(2)]
                    Cbdd = [self.sb("s5Cbd", [128, 2, 4, 128], BF16, st2) for _ in range(2)]
                    wA = [self.sb("s5wA", [128, 1024], BF16, st2) for _ in range(2)]
                    wB = [self.sb("s5wB", [128, 1024], BF16, st2) for _ in range(2)]
                    xA = [self.sb("s5xA", [128, 1024], BF16, st2) for _ in range(2)]
                    xB = [self.sb("s5xB", [128, 1024], BF16, st2) for _ in range(2)]
                    cu = [[self.sb("s5cu", [128, 8], F32, st2) for _ in range(2)] for _ in range(2)]
                    t13d = [self.sb("s5t13", [128, 1024], F32, st2) for _ in range(1)]
                    t24d = [self.sb("s5t24", [128, 1024], F32, st2) for _ in range(1)]
                    Psd = [self.sb("s5Ps", [128, 2, 4, 128], F32, st2) for _ in range(2)]
                    Eisd = [self.sb("s5Eis", [128, 1024], F32, st2) for _ in range(2)]
                    Zcd = [self.sb("s5Zc", [128, 1024], F32, st2) for _ in range(2)]
                    carry = [[self.sb("s5cy", [128, 8], F32, st2) for _ in range(2)] for _ in range(2)]
                    psA = [self.ps("psCA", [128, 512], F32, st2) for _ in range(2)]
                    psZd = [[self.ps("psCZ", [128, 512], F32, st2) for _ in range(2)] for _ in range(2)]
                    psYd = [self.ps("psCY", [128, 128], F32, st2) for _ in range(2)]
                    t1, t2 = t13d[0], t24d[0]
                    for ci, (t0, n) in enumerate(TOKCH):
                        ps = psA[ci % 2]
                        pt = "psCA%d" % (ci % 2)
                        for k in range(8):
                            self.mm(ps[:, 0:n], wsu[:, k, c * 128:(c + 1) * 128], hT[:, k, t0:t0 + n], k == 0, k == 7,
                                    ["wsu", ("hT", ci)], [pt])
                        self.cp(suT[:, t0:t0 + n], ps[:, 0:n], [pt], ["suT"], eng="act")
                        self.ts(yacc[:, t0:t0 + n], ps[:, 0:n], dsk[:, c:c + 1], None, ALU.mult, None, [pt, "dsk"],
                                [("yacc", j) for j in range(t0 // 128, (t0 + n) // 128)])
                    for d in range(2):
                        P, Eit, Bbd, Cbd = Pd[d], Eitd[d], Bbdd[d], Cbdd[d]
                        ptk, etk, btk, ctk = "s5P%d" % d, "s5Eit%d" % d, "s5Bbd%d" % d, "s5Cbd%d" % d
                        psT = psZd[d]
                        psTt = ["psCZ%d%d" % (d, 0), "psCZ%d%d" % (d, 1)]
                        tc = self.tabcache
                        if tc is not None and s == 1:
                            self.dma(P[:].rearrange("p a k j -> p (a k j)"), tc["P"][c, d], [("tabc", c, d)], [ptk])
                            self.dma(Eit[:], tc["E"][c, d], [("tabc", c, d)], [etk])
                            self.dma(Bbd[:], tc["B"][c, d], [("tabc", c, d)], [btk])
                            self.dma(Cbd[:].rearrange("p a k j -> p (a k j)"), tc["C"][c, d], [("tabc", c, d)], [ctk])
                            self.ts(Eisd[d][:, 0:512], Eit[:, 512:1024], -1.0, None, ALU.mult, None, [etk], ["s5Eis%d" % d])
                            self.cp(Eisd[d][:, 512:1024], Eit[:, 0:512], [etk], ["s5Eis%d" % d], eng="pool")
                            self.ts(Psd[d][:, 0], P[:, 1], -1.0, None, ALU.mult, None, [ptk], ["s5Ps%d" % d])
                            self.cp(Psd[d][:, 1], P[:, 0], [ptk], ["s5Ps%d" % d], eng="pool")
                            continue
                        def col(i):
                            return sc[:, i, :]
                        LRE, LIM, DT, LDR, LDI, MAG, C8, S8, ABR, ABI, DEN, CR, CI, TA, TB, TC = [col(i) for i in range(16)]
                        gs = slice(8 * c, 8 * c + 8)
                        self.dma(LRE, W["ssm_lam_re"][d, gs].rearrange("(k g) p -> (g p) k", g=2), [], ["sc"], slow=True)
                        self.dma(LIM, W["ssm_lam_im"][d, gs].rearrange("(k g) p -> (g p) k", g=2), [], ["sc"], slow=True)
                        for g2 in range(2):
                            src = W["ssm_log_step"][d, gs].rearrange("(o k g) -> o g k", o=1, g=2)[:, g2, :]
                            self.dma(sc[g2 * 64:(g2 + 1) * 64, 2, :], src.to_broadcast([64, 4]), [], ["sc"], slow=True)
                        T_ = ["sc"]
                        self.ts(LRE, LRE, -1e-4, None, ALU.min, None, T_, T_)
                        self.act(DT, DT, AF.Exp, T_, T_)
                        self.tt(LDR, LRE, DT, ALU.mult, T_, T_)
                        self.tt(LDI, LIM, DT, ALU.mult, T_, T_)
                        self.act(MAG, LDR, AF.Exp, T_, T_)
                        self.act(S8, LDI, AF.Sin, T_, T_, scale=1.0 / 16)
                        self.act(TA, LDI, AF.Sin, T_, T_, scale=1.0 / 32)
                        self.tt(TA, TA, TA, ALU.mult, T_, T_)
                        self.ts(C8, TA, -2.0, 1.0, ALU.mult, ALU.add, T_, T_)
                        for _ in range(4):
                            self.tt(TA, C8, C8, ALU.mult, T_, T_)
                            self.tt(TB, S8, S8, ALU.mult, T_, T_)
                            self.tt(TC, C8, S8, ALU.mult, T_, T_)
                            self.tt(C8, TA, TB, ALU.subtract, T_, T_)
                            self.ts(S8, TC, 2.0, None, ALU.mult, None, T_, T_)
                        self.tt(ABR, MAG, C8, ALU.mult, T_, T_)
                        self.tt(ABI, MAG, S8, ALU.mult, T_, T_)
                        self.tt(TA, LRE, LRE, ALU.mult, T_, T_)
                        self.tt(TB, LIM, LIM, ALU.mult, T_, T_)
                        self.tt(DEN, TA, TB, ALU.add, T_, T_)
                        S.add("dve", lambda e, DEN=DEN: e.reciprocal(out=DEN, in_=DEN), T_, T_)
                        self.ts(TC, ABR, -1.0, None, ALU.add, None, T_, T_)
                        self.tt(TA, TC, LRE, ALU.mult, T_, T_)
                        self.tt(TB, ABI, LIM, ALU.mult, T_, T_)
                        self.tt(CR, TA, TB, ALU.add, T_, T_)
                        self.tt(CR, CR, DEN, ALU.mult, T_, T_)
                        self.tt(TA, ABI, LRE, ALU.mult, T_, T_)
                        self.tt(TB, TC, LIM, ALU.mult, T_, T_)
                        self.tt(CI, TA, TB, ALU.subtract, T_, T_)
                        self.tt(CI, CI, DEN, ALU.mult, T_, T_)
                        self.tt(TA, MAG, MAG, ALU.mult, T_, T_)
                        S.add("dve", lambda e, TA=TA: e.reciprocal(out=TA, in_=TA), T_, T_)
                        self.tt(LDR, ABR, TA, ALU.mult, T_, T_)
                        self.tt(LDI, ABI, TA, ALU.mult, T_, T_)
                        self.ts(LDI, LDI, -1.0, None, ALU.mult, None, T_, T_)
                        for (Tb, ar, ai, tk) in ((P, ABR, ABI, ptk), (N, LDR, LDI, "s5N")):
                            for kq in range(4):
                                self.cp(Tb[:, 0, kq, 0:1], ar[:, kq:kq + 1], T_, [tk])
                                self.cp(Tb[:, 1, kq, 0:1], ai[:, kq:kq + 1], T_, [tk])
                            L = 1
                            tv1 = t1[:, 0:512].rearrange("p (k j) -> p k j", j=128)
                            tv2 = t2[:, 0:512].rearrange("p (k j) -> p k j", j=128)
                            while L < 128:
                                mr = Tb[:, 0, :, L - 1:L].to_broadcast([128, 4, L])
                                mi = Tb[:, 1, :, L - 1:L].to_broadcast([128, 4, L])
                                sr = Tb[:, 0, :, 0:L]
                                si = Tb[:, 1, :, 0:L]
                                dr = Tb[:, 0, :, L:2 * L]
                                di = Tb[:, 1, :, L:2 * L]
                                self.tt(tv1[:, :, 0:L], si, mi, ALU.mult, [tk], ["s5t1"])
                                self.tt(tv2[:, :, 0:L], sr, mr, ALU.mult, [tk], ["s5t2"])
                                self.tt(dr, tv2[:, :, 0:L], tv1[:, :, 0:L], ALU.subtract, ["s5t1", "s5t2"], [tk])
                                self.tt(tv1[:, :, 0:L], si, mr, ALU.mult, [tk], ["s5t1"])
                                self.tt(tv2[:, :, 0:L], sr, mi, ALU.mult, [tk], ["s5t2"])
                                self.tt(di, tv2[:, :, 0:L], tv1[:, :, 0:L], ALU.add, ["s5t1", "s5t2"], [tk])
                                L *= 2
                        if d == 0:
                            Nsrc, ntk = N, "s5N"
                        else:
                            self.cp(Nr[:].rearrange("p a k j -> p (a k) j"), N[:].rearrange("p a k j -> p (a k) j")[:, :, ::-1], ["s5N"], ["s5Nr"])
                            Nsrc, ntk = Nr, "s5Nr"
                        for part in range(2):
                            for kq in range(4):
                                self.tr(psT[part][:, kq * 128:(kq + 1) * 128], Nsrc[:, part, kq, :], self.identF[:], [ntk, "identF"], [psTt[part]])
                            self.cp(Eit[:, part * 512:(part + 1) * 512], psT[part][:], [psTt[part]], [etk], eng="act")
                        self.ts(Eisd[d][:, 0:512], Eit[:, 512:1024], -1.0, None, ALU.mult, None, [etk], ["s5Eis%d" % d])
                        self.cp(Eisd[d][:, 512:1024], Eit[:, 0:512], [etk], ["s5Eis%d" % d], eng="pool")
                        self.ts(Psd[d][:, 0], P[:, 1], -1.0, None, ALU.mult, None, [ptk], ["s5Ps%d" % d])
                        self.cp(Psd[d][:, 1], P[:, 0], [ptk], ["s5Ps%d" % d], eng="pool")
                        self.dma(braw[:, 0], W["ssm_b_re"][d, gs].rearrange("(k g) p m -> (g p) k m", g=2), [], ["s5braw"], slow=True)
                        self.dma(braw[:, 1], W["ssm_b_im"][d, gs].rearrange("(k g) p m -> (g p) k m", g=2), [], ["s5braw"], slow=True)
                        crb = CR.unsqueeze(2).to_broadcast([128, 4, 16])
                        cib = CI.unsqueeze(2).to_broadcast([128, 4, 16])
                        self.tt(bbar[:, 0], braw[:, 0], crb, ALU.mult, ["s5braw", "sc"], ["s5bbar"])
                        self.tt(bt[:, 0], braw[:, 1], cib, ALU.mult, ["s5braw", "sc"], ["s5bt"])
                        self.tt(bbar[:, 0], bbar[:, 0], bt[:, 0], ALU.subtract, ["s5bbar", "s5bt"], ["s5bbar"])
                        self.tt(bbar[:, 1], braw[:, 1], crb, ALU.mult, ["s5braw", "sc"], ["s5bbar"])
                        self.tt(bt[:, 1], braw[:, 0], cib, ALU.mult, ["s5braw", "sc"], ["s5bt"])
                        self.tt(bbar[:, 1], bbar[:, 1], bt[:, 1], ALU.add, ["s5bbar", "s5bt"], ["s5bbar"])
                        self.memset(Zp[:], 0.0, ["s5Zp"])
                        self.memset(Yp[:], 0.0, ["s5Yp"], eng="pool")
                        for part in range(2):
                            for kq in range(4):
                                for g2 in range(2):
                                    rs_ = slice(g2 * 64, (g2 + 1) * 64)
                                    c0 = 32 * kq + 16 * g2
                                    self.cp(Zp[rs_, part, kq, c0:c0 + 16], bbar[rs_, part, kq, :], ["s5bbar"], ["s5Zp"])
                        for part in range(2):
                            for kq in range(4):
                                self.tr(psT[part][:, kq * 128:(kq + 1) * 128], Zp[:, part, kq, :], self.identF[:], ["s5Zp", "identF"], [psTt[part]])
                            self.cp(Bbd[:, part * 512:(part + 1) * 512], psT[part][:], [psTt[part]], [btk], eng="act")
                        for part, nm in enumerate(("ssm_c_re", "ssm_c_im")):
                            for kq in range(4):
                                for g2 in range(2):
                                    g = 8 * c + 2 * kq + g2
                                    r0 = 32 * kq + 16 * g2
                                    self.dma(Yp[r0:r0 + 16, part, kq, 64 * g2:64 * g2 + 64], W[nm][d, g], [], ["s5Yp"])
                        for part in range(2):
                            for kq in range(4):
                                self.tr(psT[part][:, kq * 128:(kq + 1) * 128], Yp[:, part, kq, :], self.identF[:], ["s5Yp", "identF"], [psTt[part]])
                            if part == 0:
                                self.cp(Cbd[:, 0].rearrange("p k j -> p (k j)"), psT[0][:], [psTt[0]], [ctk], eng="act")
                            else:
                                self.ts(Cbd[:, 1].rearrange("p k j -> p (k j)"), psT[1][:], -1.0, None, ALU.mult, None, [psTt[1]], [ctk])
                    if self.tabcache is not None and s == 0:
                        tc = self.tabcache
                        for d in range(2):
                            self.dma(tc["P"][c, d], Pd[d][:].rearrange("p a k j -> p (a k j)"), ["s5P%d" % d], [("tabc", c, d)])
                            self.dma(tc["E"][c, d], Eitd[d][:], ["s5Eit%d" % d], [("tabc", c, d)])
                            self.dma(tc["B"][c, d], Bbdd[d][:], ["s5Bbd%d" % d], [("tabc", c, d)])
                            self.dma(tc["C"][c, d], Cbdd[d][:].rearrange("p a k j -> p (a k j)"), ["s5Cbd%d" % d], [("tabc", c, d)])
                    def half1(d, ci_):
                        mc = order[d][ci_]
                        tsl = slice(mc * 128, (mc + 1) * 128)
                        Ei, Eis, Bbd = Eitd[d], Eisd[d], Bbdd[d]
                        tri = self.triL if d == 0 else self.triU
                        for part in range(2):
                            self.mm(psA[part][:], suT[:, tsl], Bbd[:, part * 512:(part + 1) * 512], True, True, ["suT", "s5Bbd%d" % d], ["psCA%d" % part])
                        v2 = lambda a: a.rearrange("p (a b) -> p a b", a=2)
                        ta, tb = wA[d], wB[d]
                        ka, kb = "s5wA%d" % d, "s5wB%d" % d
                        self.tt(v2(ta[:]), psA[0][:].unsqueeze(1).to_broadcast([128, 2, 512]), v2(Ei[:]), ALU.mult, ["psCA0", "s5Eit%d" % d], [ka])
                        self.tt(v2(tb[:]), psA[1][:].unsqueeze(1).to_broadcast([128, 2, 512]), v2(Eis[:]), ALU.mult, ["psCA1", "s5Eis%d" % d], [kb])
                        for part in range(2):
                            for kq in range(4):
                                cs = slice(part * 512 + kq * 128, part * 512 + (kq + 1) * 128)
                                self.mm(psZd[d][part][:, kq * 128:(kq + 1) * 128], ta[:, cs], tri[:], True, False, [ka, "triL", "triU"], ["psCZ%d%d" % (d, part)])
                                self.mm(psZd[d][part][:, kq * 128:(kq + 1) * 128], tb[:, cs], tri[:], False, True, [kb, "triL", "triU"], ["psCZ%d%d" % (d, part)])

                    v4 = lambda a: a.rearrange("p (a k j) -> p a k j", a=2, k=4)

                    def half2a(d, ci_):
                        P, Ps, Zc = Pd[d], Psd[d], Zcd[d]
                        cp_ = carry[d][(ci_ + 1) % 2]
                        cpt = "s5cy%d%d" % (d, (ci_ + 1) % 2)
                        cn_ = carry[d][ci_ % 2]
                        cnt_ = "s5cy%d%d" % (d, ci_ % 2)
                        jc = 127 if d == 0 else 0
                        zct = "s5Zc%d" % d
                        for part in range(2):
                            for kq in range(4):
                                o0 = part * 512 + kq * 128
                                self.act(Zc[:, o0:o0 + 128], psZd[d][part][:, kq * 128:(kq + 1) * 128], AF.Identity,
                                         ["psCZ%d%d" % (d, part), cpt], [zct], bias=cp_[:, part * 4 + kq:part * 4 + kq + 1])
                        zrc = Zc[:, 0:512].rearrange("p (k j) -> p k j", j=128)[:, :, jc].unsqueeze(1).to_broadcast([128, 2, 4])
                        zic = Zc[:, 512:1024].rearrange("p (k j) -> p k j", j=128)[:, :, jc].unsqueeze(1).to_broadcast([128, 2, 4])
                        u1, u2 = cu[d][0], cu[d][1]
                        c3 = lambda a: a.rearrange("p (a k) -> p a k", a=2)
                        self.tt(c3(u1[:]), zrc, P[:, :, :, 127], ALU.mult, [zct, "s5P%d" % d], ["s5u1%d" % d])
                        self.tt(c3(u2[:]), zic, Ps[:, :, :, 127], ALU.mult, [zct, "s5Ps%d" % d], ["s5u2%d" % d])
                        self.tt(cn_[:], u1[:], u2[:], ALU.add, ["s5u1%d" % d, "s5u2%d" % d], [cnt_])
                        if d == 0:
                            pa, pb = P[:], Ps[:]
                        else:
                            pa, pb = P[:, :, :, ::-1], Ps[:, :, :, ::-1]
                        zr = Zc[:, 0:512].rearrange("p (k j) -> p k j", j=128).unsqueeze(1).to_broadcast([128, 2, 4, 128])
                        zi = Zc[:, 512:1024].rearrange("p (k j) -> p k j", j=128).unsqueeze(1).to_broadcast([128, 2, 4, 128])
                        self.tt(v4(xB[d][:]), zi, pb, ALU.mult, [zct, "s5Ps%d" % d], ["s5xB%d" % d], eng="pool")
                        self.tt(v4(xA[d][:]), zr, pa, ALU.mult, [zct, "s5P%d" % d], ["s5xA%d" % d])

                    def half2b(d, ci_):
                        Cbd = Cbdd[d]
                        n8 = 0
                        for (xt, xk) in ((xA[d], "s5xA%d" % d), (xB[d], "s5xB%d" % d)):
                            for part in range(2):
                                for kq in range(4):
                                    o0 = part * 512 + kq * 128
                                    self.mm(psYd[d][:], Cbd[:, part, kq, :], xt[:, o0:o0 + 128], n8 == 0, n8 == 15, ["s5Cbd%d" % d, xk], ["psCY%d" % d])
                                    n8 += 1

                    def yadd(d, ci_):
                        mc = order[d][ci_]
                        tsl = slice(mc * 128, (mc + 1) * 128)
                        self.tt(yacc[:, tsl], yacc[:, tsl], psYd[d][:], ALU.add, [("yacc", mc), "psCY%d" % d], [("yacc", mc)])

                    for d in range(2):
                        self.memset(carry[d][1][:], 0.0, ["s5cy%d1" % d])
                    half1(0, 0)
                    half1(1, 0)
                    pend = []
                    for ci_ in range(NT):
                        for d in range(2):
                            half2a(d, ci_)
                            if ci_ + 1 < NT:
                                half1(d, ci_ + 1)
                            while pend:
                                yadd(*pend.pop(0))
                            half2b(d, ci_)
                            pend.append((d, ci_))
                    while pend:
                        yadd(*pend.pop(0))
                    Zc = Zcd[0]
                    yall = [("yacc", j) for j in range(NT)]
                    for a0 in range(0, LT, 1024):
                        a1 = min(LT, a0 + 1024)
                        w_ = a1 - a0
                        self.tt(Zc[:, 0:w_], yacc[:, a0:a1], yacc[:, a0:a1], ALU.mult, yall, ["s5Zc0"])
                        self.ts(Zc[:, 0:w_], Zc[:, 0:w_], 0.044715, 1.0, ALU.mult, ALU.add, ["s5Zc0"], ["s5Zc0"])
                        self.tt(Zc[:, 0:w_], Zc[:, 0:w_], yacc[:, a0:a1], ALU.mult, ["s5Zc0"] + yall, ["s5Zc0"])
                        self.act(Zc[:, 0:w_], Zc[:, 0:w_], AF.Sigmoid, ["s5Zc0"], ["s5Zc0"], scale=1.5957691216057308)
                        self.tt(gel[:, c, a0:a1], Zc[:, 0:w_], yacc[:, a0:a1], ALU.mult, ["s5Zc0"] + yall, [("gel", c)])
                    S.barrier()
            if self.dbg.get("s5_noglu"):
                S.barrier()
                return
            with ExitStack() as st4:
                stage = self.sb("stgC", [128, 8, 512], F32, st4)
                wgl = [self.sb("wglu", [128, 4, 512], BF16, st4) for _ in range(2)]
                gbias = self.sb("gbias", [128, 8], F32, st4)
                sg = [self.sb("sgC", [128, 512], F32, st4) for _ in range(2)]
                yst = [self.sb("ystC", [128, 512], F32, st4) for _ in range(2)]
                psa = [self.ps("psGa", [128, 512], F32, st4) for _ in range(2)]
                psg = [self.ps("psGg", [128, 512], F32, st4) for _ in range(2)]
                self.dma(gbias[:], W["ssm_glu_b"].rearrange("(j p) -> p j", p=128), [], ["gbias"], slow=True)
                for half in range(2):
                    self.load_w("wglu", W["ssm_glu_w"], half * 512, 512, 4, wgl[half], "wglu%d" % half, stage, "stgC")
                cnt = 0
                for j in range(4):
                    for ci, (t0, n) in enumerate(TOKCH):
                        i2 = cnt % 2
                        cnt += 1
                        for k in range(4):
                            self.mm(psa[i2][:, 0:n], wgl[0][:, k, j * 128:(j + 1) * 128], gel[:, k, t0:t0 + n], k == 0, k == 3,
                                    ["wglu0", ("gel", k)], ["psGa%d" % i2])
                        for k in range(4):
                            self.mm(psg[i2][:, 0:n], wgl[1][:, k, j * 128:(j + 1) * 128], gel[:, k, t0:t0 + n], k == 0, k == 3,
                                    ["wglu1", ("gel", k)], ["psGg%d" % i2])
                        self.act(sg[i2][:, 0:n], psg[i2][:, 0:n], AF.Sigmoid, ["psGg%d" % i2, "gbias"], ["sgC%d" % i2], bias=gbias[:, 4 + j:5 + j])
                        self.stt(yst[i2][:, 0:n], psa[i2][:, 0:n], gbias[:, j:j + 1], sg[i2][:, 0:n], ALU.add, ALU.mult,
                                 ["psGa%d" % i2, "gbias", "sgC%d" % i2], ["ystC%d" % i2])
                        self.dma(ybr_a[j * 128:(j + 1) * 128, t0:t0 + n], yst[i2][:, 0:n], ["ystC%d" % i2], [("ybr", s, 0)])
                S.barrier()
            S.barrier()

CONST_SHAPES = {"ident": [128, 128], "ones": [128, 128], "tril": [128, 128], "triu": [128, 128],
                "ropeC": [128, 16, 64], "ropeS": [128, 16, 64], "lamc": [DEPTH, 128, 2]}
LAYER_W = [("norm_g", [D]), ("ada_w", [D, 3 * D]), ("ada_b", [3 * D]), ("w_in", [D, D_IN]),
           ("w_ssm_out", [512, D]), ("w_ml_out", [512, D]), ("w_da_out", [512, D]), ("w_out", [D, D]),
           ("da_qnorm_g", [64]), ("da_knorm_g", [64]), ("da_lambda", [4, 64]), ("da_subln_g", [128]),
           ("ml_conv_w", [3, 1024]), ("ml_conv_b", [1024]), ("ml_gate_b", [4, 4]), ("ml_norm_g", [512]),
           ("ssm_lam_re", [2, 32, 64]), ("ssm_lam_im", [2, 32, 64]), ("ssm_log_step", [2, 32]),
           ("ssm_b_re", [2, 32, 64, 16]), ("ssm_b_im", [2, 32, 64, 16]), ("ssm_c_re", [2, 32, 16, 64]),
           ("ssm_c_im", [2, 32, 16, 64]), ("ssm_d", [512]), ("ssm_glu_w", [512, 1024]), ("ssm_glu_b", [1024])]


def host_consts(li=0):
    i = np.arange(128)
    c = {
        "ident": np.eye(128, dtype=np.float32),
        "ones": np.ones((128, 128), np.float32),
        "tril": (i[:, None] <= i[None, :]).astype(np.float32),
        "triu": (i[:, None] >= i[None, :]).astype(np.float32),
    }
    t = np.arange(LL)
    row = (t // 64).astype(np.float32)
    col = (t % 64).astype(np.float32)
    half = 32
    inv = np.power(np.float32(10000.0), -np.arange(0, half, 2, dtype=np.float32) / np.float32(half)).astype(np.float32)
    ar = (row[:, None] * inv).astype(np.float32)
    ac = (col[:, None] * inv).astype(np.float32)
    cr, sr, cc, sc = np.cos(ar), np.sin(ar), np.cos(ac), np.sin(ac)
    C64 = np.concatenate([cr, cr, cc, cc], axis=1).astype(np.float32)
    S64 = np.concatenate([-sr, sr, -sc, sc], axis=1).astype(np.float32)
    c["ropeC"] = np.ascontiguousarray(C64.reshape(16, 128, 64).transpose(1, 0, 2))
    c["ropeS"] = np.ascontiguousarray(S64.reshape(16, 128, 64).transpose(1, 0, 2))
    lam = [0.8 - 0.6 * math.exp(-0.3 * l) for l in range(DEPTH)]
    c["lamc"] = np.stack([np.tile(np.array([[-v, 1.0 - v]], np.float32), (128, 1)) for v in lam])
    return c


def build_layer_program(li=0, branches=("a", "b", "c"), dump_ybr=False, do_merge=True, dbg=None):
    nc = bass.Bass("TRN2", target_bir_lowering=False)
    S = Sched()
    W = {}
    for nm, shp in LAYER_W:
        W[nm] = nc.dram_tensor(nm, shp, F32, kind="ExternalInput").ap()
    W["cvec"] = nc.dram_tensor("cvec", [3, D], F32, kind="ExternalInput").ap()
    cin = {nm: nc.dram_tensor(nm, shp, F32, kind="ExternalInput").ap() for nm, shp in CONST_SHAPES.items()}
    xT = nc.dram_tensor("xT", [NSEQ, D, LT], F32, kind="ExternalInput").ap()
    xo = nc.dram_tensor("xo", [NSEQ, D, LT], F32, kind="ExternalOutput").ap()
    ybr_in = None
    if len(branches) < 3:
        ybr_in = nc.dram_tensor("ybr", [NSEQ, 3, 512, LT], F32, kind="ExternalInput").ap()
    ybr_dev = nc.dram_tensor("ybr_dev", [NSEQ, 3, 512, LT], F32, kind="ExternalOutput" if dump_ybr else "Internal").ap()
    with ExitStack() as stack:
        B = LayerBuilder(nc, S, stack, dbg=dbg)
        if (dbg or {}).get("tabcache"):
            B.tabcache = {"P": nc.dram_tensor("tabP", [4, 2, 128, 1024], F32).ap(), "E": nc.dram_tensor("tabE", [4, 2, 128, 1024], F32).ap(),
                          "B": nc.dram_tensor("tabB", [4, 2, 128, 1024], BF16).ap(), "C": nc.dram_tensor("tabC", [4, 2, 128, 1024], BF16).ap()}
        B.setup_consts(cin)
        B.hT = B.sb("hT", [128, 8, LT], BF16)
        B.stage_mod(W)
        for s in range((dbg or {}).get("nseq", NSEQ)):
            B.stage_norm(xT[s], s)
            srcs = []
            for bi, b in enumerate("abc"):
                srcs.append(ybr_dev[s, bi] if b in branches else ybr_in[s, bi])
            if "a" in branches:
                B.stage_s5(W, cin, ybr_dev[s, 0], s, li)
            if "b" in branches:
                B.stage_mlstm(W, cin, ybr_dev[s, 1], s, li)
            if "c" in branches:
                B.stage_attn(W, cin, ybr_dev[s, 2], s, li)
            if do_merge:
                B.stage_merge(W, xT[s], xo[s], srcs, s)
        S.emit(nc, stack)
    return nc, S


_PROG = {}


def build_program(n_layers=DEPTH, layer0=0):
    nc = bass.Bass("TRN2", target_bir_lowering=False)
    S = Sched()
    Wall = {}
    for nm, shp in LAYER_W:
        Wall[nm] = nc.dram_tensor(nm, [DEPTH] + list(shp), F32, kind="ExternalInput").ap()
    cvec = nc.dram_tensor("cvec", [3, D], F32, kind="ExternalInput").ap()
    cin = {nm: nc.dram_tensor(nm, shp, F32, kind="ExternalInput").ap() for nm, shp in CONST_SHAPES.items()}
    xT = nc.dram_tensor("xT", [NSEQ, D, LT], F32, kind="ExternalInput").ap()
    xo = nc.dram_tensor("xo", [NSEQ, D, LT], F32, kind="ExternalOutput").ap()
    xs = [nc.dram_tensor("xs%d" % i, [NSEQ, D, LT], F32).ap() for i in range(2)]
    ybr_dev = nc.dram_tensor("ybr_dev", [NSEQ, 3, 512, LT], F32).ap()
    tabc = {"P": nc.dram_tensor("tabP", [4, 2, 128, 1024], F32).ap(), "E": nc.dram_tensor("tabE", [4, 2, 128, 1024], F32).ap(),
            "B": nc.dram_tensor("tabB", [4, 2, 128, 1024], BF16).ap(), "C": nc.dram_tensor("tabC", [4, 2, 128, 1024], BF16).ap()}
    with ExitStack() as stack:
        B = LayerBuilder(nc, S, stack)
        B.tabcache = tabc
        B.setup_consts(cin)
        B.hT = B.sb("hT", [128, 8, LT], BF16)
        for j in range(n_layers):
            li = layer0 + j
            S.epoch = j
            W = {nm: Wall[nm][li] for nm, _ in LAYER_W}
            W["cvec"] = cvec
            x_in, tin = (xT, "xT") if j == 0 else (xs[(j - 1) % 2], "xs%d" % ((j - 1) % 2))
            x_out, tout = (xo, "xo") if j == n_layers - 1 else (xs[j % 2], "xs%d" % (j % 2))
            B.stage_mod(W)
            for s in range(NSEQ):
                B.stage_norm(x_in[s], s, tin)
                B.stage_s5(W, cin, ybr_dev[s, 0], s, li)
                B.stage_mlstm(W, cin, ybr_dev[s, 1], s, li)
                B.stage_attn(W, cin, ybr_dev[s, 2], s, li)
                B.stage_merge(W, x_in[s], x_out[s], [ybr_dev[s, b] for b in range(3)], s, tin, tout)
        S.emit(nc, stack)
    return nc, S


def _fm(ctx, lat):
    return np.ascontiguousarray(np.concatenate([ctx, lat], axis=1).transpose(0, 2, 1)).astype(np.float32)


def kernel(**inputs):
    x = np.asarray(inputs["x"], np.float32)
    c = np.asarray(inputs["c"], np.float32)
    ctx = np.asarray(inputs["ctx"], np.float32)
    c_ctx = np.asarray(inputs["c_ctx"], np.float32)
    ncores = 8
    if "p" not in _PROG:
        _PROG["p"] = build_program()
    nc, _ = _PROG["p"]
    consts = host_consts()
    wl = {nm: np.ascontiguousarray(np.asarray(inputs[nm], np.float32)) for nm, _ in LAYER_W}
    in_maps = []
    for i in range(ncores):
        m = {"xT": _fm(ctx[2 * i:2 * i + 2], x[2 * i:2 * i + 2]),
             "cvec": np.stack([c[2 * i], c[2 * i + 1], c_ctx]).astype(np.float32)}
        m.update(wl)
        m.update(consts)
        in_maps.append(m)
    res = run_bass_kernel_spmd(nc, in_maps, core_ids=list(range(ncores)))
    out = np.concatenate([np.ascontiguousarray(np.asarray(r["xo"], np.float32)[:, :, LC:].transpose(0, 2, 1))
                          for r in res.results], axis=0)
    return out.astype(np.float32)
```
